# Optimizing a Trainium2 kernel written in Bass

```python
import jax
import jax.numpy as jnp
from jax import lax
import numpy as np

D_MODEL = 1024
BATCH = 8
SEQ = 2048
DEPTH = 2

GRID_W = 64
CTX_LEN = 256
N_EVEN = (DEPTH + 1) // 2
N_ODD = DEPTH // 2
HEAD_DIM = 64
BLK = 128
A_HEADS = 4
A_DIM = 128
A_WIDTH = A_HEADS * A_DIM
A_CHUNK = 64
B_HEADS = 8
B_KV = 2
B_WIN = 128
C_HEADS = 8
C_KV = 2
D_HEADS = 8
NA_ROWS = 8
NA_COLS = 16
MIX_WIDTH = A_WIDTH + B_HEADS * HEAD_DIM
D_FF = 2816
ROPE_THETA = 10000.0
EPS = 1e-6
NEG = -1e30
EVEN_SPLITS = (A_WIDTH, A_WIDTH, A_WIDTH, A_WIDTH, 4 * A_HEADS, B_HEADS * HEAD_DIM, B_KV * HEAD_DIM, B_KV * HEAD_DIM)
ODD_SPLITS = (C_HEADS * HEAD_DIM, C_KV * HEAD_DIM, C_KV * HEAD_DIM, D_HEADS * HEAD_DIM, D_HEADS * HEAD_DIM, D_HEADS * HEAD_DIM)
EVEN_COLS = sum(EVEN_SPLITS)
ODD_COLS = sum(ODD_SPLITS)

kernel_name = 'hybrid_mlstm_swa_gqa_natten_prefix'

f32 = jnp.float32


def split_cols(u, sizes):
    return jnp.split(u, np.cumsum(sizes)[:-1].tolist(), axis=-1)


def heads(a, n):
    return a.reshape(a.shape[:-1] + (n, a.shape[-1] // n))


def group(q, n_kv):
    return q.reshape(q.shape[:2] + (n_kv, q.shape[2] // n_kv, q.shape[3]))


def rms_norm(x, g):
    xf = x.astype(f32)
    y = xf * lax.rsqrt(jnp.mean(xf * xf, axis=-1, keepdims=True) + EPS)
    return (y * g.astype(f32)).astype(x.dtype)


def modulate(x, g, shift, scale):
    return rms_norm(x, g) * (1 + scale) + shift


def axial_rope(n):
    t = jnp.arange(n)
    row = (t // GRID_W).astype(f32)
    col = (t % GRID_W).astype(f32)
    half = HEAD_DIM // 2
    freq = ROPE_THETA ** (-jnp.arange(0, half, 2, dtype=f32) / half)
    ang_r = row[:, None] * freq[None, :]
    ang_c = col[:, None] * freq[None, :]
    ang = jnp.concatenate([ang_r, ang_r, ang_c, ang_c], axis=-1)
    return jnp.cos(ang), jnp.sin(ang)


def apply_rope(x, cos, sin):
    xr = x.reshape(x.shape[:-1] + (2, 2, HEAD_DIM // 4))
    rot = jnp.stack([-xr[..., 1, :], xr[..., 0, :]], axis=-2).reshape(x.shape)
    return x * cos[:, None, :].astype(x.dtype) + rot * sin[:, None, :].astype(x.dtype)


def qk_prep(a, g, rope, scale=1.0):
    a = rms_norm(a, g)
    if rope is not None:
        a = apply_rope(a, rope[0], rope[1])
    return a * scale


def joint_softmax(parts):
    sizes = [p.shape[-1] for p in parts]
    logits = jnp.concatenate([p.astype(f32) for p in parts], axis=-1)
    probs = jax.nn.softmax(logits, axis=-1)
    return jnp.split(probs, np.cumsum(sizes)[:-1].tolist(), axis=-1)


def sink_column(sink, scores):
    hkv, g = scores.shape[-4], scores.shape[-3]
    return jnp.broadcast_to(sink.astype(f32).reshape(hkv, g, 1, 1), scores.shape[:-1] + (1,))


def context_attention(q, k, v, sink=None):
    s = jnp.einsum('bqhgd,bkhd->bhgqk', q, k).astype(f32)
    if sink is None:
        p = jax.nn.softmax(s, axis=-1)
    else:
        _, p = joint_softmax([sink_column(sink, s), s])
    o = jnp.einsum('bhgqk,bkhd->bqhgd', p.astype(v.dtype), v)
    return o.reshape(o.shape[:2] + (-1,))


def window_attention(q, k, v, k_ctx, v_ctx, sink):
    B, S, Hkv, G, d = q.shape
    nb = S // BLK
    qb = q.reshape(B, nb, BLK, Hkv, G, d)

    def band(a):
        ap = jnp.pad(a, ((0, 0), (BLK, BLK), (0, 0), (0, 0))).reshape(B, nb + 2, BLK, Hkv, d)
        return jnp.concatenate([ap[:, :-2], ap[:, 1:-1], ap[:, 2:]], axis=2)

    kb, vb = band(k), band(v)
    t_pos = jnp.arange(S).reshape(nb, BLK)
    s_pos = jnp.arange(nb)[:, None] * BLK - BLK + jnp.arange(3 * BLK)[None, :]
    sp = s_pos[:, None, :]
    valid = (sp >= 0) & (sp < S) & (jnp.abs(t_pos[:, :, None] - sp) <= B_WIN)
    s_band = jnp.einsum('bnqhgd,bnkhd->bnhgqk', qb, kb).astype(f32)
    s_band = jnp.where(valid[:, None, None, :, :], s_band, NEG)
    s_ctx = jnp.einsum('bnqhgd,bchd->bnhgqc', qb, k_ctx)
    _, p_ctx, p_band = joint_softmax([sink_column(sink, s_band), s_ctx, s_band])
    o = (jnp.einsum('bnhgqc,bchd->bnqhgd', p_ctx.astype(v.dtype), v_ctx)
         + jnp.einsum('bnhgqk,bnkhd->bnqhgd', p_band.astype(v.dtype), vb))
    return o.reshape(B, S, Hkv * G * d)


def global_attention(q, k, v, k_ctx, v_ctx):
    B, S, Hkv, G, d = q.shape
    nb = S // BLK
    kk = jnp.concatenate([k_ctx, k], axis=1)
    vv = jnp.concatenate([v_ctx, v], axis=1)

    def one_block(qi):
        s = jnp.einsum('bqhgd,bkhd->bhgqk', qi, kk).astype(f32)
        p = jax.nn.softmax(s, axis=-1).astype(vv.dtype)
        return jnp.einsum('bhgqk,bkhd->bqhgd', p, vv)

    qb = jnp.moveaxis(q.reshape(B, nb, BLK, Hkv, G, d), 1, 0)
    o = lax.map(one_block, qb)
    return jnp.moveaxis(o, 0, 1).reshape(B, S, Hkv * G * d)


def neighbourhood_attention(q, k, v, k_ctx, v_ctx, rpb):
    B, S, H, d = q.shape
    rows = S // GRID_W
    wr = min(NA_ROWS, rows)
    r = jnp.arange(rows)
    row_idx = jnp.clip(r - wr // 2, 0, rows - wr)[:, None] + jnp.arange(wr)[None, :]
    col = jnp.arange(GRID_W)
    col_start = jnp.clip(col - NA_COLS // 2, 0, GRID_W - NA_COLS)
    col_ok = (col[None, :] >= col_start[:, None]) & (col[None, :] < col_start[:, None] + NA_COLS)

    def gather_rows(a):
        return a.reshape(B, rows, GRID_W, H, d)[:, row_idx].reshape(B, rows, wr * GRID_W, H, d)

    kg, vg = gather_rows(k), gather_rows(v)
    dr = row_idx - r[:, None] + NA_ROWS - 1
    dc = jnp.clip(col[None, :] - col[:, None] + NA_COLS - 1, 0, 2 * NA_COLS - 2)
    bias = rpb[:, dr[:, None, :, None], dc[None, :, None, :]].astype(f32)
    bias = jnp.where(col_ok[None, None, :, None, :], bias, NEG).reshape(H, rows, GRID_W, wr * GRID_W)
    qg = q.reshape(B, rows, GRID_W, H, d)
    s_nb = jnp.einsum('brqhd,brkhd->bhrqk', qg, kg).astype(f32) + bias
    s_ctx = jnp.einsum('brqhd,bchd->bhrqc', qg, k_ctx)
    p_ctx, p_nb = joint_softmax([s_ctx, s_nb])
    o = (jnp.einsum('bhrqc,bchd->brqhd', p_ctx.astype(v.dtype), v_ctx)
         + jnp.einsum('bhrqk,brkhd->brqhd', p_nb.astype(v.dtype), vg))
    return o.reshape(B, S, H * d)


def mlstm_scan(q, k, v, log_i, log_f, state):
    B, H, T, dh = q.shape
    nc = T // A_CHUNK

    def to_chunks(a):
        return jnp.moveaxis(a.reshape(a.shape[:2] + (nc, A_CHUNK) + a.shape[3:]), 2, 0)

    xs = (to_chunks(q), to_chunks(k), to_chunks(v), to_chunks(log_i), to_chunks(log_f))
    tri = jnp.tril(jnp.ones((A_CHUNK, A_CHUNK), dtype=bool))

    def step(carry, inp):
        C, n, m = carry
        qc, kc, vc, li, lf = inp
        b = jnp.cumsum(lf, axis=-1)
        dmat = jnp.where(tri, b[..., :, None] - b[..., None, :] + li[..., None, :], -jnp.inf)
        inter = b + m[..., None]
        m_t = jnp.maximum(inter, jnp.max(dmat, axis=-1))
        w = jnp.exp(dmat - m_t[..., None])
        a = jnp.exp(inter - m_t)
        s = jnp.einsum('bhtd,bhsd->bhts', qc, kc) * w
        num = a[..., None] * jnp.einsum('bhtd,bhde->bhte', qc, C) + jnp.einsum('bhts,bhse->bhte', s, vc)
        den = a * jnp.einsum('bhtd,bhd->bht', qc, n) + jnp.sum(s, axis=-1)
        h = num / jnp.maximum(jnp.abs(den), jnp.exp(-m_t))[..., None]
        b_last = b[..., -1]
        g = b_last[..., None] - b + li
        m_new = jnp.maximum(b_last + m, jnp.max(g, axis=-1))
        decay = jnp.exp(b_last + m - m_new)
        wk = jnp.exp(g - m_new[..., None])[..., None] * kc
        C_new = decay[..., None, None] * C + jnp.einsum('bhsd,bhse->bhde', wk, vc)
        n_new = decay[..., None] * n + jnp.sum(wk, axis=2)
        return (C_new, n_new, m_new), h

    state, hs = lax.scan(step, state, xs)
    return jnp.moveaxis(hs, 0, 2).reshape(B, H, T, dh), state


def mlstm_prep(q, k, v, gates, gate_b):
    to_h = lambda a: jnp.swapaxes(heads(a.astype(f32), A_HEADS), 1, 2)
    g = (gates.astype(f32) + gate_b.astype(f32)).reshape(gates.shape[:2] + (4, A_HEADS))
    g = jnp.transpose(g, (2, 0, 3, 1))
    return (to_h(q), to_h(k) * A_DIM ** -0.5, to_h(v),
            g[0], jax.nn.log_sigmoid(g[1]), g[2], jax.nn.log_sigmoid(g[3]))


def mlstm_bidirectional(ctx_s, lat_s):
    qc, kc, vc, icf, fcf, icb, fcb = ctx_s
    ql, kl, vl, ilf, flf, ilb, flb = lat_s
    b = qc.shape[0]
    zero = (jnp.zeros((b, A_HEADS, A_DIM, A_DIM), f32), jnp.zeros((b, A_HEADS, A_DIM), f32), jnp.zeros((b, A_HEADS), f32))
    flip = lambda a: jnp.flip(a, axis=2)
    hcf, st_f = mlstm_scan(qc, kc, vc, icf, fcf, zero)
    hlf, _ = mlstm_scan(ql, kl, vl, ilf, flf, st_f)
    hcb, st_b = mlstm_scan(flip(qc), flip(kc), flip(vc), flip(icb), flip(fcb), zero)
    hlb, _ = mlstm_scan(flip(ql), flip(kl), flip(vl), flip(ilb), flip(flb), st_b)
    return hcf + flip(hcb), hlf + flip(hlb)


def mlstm_out(h, o, head_g, dtype):
    h = rms_norm(jnp.swapaxes(h, 1, 2), head_g.reshape(A_HEADS, A_DIM))
    return (jax.nn.sigmoid(o.astype(f32)) * h.reshape(h.shape[:2] + (A_WIDTH,))).astype(dtype)


def even_mixer(hc, hl, w_in, gate_b, head_g, qk_g, sink, rope, need_ctx):
    aq_l, ak_l, av_l, ao_l, ag_l, bq_l, bk_l, bv_l = split_cols(hl @ w_in, EVEN_SPLITS)
    aq_c, ak_c, av_c, ao_c, ag_c, bq_c, bk_c, bv_c = split_cols(hc @ w_in, EVEN_SPLITS)
    h_ctx, h_lat = mlstm_bidirectional(mlstm_prep(aq_c, ak_c, av_c, ag_c, gate_b),
                                       mlstm_prep(aq_l, ak_l, av_l, ag_l, gate_b))
    scale = HEAD_DIM ** -0.5
    k_c = qk_prep(heads(bk_c, B_KV), qk_g[1], None)
    v_c = heads(bv_c, B_KV)
    q_l = qk_prep(heads(bq_l, B_HEADS), qk_g[0], rope, scale)
    k_l = qk_prep(heads(bk_l, B_KV), qk_g[1], rope)
    b_lat = window_attention(group(q_l, B_KV), k_l, heads(bv_l, B_KV), k_c, v_c, sink)
    lat = jnp.concatenate([mlstm_out(h_lat, ao_l, head_g, hl.dtype), b_lat], axis=-1)
    if not need_ctx:
        return None, lat
    q_c = qk_prep(heads(bq_c, B_HEADS), qk_g[0], None, scale)
    b_ctx = context_attention(group(q_c, B_KV), k_c, v_c, sink)
    ctx_out = jnp.concatenate([mlstm_out(h_ctx, ao_c, head_g, hc.dtype), b_ctx], axis=-1)
    return ctx_out, lat


def odd_mixer(hc, hl, w_in, gqa_g, na_g, rpb, rope, need_ctx):
    cq_l, ck_l, cv_l, nq_l, nk_l, nv_l = split_cols(hl @ w_in, ODD_SPLITS)
    cq_c, ck_c, cv_c, nq_c, nk_c, nv_c = split_cols(hc @ w_in, ODD_SPLITS)
    scale = HEAD_DIM ** -0.5
    gk_c = qk_prep(heads(ck_c, C_KV), gqa_g[1], None)
    gv_c = heads(cv_c, C_KV)
    dk_c = qk_prep(heads(nk_c, D_HEADS), na_g[1], None)
    dv_c = heads(nv_c, D_HEADS)
    gq_l = qk_prep(heads(cq_l, C_HEADS), gqa_g[0], rope, scale)
    gk_l = qk_prep(heads(ck_l, C_KV), gqa_g[1], rope)
    c_lat = global_attention(group(gq_l, C_KV), gk_l, heads(cv_l, C_KV), gk_c, gv_c)
    dq_l = qk_prep(heads(nq_l, D_HEADS), na_g[0], None, scale)
    dk_l = qk_prep(heads(nk_l, D_HEADS), na_g[1], None)
    d_lat = neighbourhood_attention(dq_l, dk_l, heads(nv_l, D_HEADS), dk_c, dv_c, rpb)
    lat = jnp.concatenate([c_lat, d_lat], axis=-1)
    if not need_ctx:
        return None, lat
    gq_c = qk_prep(heads(cq_c, C_HEADS), gqa_g[0], None, scale)
    dq_c = qk_prep(heads(nq_c, D_HEADS), na_g[0], None, scale)
    ctx_out = jnp.concatenate([context_attention(group(gq_c, C_KV), gk_c, gv_c),
                               context_attention(dq_c[:, :, :, None, :], dk_c, dv_c)], axis=-1)
    return ctx_out, lat


def conv_ffn(h, w_up, conv_w, conv_b, w_down):
    u = h @ w_up
    u = lax.conv_general_dilated(u, conv_w[:, None, :].astype(u.dtype), window_strides=(1,), padding=((1, 1),),
                                 dimension_numbers=('NWC', 'WIO', 'NWC'), feature_group_count=u.shape[-1]) + conv_b
    gate, val = jnp.split(u, 2, axis=-1)
    return (jax.nn.silu(gate) * val) @ w_down


def setup_inputs(seed: int = 0) -> dict:
    key = jax.random.key(seed)
    ks = jax.random.split(key, 24)
    nrm = lambda k, shape, s: jax.random.normal(k, shape, f32) * s
    D = D_MODEL
    gk = jax.random.split(ks[13], 4)
    mlstm_gate_b = jnp.concatenate([
        nrm(gk[0], (N_EVEN, A_HEADS), 0.1),
        3.0 + 3.0 * jax.random.uniform(gk[1], (N_EVEN, A_HEADS), f32),
        nrm(gk[2], (N_EVEN, A_HEADS), 0.1),
        3.0 + 3.0 * jax.random.uniform(gk[3], (N_EVEN, A_HEADS), f32)], axis=-1)
    return {
        'x': nrm(ks[0], (BATCH, SEQ, D), 1.0),
        'c': nrm(ks[1], (BATCH, D), 1.0),
        'ctx': nrm(ks[2], (BATCH, CTX_LEN, D), 1.0),
        'c_ctx': nrm(ks[3], (D,), 1.0),
        'ada_w': nrm(ks[4], (DEPTH, D, 6 * D), 0.3 * D ** -0.5),
        'ada_b': nrm(ks[5], (DEPTH, 6 * D), 0.02),
        'norm_g': 1.0 + nrm(ks[6], (DEPTH, 2, D), 0.02),
        'w_out': nrm(ks[7], (DEPTH, MIX_WIDTH, D), MIX_WIDTH ** -0.5),
        'ffn_up': nrm(ks[8], (DEPTH, D, 2 * D_FF), D ** -0.5),
        'ffn_conv_w': nrm(ks[9], (DEPTH, 3, 2 * D_FF), 3 ** -0.5),
        'ffn_conv_b': nrm(ks[10], (DEPTH, 2 * D_FF), 0.02),
        'ffn_down': nrm(ks[11], (DEPTH, D_FF, D), D_FF ** -0.5),
        'even_w_in': nrm(ks[12], (N_EVEN, D, EVEN_COLS), D ** -0.5),
        'mlstm_gate_b': mlstm_gate_b,
        'mlstm_head_g': 1.0 + nrm(ks[14], (N_EVEN, A_WIDTH), 0.02),
        'swa_qk_g': 1.0 + nrm(ks[15], (N_EVEN, 2, HEAD_DIM), 0.02),
        'swa_sink': nrm(ks[16], (N_EVEN, B_HEADS), 0.5),
        'odd_w_in': nrm(ks[17], (N_ODD, D, ODD_COLS), D ** -0.5),
        'gqa_qk_g': 1.0 + nrm(ks[18], (N_ODD, 2, HEAD_DIM), 0.02),
        'na_qk_g': 1.0 + nrm(ks[19], (N_ODD, 2, HEAD_DIM), 0.02),
        'na_rpb': nrm(ks[20], (N_ODD, D_HEADS, 2 * NA_ROWS - 1, 2 * NA_COLS - 1), 0.5),
    }


def reference(x, c, ctx, c_ctx, ada_w, ada_b, norm_g, w_out, ffn_up, ffn_conv_w, ffn_conv_b, ffn_down,
              even_w_in, mlstm_gate_b, mlstm_head_g, swa_qk_g, swa_sink,
              odd_w_in, gqa_qk_g, na_qk_g, na_rpb):
    rope = axial_rope(x.shape[1])
    x_lat, x_ctx = x, ctx
    for l in range(DEPTH):
        need_ctx = l < DEPTH - 1
        mod_l = jnp.split((jax.nn.silu(c) @ ada_w[l] + ada_b[l])[:, None, :], 6, axis=-1)
        mod_c = jnp.split((jax.nn.silu(c_ctx) @ ada_w[l] + ada_b[l])[None, None, :], 6, axis=-1)
        hl = modulate(x_lat, norm_g[l, 0], mod_l[0], mod_l[1])
        hc = modulate(x_ctx, norm_g[l, 0], mod_c[0], mod_c[1])
        if l % 2 == 0:
            e = l // 2
            mix_c, mix_l = even_mixer(hc, hl, even_w_in[e], mlstm_gate_b[e], mlstm_head_g[e],
                                      swa_qk_g[e], swa_sink[e], rope, need_ctx)
        else:
            o = l // 2
            mix_c, mix_l = odd_mixer(hc, hl, odd_w_in[o], gqa_qk_g[o], na_qk_g[o], na_rpb[o], rope, need_ctx)
        x_lat = x_lat + mod_l[2] * (mix_l @ w_out[l])
        x_lat = x_lat + mod_l[5] * conv_ffn(modulate(x_lat, norm_g[l, 1], mod_l[3], mod_l[4]),
                                            ffn_up[l], ffn_conv_w[l], ffn_conv_b[l], ffn_down[l])
        if need_ctx:
            x_ctx = x_ctx + mod_c[2] * (mix_c @ w_out[l])
            x_ctx = x_ctx + mod_c[5] * conv_ffn(modulate(x_ctx, norm_g[l, 1], mod_c[3], mod_c[4]),
                                                ffn_up[l], ffn_conv_w[l], ffn_conv_b[l], ffn_down[l])
    return x_lat
```

```python
import numpy as np
from contextlib import ExitStack
import concourse.bass as bass
import concourse.mybir as mybir
from concourse.bass_utils import run_bass_kernel_spmd

F32 = mybir.dt.float32
BF16 = mybir.dt.bfloat16
AF = mybir.ActivationFunctionType
ALU = mybir.AluOpType
AX = mybir.AxisListType

ENGS = ("pe", "act", "dve", "pool", "sp")
NT = 18
EPS = 1e-6
NEGM = -30000.0


class Buf:
    __slots__ = ("t", "name", "w", "r", "sem", "psum")

    def __init__(self, t, name):
        self.t = t
        self.name = name
        self.w = None
        self.r = {}
        self.sem = None
        self.psum = False

    def __getitem__(self, idx):
        return self.t[idx]


class Ring:
    def __init__(self, bufs):
        self.bufs = bufs
        self.i = 0

    def get(self):
        b = self.bufs[self.i % len(self.bufs)]
        self.i += 1
        return b


SEM_LIMIT = 1500


class Sched:
    def __init__(self, nc, stack):
        self.nc = nc
        self.stack = stack
        self.eng = {"pe": nc.tensor, "act": nc.scalar, "dve": nc.vector,
                    "pool": nc.gpsimd, "sp": nc.sync}
        self.sems = {}
        self.cnt = {}
        self.epoch = {}
        self.cur = {}
        for e in ENGS:
            self.epoch[e] = 0
            self._new_epoch(e)
        self.seen = {e: {} for e in ENGS}
        self.ninst = 0
        self.nwait = 0
        self.nsem = 0
        self.nalloc = 0

    def _new_epoch(self, e):
        self.epoch[e] += 1
        key = f"{e}#{self.epoch[e]}"
        self.sems[key] = self.stack.enter_context(self.nc.semaphore("s_" + key.replace("#", "_")))
        self.cnt[key] = 0
        self.cur[e] = key

    def sb(self, name, shape, dt, stack=None):
        self.nalloc += 1
        name = f"{name}_{self.nalloc}"
        t = (stack or self.stack).enter_context(self.nc.sbuf_tensor(name, list(shape), dt))
        return Buf(t, name)

    def ps(self, name, shape, dt=F32):
        t = self.stack.enter_context(self.nc.psum_tensor(name, list(shape), dt))
        b = Buf(t, name)
        b.psum = True
        return b

    def newsem(self, name=None):
        self.nsem += 1
        name = name or f"d{self.nsem}"
        s = self.stack.enter_context(self.nc.semaphore(name))
        self.sems[name] = s
        self.cnt[name] = 0
        return name

    def sbd(self, name, shape, dt, stack=None):
        b = self.sb(name, shape, dt, stack)
        b.sem = self.newsem("d_" + name)
        return b

    @staticmethod
    def _eng_of(key):
        return key.split("#")[0] if "#" in key else None

    def _need(self, e, key, val):
        if self.seen[e].get(key, 0) >= val:
            return
        ke = self._eng_of(key)
        if ke is not None:
            ep = int(key.split("#")[1])
            for k2, v2 in self.seen[e].items():
                if v2 > 0 and self._eng_of(k2) == ke and int(k2.split("#")[1]) > ep:
                    return
        self.seen[e][key] = val
        self.eng[e].wait_ge(self.sems[key], val)
        self.nwait += 1

    def deps(self, e, reads, writes):
        for b in reads:
            if b.w is not None:
                k, v = b.w
                if not (self._eng_of(k) == e and e == "pe"):
                    self._need(e, k, v)
        for b in writes:
            if b.w is not None:
                k, v = b.w
                if self._eng_of(k) != e:
                    self._need(e, k, v)
            for k, v in b.r.items():
                if self._eng_of(k) != e:
                    self._need(e, k, v)

    def op(self, e, reads, writes, fn):
        pr = [b for b in reads if b.psum]
        if pr:
            reads = [b for b in reads if not b.psum]
            writes = list(writes) + [b for b in pr if b not in writes]
        self.deps(e, reads, writes)
        ins = fn(self.eng[e])
        if self.cnt[self.cur[e]] >= SEM_LIMIT:
            self._new_epoch(e)
        key = self.cur[e]
        self.cnt[key] += 1
        ins.then_inc(self.sems[key], 1)
        v = self.cnt[key]
        for b in reads:
            for k2 in [k2 for k2 in b.r if self._eng_of(k2) == e]:
                del b.r[k2]
            b.r[key] = v
        for b in writes:
            b.w = (key, v)
            b.r = {}
        self.ninst += 1
        return ins

    def dma(self, q, semkey, out_ap, in_ap, reads, writes, **kw):
        self.deps(q, reads, writes)
        ins = self.eng[q].dma_start(out=out_ap, in_=in_ap, **kw)
        self.cnt[semkey] += 16
        assert self.cnt[semkey] <= 2000, semkey
        ins.then_inc(self.sems[semkey], 16)
        v = self.cnt[semkey]
        for b in reads:
            b.r[semkey] = v
        for b in writes:
            b.w = (semkey, v)
            b.r = {}
        self.ninst += 1
        return ins

    def barrier(self):
        for e in ENGS:
            for k, v in list(self.cnt.items()):
                ke = self._eng_of(k)
                if ke == e or v == 0:
                    continue
                if ke is not None and k != self.cur[ke]:
                    if not (self.cnt[self.cur[ke]] == 0 and int(k.split("#")[1]) == self.epoch[ke] - 1):
                        continue
                self._need(e, k, v)


def _rope_tables():
    t = np.arange(2048)
    row = (t // 64).astype(np.float32)
    col = (t % 64).astype(np.float32)
    half = 32
    freq = (np.float32(10000.0) ** (-np.arange(0, half, 2, dtype=np.float32) / np.float32(half))).astype(np.float32)
    ang_r = row[:, None] * freq[None, :]
    ang_c = col[:, None] * freq[None, :]
    ang = np.concatenate([ang_r, ang_r, ang_c, ang_c], axis=-1).astype(np.float32)
    cos = np.cos(ang).astype(np.float32)
    sin = np.sin(ang).astype(np.float32)
    sgn = np.ones(64, np.float32)
    sgn[0:16] = -1.0
    sgn[32:48] = -1.0
    sinS = sin * sgn[None, :]
    cos = cos.reshape(16, 128, 64).transpose(1, 0, 2).copy()
    sinS = sinS.reshape(16, 128, 64).transpose(1, 0, 2).copy()
    return cos, sinS


def _na_tables(rpb):
    rows = 32
    wr = 8
    r = np.arange(rows)
    row_start = np.clip(r - wr // 2, 0, rows - wr)
    col = np.arange(64)
    col_start = np.clip(col - 8, 0, 48)
    col_ok = (col[None, :] >= col_start[:, None]) & (col[None, :] < col_start[:, None] + 16)
    dc = np.clip(col[None, :] - col[:, None] + 15, 0, 30)
    classes = [0, 1, 2, 14, 15]
    blocks = {}
    tab = np.full((8, 128, 25, 128), NEGM, np.float32)
    for ci, j in enumerate(classes):
        qrows = [2 * j, 2 * j + 1]
        lo = min(row_start[q] for q in qrows)
        hi = max(row_start[q] + wr - 1 for q in qrows)
        mlist = list(range(lo // 2, hi // 2 + 1))
        assert len(mlist) <= 5
        blocks[j] = mlist
        for si, m in enumerate(mlist):
            for kr in range(2):
                krow = 2 * m + kr
                for qr in range(2):
                    qrow = qrows[qr]
                    if not (row_start[qrow] <= krow < row_start[qrow] + wr):
                        continue
                    dr = krow - qrow + 7
                    sub = rpb[:, dr, :][:, dc]
                    sub = np.where(col_ok[None], sub, np.float32(NEGM))
                    tab[:, kr * 64:(kr + 1) * 64, ci * 5 + si, qr * 64:(qr + 1) * 64] = sub.transpose(0, 2, 1)
    return tab, blocks, classes


def _na_blocks():
    _, blocks, classes = _na_tables(np.zeros((8, 15, 31), np.float32))
    return blocks, classes


def host_prepare(inp):
    f = np.float32
    shared = {}
    shared["ada_w"] = np.ascontiguousarray(inp["ada_w"], f)
    shared["ada_bT"] = np.ascontiguousarray(inp["ada_b"].reshape(2, 48, 128).transpose(2, 0, 1), f)
    shared["norm_gT"] = np.ascontiguousarray(inp["norm_g"].reshape(2, 2, 8, 128).transpose(3, 0, 1, 2), f)
    shared["w_out"] = np.ascontiguousarray(inp["w_out"], f)
    shared["ffn_up"] = np.ascontiguousarray(inp["ffn_up"], f)
    shared["ffn_down"] = np.ascontiguousarray(inp["ffn_down"], f)
    shared["conv_wT"] = np.ascontiguousarray(inp["ffn_conv_w"].reshape(2, 3, 44, 128).transpose(3, 0, 1, 2), f)
    shared["conv_bT"] = np.ascontiguousarray(inp["ffn_conv_b"].reshape(2, 44, 128).transpose(2, 0, 1), f)
    shared["even_w"] = np.ascontiguousarray(inp["even_w_in"][0], f)
    shared["odd_w"] = np.ascontiguousarray(inp["odd_w_in"][0], f)
    bc = lambda a: np.ascontiguousarray(np.broadcast_to(np.asarray(a, f).reshape(1, -1), (128, a.size)))
    shared["gate_b_bc"] = bc(inp["mlstm_gate_b"][0])
    shared["head_g_bc"] = bc(inp["mlstm_head_g"][0])
    shared["swa_g_bc"] = bc(inp["swa_qk_g"][0])
    shared["sink_bc"] = bc(inp["swa_sink"][0])
    shared["gqa_g_bc"] = bc(inp["gqa_qk_g"][0])
    shared["na_g_bc"] = bc(inp["na_qk_g"][0])
    tab, _, _ = _na_tables(np.asarray(inp["na_rpb"][0], f))
    shared["na_bias"] = tab
    ident = np.eye(128, dtype=f)
    s = np.arange(128)
    triU = (s[:, None] <= s[None, :]).astype(f)
    triL = (s[:, None] >= s[None, :]).astype(f)
    wm = np.zeros((128, 2, 128), f)
    wm[:, 0, :] = np.where(s[None, :] <= s[:, None], 0.0, NEGM)
    wm[:, 1, :] = np.where(s[:, None] <= s[None, :], 0.0, NEGM)
    shared["consts"] = np.ascontiguousarray(np.concatenate([ident, triU, triL, wm.reshape(128, 256)], axis=1))
    cos, sinS = _rope_tables()
    shared["rope"] = np.ascontiguousarray(np.stack([cos, sinS], axis=1))
    percore = []
    for b in range(8):
        cc = np.stack([inp["c"][b].reshape(8, 128).T, inp["c_ctx"].reshape(8, 128).T], axis=-1)
        percore.append({"x": np.ascontiguousarray(inp["x"][b], f), "ctx": np.ascontiguousarray(inp["ctx"][b], f),
                        "cc": np.ascontiguousarray(cc, f)})
    return shared, percore


SHARED_SHAPES = {
    "ada_w": [2, 1024, 6144], "ada_bT": [128, 2, 48], "norm_gT": [128, 2, 2, 8], "w_out": [2, 1024, 1024],
    "ffn_up": [2, 1024, 5632], "ffn_down": [2, 2816, 1024], "conv_wT": [128, 2, 3, 44], "conv_bT": [128, 2, 44],
    "even_w": [1024, 2832], "odd_w": [1024, 2304], "gate_b_bc": [128, 16], "head_g_bc": [128, 512],
    "swa_g_bc": [128, 128], "sink_bc": [128, 8], "gqa_g_bc": [128, 128], "na_g_bc": [128, 128],
    "na_bias": [8, 128, 25, 128], "consts": [128, 640], "rope": [128, 2, 16, 64],
    "x": [2048, 1024], "ctx": [256, 1024], "cc": [128, 8, 2],
}


GROUPS = [(0, 0, 256), (1, 256, 512), (2, 768, 512), (3, 1280, 512), (4, 1792, 512)]


def tok_group(i):
    return (0, i * 128) if i < 2 else (1 + (i - 2) // 4, ((i - 2) % 4) * 128)


def build_program(stage="full"):
    nc = bass.Bass("TRN2", target_bir_lowering=False)
    D = {k: nc.dram_tensor(k, shp, F32, kind="ExternalInput").ap() for k, shp in SHARED_SHAPES.items()}
    out = nc.dram_tensor("out", [2048, 1024], F32, kind="ExternalOutput").ap()
    dbg = stage != "full"
    if dbg:
        octx = nc.dram_tensor("octx", [256, 1024], F32, kind="ExternalOutput").ap()
        dbgd = nc.dram_tensor("dbgd", [128, 8192], F32, kind="ExternalOutput").ap()
    na_blocks, na_classes = _na_blocks()

    with ExitStack() as st:
        S = Sched(nc, st)
        xs = [S.sbd(f"xs{i}", [128, 1024], F32) for i in range(NT)]
        cst = S.sbd("cst", [128, 640], F32)
        cc = S.sbd("cc", [128, 8, 2], F32)
        adab = S.sbd("adab", [128, 2, 48], F32)
        ngT = S.sbd("ngT", [128, 2, 2, 8], F32)
        cw = S.sbd("cw", [128, 2, 3, 44], F32)
        cb = S.sbd("cb", [128, 2, 44], F32)
        identb = S.sb("identb", [128, 128], BF16)
        wmb = S.sb("wmb", [128, 2, 128], BF16)
        ones_f = S.sb("ones_f", [128, 128], F32)
        ones_b = S.sb("ones_b", [128, 128], BF16)
        sc = S.sb("sc", [128, 8, 2], F32)
        modT = [S.sb(f"modT{l}", [128, 48, 2], F32) for l in range(2)]
        gbc = S.sb("gbc", [128, 2, 1024], F32)
        AB = S.sb("AB", [128, 8, 2], F32)

        psT = Ring([S.ps(f"psT{i}", [128, 8, 128], BF16) for i in range(2)])
        psA = Ring([S.ps(f"psA{i}", [128, 512], F32) for i in range(2)])
        psS = Ring([S.ps(f"psS{i}", [128, 512], F32) for i in range(2)])
        psO = Ring([S.ps(f"psO{i}", [128, 512], F32) for i in range(2)])

        IDF = lambda: cst[:, 0:128]
        TRIU = lambda: cst[:, 128:256]
        TRIL = lambda: cst[:, 256:384]

        S.dma("sp", cst.sem, cst[:], D["consts"], [], [cst])
        S.dma("sp", cc.sem, cc[:], D["cc"], [], [cc])
        S.dma("sp", adab.sem, adab[:], D["ada_bT"], [], [adab])
        S.dma("sp", ngT.sem, ngT[:], D["norm_gT"], [], [ngT])
        S.dma("sp", cw.sem, cw[:], D["conv_wT"], [], [cw])
        S.dma("sp", cb.sem, cb[:], D["conv_bT"], [], [cb])
        for i in range(NT):
            src = D["ctx"][i * 128:(i + 1) * 128, :] if i < 2 else D["x"][(i - 2) * 128:(i - 1) * 128, :]
            S.dma("sp", xs[i].sem, xs[i][:], src, [], [xs[i]])
        S.op("dve", [cst], [identb], lambda e: e.tensor_copy(out=identb[:], in_=cst[:, 0:128]))
        S.op("dve", [cst], [wmb], lambda e: e.tensor_copy(out=wmb[:], in_=cst[:, 384:640].rearrange("p (a b) -> p a b", a=2)))
        S.op("dve", [], [ones_f], lambda e: e.memset(ones_f[:], 1.0))
        S.op("dve", [], [ones_b], lambda e: e.memset(ones_b[:], 1.0))
        S.op("act", [cc], [sc], lambda e: e.activation(out=sc[:], in_=cc[:], func=AF.Silu))

        dstg = S.sb("dstg", [128, 128], F32) if dbg else None
        dstate = {"col": 0, "items": []}

        def dump(name, buf, ap, n):
            if not dbg:
                return
            stg = dstg
            sem = S.newsem()
            S.op("act", [buf], [stg], lambda e: e.activation(out=stg[:, 0:n], in_=ap, func=AF.Copy))
            c0 = dstate["col"]
            S.dma("sp", sem, dbgd[:, c0:c0 + n], stg[:, 0:n], [stg], [])
            S._need("sp", sem, S.cnt[sem])
            dstate["items"].append((name, c0, n))
            dstate["col"] = c0 + n
            print("DUMP", name, c0, n, flush=True)

        def wview(wb, shape_str, **kw):
            n = 1
            for v in kw.values():
                n *= v
            return wb

        def mod_phase(l):
            with ExitStack() as ph:
                ring = Ring([S.sbd(f"adaw{l}_{i}", [128, 8, 512], F32, ph) for i in range(2)])
                for cg in range(12):
                    wb = ring.get()
                    S.dma("sp", wb.sem, wb[:], D["ada_w"][l, :, cg * 512:(cg + 1) * 512].rearrange("(k p) n -> p k n", p=128), [], [wb])
                    ps = psA.get()
                    for c4 in range(4):
                        for k in range(8):
                            S.op("pe", [wb, sc], [ps], lambda e: e.matmul(ps[:, c4 * 2:c4 * 2 + 2], lhsT=wb[:, k, c4 * 128:(c4 + 1) * 128], rhs=sc[:, k, :], start=(k == 0), stop=(k == 7)))
                    S.op("dve", [ps, adab], [modT[l]], lambda e: e.tensor_tensor(
                        out=modT[l][:, cg * 4:(cg + 1) * 4, :], in0=ps[:, 0:8].rearrange("p (c j) -> p c j", j=2),
                        in1=adab[:, l, cg * 4:(cg + 1) * 4].unsqueeze(2).to_broadcast([128, 4, 2]), op=ALU.add))
                S.barrier()

        def mk_AB(l, which):
            scl = 8 if which == 0 else 32
            S.op("dve", [modT[l]], [AB], lambda e: e.tensor_scalar(out=AB[:], in0=modT[l][:, scl:scl + 8, :], scalar1=1.0, scalar2=None, op0=ALU.add))
            S.op("dve", [AB, ngT], [AB], lambda e: e.tensor_tensor(out=AB[:], in0=AB[:], in1=ngT[:, l, which, :].unsqueeze(2).to_broadcast([128, 8, 2]), op=ALU.mult))

        def mk_gate(l, gchunk, ph):
            hl = S.sb(f"ghl{l}_{gchunk}", [128, 8, 2], F32, ph)
            hb = S.sb(f"ghb{l}_{gchunk}", [128, 8, 2], BF16, ph)
            hf = S.sb(f"ghf{l}_{gchunk}", [128, 8, 2], F32, ph)
            lo = S.sb(f"glo{l}_{gchunk}", [128, 8, 2], F32, ph)
            lb = S.sb(f"glb{l}_{gchunk}", [128, 8, 2], BF16, ph)
            lf = S.sb(f"glf{l}_{gchunk}", [128, 8, 2], F32, ph)
            S.op("dve", [modT[l]], [hl], lambda e: e.tensor_copy(out=hl[:], in_=modT[l][:, gchunk:gchunk + 8, :]))
            S.op("dve", [hl], [hb], lambda e: e.tensor_copy(out=hb[:], in_=hl[:]))
            S.op("dve", [hb], [hf], lambda e: e.tensor_copy(out=hf[:], in_=hb[:]))
            S.op("dve", [hl, hf], [lo], lambda e: e.tensor_tensor(out=lo[:], in0=hl[:], in1=hf[:], op=ALU.subtract))
            S.op("dve", [lo], [lb], lambda e: e.tensor_copy(out=lb[:], in_=lo[:]))
            S.op("dve", [lb], [lf], lambda e: e.tensor_copy(out=lf[:], in_=lb[:]))
            dgr = Ring([S.sb(f"dg{l}_{gchunk}_{i}", [128, 2, 128], BF16, ph) for i in range(2)])
            for j in range(2):
                for half in range(2):
                    ps = psA.get()
                    for k4 in range(4):
                        kk = half * 4 + k4
                        dg = dgr.get()
                        S.op("dve", [identb, hf], [dg], lambda e: e.tensor_scalar(out=dg[:, 0, :], in0=identb[:], scalar1=hf[:, kk, j:j + 1], scalar2=None, op0=ALU.mult))
                        S.op("dve", [identb, lf], [dg], lambda e: e.tensor_scalar(out=dg[:, 1, :], in0=identb[:], scalar1=lf[:, kk, j:j + 1], scalar2=None, op0=ALU.mult))
                        S.op("pe", [ones_b, dg], [ps], lambda e: e.matmul(ps[:, k4 * 128:(k4 + 1) * 128], lhsT=ones_b[:], rhs=dg[:, 0, :], start=True, stop=False))
                        S.op("pe", [ones_b, dg], [ps], lambda e: e.matmul(ps[:, k4 * 128:(k4 + 1) * 128], lhsT=ones_b[:], rhs=dg[:, 1, :], start=False, stop=True))
                    S.op("act", [ps], [gbc], lambda e: e.activation(out=gbc[:, j, half * 512:(half + 1) * 512], in_=ps[:], func=AF.Copy))

        def rstd_of(t, n_ap, dim):
            S.op("dve", [t], [t], lambda e: e.tensor_scalar(out=n_ap(), in0=n_ap(), scalar1=1.0 / dim, scalar2=EPS, op0=ALU.mult, op1=ALU.add))
            S.op("act", [t], [t], lambda e: e.activation(out=n_ap(), in_=n_ap(), func=AF.Ln))
            S.op("act", [t], [t], lambda e: e.activation(out=n_ap(), in_=n_ap(), func=AF.Exp, scale=-0.5))

        def norm_phase(l, which, hTg, ph, tiles=range(NT)):
            mk_AB(l, which)
            sh = 0 if which == 0 else 24
            ss = S.sb(f"nss{l}{which}", [128, NT], F32, ph)
            junk = S.sb(f"njunk{l}{which}", [128, 1024], BF16, ph)
            xnr = Ring([S.sb(f"xn{l}{which}_{i}", [128, 1024], BF16, ph) for i in range(2)])
            S.op("dve", [], [ss], lambda e: e.memset(ss[:], 1.0))
            for i in tiles:
                S.op("act", [xs[i]], [junk, ss], lambda e: e.activation(out=junk[:], in_=xs[i][:], func=AF.Square, accum_out=ss[:, i:i + 1]))
            rstd_of(ss, lambda: ss[:], 1024)
            import os
            if os.environ.get("KSUB") in ("a", "c"):
                return
            for i in tiles:
                xn = xnr.get()
                S.op("dve", [xs[i], ss], [xn], lambda e: e.tensor_scalar(out=xn[:], in0=xs[i][:], scalar1=ss[:, i:i + 1], scalar2=None, op0=ALU.mult))
                pt = psT.get()
                for k in range(8):
                    S.op("pe", [xn, identb], [pt], lambda e: e.transpose(out=pt[:, k, :], in_=xn[:, k * 128:(k + 1) * 128], identity=identb[:]))
                g, off = tok_group(i)
                j = 1 if i < 2 else 0
                for k in range(8):
                    if k % 2 == 0:
                        S.op("dve", [pt, AB, modT[l]], [hTg[g]], lambda e: e.tensor_scalar(
                            out=hTg[g][:, k, off:off + 128], in0=pt[:, k, :], scalar1=AB[:, k, j:j + 1], scalar2=modT[l][:, sh + k, j:j + 1], op0=ALU.mult, op1=ALU.add))
                    else:
                        S.op("act", [pt, AB, modT[l]], [hTg[g]], lambda e: e.activation(
                            out=hTg[g][:, k, off:off + 128], in_=pt[:, k, :], func=AF.Identity, scale=AB[:, k, j:j + 1], bias=modT[l][:, sh + k, j:j + 1]))

        def wload(wb, n, src):
            dst = wb[:, 0:8 * n].rearrange("p (k n) -> p k n", k=8)
            S.dma("pool", wb.sem, dst, src, [], [wb])
            return dst

        def qk_prep(ps, ps_ap, nh, g_ap, rope_tile, out_ap, wk, rope):
            sq, ssq, qn, t1 = wk
            n = nh * 64
            v3 = lambda ap: ap.rearrange("p (h d) -> p h d", d=64)
            S.op("act", [ps], [sq], lambda e: e.activation(out=sq[:, 0:n], in_=ps_ap, func=AF.Square))
            S.op("dve", [sq], [ssq], lambda e: e.tensor_reduce(out=ssq[:, 0:nh], in_=v3(sq[:, 0:n]), axis=AX.X, op=ALU.add))
            rstd_of(ssq, lambda: ssq[:, 0:nh], 64)
            S.op("dve", [ps, ssq], [qn], lambda e: e.tensor_tensor(out=v3(qn[:, 0:n]), in0=v3(ps_ap), in1=ssq[:, 0:nh].unsqueeze(2).to_broadcast([128, nh, 64]), op=ALU.mult))
            if rope_tile is None:
                S.op("pool", [qn], [out_ap[0]], lambda e: e.tensor_tensor(out=out_ap[1], in0=v3(qn[:, 0:n]), in1=g_ap.unsqueeze(1).to_broadcast([128, nh, 64]), op=ALU.mult))
                return
            S.op("pool", [qn], [qn], lambda e: e.tensor_tensor(out=v3(qn[:, 0:n]), in0=v3(qn[:, 0:n]), in1=g_ap.unsqueeze(1).to_broadcast([128, nh, 64]), op=ALU.mult))
            cos_ap = rope[:, 0, :]
            sin_ap = rope[:, 1, :]
            S.op("dve", [qn, rope], [t1], lambda e: e.tensor_tensor(out=v3(t1[:, 0:n]), in0=v3(qn[:, 0:n]), in1=cos_ap.unsqueeze(1).to_broadcast([128, nh, 64]), op=ALU.mult))
            v5 = lambda ap: ap.rearrange("p (h x y d) -> p h x y d", x=2, y=2, d=16)
            s4 = sin_ap.rearrange("p (x y d) -> p x y d", x=2, y=2)
            for y in range(2):
                S.op("pool", [qn, rope], [sq], lambda e: e.tensor_tensor(
                    out=v5(sq[:, 0:n])[:, :, :, y, :], in0=v5(qn[:, 0:n])[:, :, :, 1 - y, :],
                    in1=s4[:, :, y, :].unsqueeze(1).to_broadcast([128, nh, 2, 16]), op=ALU.mult))
            S.op("dve", [t1, sq], [out_ap[0]], lambda e: e.tensor_tensor(out=out_ap[1], in0=v3(t1[:, 0:n]), in1=v3(sq[:, 0:n]), op=ALU.add))

        def residual(i, ps, cgi, j):
            tmp = restmp.get()
            S.op("dve", [ps, gbc], [tmp], lambda e: e.tensor_tensor(out=tmp[:], in0=ps[:], in1=gbc[:, j, cgi * 512:(cgi + 1) * 512], op=ALU.mult))
            S.op("pool", [tmp, xs[i]], [xs[i]], lambda e: e.tensor_tensor(out=xs[i][:, cgi * 512:(cgi + 1) * 512], in0=xs[i][:, cgi * 512:(cgi + 1) * 512], in1=tmp[:], op=ALU.add))

        restmp = Ring([S.sb(f"restmp{i}", [128, 512], F32) for i in range(2)])

        def mixer0():
            l = 0
            with ExitStack() as ph:
                hTg = [S.sb("hT0_0", [128, 8, 256], BF16, ph)] + [S.sb(f"hT0_{g}", [128, 8, 512], BF16, ph) for g in range(1, 5)]
                with ExitStack() as ph2:
                    norm_phase(0, 0, hTg, ph2)
                    import os
                    if os.environ.get("KSUB") not in ("a", "b"):
                        mk_gate(0, 16, ph2)
                    S.barrier()
                if stage == "norm":
                    return
                mixTa = S.sb("mixTa", [128, 4, NT * 128], BF16, ph)
                with ExitStack() as ph2:
                    wring = Ring([S.sbd(f"w0_{i}", [128, 8 * 384], BF16, ph2) for i in range(2)])
                    gateb = S.sbd("gateb", [128, 16], F32, ph2)
                    headg = S.sbd("headg", [128, 512], F32, ph2)
                    S.dma("sp", gateb.sem, gateb[:], D["gate_b_bc"], [], [gateb])
                    S.dma("sp", headg.sem, headg[:], D["head_g_bc"], [], [headg])
                    mlstm(hTg, mixTa, gateb, headg, wring, ph2)
                    S.barrier()
                if stage == "mlstm":
                    return
                with ExitStack() as ph2:
                    gqa_attn(0, hTg, mixTa, None, ph2)
                    S.barrier()

        def mlstm(hTg, mixTa, gateb, headg, wring, ph):
            G = S.sb("G", [128, NT, 16], F32, ph)
            wg = wload(wring.get(), 16, D["even_w"][:, 2048:2064].rearrange("(k p) n -> p k n", p=128))
            wgb = wring.bufs[(wring.i - 1) % len(wring.bufs)]
            for i in range(NT):
                g, off = tok_group(i)
                ps = psO.get()
                for k in range(8):
                    S.op("pe", [hTg[g], wgb], [ps], lambda e: e.matmul(ps[:, 0:16], lhsT=hTg[g][:, k, off:off + 128], rhs=wg[:, k, :], start=(k == 0), stop=(k == 7)))
                S.op("dve", [ps, gateb], [G], lambda e: e.tensor_tensor(out=G[:, i, :], in0=ps[:, 0:16], in1=gateb[:], op=ALU.add))
            E = S.sb("E", [128, 2, NT, 4], F32, ph)
            for d in range(2):
                S.op("act", [G], [E], lambda e: e.activation(out=E[:, d], in_=G[:, :, 4 + 8 * d:8 + 8 * d], func=AF.Exp, scale=-1.0))
            S.op("dve", [E], [E], lambda e: e.tensor_scalar(out=E[:], in0=E[:], scalar1=1.0, scalar2=None, op0=ALU.add))
            S.op("act", [E], [E], lambda e: e.activation(out=E[:], in_=E[:], func=AF.Ln))
            es = S.sb("es", [128, 2, NT, 4], F32, ph)
            eb = S.sb("eb", [128, 2, NT, 4], F32, ph)
            edec = S.sb("edec", [128, 2, NT, 4], F32, ph)
            ekw = S.sb("ekw", [128, 2, NT, 4], F32, ph)
            tg = S.sb("tg", [128, NT, 4], F32, ph)
            f72 = lambda ap: ap.rearrange("p t h -> p (t h)")
            trib = S.sb("trib", [128, 2, 128], BF16, ph)
            S.op("dve", [cst], [trib], lambda e: e.tensor_copy(out=trib[:], in_=cst[:, 128:384].rearrange("p (a b) -> p a b", a=2)))
            Ehi = S.sb("Ehi", [128, 2, NT, 4], BF16, ph)
            Ehf = S.sb("Ehf", [128, 2, NT, 4], F32, ph)
            Elo = S.sb("Elo", [128, 2, NT, 4], BF16, ph)
            S.op("dve", [E], [Ehi], lambda e: e.tensor_copy(out=Ehi[:], in_=E[:]))
            S.op("dve", [Ehi], [Ehf], lambda e: e.tensor_copy(out=Ehf[:], in_=Ehi[:]))
            S.op("dve", [E, Ehf], [Ehf], lambda e: e.tensor_tensor(out=Ehf[:], in0=E[:], in1=Ehf[:], op=ALU.subtract))
            S.op("dve", [Ehf], [Elo], lambda e: e.tensor_copy(out=Elo[:], in_=Ehf[:]))
            for d in range(2):
                psb = psO.get()
                S.op("pe", [trib, Ehi], [psb], lambda e: e.matmul(psb[:, 0:72], lhsT=trib[:, d, :], rhs=f72(Ehi[:, d]), start=True, stop=False))
                S.op("pe", [trib, Elo], [psb], lambda e: e.matmul(psb[:, 0:72], lhsT=trib[:, d, :], rhs=f72(Elo[:, d]), start=False, stop=True))
                S.op("pe", [ones_b, Ehi], [psb], lambda e: e.matmul(psb[:, 72:144], lhsT=ones_b[:], rhs=f72(Ehi[:, d]), start=True, stop=False))
                S.op("pe", [ones_b, Elo], [psb], lambda e: e.matmul(psb[:, 72:144], lhsT=ones_b[:], rhs=f72(Elo[:, d]), start=False, stop=True))
                S.op("dve", [psb, G], [tg], lambda e: e.tensor_tensor(out=tg[:], in0=psb[:, 0:72].rearrange("p (t h) -> p t h", h=4), in1=G[:, :, 8 * d:8 * d + 4], op=ALU.add))
                S.op("act", [tg], [es], lambda e: e.activation(out=es[:, d], in_=tg[:], func=AF.Exp))
                S.op("act", [psb], [eb], lambda e: e.activation(out=f72(eb[:, d]), in_=psb[:, 0:72], func=AF.Exp, scale=-1.0))
                S.op("act", [psb], [edec], lambda e: e.activation(out=f72(edec[:, d]), in_=psb[:, 72:144], func=AF.Exp, scale=-1.0))
                S.op("dve", [es, edec], [ekw], lambda e: e.tensor_tensor(out=ekw[:, d], in0=es[:, d], in1=edec[:, d], op=ALU.mult))

            pass
            pass
            pass
            pass
            pass
            import os
            KS_ = os.environ.get("KSUB", "")
            KH_ = int(os.environ.get("KHEAD", "0"))
            if KS_ == "m1":
                return
            qT = S.sb("qTa", [128, NT * 128], BF16, ph)
            kT = S.sb("kTa", [128, NT * 128], BF16, ph)
            ktok = S.sb("ktok", [128, NT, 128], BF16, ph)
            vaug = S.sb("vaug", [128, NT, 130], BF16, ph)
            hsum = [S.sb(f"hsum{i}", [128, 128], F32, ph) for i in range(NT)]
            Cst = [S.sb(f"Cst{d}", [128, 129], F32, ph) for d in range(2)]
            Cbf = [S.sb(f"Cbf{d}", [128, 129], BF16, ph) for d in range(2)]
            PTr = Ring([S.sb(f"PTm{i}", [128, 128], BF16, ph) for i in range(3)])
            kwr = Ring([S.sb(f"kwm{i}", [128, 128], BF16, ph) for i in range(3)])
            smr = Ring([S.sb(f"smm{i}", [128, 4], F32, ph) for i in range(4)])
            hss = S.sb("hss", [128, NT], F32, ph)
            hjunk = S.sb("hjunk", [128, 128], F32, ph)
            ogr = Ring([S.sb(f"og{i}", [128, 128], F32, ph) for i in range(2)])
            t1r = Ring([S.sb(f"mt1{i}", [128, 128], F32, ph) for i in range(2)])
            mxr = Ring([S.sb(f"mmx{i}", [128, 128], BF16, ph) for i in range(2)])
            S.op("dve", [], [vaug], lambda e: e.memset(vaug[:, :, 128:129], 1.0))
            orders = [list(range(NT)), [1, 0] + list(range(NT - 1, 1, -1))]
            KS = 128.0 ** -0.5

            for h in range(4):
                wb = wring.get()
                src = D["even_w"][:, 0:1536].rearrange("(k p) (g h n) -> p k g h n", p=128, g=3, h=4)[:, :, :, h, :]
                wq = wb[:, 0:8 * 384].rearrange("p (k g n) -> p k g n", k=8, g=3)
                for g3 in range(3):
                    S.dma("pool", wb.sem, wq[:, :, g3, :], src[:, :, g3, :], [], [wb])
                wob = wring.get()
                wo = wload(wob, 128, D["even_w"][:, 1536 + h * 128:1536 + (h + 1) * 128].rearrange("(k p) n -> p k n", p=128))
                flip = 0
                for (g, c0, n) in GROUPS:
                    for which, dst, scl in ((0, qT, 1.0), (1, kT, KS)):
                        ps = psA.get()
                        for k in range(8):
                            S.op("pe", [wb, hTg[g]], [ps], lambda e: e.matmul(ps[:, 0:n], lhsT=wq[:, k, which, :], rhs=hTg[g][:, k, 0:n], start=(k == 0), stop=(k == 7)))
                        if flip % 2 == 0:
                            S.op("act", [ps], [dst], lambda e: e.activation(out=dst[:, c0:c0 + n], in_=ps[:, 0:n], func=AF.Copy, scale=scl))
                        else:
                            S.op("dve", [ps], [dst], lambda e: e.tensor_scalar(out=dst[:, c0:c0 + n], in0=ps[:, 0:n], scalar1=scl, scalar2=None, op0=ALU.mult))
                        flip += 1
                if KS_ == "m2a" and h == KH_:
                    return
                for i in range(NT):
                    g, off = tok_group(i)
                    ps = psA.get()
                    for k in range(8):
                        S.op("pe", [wb, hTg[g]], [ps], lambda e: e.matmul(ps[:, 0:256], lhsT=hTg[g][:, k, off:off + 128], rhs=wb[:, k * 384 + 128:k * 384 + 384], start=(k == 0), stop=(k == 7)))
                    if os.environ.get("KSUB2") != "noact":
                        S.op("act", [ps], [ktok], lambda e: e.activation(out=ktok[:, i, :], in_=ps[:, 0:128], func=AF.Copy, scale=KS))
                    if os.environ.get("KSUB2") != "nodve":
                        S.op("dve", [ps], [vaug], lambda e: e.tensor_copy(out=vaug[:, i, 0:128], in_=ps[:, 128:256]))
                if KS_ == "m2b" and h == KH_:
                    return
                if h == 0:
                    pass
                    pass
                    pass
                    pass
                if KS_ == "m2" and h == KH_:
                    return
                written = [False] * NT
                for step in range(NT):
                    for d in range(2):
                        i = orders[d][step]
                        col = lambda a: a[:, d, i, h:h + 1]
                        cs = slice(i * 128, (i + 1) * 128)
                        pss = psS.get()
                        S.op("pe", [kT, qT], [pss], lambda e: e.matmul(pss[:, 0:128], lhsT=kT[:, cs], rhs=qT[:, cs], start=True, stop=True))
                        PT = PTr.get()
                        msk = TRIU() if d == 0 else TRIL()
                        S.op("dve", [pss, es, cst], [PT], lambda e: e.scalar_tensor_tensor(out=PT[:], in0=pss[:, 0:128], scalar=col(es), in1=msk, op0=ALU.mult, op1=ALU.mult))
                        acc = psO.get()
                        if step > 0:
                            S.op("pe", [qT, Cbf[d]], [acc], lambda e: e.matmul(acc[:, 0:129], lhsT=qT[:, cs], rhs=Cbf[d][:], start=True, stop=False))
                        S.op("pe", [PT, vaug], [acc], lambda e: e.matmul(acc[:, 0:129], lhsT=PT[:], rhs=vaug[:, i, 0:129], start=(step == 0), stop=True))
                        sm = smr.get()
                        S.op("act", [acc, eb], [sm], lambda e: e.activation(out=sm[:, 0:1], in_=acc[:, 128:129], func=AF.Abs, scale=col(eb)))
                        S.op("dve", [sm], [sm], lambda e: e.tensor_scalar(out=sm[:, 1:2], in0=sm[:, 0:1], scalar1=1.0, scalar2=None, op0=ALU.max))
                        S.op("dve", [sm], [sm], lambda e: e.reciprocal(out=sm[:, 2:3], in_=sm[:, 1:2]))
                        S.op("dve", [sm, eb], [sm], lambda e: e.tensor_tensor(out=sm[:, 3:4], in0=sm[:, 2:3], in1=col(eb), op=ALU.mult))
                        if not written[i]:
                            S.op("act", [acc, sm], [hsum[i]], lambda e: e.activation(out=hsum[i][:], in_=acc[:, 0:128], func=AF.Copy, scale=sm[:, 3:4]))
                            written[i] = True
                        else:
                            S.op("dve", [acc, sm, hsum[i]], [hsum[i]], lambda e: e.scalar_tensor_tensor(out=hsum[i][:], in0=acc[:, 0:128], scalar=sm[:, 3:4], in1=hsum[i][:], op0=ALU.mult, op1=ALU.add))
                        if step < NT - 1:
                            kw = kwr.get()
                            S.op("pool", [ktok, ekw], [kw], lambda e: e.tensor_scalar(out=kw[:], in0=ktok[:, i, :], scalar1=col(ekw), scalar2=None, op0=ALU.mult))
                            psc = psA.get()
                            S.op("pe", [kw, vaug], [psc], lambda e: e.matmul(psc[:, 0:129], lhsT=kw[:], rhs=vaug[:, i, 0:129], start=True, stop=True))
                            if step == 0:
                                S.op("dve", [psc], [Cst[d]], lambda e: e.tensor_copy(out=Cst[d][:], in_=psc[:, 0:129]))
                            else:
                                S.op("dve", [psc, Cst[d], edec], [Cst[d]], lambda e: e.scalar_tensor_tensor(out=Cst[d][:], in0=Cst[d][:], scalar=col(edec), in1=psc[:, 0:129], op0=ALU.mult, op1=ALU.add))
                            S.op("act", [Cst[d]], [Cbf[d]], lambda e: e.activation(out=Cbf[d][:], in_=Cst[d][:], func=AF.Copy))
                if h == 0:
                    pass
                    pass
                if KS_ == "m3" and h == KH_:
                    return
                S.op("dve", [], [hss], lambda e: e.memset(hss[:], 1.0))
                for i in range(NT):
                    S.op("act", [hsum[i]], [hjunk, hss], lambda e: e.activation(out=hjunk[:], in_=hsum[i][:], func=AF.Square, accum_out=hss[:, i:i + 1]))
                rstd_of(hss, lambda: hss[:], 128)
                for i in range(NT):
                    g, off = tok_group(i)
                    ps = psA.get()
                    for k in range(8):
                        S.op("pe", [wob, hTg[g]], [ps], lambda e: e.matmul(ps[:, 0:128], lhsT=hTg[g][:, k, off:off + 128], rhs=wo[:, k, :], start=(k == 0), stop=(k == 7)))
                    og = ogr.get()
                    S.op("act", [ps], [og], lambda e: e.activation(out=og[:], in_=ps[:, 0:128], func=AF.Sigmoid))
                    t1 = t1r.get()
                    S.op("dve", [hsum[i], hss, headg], [t1], lambda e: e.scalar_tensor_tensor(out=t1[:], in0=hsum[i][:], scalar=hss[:, i:i + 1], in1=headg[:, h * 128:(h + 1) * 128], op0=ALU.mult, op1=ALU.mult))
                    mx = mxr.get()
                    S.op("pool", [t1, og], [mx], lambda e: e.tensor_tensor(out=mx[:], in0=t1[:], in1=og[:], op=ALU.mult))
                    pt = psT.get()
                    S.op("pe", [mx, identb], [pt], lambda e: e.transpose(out=pt[:, 0, :], in_=mx[:], identity=identb[:]))
                    S.op("act", [pt], [mixTa], lambda e: e.activation(out=mixTa[:, h, i * 128:(i + 1) * 128], in_=pt[:, 0, :], func=AF.Copy))
                if KS_ == "m4" and h == KH_:
                    return

        def attn_scores_exp_pv(kv_specs, nheads_per_kv, qT, q_sl, PTr, accs, first, last):
            pass

        def gqa_attn(l, hTg, other, wring, ph):
            wname = "even_w" if l == 0 else "odd_w"
            qc0, kc0 = (2064, 2576) if l == 0 else (0, 512)
            swag = S.sbd(f"swag{l}", [128, 128], F32, ph)
            roper = Ring([S.sbd(f"rope{l}_{i}", [128, 2, 64], F32, ph) for i in range(2)])

            def get_rope(jt):
                rb = roper.get()
                S.dma("sp", rb.sem, rb[:], D["rope"][:, :, jt, :], [], [rb])
                return rb
            S.dma("sp", swag.sem, swag[:], D["swa_g_bc" if l == 0 else "gqa_g_bc"], [], [swag])
            gq = S.sb(f"gq{l}", [128, 64], F32, ph)
            S.op("dve", [swag], [gq], lambda e: e.tensor_scalar(out=gq[:], in0=swag[:, 0:64], scalar1=0.125, scalar2=None, op0=ALU.mult))
            esink = S.sb(f"esink{l}", [128, 8], F32, ph)
            if l == 0:
                sinkb = S.sbd("sinkb", [128, 8], F32, ph)
                S.dma("sp", sinkb.sem, sinkb[:], D["sink_bc"], [], [sinkb])
                S.op("act", [sinkb], [esink], lambda e: e.activation(out=esink[:], in_=sinkb[:], func=AF.Exp))
            else:
                S.op("dve", [], [esink], lambda e: e.memset(esink[:], 0.0))
            wout = S.sbd(f"wout{l}", [128, 8 * 1024], BF16, ph)
            woutv = wout[:, :].rearrange("p (k n) -> p k n", k=8)
            S.dma("pool", wout.sem, woutv, D["w_out"][l].rearrange("(k p) n -> p k n", p=128), [], [wout])
            wkb = S.sbd(f"wkv{l}", [128, 8 * 256], BF16, ph)
            wkv = wload(wkb, 256, D[wname][:, kc0:kc0 + 256].rearrange("(k p) n -> p k n", p=128))
            wqb = S.sbd(f"wqq{l}", [128, 8 * 512], BF16, ph)
            wq = wload(wqb, 512, D[wname][:, qc0:qc0 + 512].rearrange("(k p) n -> p k n", p=128))
            kTd = [S.sb(f"kTd{g}", [128, NT * 128], BF16, ph) for g in range(2)]
            vb = S.sb("vb", [128, NT, 2, 66], BF16, ph)
            S.op("dve", [], [vb], lambda e: e.memset(vb[:, :, :, 64:65], 1.0))
            wk = (S.sb("wk_sq", [128, 512], F32, ph), S.sb("wk_ss", [128, 8], F32, ph), S.sb("wk_qn", [128, 512], F32, ph), S.sb("wk_t1", [128, 512], F32, ph))
            kd = S.sb("kd", [128, 2, 2, 64], BF16, ph)
            kn = S.sb("kn", [128, 2, 64], BF16, ph)
            for i in range(NT):
                g, off = tok_group(i)
                ps = psA.get()
                for k in range(8):
                    S.op("pe", [wkb, hTg[g]], [ps], lambda e: e.matmul(ps[:, 0:256], lhsT=hTg[g][:, k, off:off + 128], rhs=wkv[:, k, :], start=(k == 0), stop=(k == 7)))
                S.op("act", [ps], [vb], lambda e: e.activation(out=vb[:, i, :, 0:64], in_=ps[:, 128:256].rearrange("p (g d) -> p g d", g=2), func=AF.Copy))
                qk_prep(ps, ps[:, 0:128], 2, swag[:, 64:128], (i - 2) if i >= 2 else None, (kn, kn[:]), wk, get_rope(i - 2) if i >= 2 else None)
                for dup in range(2):
                    S.op("pool", [kn], [kd], lambda e: e.tensor_copy(out=kd[:, :, dup, :], in_=kn[:]))
                pt = psT.get()
                for g2 in range(2):
                    S.op("pe", [kd, identb], [pt], lambda e: e.transpose(out=pt[:, g2, :], in_=kd[:, g2].rearrange("p a d -> p (a d)"), identity=identb[:]))
                for g2 in range(2):
                    S.op("act" if g2 == 0 else "dve", [pt], [kTd[g2]],
                         (lambda e: e.activation(out=kTd[0][:, i * 128:(i + 1) * 128], in_=pt[:, 0, :], func=AF.Copy)) if g2 == 0 else
                         (lambda e: e.tensor_copy(out=kTd[1][:, i * 128:(i + 1) * 128], in_=pt[:, 1, :])))
            import os
            KS_ = os.environ.get("KSUB", "")
            if KS_ == "w1":
                return
            qb = S.sb("qb", [128, 8, 64], BF16, ph)
            qz = S.sb("qz", [128, 2, 4, 128], BF16, ph)
            S.op("dve", [], [qz], lambda e: e.memset(qz[:], 0.0))
            wmb4 = S.sb("wmb4", [128, 2, 4, 128], BF16, ph)
            S.op("dve", [wmb], [wmb4], lambda e: e.tensor_copy(out=wmb4[:], in_=wmb[:, :, :].unsqueeze(2).to_broadcast([128, 2, 4, 128])))
            PTr = Ring([S.sb(f"PTw{i}", [128, 512], BF16, ph) for i in range(2)])
            den = S.sb("wden", [128, 8], F32, ph)
            mixb = S.sb("mixb", [128, 512], BF16, ph)
            mixTb = S.sb("mixTb", [128, 4, 128], BF16, ph)
            for i in (range(NT) if l == 0 else range(2, NT)):
                g, off = tok_group(i)
                lat = i >= 2
                j = i - 2
                ps = psA.get()
                for k in range(8):
                    S.op("pe", [wqb, hTg[g]], [ps], lambda e: e.matmul(ps[:, 0:512], lhsT=hTg[g][:, k, off:off + 128], rhs=wq[:, k, :], start=(k == 0), stop=(k == 7)))
                qk_prep(ps, ps[:, 0:512], 8, gq[:], j if lat else None, (qb, qb[:]), wk, get_rope(j) if lat else None)
                pt = psT.get()
                for pr in range(4):
                    S.op("pe", [qb, identb], [pt], lambda e: e.transpose(out=pt[:, pr, :], in_=qb[:, 2 * pr:2 * pr + 2, :].rearrange("p a d -> p (a d)"), identity=identb[:]))
                S.op("act", [pt], [qz], lambda e: e.activation(out=qz[0:64, 0, :, :], in_=pt[0:64, 0:4, :], func=AF.Copy))
                S.op("dve", [pt], [qz], lambda e: e.tensor_copy(out=qz[64:128, 1, :, :], in_=pt[64:128, 0:4, :]))
                if KS_ == "w2a":
                    return
                if l == 1:
                    blocks = [(m, None) for m in range(NT)]
                elif lat:
                    blocks = [(0, None), (1, None)]
                    if j > 0:
                        blocks.append((i - 1, 0))
                    blocks.append((i, None))
                    if j < 15:
                        blocks.append((i + 1, 1))
                else:
                    blocks = [(0, None), (1, None)]
                for g2 in range(2):
                    acc = psO.get()
                    for bi, (m, msk) in enumerate(blocks):
                        pss = psS.get()
                        for half in range(2):
                            S.op("pe", [kTd[g2], qz], [pss], lambda e: e.matmul(
                                pss[:, half * 256:(half + 1) * 256], lhsT=kTd[g2][:, m * 128:(m + 1) * 128],
                                rhs=qz[:, half, 2 * g2:2 * g2 + 2, :].rearrange("p a q -> p (a q)"),
                                start=(half == 0), stop=(half == 1 and msk is None)))
                        if msk is not None:
                            S.op("pe", [identb, wmb4], [pss], lambda e: e.matmul(pss[:, 0:512], lhsT=identb[:], rhs=wmb4[:, msk, :, :].rearrange("p a q -> p (a q)"), start=False, stop=True))
                        PT = PTr.get()
                        S.op("act", [pss], [PT], lambda e: e.activation(out=PT[:], in_=pss[:], func=AF.Exp))
                        if KS_ == "w2b":
                            return
                        for hh in range(4):
                            S.op("pe", [PT, vb], [acc], lambda e: e.matmul(acc[:, hh * 128:hh * 128 + 65], lhsT=PT[:, hh * 128:(hh + 1) * 128], rhs=vb[:, m, g2, 0:65], start=(bi == 0 and hh == 0), stop=(bi == len(blocks) - 1)))
                    if KS_ == "w2c":
                        return
                    a3 = acc[:, :].rearrange("p (h c) -> p h c", h=4)
                    S.op("dve", [acc, esink], [den], lambda e: e.tensor_tensor(out=den[:, g2 * 4:(g2 + 1) * 4].rearrange("p (b a) -> p b a", b=2), in0=a3[:, :, 64].rearrange("p (b a) -> p b a", b=2),
                                                                            in1=esink[:, g2 * 4:(g2 + 1) * 4].rearrange("p (a b) -> p b a", a=2), op=ALU.add))
                    S.op("dve", [den], [den], lambda e: e.reciprocal(out=den[:, g2 * 4:(g2 + 1) * 4], in_=den[:, g2 * 4:(g2 + 1) * 4]))
                    S.op("dve", [acc, den], [mixb], lambda e: e.tensor_tensor(
                        out=mixb[:, g2 * 256:(g2 + 1) * 256].rearrange("p (a b d) -> p b a d", a=2, b=2), in0=a3[:, :, 0:64].rearrange("p (b a) d -> p b a d", b=2),
                        in1=den[:, g2 * 4:(g2 + 1) * 4].rearrange("p (b a) -> p b a", b=2).unsqueeze(3).to_broadcast([128, 2, 2, 64]), op=ALU.mult))
                if KS_ == "w2d":
                    return
                pt2 = psT.get()
                for c in range(4):
                    S.op("pe", [mixb, identb], [pt2], lambda e: e.transpose(out=pt2[:, c, :], in_=mixb[:, c * 128:(c + 1) * 128], identity=identb[:]))
                S.op("act", [pt2], [mixTb], lambda e: e.activation(out=mixTb[:], in_=pt2[:, 0:4, :], func=AF.Copy))
                if KS_ == "w2" and i == 2:
                    return
                for cgi in range(2):
                    pso = psA.get()
                    for k in range(8):
                        if l == 0:
                            lhs = other[:, k, i * 128:(i + 1) * 128] if k < 4 else mixTb[:, k - 4, :]
                        else:
                            lhs = mixTb[:, k, :] if k < 4 else other[:, k - 4, j * 128:(j + 1) * 128]
                        S.op("pe", [other, mixTb, wout], [pso], lambda e: e.matmul(pso[:, 0:512], lhsT=lhs, rhs=woutv[:, k, cgi * 512:(cgi + 1) * 512], start=(k == 0), stop=(k == 7)))
                    residual(i, pso, cgi, 0 if lat else 1)

        def ffn_phase(l, tiles):
            tiles = list(tiles)
            with ExitStack() as ph:
                hTg = [S.sb(f"hF{l}_0", [128, 8, 256], BF16, ph)] + [S.sb(f"hF{l}_{g}", [128, 8, 512], BF16, ph) for g in range(1, 5)]
                with ExitStack() as ph2:
                    norm_phase(l, 1, hTg, ph2, tiles)
                    mk_gate(l, 40, ph2)
                    S.barrier()
                segs = [gg for gg in GROUPS if (gg[0] > 0 or 0 in tiles)]
                yr = Ring([S.sb(f"fy{l}_{i}", [128, 2304], F32, ph) for i in range(2)])
                actT = S.sb(f"actT{l}", [128, 4, 2304], BF16, ph)
                wur = Ring([S.sbd(f"wu{l}_{i}", [128, 8 * 256], BF16, ph) for i in range(3)])
                wdr = Ring([S.sbd(f"wd{l}_{i}", [128, 4 * 1024], BF16, ph) for i in range(2)])
                edr = Ring([S.sb(f"fe{l}_{i}", [128, 5, 2], F32, ph) for i in range(2)])
                upsrc = D["ffn_up"][l].rearrange("(k p) (g c n) -> p k g c n", p=128, g=2, c=22)
                for c0 in range(0, 22, 4):
                    ncg = min(4, 22 - c0)
                    wdb = wdr.get()
                    wd = wdb[:, 0:ncg * 1024].rearrange("p (c n) -> p c n", c=ncg)
                    S.dma("pool", wdb.sem, wd, D["ffn_down"][l, c0 * 128:(c0 + ncg) * 128, :].rearrange("(c p) n -> p c n", p=128), [], [wdb])
                    for ci in range(ncg):
                        cp = c0 + ci
                        wub = wur.get()
                        wu = wub[:, :].rearrange("p (k g n) -> p k g n", k=8, g=2)
                        for g3 in range(2):
                            S.dma("pool", wub.sem, wu[:, :, g3, :], upsrc[:, :, g3, cp, :], [], [wub])
                        ys = []
                        for gv in range(2):
                            ch = gv * 22 + cp
                            y = yr.get()
                            ed = edr.get()
                            w0 = cw[:, l, 0, ch:ch + 1]
                            w1 = cw[:, l, 1, ch:ch + 1]
                            w2 = cw[:, l, 2, ch:ch + 1]
                            for (g, t0, n) in segs:
                                ps = psA.get()
                                for k in range(8):
                                    S.op("pe", [wub, hTg[g]], [ps], lambda e: e.matmul(ps[:, 0:n], lhsT=wu[:, k, gv, :], rhs=hTg[g][:, k, 0:n], start=(k == 0), stop=(k == 7)))
                                S.op("act", [ps, cw, cb], [y], lambda e: e.activation(out=y[:, t0:t0 + n], in_=ps[:, 0:n], func=AF.Identity, scale=w1, bias=cb[:, l, ch:ch + 1]))
                                S.op("act", [ps], [ed], lambda e: e.activation(out=ed[:, g, 0:1], in_=ps[:, 0:1], func=AF.Copy))
                                S.op("act", [ps], [ed], lambda e: e.activation(out=ed[:, g, 1:2], in_=ps[:, n - 1:n], func=AF.Copy))
                                S.op("dve", [ps, cw, y], [y], lambda e: e.scalar_tensor_tensor(out=y[:, t0 + 1:t0 + n], in0=ps[:, 0:n - 1], scalar=w0, in1=y[:, t0 + 1:t0 + n], op0=ALU.mult, op1=ALU.add))
                                S.op("dve", [ps, cw, y], [y], lambda e: e.scalar_tensor_tensor(out=y[:, t0:t0 + n - 1], in0=ps[:, 1:n], scalar=w2, in1=y[:, t0:t0 + n - 1], op0=ALU.mult, op1=ALU.add))
                            for g in range(1, 4):
                                cB = 256 + g * 512
                                S.op("dve", [ed, cw, y], [y], lambda e: e.scalar_tensor_tensor(out=y[:, cB:cB + 1], in0=ed[:, g, 1:2], scalar=w0, in1=y[:, cB:cB + 1], op0=ALU.mult, op1=ALU.add))
                                S.op("dve", [ed, cw, y], [y], lambda e: e.scalar_tensor_tensor(out=y[:, cB - 1:cB], in0=ed[:, g + 1, 0:1], scalar=w2, in1=y[:, cB - 1:cB], op0=ALU.mult, op1=ALU.add))
                            ys.append(y)
                        lo = segs[0][1]
                        S.op("act", [ys[0]], [ys[0]], lambda e: e.activation(out=ys[0][:, lo:2304], in_=ys[0][:, lo:2304], func=AF.Silu))
                        S.op("pool", [ys[0], ys[1]], [actT], lambda e: e.tensor_tensor(out=actT[:, ci, lo:2304], in0=ys[0][:, lo:2304], in1=ys[1][:, lo:2304], op=ALU.mult))
                    for i in tiles:
                        for cgi in range(2):
                            ps = psO.get()
                            for ci in range(ncg):
                                S.op("pe", [actT, wdb], [ps], lambda e: e.matmul(ps[:, 0:512], lhsT=actT[:, ci, i * 128:(i + 1) * 128], rhs=wd[:, ci, cgi * 512:(cgi + 1) * 512], start=(ci == 0), stop=(ci == ncg - 1)))
                            residual(i, ps, cgi, 0 if i >= 2 else 1)
                S.barrier()

        def na_attn(hTg, mixTd, wring, ph):
            nag = S.sbd("nag", [128, 128], F32, ph)
            S.dma("sp", nag.sem, nag[:], D["na_g_bc"], [], [nag])
            gq = S.sb("nagq", [128, 64], F32, ph)
            S.op("dve", [nag], [gq], lambda e: e.tensor_scalar(out=gq[:], in0=nag[:, 0:64], scalar1=0.125, scalar2=None, op0=ALU.mult))
            kTn = S.sb("kTn", [128, NT * 128], BF16, ph)
            vn = S.sb("vn", [128, NT, 2, 66], BF16, ph)
            qTn = S.sb("qTn", [128, 2048], BF16, ph)
            S.op("dve", [], [vn], lambda e: e.memset(vn[:, :, :, 64:65], 1.0))
            wk = (S.sb("nk_sq", [128, 512], F32, ph), S.sb("nk_ss", [128, 8], F32, ph), S.sb("nk_qn", [128, 512], F32, ph), S.sb("nk_t1", [128, 512], F32, ph))
            kn = S.sb("nkn", [128, 2, 64], BF16, ph)
            qn = S.sb("nqn", [128, 2, 64], BF16, ph)
            biasr = Ring([S.sbd(f"nbias{i}", [128, 25, 128], F32, ph) for i in range(1)])
            stmp = Ring([S.sb(f"nstmp{i}", [128, 5, 128], F32, ph) for i in range(2)])
            PTr = Ring([S.sb(f"PTn{i}", [128, 7, 128], BF16, ph) for i in range(3)])
            rdn = Ring([S.sb(f"nrd{i}", [128, 1], F32, ph) for i in range(3)])
            mixd = S.sb("mixd", [128, 16, 2, 64], BF16, ph)
            wsrc = D["odd_w"][:, 768:2304].rearrange("(k p) (g h n) -> p k g h n", p=128, g=3, h=4)
            for pr in range(4):
                wb = wring.get()
                wq = wb[:, 0:8 * 384].rearrange("p (k g n) -> p k g n", k=8, g=3)
                for g3 in range(3):
                    S.dma("pool", wb.sem, wq[:, :, g3, :], wsrc[:, :, g3, pr, :], [], [wb])
                for i in range(NT):
                    g, off = tok_group(i)
                    ps = psA.get()
                    for k in range(8):
                        S.op("pe", [wb, hTg[g]], [ps], lambda e: e.matmul(ps[:, 0:256], lhsT=hTg[g][:, k, off:off + 128], rhs=wb[:, k * 384 + 128:k * 384 + 384], start=(k == 0), stop=(k == 7)))
                    S.op("act", [ps], [vn], lambda e: e.activation(out=vn[:, i, :, 0:64], in_=ps[:, 128:256].rearrange("p (g d) -> p g d", g=2), func=AF.Copy))
                    qk_prep(ps, ps[:, 0:128], 2, nag[:, 64:128], None, (kn, kn[:]), wk, None)
                    pt = psT.get()
                    S.op("pe", [kn, identb], [pt], lambda e: e.transpose(out=pt[:, 0, :], in_=kn[:, :, :].rearrange("p a d -> p (a d)"), identity=identb[:]))
                    S.op("act", [pt], [kTn], lambda e: e.activation(out=kTn[:, i * 128:(i + 1) * 128], in_=pt[:, 0, :], func=AF.Copy))
                    if i >= 2:
                        j = i - 2
                        ps2 = psA.get()
                        for k in range(8):
                            S.op("pe", [wb, hTg[g]], [ps2], lambda e: e.matmul(ps2[:, 0:128], lhsT=hTg[g][:, k, off:off + 128], rhs=wq[:, k, 0, :], start=(k == 0), stop=(k == 7)))
                        qk_prep(ps2, ps2[:, 0:128], 2, gq[:], None, (qn, qn[:]), wk, None)
                        pt2 = psT.get()
                        S.op("pe", [qn, identb], [pt2], lambda e: e.transpose(out=pt2[:, 0, :], in_=qn[:, :, :].rearrange("p a d -> p (a d)"), identity=identb[:]))
                        S.op("dve", [pt2], [qTn], lambda e: e.tensor_copy(out=qTn[:, j * 128:(j + 1) * 128], in_=pt2[:, 0, :]))
                for hh in range(2):
                    head = 2 * pr + hh
                    bt = biasr.get()
                    S.dma("sp", bt.sem, bt[:], D["na_bias"][head], [], [bt])
                    prs = slice(hh * 64, (hh + 1) * 64)
                    for j in range(16):
                        ci = 0 if j == 0 else 1 if j == 1 else 3 if j == 14 else 4 if j == 15 else 2
                        mlist = list(range(j - 2, j + 3)) if ci == 2 else na_blocks[j]
                        nb = len(mlist)
                        keyt = [0, 1] + [m + 2 for m in mlist]
                        pA = psS.get()
                        pB = psS.get()
                        for bi, kt in enumerate(keyt):
                            pp, off2 = (pA, bi) if bi < 4 else (pB, bi - 4)
                            S.op("pe", [kTn, qTn], [pp], lambda e: e.matmul(pp[:, off2 * 128:(off2 + 1) * 128], lhsT=kTn[prs, kt * 128:(kt + 1) * 128], rhs=qTn[prs, j * 128:(j + 1) * 128], start=True, stop=True))
                        stp = stmp.get()
                        S.op("dve", [pA, bt], [stp], lambda e: e.tensor_tensor(out=stp[:, 0:2, :], in0=pA[:, 256:512].rearrange("p (b q) -> p b q", b=2), in1=bt[:, ci * 5:ci * 5 + 2, :], op=ALU.add))
                        S.op("dve", [pB, bt], [stp], lambda e: e.tensor_tensor(out=stp[:, 2:nb, :], in0=pB[:, 0:(nb - 2) * 128].rearrange("p (b q) -> p b q", b=nb - 2), in1=bt[:, ci * 5 + 2:ci * 5 + nb, :], op=ALU.add))
                        PT = PTr.get()
                        S.op("act", [pA], [PT], lambda e: e.activation(out=PT[:, 0:2, :], in_=pA[:, 0:256].rearrange("p (b q) -> p b q", b=2), func=AF.Exp))
                        S.op("act", [stp], [PT], lambda e: e.activation(out=PT[:, 2:2 + nb, :], in_=stp[:, 0:nb, :], func=AF.Exp))
                        acc = psO.get()
                        for bi, kt in enumerate(keyt):
                            S.op("pe", [PT, vn], [acc], lambda e: e.matmul(acc[:, 0:65], lhsT=PT[:, bi, :], rhs=vn[:, kt, hh, 0:65], start=(bi == 0), stop=(bi == len(keyt) - 1)))
                        rd = rdn.get()
                        S.op("dve", [acc], [rd], lambda e: e.reciprocal(out=rd[:], in_=acc[:, 64:65]))
                        S.op("act", [acc, rd], [mixd], lambda e: e.activation(out=mixd[:, j, hh, :], in_=acc[:, 0:64], func=AF.Copy, scale=rd[:, 0:1]))
                for j in range(16):
                    pt = psT.get()
                    S.op("pe", [mixd, identb], [pt], lambda e: e.transpose(out=pt[:, 0, :], in_=mixd[:, j, :, :].rearrange("p a d -> p (a d)"), identity=identb[:]))
                    S.op("dve", [pt], [mixTd], lambda e: e.tensor_copy(out=mixTd[:, pr, j * 128:(j + 1) * 128], in_=pt[:, 0, :]))

        def mixer1():
            with ExitStack() as ph:
                hTg = [S.sb("hT1_0", [128, 8, 256], BF16, ph)] + [S.sb(f"hT1_{g}", [128, 8, 512], BF16, ph) for g in range(1, 5)]
                with ExitStack() as ph2:
                    norm_phase(1, 0, hTg, ph2)
                    mk_gate(1, 16, ph2)
                    S.barrier()
                mixTd = S.sb("mixTd", [128, 4, 2048], BF16, ph)
                with ExitStack() as ph2:
                    wring = Ring([S.sbd(f"w1_{i}", [128, 8 * 384], BF16, ph2) for i in range(2)])
                    na_attn(hTg, mixTd, wring, ph2)
                    S.barrier()
                with ExitStack() as ph2:
                    gqa_attn(1, hTg, mixTd, None, ph2)
                    S.barrier()

        mod_phase(0)
        if stage != "mod":
            mixer0()
        if stage not in ("l0mix", "mod", "norm", "mlstm"):
            ffn_phase(0, range(NT))
        if stage not in ("l0mix", "l0", "mod", "norm", "mlstm"):
            mod_phase(1)
            mixer1()
            if stage != "l1mix":
                ffn_phase(1, range(2, NT))
        osem = S.newsem("d_out")
        for i in range(2, NT):
            S.dma("sp", osem, out[(i - 2) * 128:(i - 1) * 128, :], xs[i][:], [xs[i]], [])
        if dbg:
            for i in range(2):
                S.dma("sp", osem, octx[i * 128:(i + 1) * 128, :], xs[i][:], [xs[i]], [])
        S._need("sp", osem, S.cnt[osem])
        S.barrier()
        print(f"[kernel] instructions={S.ninst} waits={S.nwait} sems={len(S.sems)}", flush=True)
    return nc


_CACHE = {}


def kernel(**inputs):
    shared, percore = host_prepare({k: np.asarray(v) for k, v in inputs.items()})
    if "nc" not in _CACHE:
        _CACHE["nc"] = build_program("full")
    nc = _CACHE["nc"]
    in_maps = []
    for b in range(8):
        m = dict(shared)
        m.update(percore[b])
        in_maps.append(m)
    res = run_bass_kernel_spmd(nc, in_maps, core_ids=list(range(8)))
    return np.stack([np.asarray(r["out"], np.float32) for r in res.results], axis=0)
```

```python
import numpy as np
from contextlib import ExitStack
import concourse.bass as bass
import concourse.mybir as mybir
from concourse.bass_utils import run_bass_kernel_spmd

F32 = mybir.dt.float32
BF16 = mybir.dt.bfloat16
AF = mybir.ActivationFunctionType
ALU = mybir.AluOpType
AX = mybir.AxisListType

ENGS = ("pe", "act", "dve", "pool", "sp")
NT = 18
EPS = 1e-6
NEGM = -30000.0


class Buf:
    __slots__ = ("t", "name", "w", "r", "sem", "psum")

    def __init__(self, t, name):
        self.t = t
        self.name = name
        self.w = None
        self.r = {}
        self.sem = None
        self.psum = False

    def __getitem__(self, idx):
        return self.t[idx]


class Ring:
    def __init__(self, bufs):
        self.bufs = bufs
        self.i = 0

    def get(self):
        b = self.bufs[self.i % len(self.bufs)]
        self.i += 1
        return b


SEM_LIMIT = 1500


class Sched:
    def __init__(self, nc, stack):
        self.nc = nc
        self.stack = stack
        self.eng = {"pe": nc.tensor, "act": nc.scalar, "dve": nc.vector,
                    "pool": nc.gpsimd, "sp": nc.sync}
        self.sems = {}
        self.cnt = {}
        self.epoch = {}
        self.cur = {}
        for e in ENGS:
            self.epoch[e] = 0
            self._new_epoch(e)
        self.seen = {e: {} for e in ENGS}
        self.ninst = 0
        self.nwait = 0
        self.nsem = 0
        self.nalloc = 0

    def _new_epoch(self, e):
        self.epoch[e] += 1
        key = f"{e}#{self.epoch[e]}"
        self.sems[key] = self.stack.enter_context(self.nc.semaphore("s_" + key.replace("#", "_")))
        self.cnt[key] = 0
        self.cur[e] = key

    def sb(self, name, shape, dt, stack=None):
        self.nalloc += 1
        name = f"{name}_{self.nalloc}"
        t = (stack or self.stack).enter_context(self.nc.sbuf_tensor(name, list(shape), dt))
        return Buf(t, name)

    def ps(self, name, shape, dt=F32):
        t = self.stack.enter_context(self.nc.psum_tensor(name, list(shape), dt))
        b = Buf(t, name)
        b.psum = True
        return b

    def newsem(self, name=None):
        self.nsem += 1
        name = name or f"d{self.nsem}"
        s = self.stack.enter_context(self.nc.semaphore(name))
        self.sems[name] = s
        self.cnt[name] = 0
        return name

    def sbd(self, name, shape, dt, stack=None):
        b = self.sb(name, shape, dt, stack)
        b.sem = self.newsem("d_" + name)
        return b

    @staticmethod
    def _eng_of(key):
        return key.split("#")[0] if "#" in key else None

    def _need(self, e, key, val):
        if self.seen[e].get(key, 0) >= val:
            return
        ke = self._eng_of(key)
        if ke is not None:
            ep = int(key.split("#")[1])
            for k2, v2 in self.seen[e].items():
                if v2 > 0 and self._eng_of(k2) == ke and int(k2.split("#")[1]) > ep:
                    return
        self.seen[e][key] = val
        self.eng[e].wait_ge(self.sems[key], val)
        self.nwait += 1

    def deps(self, e, reads, writes):
        for b in reads:
            if b.w is not None:
                k, v = b.w
                if not (self._eng_of(k) == e and e == "pe"):
                    self._need(e, k, v)
        for b in writes:
            if b.w is not None:
                k, v = b.w
                if self._eng_of(k) != e:
                    self._need(e, k, v)
            for k, v in b.r.items():
                if self._eng_of(k) != e:
                    self._need(e, k, v)

    def op(self, e, reads, writes, fn):
        pr = [b for b in reads if b.psum]
        if pr:
            reads = [b for b in reads if not b.psum]
            writes = list(writes) + [b for b in pr if b not in writes]
        self.deps(e, reads, writes)
        ins = fn(self.eng[e])
        if self.cnt[self.cur[e]] >= SEM_LIMIT:
            self._new_epoch(e)
        key = self.cur[e]
        self.cnt[key] += 1
        ins.then_inc(self.sems[key], 1)
        v = self.cnt[key]
        for b in reads:
            for k2 in [k2 for k2 in b.r if self._eng_of(k2) == e]:
                del b.r[k2]
            b.r[key] = v
        for b in writes:
            b.w = (key, v)
            b.r = {}
        self.ninst += 1
        return ins

    def dma(self, q, semkey, out_ap, in_ap, reads, writes, **kw):
        self.deps(q, reads, writes)
        ins = self.eng[q].dma_start(out=out_ap, in_=in_ap, **kw)
        self.cnt[semkey] += 16
        assert self.cnt[semkey] <= 2000, semkey
        ins.then_inc(self.sems[semkey], 16)
        v = self.cnt[semkey]
        for b in reads:
            b.r[semkey] = v
        for b in writes:
            b.w = (semkey, v)
            b.r = {}
        self.ninst += 1
        return ins

    def barrier(self):
        for e in ENGS:
            for k, v in list(self.cnt.items()):
                ke = self._eng_of(k)
                if ke == e or v == 0:
                    continue
                if ke is not None and k != self.cur[ke]:
                    if not (self.cnt[self.cur[ke]] == 0 and int(k.split("#")[1]) == self.epoch[ke] - 1):
                        continue
                self._need(e, k, v)


def _rope_tables():
    t = np.arange(2048)
    row = (t // 64).astype(np.float32)
    col = (t % 64).astype(np.float32)
    half = 32
    freq = (np.float32(10000.0) ** (-np.arange(0, half, 2, dtype=np.float32) / np.float32(half))).astype(np.float32)
    ang_r = row[:, None] * freq[None, :]
    ang_c = col[:, None] * freq[None, :]
    ang = np.concatenate([ang_r, ang_r, ang_c, ang_c], axis=-1).astype(np.float32)
    cos = np.cos(ang).astype(np.float32)
    sin = np.sin(ang).astype(np.float32)
    sgn = np.ones(64, np.float32)
    sgn[0:16] = -1.0
    sgn[32:48] = -1.0
    sinS = sin * sgn[None, :]
    cos = cos.reshape(16, 128, 64).transpose(1, 0, 2).copy()
    sinS = sinS.reshape(16, 128, 64).transpose(1, 0, 2).copy()
    return cos, sinS


def _na_tables(rpb):
    rows = 32
    wr = 8
    r = np.arange(rows)
    row_start = np.clip(r - wr // 2, 0, rows - wr)
    col = np.arange(64)
    col_start = np.clip(col - 8, 0, 48)
    col_ok = (col[None, :] >= col_start[:, None]) & (col[None, :] < col_start[:, None] + 16)
    dc = np.clip(col[None, :] - col[:, None] + 15, 0, 30)
    classes = [0, 1, 2, 14, 15]
    blocks = {}
    tab = np.full((8, 128, 25, 128), NEGM, np.float32)
    for ci, j in enumerate(classes):
        qrows = [2 * j, 2 * j + 1]
        lo = min(row_start[q] for q in qrows)
        hi = max(row_start[q] + wr - 1 for q in qrows)
        mlist = list(range(lo // 2, hi // 2 + 1))
        assert len(mlist) <= 5
        blocks[j] = mlist
        for si, m in enumerate(mlist):
            for kr in range(2):
                krow = 2 * m + kr
                for qr in range(2):
                    qrow = qrows[qr]
                    if not (row_start[qrow] <= krow < row_start[qrow] + wr):
                        continue
                    dr = krow - qrow + 7
                    sub = rpb[:, dr, :][:, dc]
                    sub = np.where(col_ok[None], sub, np.float32(NEGM))
                    tab[:, kr * 64:(kr + 1) * 64, ci * 5 + si, qr * 64:(qr + 1) * 64] = sub.transpose(0, 2, 1)
    return tab, blocks, classes


def _na_blocks():
    _, blocks, classes = _na_tables(np.zeros((8, 15, 31), np.float32))
    return blocks, classes


def host_prepare(inp):
    f = np.float32
    shared = {}
    shared["ada_w"] = np.ascontiguousarray(inp["ada_w"], f)
    shared["ada_bT"] = np.ascontiguousarray(inp["ada_b"].reshape(2, 48, 128).transpose(2, 0, 1), f)
    shared["norm_gT"] = np.ascontiguousarray(inp["norm_g"].reshape(2, 2, 8, 128).transpose(3, 0, 1, 2), f)
    shared["w_out"] = np.ascontiguousarray(inp["w_out"], f)
    shared["ffn_up"] = np.ascontiguousarray(inp["ffn_up"], f)
    shared["ffn_down"] = np.ascontiguousarray(inp["ffn_down"], f)
    shared["conv_wT"] = np.ascontiguousarray(inp["ffn_conv_w"].reshape(2, 3, 44, 128).transpose(3, 0, 1, 2), f)
    shared["conv_bT"] = np.ascontiguousarray(inp["ffn_conv_b"].reshape(2, 44, 128).transpose(2, 0, 1), f)
    shared["even_w"] = np.ascontiguousarray(inp["even_w_in"][0], f)
    shared["odd_w"] = np.ascontiguousarray(inp["odd_w_in"][0], f)
    bc = lambda a: np.ascontiguousarray(np.broadcast_to(np.asarray(a, f).reshape(1, -1), (128, a.size)))
    shared["gate_b_bc"] = bc(inp["mlstm_gate_b"][0])
    shared["head_g_bc"] = bc(inp["mlstm_head_g"][0])
    shared["swa_g_bc"] = bc(inp["swa_qk_g"][0])
    shared["sink_bc"] = bc(inp["swa_sink"][0])
    shared["gqa_g_bc"] = bc(inp["gqa_qk_g"][0])
    shared["na_g_bc"] = bc(inp["na_qk_g"][0])
    tab, _, _ = _na_tables(np.asarray(inp["na_rpb"][0], f))
    shared["na_bias"] = tab
    ident = np.eye(128, dtype=f)
    s = np.arange(128)
    triU = (s[:, None] <= s[None, :]).astype(f)
    triL = (s[:, None] >= s[None, :]).astype(f)
    wm = np.zeros((128, 2, 128), f)
    wm[:, 0, :] = np.where(s[None, :] <= s[:, None], 0.0, NEGM)
    wm[:, 1, :] = np.where(s[:, None] <= s[None, :], 0.0, NEGM)
    shared["consts"] = np.ascontiguousarray(np.concatenate([ident, triU, triL, wm.reshape(128, 256)], axis=1))
    cos, sinS = _rope_tables()
    shared["rope"] = np.ascontiguousarray(np.stack([cos, sinS], axis=1))
    percore = []
    for b in range(8):
        cc = np.stack([inp["c"][b].reshape(8, 128).T, inp["c_ctx"].reshape(8, 128).T], axis=-1)
        percore.append({"x": np.ascontiguousarray(inp["x"][b], f), "ctx": np.ascontiguousarray(inp["ctx"][b], f),
                        "cc": np.ascontiguousarray(cc, f)})
    return shared, percore


SHARED_SHAPES = {
    "ada_w": [2, 1024, 6144], "ada_bT": [128, 2, 48], "norm_gT": [128, 2, 2, 8], "w_out": [2, 1024, 1024],
    "ffn_up": [2, 1024, 5632], "ffn_down": [2, 2816, 1024], "conv_wT": [128, 2, 3, 44], "conv_bT": [128, 2, 44],
    "even_w": [1024, 2832], "odd_w": [1024, 2304], "gate_b_bc": [128, 16], "head_g_bc": [128, 512],
    "swa_g_bc": [128, 128], "sink_bc": [128, 8], "gqa_g_bc": [128, 128], "na_g_bc": [128, 128],
    "na_bias": [8, 128, 25, 128], "consts": [128, 640], "rope": [128, 2, 16, 64],
    "x": [2048, 1024], "ctx": [256, 1024], "cc": [128, 8, 2],
}


GROUPS = [(0, 0, 256), (1, 256, 512), (2, 768, 512), (3, 1280, 512), (4, 1792, 512)]


def tok_group(i):
    return (0, i * 128) if i < 2 else (1 + (i - 2) // 4, ((i - 2) % 4) * 128)


def build_program(stage="full"):
    nc = bass.Bass("TRN2", target_bir_lowering=False)
    D = {k: nc.dram_tensor(k, shp, F32, kind="ExternalInput").ap() for k, shp in SHARED_SHAPES.items()}
    out = nc.dram_tensor("out", [2048, 1024], F32, kind="ExternalOutput").ap()
    dbg = stage != "full"
    if dbg:
        octx = nc.dram_tensor("octx", [256, 1024], F32, kind="ExternalOutput").ap()
        dbgd = nc.dram_tensor("dbgd", [128, 8192], F32, kind="ExternalOutput").ap()
    na_blocks, na_classes = _na_blocks()

    with ExitStack() as st:
        S = Sched(nc, st)
        xs = [S.sbd(f"xs{i}", [128, 1024], F32) for i in range(NT)]
        cst = S.sbd("cst", [128, 640], F32)
        cc = S.sbd("cc", [128, 8, 2], F32)
        adab = S.sbd("adab", [128, 2, 48], F32)
        ngT = S.sbd("ngT", [128, 2, 2, 8], F32)
        cw = S.sbd("cw", [128, 2, 3, 44], F32)
        cb = S.sbd("cb", [128, 2, 44], F32)
        identb = S.sb("identb", [128, 128], BF16)
        wmb = S.sb("wmb", [128, 2, 128], BF16)
        ones_f = S.sb("ones_f", [128, 128], F32)
        ones_b = S.sb("ones_b", [128, 128], BF16)
        sc = S.sb("sc", [128, 8, 2], F32)
        modT = [S.sb(f"modT{l}", [128, 48, 2], F32) for l in range(2)]
        gbc = S.sb("gbc", [128, 2, 1024], F32)
        AB = S.sb("AB", [128, 8, 2], F32)

        psT = Ring([S.ps(f"psT{i}", [128, 8, 128], BF16) for i in range(2)])
        psA = Ring([S.ps(f"psA{i}", [128, 512], F32) for i in range(2)])
        psS = Ring([S.ps(f"psS{i}", [128, 512], F32) for i in range(2)])
        psO = Ring([S.ps(f"psO{i}", [128, 512], F32) for i in range(2)])

        IDF = lambda: cst[:, 0:128]
        TRIU = lambda: cst[:, 128:256]
        TRIL = lambda: cst[:, 256:384]

        S.dma("sp", cst.sem, cst[:], D["consts"], [], [cst])
        S.dma("sp", cc.sem, cc[:], D["cc"], [], [cc])
        S.dma("sp", adab.sem, adab[:], D["ada_bT"], [], [adab])
        S.dma("sp", ngT.sem, ngT[:], D["norm_gT"], [], [ngT])
        S.dma("sp", cw.sem, cw[:], D["conv_wT"], [], [cw])
        S.dma("sp", cb.sem, cb[:], D["conv_bT"], [], [cb])
        for i in range(NT):
            src = D["ctx"][i * 128:(i + 1) * 128, :] if i < 2 else D["x"][(i - 2) * 128:(i - 1) * 128, :]
            S.dma("sp", xs[i].sem, xs[i][:], src, [], [xs[i]])
        S.op("dve", [cst], [identb], lambda e: e.tensor_copy(out=identb[:], in_=cst[:, 0:128]))
        S.op("dve", [cst], [wmb], lambda e: e.tensor_copy(out=wmb[:], in_=cst[:, 384:640].rearrange("p (a b) -> p a b", a=2)))
        S.op("dve", [], [ones_f], lambda e: e.memset(ones_f[:], 1.0))
        S.op("dve", [], [ones_b], lambda e: e.memset(ones_b[:], 1.0))
        S.op("act", [cc], [sc], lambda e: e.activation(out=sc[:], in_=cc[:], func=AF.Silu))

        dstg = S.sb("dstg", [128, 128], F32) if dbg else None
        dstate = {"col": 0, "items": []}

        def dump(name, buf, ap, n):
            if not dbg:
                return
            stg = dstg
            sem = S.newsem()
            S.op("act", [buf], [stg], lambda e: e.activation(out=stg[:, 0:n], in_=ap, func=AF.Copy))
            c0 = dstate["col"]
            S.dma("sp", sem, dbgd[:, c0:c0 + n], stg[:, 0:n], [stg], [])
            S._need("sp", sem, S.cnt[sem])
            dstate["items"].append((name, c0, n))
            dstate["col"] = c0 + n
            print("DUMP", name, c0, n, flush=True)

        def wview(wb, shape_str, **kw):
            n = 1
            for v in kw.values():
                n *= v
            return wb

        def mod_phase(l):
            with ExitStack() as ph:
                ring = Ring([S.sbd(f"adaw{l}_{i}", [128, 8, 512], F32, ph) for i in range(2)])
                for cg in range(12):
                    wb = ring.get()
                    S.dma("sp", wb.sem, wb[:], D["ada_w"][l, :, cg * 512:(cg + 1) * 512].rearrange("(k p) n -> p k n", p=128), [], [wb])
                    ps = psA.get()
                    for c4 in range(4):
                        for k in range(8):
                            S.op("pe", [wb, sc], [ps], lambda e: e.matmul(ps[:, c4 * 2:c4 * 2 + 2], lhsT=wb[:, k, c4 * 128:(c4 + 1) * 128], rhs=sc[:, k, :], start=(k == 0), stop=(k == 7)))
                    S.op("dve", [ps, adab], [modT[l]], lambda e: e.tensor_tensor(
                        out=modT[l][:, cg * 4:(cg + 1) * 4, :], in0=ps[:, 0:8].rearrange("p (c j) -> p c j", j=2),
                        in1=adab[:, l, cg * 4:(cg + 1) * 4].unsqueeze(2).to_broadcast([128, 4, 2]), op=ALU.add))
                S.barrier()

        def mk_AB(l, which):
            scl = 8 if which == 0 else 32
            S.op("dve", [modT[l]], [AB], lambda e: e.tensor_scalar(out=AB[:], in0=modT[l][:, scl:scl + 8, :], scalar1=1.0, scalar2=None, op0=ALU.add))
            S.op("dve", [AB, ngT], [AB], lambda e: e.tensor_tensor(out=AB[:], in0=AB[:], in1=ngT[:, l, which, :].unsqueeze(2).to_broadcast([128, 8, 2]), op=ALU.mult))

        def mk_gate(l, gchunk, ph):
            hl = S.sb(f"ghl{l}_{gchunk}", [128, 8, 2], F32, ph)
            hb = S.sb(f"ghb{l}_{gchunk}", [128, 8, 2], BF16, ph)
            hf = S.sb(f"ghf{l}_{gchunk}", [128, 8, 2], F32, ph)
            lo = S.sb(f"glo{l}_{gchunk}", [128, 8, 2], F32, ph)
            lb = S.sb(f"glb{l}_{gchunk}", [128, 8, 2], BF16, ph)
            lf = S.sb(f"glf{l}_{gchunk}", [128, 8, 2], F32, ph)
            S.op("dve", [modT[l]], [hl], lambda e: e.tensor_copy(out=hl[:], in_=modT[l][:, gchunk:gchunk + 8, :]))
            S.op("dve", [hl], [hb], lambda e: e.tensor_copy(out=hb[:], in_=hl[:]))
            S.op("dve", [hb], [hf], lambda e: e.tensor_copy(out=hf[:], in_=hb[:]))
            S.op("dve", [hl, hf], [lo], lambda e: e.tensor_tensor(out=lo[:], in0=hl[:], in1=hf[:], op=ALU.subtract))
            S.op("dve", [lo], [lb], lambda e: e.tensor_copy(out=lb[:], in_=lo[:]))
            S.op("dve", [lb], [lf], lambda e: e.tensor_copy(out=lf[:], in_=lb[:]))
            dgr = Ring([S.sb(f"dg{l}_{gchunk}_{i}", [128, 2, 128], BF16, ph) for i in range(2)])
            for j in range(2):
                for half in range(2):
                    ps = psA.get()
                    for k4 in range(4):
                        kk = half * 4 + k4
                        dg = dgr.get()
                        S.op("dve", [identb, hf], [dg], lambda e: e.tensor_scalar(out=dg[:, 0, :], in0=identb[:], scalar1=hf[:, kk, j:j + 1], scalar2=None, op0=ALU.mult))
                        S.op("dve", [identb, lf], [dg], lambda e: e.tensor_scalar(out=dg[:, 1, :], in0=identb[:], scalar1=lf[:, kk, j:j + 1], scalar2=None, op0=ALU.mult))
                        S.op("pe", [ones_b, dg], [ps], lambda e: e.matmul(ps[:, k4 * 128:(k4 + 1) * 128], lhsT=ones_b[:], rhs=dg[:, 0, :], start=True, stop=False))
                        S.op("pe", [ones_b, dg], [ps], lambda e: e.matmul(ps[:, k4 * 128:(k4 + 1) * 128], lhsT=ones_b[:], rhs=dg[:, 1, :], start=False, stop=True))
                    S.op("act", [ps], [gbc], lambda e: e.activation(out=gbc[:, j, half * 512:(half + 1) * 512], in_=ps[:], func=AF.Copy))

        def rstd_of(t, n_ap, dim):
            S.op("dve", [t], [t], lambda e: e.tensor_scalar(out=n_ap(), in0=n_ap(), scalar1=1.0 / dim, scalar2=EPS, op0=ALU.mult, op1=ALU.add))
            S.op("act", [t], [t], lambda e: e.activation(out=n_ap(), in_=n_ap(), func=AF.Ln))
            S.op("act", [t], [t], lambda e: e.activation(out=n_ap(), in_=n_ap(), func=AF.Exp, scale=-0.5))

        def norm_phase(l, which, hTg, ph, tiles=range(NT)):
            mk_AB(l, which)
            sh = 0 if which == 0 else 24
            ss = S.sb(f"nss{l}{which}", [128, NT], F32, ph)
            junk = S.sb(f"njunk{l}{which}", [128, 1024], BF16, ph)
            xnr = Ring([S.sb(f"xn{l}{which}_{i}", [128, 1024], BF16, ph) for i in range(2)])
            S.op("dve", [], [ss], lambda e: e.memset(ss[:], 1.0))
            for i in tiles:
                S.op("act", [xs[i]], [junk, ss], lambda e: e.activation(out=junk[:], in_=xs[i][:], func=AF.Square, accum_out=ss[:, i:i + 1]))
            rstd_of(ss, lambda: ss[:], 1024)
            import os
            if os.environ.get("KSUB") in ("a", "c"):
                return
            for i in tiles:
                xn = xnr.get()
                S.op("dve", [xs[i], ss], [xn], lambda e: e.tensor_scalar(out=xn[:], in0=xs[i][:], scalar1=ss[:, i:i + 1], scalar2=None, op0=ALU.mult))
                pt = psT.get()
                for k in range(8):
                    S.op("pe", [xn, identb], [pt], lambda e: e.transpose(out=pt[:, k, :], in_=xn[:, k * 128:(k + 1) * 128], identity=identb[:]))
                g, off = tok_group(i)
                j = 1 if i < 2 else 0
                for k in range(8):
                    if k % 2 == 0:
                        S.op("dve", [pt, AB, modT[l]], [hTg[g]], lambda e: e.tensor_scalar(
                            out=hTg[g][:, k, off:off + 128], in0=pt[:, k, :], scalar1=AB[:, k, j:j + 1], scalar2=modT[l][:, sh + k, j:j + 1], op0=ALU.mult, op1=ALU.add))
                    else:
                        S.op("act", [pt, AB, modT[l]], [hTg[g]], lambda e: e.activation(
                            out=hTg[g][:, k, off:off + 128], in_=pt[:, k, :], func=AF.Identity, scale=AB[:, k, j:j + 1], bias=modT[l][:, sh + k, j:j + 1]))

        def wload(wb, n, src):
            dst = wb[:, 0:8 * n].rearrange("p (k n) -> p k n", k=8)
            S.dma("pool", wb.sem, dst, src, [], [wb])
            return dst

        def qk_prep(ps, ps_ap, nh, g_ap, rope_tile, out_ap, wk, rope):
            sq, ssq, qn, t1 = wk
            n = nh * 64
            v3 = lambda ap: ap.rearrange("p (h d) -> p h d", d=64)
            S.op("act", [ps], [sq], lambda e: e.activation(out=sq[:, 0:n], in_=ps_ap, func=AF.Square))
            S.op("dve", [sq], [ssq], lambda e: e.tensor_reduce(out=ssq[:, 0:nh], in_=v3(sq[:, 0:n]), axis=AX.X, op=ALU.add))
            rstd_of(ssq, lambda: ssq[:, 0:nh], 64)
            S.op("dve", [ps, ssq], [qn], lambda e: e.tensor_tensor(out=v3(qn[:, 0:n]), in0=v3(ps_ap), in1=ssq[:, 0:nh].unsqueeze(2).to_broadcast([128, nh, 64]), op=ALU.mult))
            if rope_tile is None:
                S.op("pool", [qn], [out_ap[0]], lambda e: e.tensor_tensor(out=out_ap[1], in0=v3(qn[:, 0:n]), in1=g_ap.unsqueeze(1).to_broadcast([128, nh, 64]), op=ALU.mult))
                return
            S.op("pool", [qn], [qn], lambda e: e.tensor_tensor(out=v3(qn[:, 0:n]), in0=v3(qn[:, 0:n]), in1=g_ap.unsqueeze(1).to_broadcast([128, nh, 64]), op=ALU.mult))
            cos_ap = rope[:, 0, :]
            sin_ap = rope[:, 1, :]
            S.op("dve", [qn, rope], [t1], lambda e: e.tensor_tensor(out=v3(t1[:, 0:n]), in0=v3(qn[:, 0:n]), in1=cos_ap.unsqueeze(1).to_broadcast([128, nh, 64]), op=ALU.mult))
            v5 = lambda ap: ap.rearrange("p (h x y d) -> p h x y d", x=2, y=2, d=16)
            s4 = sin_ap.rearrange("p (x y d) -> p x y d", x=2, y=2)
            for y in range(2):
                S.op("pool", [qn, rope], [sq], lambda e: e.tensor_tensor(
                    out=v5(sq[:, 0:n])[:, :, :, y, :], in0=v5(qn[:, 0:n])[:, :, :, 1 - y, :],
                    in1=s4[:, :, y, :].unsqueeze(1).to_broadcast([128, nh, 2, 16]), op=ALU.mult))
            S.op("dve", [t1, sq], [out_ap[0]], lambda e: e.tensor_tensor(out=out_ap[1], in0=v3(t1[:, 0:n]), in1=v3(sq[:, 0:n]), op=ALU.add))

        def residual(i, ps, cgi, j):
            tmp = restmp.get()
            S.op("dve", [ps, gbc], [tmp], lambda e: e.tensor_tensor(out=tmp[:], in0=ps[:], in1=gbc[:, j, cgi * 512:(cgi + 1) * 512], op=ALU.mult))
            rstate["n"] += 1
            S.op("pool" if rstate["n"] % 2 == 0 else "dve", [tmp, xs[i]], [xs[i]], lambda e: e.tensor_tensor(out=xs[i][:, cgi * 512:(cgi + 1) * 512], in0=xs[i][:, cgi * 512:(cgi + 1) * 512], in1=tmp[:], op=ALU.add))

        restmp = Ring([S.sb(f"restmp{i}", [128, 512], F32) for i in range(2)])
        rstate = {"n": 0}

        def mixer0():
            l = 0
            with ExitStack() as ph:
                hTg = [S.sb("hT0_0", [128, 8, 256], BF16, ph)] + [S.sb(f"hT0_{g}", [128, 8, 512], BF16, ph) for g in range(1, 5)]
                with ExitStack() as ph2:
                    norm_phase(0, 0, hTg, ph2)
                    import os
                    if os.environ.get("KSUB") not in ("a", "b"):
                        mk_gate(0, 16, ph2)
                    S.barrier()
                if stage == "norm":
                    return
                mixTa = S.sb("mixTa", [128, 4, NT * 128], BF16, ph)
                with ExitStack() as ph2:
                    wring = Ring([S.sbd(f"w0_{i}", [128, 8 * 384], BF16, ph2) for i in range(2)])
                    gateb = S.sbd("gateb", [128, 16], F32, ph2)
                    headg = S.sbd("headg", [128, 512], F32, ph2)
                    S.dma("sp", gateb.sem, gateb[:], D["gate_b_bc"], [], [gateb])
                    S.dma("sp", headg.sem, headg[:], D["head_g_bc"], [], [headg])
                    mlstm(hTg, mixTa, gateb, headg, wring, ph2)
                    S.barrier()
                if stage == "mlstm":
                    return
                with ExitStack() as ph2:
                    gqa_attn(0, hTg, mixTa, None, ph2)
                    S.barrier()

        def mlstm(hTg, mixTa, gateb, headg, wring, ph):
            G = S.sb("G", [128, NT, 16], F32, ph)
            wg = wload(wring.get(), 16, D["even_w"][:, 2048:2064].rearrange("(k p) n -> p k n", p=128))
            wgb = wring.bufs[(wring.i - 1) % len(wring.bufs)]
            for i in range(NT):
                g, off = tok_group(i)
                ps = psO.get()
                for k in range(8):
                    S.op("pe", [hTg[g], wgb], [ps], lambda e: e.matmul(ps[:, 0:16], lhsT=hTg[g][:, k, off:off + 128], rhs=wg[:, k, :], start=(k == 0), stop=(k == 7)))
                S.op("dve", [ps, gateb], [G], lambda e: e.tensor_tensor(out=G[:, i, :], in0=ps[:, 0:16], in1=gateb[:], op=ALU.add))
            E = S.sb("E", [128, 2, NT, 4], F32, ph)
            for d in range(2):
                S.op("act", [G], [E], lambda e: e.activation(out=E[:, d], in_=G[:, :, 4 + 8 * d:8 + 8 * d], func=AF.Exp, scale=-1.0))
            S.op("dve", [E], [E], lambda e: e.tensor_scalar(out=E[:], in0=E[:], scalar1=1.0, scalar2=None, op0=ALU.add))
            S.op("act", [E], [E], lambda e: e.activation(out=E[:], in_=E[:], func=AF.Ln))
            es = S.sb("es", [128, 2, NT, 4], F32, ph)
            eb = S.sb("eb", [128, 2, NT, 4], F32, ph)
            edec = S.sb("edec", [128, 2, NT, 4], F32, ph)
            ekw = S.sb("ekw", [128, 2, NT, 4], F32, ph)
            tg = S.sb("tg", [128, NT, 4], F32, ph)
            f72 = lambda ap: ap.rearrange("p t h -> p (t h)")
            trib = S.sb("trib", [128, 2, 128], BF16, ph)
            S.op("dve", [cst], [trib], lambda e: e.tensor_copy(out=trib[:], in_=cst[:, 128:384].rearrange("p (a b) -> p a b", a=2)))
            Ehi = S.sb("Ehi", [128, 2, NT, 4], BF16, ph)
            Ehf = S.sb("Ehf", [128, 2, NT, 4], F32, ph)
            Elo = S.sb("Elo", [128, 2, NT, 4], BF16, ph)
            S.op("dve", [E], [Ehi], lambda e: e.tensor_copy(out=Ehi[:], in_=E[:]))
            S.op("dve", [Ehi], [Ehf], lambda e: e.tensor_copy(out=Ehf[:], in_=Ehi[:]))
            S.op("dve", [E, Ehf], [Ehf], lambda e: e.tensor_tensor(out=Ehf[:], in0=E[:], in1=Ehf[:], op=ALU.subtract))
            S.op("dve", [Ehf], [Elo], lambda e: e.tensor_copy(out=Elo[:], in_=Ehf[:]))
            for d in range(2):
                psb = psO.get()
                S.op("pe", [trib, Ehi], [psb], lambda e: e.matmul(psb[:, 0:72], lhsT=trib[:, d, :], rhs=f72(Ehi[:, d]), start=True, stop=False))
                S.op("pe", [trib, Elo], [psb], lambda e: e.matmul(psb[:, 0:72], lhsT=trib[:, d, :], rhs=f72(Elo[:, d]), start=False, stop=True))
                S.op("pe", [ones_b, Ehi], [psb], lambda e: e.matmul(psb[:, 72:144], lhsT=ones_b[:], rhs=f72(Ehi[:, d]), start=True, stop=False))
                S.op("pe", [ones_b, Elo], [psb], lambda e: e.matmul(psb[:, 72:144], lhsT=ones_b[:], rhs=f72(Elo[:, d]), start=False, stop=True))
                S.op("dve", [psb, G], [tg], lambda e: e.tensor_tensor(out=tg[:], in0=psb[:, 0:72].rearrange("p (t h) -> p t h", h=4), in1=G[:, :, 8 * d:8 * d + 4], op=ALU.add))
                S.op("act", [tg], [es], lambda e: e.activation(out=es[:, d], in_=tg[:], func=AF.Exp))
                S.op("act", [psb], [eb], lambda e: e.activation(out=f72(eb[:, d]), in_=psb[:, 0:72], func=AF.Exp, scale=-1.0))
                S.op("act", [psb], [edec], lambda e: e.activation(out=f72(edec[:, d]), in_=psb[:, 72:144], func=AF.Exp, scale=-1.0))
                S.op("dve", [es, edec], [ekw], lambda e: e.tensor_tensor(out=ekw[:, d], in0=es[:, d], in1=edec[:, d], op=ALU.mult))

            pass
            pass
            pass
            pass
            pass
            import os
            KS_ = os.environ.get("KSUB", "")
            KH_ = int(os.environ.get("KHEAD", "0"))
            if KS_ == "m1":
                return
            qT = S.sb("qTa", [128, NT * 128], BF16, ph)
            kT = S.sb("kTa", [128, NT * 128], BF16, ph)
            ktok = S.sb("ktok", [128, NT, 128], BF16, ph)
            vaug = S.sb("vaug", [128, NT, 130], BF16, ph)
            hsum = [S.sb(f"hsum{i}", [128, 128], F32, ph) for i in range(NT)]
            Cst = [S.sb(f"Cst{d}", [128, 129], F32, ph) for d in range(2)]
            Cbf = [S.sb(f"Cbf{d}", [128, 129], BF16, ph) for d in range(2)]
            PTr = Ring([S.sb(f"PTm{i}", [128, 128], BF16, ph) for i in range(3)])
            kwr = Ring([S.sb(f"kwm{i}", [128, 128], BF16, ph) for i in range(3)])
            smr = Ring([S.sb(f"smm{i}", [128, 4], F32, ph) for i in range(4)])
            hss = S.sb("hss", [128, NT], F32, ph)
            hjunk = S.sb("hjunk", [128, 128], F32, ph)
            ogr = Ring([S.sb(f"og{i}", [128, 128], F32, ph) for i in range(2)])
            t1r = Ring([S.sb(f"mt1{i}", [128, 128], F32, ph) for i in range(2)])
            mxr = Ring([S.sb(f"mmx{i}", [128, 128], BF16, ph) for i in range(2)])
            S.op("dve", [], [vaug], lambda e: e.memset(vaug[:, :, 128:129], 1.0))
            orders = [list(range(NT)), [1, 0] + list(range(NT - 1, 1, -1))]
            KS = 128.0 ** -0.5

            for h in range(4):
                wb = wring.get()
                src = D["even_w"][:, 0:1536].rearrange("(k p) (g h n) -> p k g h n", p=128, g=3, h=4)[:, :, :, h, :]
                wq = wb[:, 0:8 * 384].rearrange("p (k g n) -> p k g n", k=8, g=3)
                for g3 in range(3):
                    S.dma("pool", wb.sem, wq[:, :, g3, :], src[:, :, g3, :], [], [wb])
                wob = wring.get()
                wo = wload(wob, 128, D["even_w"][:, 1536 + h * 128:1536 + (h + 1) * 128].rearrange("(k p) n -> p k n", p=128))
                flip = 0
                for (g, c0, n) in GROUPS:
                    for which, dst, scl in ((0, qT, 1.0), (1, kT, KS)):
                        ps = psA.get()
                        for k in range(8):
                            S.op("pe", [wb, hTg[g]], [ps], lambda e: e.matmul(ps[:, 0:n], lhsT=wq[:, k, which, :], rhs=hTg[g][:, k, 0:n], start=(k == 0), stop=(k == 7)))
                        if flip % 2 == 0:
                            S.op("act", [ps], [dst], lambda e: e.activation(out=dst[:, c0:c0 + n], in_=ps[:, 0:n], func=AF.Copy, scale=scl))
                        else:
                            S.op("dve", [ps], [dst], lambda e: e.tensor_scalar(out=dst[:, c0:c0 + n], in0=ps[:, 0:n], scalar1=scl, scalar2=None, op0=ALU.mult))
                        flip += 1
                if KS_ == "m2a" and h == KH_:
                    return
                for i in range(NT):
                    g, off = tok_group(i)
                    ps = psA.get()
                    for k in range(8):
                        S.op("pe", [wb, hTg[g]], [ps], lambda e: e.matmul(ps[:, 0:256], lhsT=hTg[g][:, k, off:off + 128], rhs=wb[:, k * 384 + 128:k * 384 + 384], start=(k == 0), stop=(k == 7)))
                    if os.environ.get("KSUB2") != "noact":
                        S.op("act", [ps], [ktok], lambda e: e.activation(out=ktok[:, i, :], in_=ps[:, 0:128], func=AF.Copy, scale=KS))
                    if os.environ.get("KSUB2") != "nodve":
                        S.op("dve", [ps], [vaug], lambda e: e.tensor_copy(out=vaug[:, i, 0:128], in_=ps[:, 128:256]))
                if KS_ == "m2b" and h == KH_:
                    return
                if h == 0:
                    pass
                    pass
                    pass
                    pass
                if KS_ == "m2" and h == KH_:
                    return
                written = [False] * NT
                for step in range(NT):
                    for d in range(2):
                        i = orders[d][step]
                        col = lambda a: a[:, d, i, h:h + 1]
                        cs = slice(i * 128, (i + 1) * 128)
                        pss = psS.get()
                        S.op("pe", [kT, qT], [pss], lambda e: e.matmul(pss[:, 0:128], lhsT=kT[:, cs], rhs=qT[:, cs], start=True, stop=True))
                        PT = PTr.get()
                        msk = TRIU() if d == 0 else TRIL()
                        S.op("dve", [pss, es, cst], [PT], lambda e: e.scalar_tensor_tensor(out=PT[:], in0=pss[:, 0:128], scalar=col(es), in1=msk, op0=ALU.mult, op1=ALU.mult))
                        acc = psO.get()
                        if step > 0:
                            S.op("pe", [qT, Cbf[d]], [acc], lambda e: e.matmul(acc[:, 0:129], lhsT=qT[:, cs], rhs=Cbf[d][:], start=True, stop=False))
                        S.op("pe", [PT, vaug], [acc], lambda e: e.matmul(acc[:, 0:129], lhsT=PT[:], rhs=vaug[:, i, 0:129], start=(step == 0), stop=True))
                        sm = smr.get()
                        S.op("act", [acc, eb], [sm], lambda e: e.activation(out=sm[:, 0:1], in_=acc[:, 128:129], func=AF.Abs, scale=col(eb)))
                        S.op("dve", [sm], [sm], lambda e: e.tensor_scalar(out=sm[:, 1:2], in0=sm[:, 0:1], scalar1=1.0, scalar2=None, op0=ALU.max))
                        S.op("dve", [sm], [sm], lambda e: e.reciprocal(out=sm[:, 2:3], in_=sm[:, 1:2]))
                        S.op("dve", [sm, eb], [sm], lambda e: e.tensor_tensor(out=sm[:, 3:4], in0=sm[:, 2:3], in1=col(eb), op=ALU.mult))
                        if not written[i]:
                            S.op("act", [acc, sm], [hsum[i]], lambda e: e.activation(out=hsum[i][:], in_=acc[:, 0:128], func=AF.Copy, scale=sm[:, 3:4]))
                            written[i] = True
                        else:
                            S.op("dve", [acc, sm, hsum[i]], [hsum[i]], lambda e: e.scalar_tensor_tensor(out=hsum[i][:], in0=acc[:, 0:128], scalar=sm[:, 3:4], in1=hsum[i][:], op0=ALU.mult, op1=ALU.add))
                        if step < NT - 1:
                            kw = kwr.get()
                            S.op("pool", [ktok, ekw], [kw], lambda e: e.tensor_scalar(out=kw[:], in0=ktok[:, i, :], scalar1=col(ekw), scalar2=None, op0=ALU.mult))
                            psc = psA.get()
                            S.op("pe", [kw, vaug], [psc], lambda e: e.matmul(psc[:, 0:129], lhsT=kw[:], rhs=vaug[:, i, 0:129], start=True, stop=True))
                            if step == 0:
                                S.op("dve", [psc], [Cst[d]], lambda e: e.tensor_copy(out=Cst[d][:], in_=psc[:, 0:129]))
                            else:
                                S.op("dve", [psc, Cst[d], edec], [Cst[d]], lambda e: e.scalar_tensor_tensor(out=Cst[d][:], in0=Cst[d][:], scalar=col(edec), in1=psc[:, 0:129], op0=ALU.mult, op1=ALU.add))
                            S.op("act", [Cst[d]], [Cbf[d]], lambda e: e.activation(out=Cbf[d][:], in_=Cst[d][:], func=AF.Copy))
                if h == 0:
                    pass
                    pass
                if KS_ == "m3" and h == KH_:
                    return
                S.op("dve", [], [hss], lambda e: e.memset(hss[:], 1.0))
                for i in range(NT):
                    S.op("act", [hsum[i]], [hjunk, hss], lambda e: e.activation(out=hjunk[:], in_=hsum[i][:], func=AF.Square, accum_out=hss[:, i:i + 1]))
                rstd_of(hss, lambda: hss[:], 128)
                for i in range(NT):
                    g, off = tok_group(i)
                    ps = psA.get()
                    for k in range(8):
                        S.op("pe", [wob, hTg[g]], [ps], lambda e: e.matmul(ps[:, 0:128], lhsT=hTg[g][:, k, off:off + 128], rhs=wo[:, k, :], start=(k == 0), stop=(k == 7)))
                    og = ogr.get()
                    S.op("act", [ps], [og], lambda e: e.activation(out=og[:], in_=ps[:, 0:128], func=AF.Sigmoid))
                    t1 = t1r.get()
                    S.op("dve", [hsum[i], hss, headg], [t1], lambda e: e.scalar_tensor_tensor(out=t1[:], in0=hsum[i][:], scalar=hss[:, i:i + 1], in1=headg[:, h * 128:(h + 1) * 128], op0=ALU.mult, op1=ALU.mult))
                    mx = mxr.get()
                    S.op("pool", [t1, og], [mx], lambda e: e.tensor_tensor(out=mx[:], in0=t1[:], in1=og[:], op=ALU.mult))
                    pt = psT.get()
                    S.op("pe", [mx, identb], [pt], lambda e: e.transpose(out=pt[:, 0, :], in_=mx[:], identity=identb[:]))
                    S.op("act", [pt], [mixTa], lambda e: e.activation(out=mixTa[:, h, i * 128:(i + 1) * 128], in_=pt[:, 0, :], func=AF.Copy))
                if KS_ == "m4" and h == KH_:
                    return

        def attn_scores_exp_pv(kv_specs, nheads_per_kv, qT, q_sl, PTr, accs, first, last):
            pass

        def gqa_attn(l, hTg, other, wring, ph):
            wname = "even_w" if l == 0 else "odd_w"
            qc0, kc0 = (2064, 2576) if l == 0 else (0, 512)
            swag = S.sbd(f"swag{l}", [128, 128], F32, ph)
            roper = Ring([S.sbd(f"rope{l}_{i}", [128, 2, 64], F32, ph) for i in range(2)])

            def get_rope(jt):
                rb = roper.get()
                S.dma("sp", rb.sem, rb[:], D["rope"][:, :, jt, :], [], [rb])
                return rb
            S.dma("sp", swag.sem, swag[:], D["swa_g_bc" if l == 0 else "gqa_g_bc"], [], [swag])
            gq = S.sb(f"gq{l}", [128, 64], F32, ph)
            S.op("dve", [swag], [gq], lambda e: e.tensor_scalar(out=gq[:], in0=swag[:, 0:64], scalar1=0.125, scalar2=None, op0=ALU.mult))
            esink = S.sb(f"esink{l}", [128, 8], F32, ph)
            if l == 0:
                sinkb = S.sbd("sinkb", [128, 8], F32, ph)
                S.dma("sp", sinkb.sem, sinkb[:], D["sink_bc"], [], [sinkb])
                S.op("act", [sinkb], [esink], lambda e: e.activation(out=esink[:], in_=sinkb[:], func=AF.Exp))
            else:
                S.op("dve", [], [esink], lambda e: e.memset(esink[:], 0.0))
            wout = S.sbd(f"wout{l}", [128, 8 * 1024], BF16, ph)
            woutv = wout[:, :].rearrange("p (k n) -> p k n", k=8)
            S.dma("pool", wout.sem, woutv, D["w_out"][l].rearrange("(k p) n -> p k n", p=128), [], [wout])
            wkb = S.sbd(f"wkv{l}", [128, 8 * 256], BF16, ph)
            wkv = wload(wkb, 256, D[wname][:, kc0:kc0 + 256].rearrange("(k p) n -> p k n", p=128))
            wqb = S.sbd(f"wqq{l}", [128, 8 * 512], BF16, ph)
            wq = wload(wqb, 512, D[wname][:, qc0:qc0 + 512].rearrange("(k p) n -> p k n", p=128))
            kTd = [S.sb(f"kTd{g}", [128, NT * 128], BF16, ph) for g in range(2)]
            vb = S.sb("vb", [128, NT, 2, 66], BF16, ph)
            S.op("dve", [], [vb], lambda e: e.memset(vb[:, :, :, 64:65], 1.0))
            wk = (S.sb("wk_sq", [128, 512], F32, ph), S.sb("wk_ss", [128, 8], F32, ph), S.sb("wk_qn", [128, 512], F32, ph), S.sb("wk_t1", [128, 512], F32, ph))
            kd = S.sb("kd", [128, 2, 2, 64], BF16, ph)
            kn = S.sb("kn", [128, 2, 64], BF16, ph)
            for i in range(NT):
                g, off = tok_group(i)
                ps = psA.get()
                for k in range(8):
                    S.op("pe", [wkb, hTg[g]], [ps], lambda e: e.matmul(ps[:, 0:256], lhsT=hTg[g][:, k, off:off + 128], rhs=wkv[:, k, :], start=(k == 0), stop=(k == 7)))
                S.op("act", [ps], [vb], lambda e: e.activation(out=vb[:, i, :, 0:64], in_=ps[:, 128:256].rearrange("p (g d) -> p g d", g=2), func=AF.Copy))
                qk_prep(ps, ps[:, 0:128], 2, swag[:, 64:128], (i - 2) if i >= 2 else None, (kn, kn[:]), wk, get_rope(i - 2) if i >= 2 else None)
                for dup in range(2):
                    S.op("pool", [kn], [kd], lambda e: e.tensor_copy(out=kd[:, :, dup, :], in_=kn[:]))
                pt = psT.get()
                for g2 in range(2):
                    S.op("pe", [kd, identb], [pt], lambda e: e.transpose(out=pt[:, g2, :], in_=kd[:, g2].rearrange("p a d -> p (a d)"), identity=identb[:]))
                for g2 in range(2):
                    S.op("act" if g2 == 0 else "dve", [pt], [kTd[g2]],
                         (lambda e: e.activation(out=kTd[0][:, i * 128:(i + 1) * 128], in_=pt[:, 0, :], func=AF.Copy)) if g2 == 0 else
                         (lambda e: e.tensor_copy(out=kTd[1][:, i * 128:(i + 1) * 128], in_=pt[:, 1, :])))
            import os
            KS_ = os.environ.get("KSUB", "")
            if KS_ == "w1":
                return
            qb = S.sb("qb", [128, 8, 64], BF16, ph)
            qz = S.sb("qz", [128, 2, 4, 128], BF16, ph)
            S.op("dve", [], [qz], lambda e: e.memset(qz[:], 0.0))
            wmb4 = S.sb("wmb4", [128, 2, 4, 128], BF16, ph)
            S.op("dve", [wmb], [wmb4], lambda e: e.tensor_copy(out=wmb4[:], in_=wmb[:, :, :].unsqueeze(2).to_broadcast([128, 2, 4, 128])))
            PTr = Ring([S.sb(f"PTw{i}", [128, 512], BF16, ph) for i in range(2)])
            den = S.sb("wden", [128, 8], F32, ph)
            mixb = S.sb("mixb", [128, 512], BF16, ph)
            mixTb = S.sb("mixTb", [128, 4, 128], BF16, ph)
            for i in (range(NT) if l == 0 else range(2, NT)):
                g, off = tok_group(i)
                lat = i >= 2
                j = i - 2
                ps = psA.get()
                for k in range(8):
                    S.op("pe", [wqb, hTg[g]], [ps], lambda e: e.matmul(ps[:, 0:512], lhsT=hTg[g][:, k, off:off + 128], rhs=wq[:, k, :], start=(k == 0), stop=(k == 7)))
                qk_prep(ps, ps[:, 0:512], 8, gq[:], j if lat else None, (qb, qb[:]), wk, get_rope(j) if lat else None)
                pt = psT.get()
                for pr in range(4):
                    S.op("pe", [qb, identb], [pt], lambda e: e.transpose(out=pt[:, pr, :], in_=qb[:, 2 * pr:2 * pr + 2, :].rearrange("p a d -> p (a d)"), identity=identb[:]))
                S.op("act", [pt], [qz], lambda e: e.activation(out=qz[0:64, 0, :, :], in_=pt[0:64, 0:4, :], func=AF.Copy))
                S.op("dve", [pt], [qz], lambda e: e.tensor_copy(out=qz[64:128, 1, :, :], in_=pt[64:128, 0:4, :]))
                if KS_ == "w2a":
                    return
                if l == 1:
                    blocks = [(m, None) for m in range(NT)]
                elif lat:
                    blocks = [(0, None), (1, None)]
                    if j > 0:
                        blocks.append((i - 1, 0))
                    blocks.append((i, None))
                    if j < 15:
                        blocks.append((i + 1, 1))
                else:
                    blocks = [(0, None), (1, None)]
                for g2 in range(2):
                    acc = psO.get()

                    def emit_scores(m, msk):
                        pss = psS.get()
                        for half in range(2):
                            S.op("pe", [kTd[g2], qz], [pss], lambda e: e.matmul(
                                pss[:, half * 256:(half + 1) * 256], lhsT=kTd[g2][:, m * 128:(m + 1) * 128],
                                rhs=qz[:, half, 2 * g2:2 * g2 + 2, :].rearrange("p a q -> p (a q)"),
                                start=(half == 0), stop=(half == 1 and msk is None)))
                        if msk is not None:
                            S.op("pe", [identb, wmb4], [pss], lambda e: e.matmul(pss[:, 0:512], lhsT=identb[:], rhs=wmb4[:, msk, :, :].rearrange("p a q -> p (a q)"), start=False, stop=True))
                        return pss

                    nxt = emit_scores(*blocks[0])
                    for bi, (m, msk) in enumerate(blocks):
                        pss = nxt
                        if bi + 1 < len(blocks):
                            nxt = emit_scores(*blocks[bi + 1])
                        PT = PTr.get()
                        S.op("act", [pss], [PT], lambda e: e.activation(out=PT[:], in_=pss[:], func=AF.Exp))
                        for hh in range(4):
                            S.op("pe", [PT, vb], [acc], lambda e: e.matmul(acc[:, hh * 128:hh * 128 + 65], lhsT=PT[:, hh * 128:(hh + 1) * 128], rhs=vb[:, m, g2, 0:65], start=(bi == 0 and hh == 0), stop=(bi == len(blocks) - 1)))
                    if KS_ == "w2c":
                        return
                    a3 = acc[:, :].rearrange("p (h c) -> p h c", h=4)
                    S.op("dve", [acc, esink], [den], lambda e: e.tensor_tensor(out=den[:, g2 * 4:(g2 + 1) * 4].rearrange("p (b a) -> p b a", b=2), in0=a3[:, :, 64].rearrange("p (b a) -> p b a", b=2),
                                                                            in1=esink[:, g2 * 4:(g2 + 1) * 4].rearrange("p (a b) -> p b a", a=2), op=ALU.add))
                    S.op("dve", [den], [den], lambda e: e.reciprocal(out=den[:, g2 * 4:(g2 + 1) * 4], in_=den[:, g2 * 4:(g2 + 1) * 4]))
                    S.op("dve", [acc, den], [mixb], lambda e: e.tensor_tensor(
                        out=mixb[:, g2 * 256:(g2 + 1) * 256].rearrange("p (a b d) -> p b a d", a=2, b=2), in0=a3[:, :, 0:64].rearrange("p (b a) d -> p b a d", b=2),
                        in1=den[:, g2 * 4:(g2 + 1) * 4].rearrange("p (b a) -> p b a", b=2).unsqueeze(3).to_broadcast([128, 2, 2, 64]), op=ALU.mult))
                if KS_ == "w2d":
                    return
                pt2 = psT.get()
                for c in range(4):
                    S.op("pe", [mixb, identb], [pt2], lambda e: e.transpose(out=pt2[:, c, :], in_=mixb[:, c * 128:(c + 1) * 128], identity=identb[:]))
                S.op("act", [pt2], [mixTb], lambda e: e.activation(out=mixTb[:], in_=pt2[:, 0:4, :], func=AF.Copy))
                if KS_ == "w2" and i == 2:
                    return
                for cgi in range(2):
                    pso = psA.get()
                    for k in range(8):
                        if l == 0:
                            lhs = other[:, k, i * 128:(i + 1) * 128] if k < 4 else mixTb[:, k - 4, :]
                        else:
                            lhs = mixTb[:, k, :] if k < 4 else other[:, k - 4, j * 128:(j + 1) * 128]
                        S.op("pe", [other, mixTb, wout], [pso], lambda e: e.matmul(pso[:, 0:512], lhsT=lhs, rhs=woutv[:, k, cgi * 512:(cgi + 1) * 512], start=(k == 0), stop=(k == 7)))
                    residual(i, pso, cgi, 0 if lat else 1)

        def ffn_phase(l, tiles):
            tiles = list(tiles)
            with ExitStack() as ph:
                hTg = [S.sb(f"hF{l}_0", [128, 8, 256], BF16, ph)] + [S.sb(f"hF{l}_{g}", [128, 8, 512], BF16, ph) for g in range(1, 5)]
                with ExitStack() as ph2:
                    norm_phase(l, 1, hTg, ph2, tiles)
                    mk_gate(l, 40, ph2)
                    S.barrier()
                segs = [gg for gg in GROUPS if (gg[0] > 0 or 0 in tiles)]
                lo = segs[0][1]
                ranges = ([(0, 256)] if lo == 0 else []) + [(256, 2304)]
                GS = 3
                ur = Ring([S.sb(f"fu{l}_{i}", [128, 2304], F32, ph) for i in range(2)])
                yr = Ring([S.sb(f"fy{l}_{i}", [128, 2304], F32, ph) for i in range(2)])
                actT = S.sb(f"actT{l}", [128, GS, 2304], BF16, ph)
                wur = Ring([S.sbd(f"wu{l}_{i}", [128, 8 * 256], BF16, ph) for i in range(3)])
                wdr = Ring([S.sbd(f"wd{l}_{i}", [128, GS * 1024], BF16, ph) for i in range(2)])
                upsrc = D["ffn_up"][l].rearrange("(k p) (g c n) -> p k g c n", p=128, g=2, c=22)
                wu_loaded = {}
                wd_loaded = {}

                def load_wu(cp):
                    if cp >= 22 or cp in wu_loaded:
                        return
                    wub = wur.get()
                    wu = wub[:, :].rearrange("p (k g n) -> p k g n", k=8, g=2)
                    for g3 in range(2):
                        S.dma("pool", wub.sem, wu[:, :, g3, :], upsrc[:, :, g3, cp, :], [], [wub])
                    wu_loaded[cp] = (wub, wu)

                def load_wd(c0):
                    if c0 >= 22 or c0 in wd_loaded:
                        return
                    ncg = min(GS, 22 - c0)
                    wdb = wdr.get()
                    wd = wdb[:, 0:ncg * 1024].rearrange("p (c n) -> p c n", c=ncg)
                    S.dma("pool", wdb.sem, wd, D["ffn_down"][l, c0 * 128:(c0 + ncg) * 128, :].rearrange("(c p) n -> p c n", p=128), [], [wdb])
                    wd_loaded[c0] = (wdb, wd)

                load_wu(0)
                load_wu(1)
                load_wd(0)
                for c0 in range(0, 22, GS):
                    ncg = min(GS, 22 - c0)
                    wdb, wd = wd_loaded[c0]
                    for ci in range(ncg):
                        cp = c0 + ci
                        load_wu(cp + 2)
                        if ci == 0:
                            load_wd(c0 + GS)
                        wub, wu = wu_loaded[cp]
                        ys = []
                        for gv in range(2):
                            ch = gv * 22 + cp
                            u = ur.get()
                            y = yr.get()
                            w0 = cw[:, l, 0, ch:ch + 1]
                            w1 = cw[:, l, 1, ch:ch + 1]
                            w2 = cw[:, l, 2, ch:ch + 1]
                            for (g, t0, n) in segs:
                                ps = psA.get()
                                for k in range(8):
                                    S.op("pe", [wub, hTg[g]], [ps], lambda e: e.matmul(ps[:, 0:n], lhsT=wu[:, k, gv, :], rhs=hTg[g][:, k, 0:n], start=(k == 0), stop=(k == 7)))
                                S.op("act", [ps], [u], lambda e: e.activation(out=u[:, t0:t0 + n], in_=ps[:, 0:n], func=AF.Copy))
                            S.op("act", [u, cw, cb], [y], lambda e: e.activation(out=y[:, lo:2304], in_=u[:, lo:2304], func=AF.Identity, scale=w1, bias=cb[:, l, ch:ch + 1]))
                            for (a, b_) in ranges:
                                S.op("dve", [u, cw, y], [y], lambda e: e.scalar_tensor_tensor(out=y[:, a + 1:b_], in0=u[:, a:b_ - 1], scalar=w0, in1=y[:, a + 1:b_], op0=ALU.mult, op1=ALU.add))
                                S.op("dve", [u, cw, y], [y], lambda e: e.scalar_tensor_tensor(out=y[:, a:b_ - 1], in0=u[:, a + 1:b_], scalar=w2, in1=y[:, a:b_ - 1], op0=ALU.mult, op1=ALU.add))
                            ys.append(y)
                        S.op("act", [ys[0]], [ys[0]], lambda e: e.activation(out=ys[0][:, lo:2304], in_=ys[0][:, lo:2304], func=AF.Silu))
                        S.op("pool", [ys[0], ys[1]], [actT], lambda e: e.tensor_tensor(out=actT[:, ci, lo:2304], in0=ys[0][:, lo:2304], in1=ys[1][:, lo:2304], op=ALU.mult))
                    for i in tiles:
                        for cgi in range(2):
                            ps = psO.get()
                            for ci in range(ncg):
                                S.op("pe", [actT, wdb], [ps], lambda e: e.matmul(ps[:, 0:512], lhsT=actT[:, ci, i * 128:(i + 1) * 128], rhs=wd[:, ci, cgi * 512:(cgi + 1) * 512], start=(ci == 0), stop=(ci == ncg - 1)))
                            residual(i, ps, cgi, 0 if i >= 2 else 1)
                S.barrier()

        def prep_batch(raw, sq, ss, T, nh, g_ap, out_buf, out_ap):
            n = T * nh
            r3 = raw[:, 0:T, :].rearrange("p t (h d) -> p (t h) d", d=64)
            s3 = sq[:, 0:T, :].rearrange("p t (h d) -> p (t h) d", d=64)
            S.op("act", [raw], [sq], lambda e: e.activation(out=sq[:, 0:T, :], in_=raw[:, 0:T, :], func=AF.Square))
            S.op("dve", [sq], [ss], lambda e: e.tensor_reduce(out=ss[:, 0:n], in_=s3, axis=AX.X, op=ALU.add))
            rstd_of(ss, lambda: ss[:, 0:n], 64)
            S.op("dve", [raw, ss], [raw], lambda e: e.tensor_tensor(out=r3, in0=r3, in1=ss[:, 0:n].unsqueeze(2).to_broadcast([128, n, 64]), op=ALU.mult))
            S.op("pool", [raw], [out_buf], lambda e: e.tensor_tensor(out=out_ap.rearrange("p t (h d) -> p (t h) d", d=64), in0=r3, in1=g_ap.unsqueeze(1).to_broadcast([128, n, 64]), op=ALU.mult))

        def na_attn(hTg, mixTd, wring, ph):
            nag = S.sbd("nag", [128, 128], F32, ph)
            S.dma("sp", nag.sem, nag[:], D["na_g_bc"], [], [nag])
            gq = S.sb("nagq", [128, 64], F32, ph)
            S.op("dve", [nag], [gq], lambda e: e.tensor_scalar(out=gq[:], in0=nag[:, 0:64], scalar1=0.125, scalar2=None, op0=ALU.mult))
            kTn = S.sb("kTn", [128, NT * 128], BF16, ph)
            vn = S.sb("vn", [128, NT, 2, 66], BF16, ph)
            qTn = S.sb("qTn", [128, 2048], BF16, ph)
            S.op("dve", [], [vn], lambda e: e.memset(vn[:, :, :, 64:65], 1.0))
            TB = 9
            raw = S.sb("nraw", [128, TB, 128], F32, ph)
            sq = S.sb("nsq", [128, TB, 128], F32, ph)
            ssb = S.sb("nss", [128, TB * 2], F32, ph)
            nrm = S.sb("nnrm", [128, TB, 128], BF16, ph)
            biasr = Ring([S.sbd(f"nbias{i}", [128, 25, 128], F32, ph) for i in range(1)])
            stmp = Ring([S.sb(f"nstmp{i}", [128, 5, 128], F32, ph) for i in range(2)])
            PTr = Ring([S.sb(f"PTn{i}", [128, 7, 128], BF16, ph) for i in range(3)])
            rdn = Ring([S.sb(f"nrd{i}", [128, 1], F32, ph) for i in range(3)])
            mixd = S.sb("mixd", [128, 16, 2, 64], BF16, ph)
            naS = Ring(psS.bufs + psA.bufs)
            wsrc = D["odd_w"][:, 768:2304].rearrange("(k p) (g h n) -> p k g h n", p=128, g=3, h=4)
            wl = {}

            def load_w(pr):
                if pr >= 4 or pr in wl:
                    return
                wb = wring.get()
                wq = wb[:, 0:8 * 384].rearrange("p (k g n) -> p k g n", k=8, g=3)
                for g3 in range(3):
                    S.dma("pool", wb.sem, wq[:, :, g3, :], wsrc[:, :, g3, pr, :], [], [wb])
                wl[pr] = (wb, wq)

            load_w(0)
            for pr in range(4):
                wb, wq = wl[pr]
                load_w(pr + 1)
                for t0 in range(0, NT, TB):
                    tl = list(range(t0, min(NT, t0 + TB)))
                    for i in tl:
                        g, off = tok_group(i)
                        ps = psA.get()
                        for k in range(8):
                            S.op("pe", [wb, hTg[g]], [ps], lambda e: e.matmul(ps[:, 0:256], lhsT=hTg[g][:, k, off:off + 128], rhs=wb[:, k * 384 + 128:k * 384 + 384], start=(k == 0), stop=(k == 7)))
                        S.op("act", [ps], [vn], lambda e: e.activation(out=vn[:, i, :, 0:64], in_=ps[:, 128:256].rearrange("p (g d) -> p g d", g=2), func=AF.Copy))
                        S.op("dve", [ps], [raw], lambda e: e.tensor_copy(out=raw[:, i - t0, :], in_=ps[:, 0:128]))
                    prep_batch(raw, sq, ssb, len(tl), 2, nag[:, 64:128], nrm, nrm[:, 0:len(tl), :])
                    for i in tl:
                        pt = psT.get()
                        S.op("pe", [nrm, identb], [pt], lambda e: e.transpose(out=pt[:, 0, :], in_=nrm[:, i - t0, :], identity=identb[:]))
                        S.op("act", [pt], [kTn], lambda e: e.activation(out=kTn[:, i * 128:(i + 1) * 128], in_=pt[:, 0, :], func=AF.Copy))
                for t0 in range(2, NT, 8):
                    tl = list(range(t0, t0 + 8))
                    for i in tl:
                        g, off = tok_group(i)
                        ps2 = psA.get()
                        for k in range(8):
                            S.op("pe", [wb, hTg[g]], [ps2], lambda e: e.matmul(ps2[:, 0:128], lhsT=hTg[g][:, k, off:off + 128], rhs=wq[:, k, 0, :], start=(k == 0), stop=(k == 7)))
                        S.op("dve", [ps2], [raw], lambda e: e.tensor_copy(out=raw[:, i - t0, :], in_=ps2[:, 0:128]))
                    prep_batch(raw, sq, ssb, 8, 2, gq[:], nrm, nrm[:, 0:8, :])
                    for i in tl:
                        j = i - 2
                        pt2 = psT.get()
                        S.op("pe", [nrm, identb], [pt2], lambda e: e.transpose(out=pt2[:, 0, :], in_=nrm[:, i - t0, :], identity=identb[:]))
                        S.op("dve", [pt2], [qTn], lambda e: e.tensor_copy(out=qTn[:, j * 128:(j + 1) * 128], in_=pt2[:, 0, :]))
                for hh in range(2):
                    head = 2 * pr + hh
                    bt = biasr.get()
                    S.dma("sp", bt.sem, bt[:], D["na_bias"][head], [], [bt])
                    prs = slice(hh * 64, (hh + 1) * 64)

                    def blocks_of(j):
                        ci = 0 if j == 0 else 1 if j == 1 else 3 if j == 14 else 4 if j == 15 else 2
                        mlist = list(range(j - 2, j + 3)) if ci == 2 else na_blocks[j]
                        return ci, len(mlist), [0, 1] + [m + 2 for m in mlist]

                    def emit_scores(j):
                        ci, nb, keyt = blocks_of(j)
                        pA = naS.get()
                        pB = naS.get()
                        for bi, kt in enumerate(keyt):
                            pp, off2 = (pA, bi) if bi < 4 else (pB, bi - 4)
                            S.op("pe", [kTn, qTn], [pp], lambda e: e.matmul(pp[:, off2 * 128:(off2 + 1) * 128], lhsT=kTn[prs, kt * 128:(kt + 1) * 128], rhs=qTn[prs, j * 128:(j + 1) * 128], start=True, stop=True))
                        return pA, pB

                    nxt = emit_scores(0)
                    for j in range(16):
                        ci, nb, keyt = blocks_of(j)
                        pA, pB = nxt
                        if j + 1 < 16:
                            nxt = emit_scores(j + 1)
                        stp = stmp.get()
                        S.op("dve", [pA, bt], [stp], lambda e: e.tensor_tensor(out=stp[:, 0:2, :], in0=pA[:, 256:512].rearrange("p (b q) -> p b q", b=2), in1=bt[:, ci * 5:ci * 5 + 2, :], op=ALU.add))
                        S.op("dve", [pB, bt], [stp], lambda e: e.tensor_tensor(out=stp[:, 2:nb, :], in0=pB[:, 0:(nb - 2) * 128].rearrange("p (b q) -> p b q", b=nb - 2), in1=bt[:, ci * 5 + 2:ci * 5 + nb, :], op=ALU.add))
                        PT = PTr.get()
                        S.op("act", [pA], [PT], lambda e: e.activation(out=PT[:, 0:2, :], in_=pA[:, 0:256].rearrange("p (b q) -> p b q", b=2), func=AF.Exp))
                        S.op("act", [stp], [PT], lambda e: e.activation(out=PT[:, 2:2 + nb, :], in_=stp[:, 0:nb, :], func=AF.Exp))
                        acc = psO.get()
                        for bi, kt in enumerate(keyt):
                            S.op("pe", [PT, vn], [acc], lambda e: e.matmul(acc[:, 0:65], lhsT=PT[:, bi, :], rhs=vn[:, kt, hh, 0:65], start=(bi == 0), stop=(bi == len(keyt) - 1)))
                        rd = rdn.get()
                        S.op("dve", [acc], [rd], lambda e: e.reciprocal(out=rd[:], in_=acc[:, 64:65]))
                        S.op("act", [acc, rd], [mixd], lambda e: e.activation(out=mixd[:, j, hh, :], in_=acc[:, 0:64], func=AF.Copy, scale=rd[:, 0:1]))
                for j in range(16):
                    pt = psT.get()
                    S.op("pe", [mixd, identb], [pt], lambda e: e.transpose(out=pt[:, 0, :], in_=mixd[:, j, :, :].rearrange("p a d -> p (a d)"), identity=identb[:]))
                    S.op("dve", [pt], [mixTd], lambda e: e.tensor_copy(out=mixTd[:, pr, j * 128:(j + 1) * 128], in_=pt[:, 0, :]))

        def mixer1():
            with ExitStack() as ph:
                hTg = [S.sb("hT1_0", [128, 8, 256], BF16, ph)] + [S.sb(f"hT1_{g}", [128, 8, 512], BF16, ph) for g in range(1, 5)]
                with ExitStack() as ph2:
                    norm_phase(1, 0, hTg, ph2)
                    mk_gate(1, 16, ph2)
                    S.barrier()
                mixTd = S.sb("mixTd", [128, 4, 2048], BF16, ph)
                with ExitStack() as ph2:
                    wring = Ring([S.sbd(f"w1_{i}", [128, 8 * 384], BF16, ph2) for i in range(2)])
                    na_attn(hTg, mixTd, wring, ph2)
                    S.barrier()
                with ExitStack() as ph2:
                    gqa_attn(1, hTg, mixTd, None, ph2)
                    S.barrier()

        mod_phase(0)
        if stage != "mod":
            mixer0()
        if stage not in ("l0mix", "mod", "norm", "mlstm"):
            ffn_phase(0, range(NT))
        if stage not in ("l0mix", "l0", "mod", "norm", "mlstm"):
            mod_phase(1)
            mixer1()
            if stage != "l1mix":
                ffn_phase(1, range(2, NT))
        osem = S.newsem("d_out")
        for i in range(2, NT):
            S.dma("sp", osem, out[(i - 2) * 128:(i - 1) * 128, :], xs[i][:], [xs[i]], [])
        if dbg:
            for i in range(2):
                S.dma("sp", osem, octx[i * 128:(i + 1) * 128, :], xs[i][:], [xs[i]], [])
        S._need("sp", osem, S.cnt[osem])
        S.barrier()
        print(f"[kernel] instructions={S.ninst} waits={S.nwait} sems={len(S.sems)}", flush=True)
    return nc


_CACHE = {}


def kernel(**inputs):
    shared, percore = host_prepare({k: np.asarray(v) for k, v in inputs.items()})
    if "nc" not in _CACHE:
        _CACHE["nc"] = build_program("full")
    nc = _CACHE["nc"]
    in_maps = []
    for b in range(8):
        m = dict(shared)
        m.update(percore[b])
        in_maps.append(m)
    res = run_bass_kernel_spmd(nc, in_maps, core_ids=list(range(8)))
    return np.stack([np.asarray(r["out"], np.float32) for r in res.results], axis=0)
```

```python
import numpy as np
from contextlib import ExitStack
import concourse.bass as bass
import concourse.mybir as mybir
from concourse.bass_utils import run_bass_kernel_spmd

F32 = mybir.dt.float32
BF16 = mybir.dt.bfloat16
AF = mybir.ActivationFunctionType
ALU = mybir.AluOpType
AX = mybir.AxisListType

ENGS = ("pe", "act", "dve", "pool", "sp")
NT = 18
EPS = 1e-6
NEGM = -30000.0


class Buf:
    __slots__ = ("t", "name", "w", "r", "sem", "psum")

    def __init__(self, t, name):
        self.t = t
        self.name = name
        self.w = None
        self.r = {}
        self.sem = None
        self.psum = False

    def __getitem__(self, idx):
        return self.t[idx]


class Ring:
    def __init__(self, bufs):
        self.bufs = bufs
        self.i = 0

    def get(self):
        b = self.bufs[self.i % len(self.bufs)]
        self.i += 1
        return b


SEM_LIMIT = 1500


class Sched:
    def __init__(self, nc, stack):
        self.nc = nc
        self.stack = stack
        self.eng = {"pe": nc.tensor, "act": nc.scalar, "dve": nc.vector,
                    "pool": nc.gpsimd, "sp": nc.sync}
        self.sems = {}
        self.cnt = {}
        self.epoch = {}
        self.cur = {}
        for e in ENGS:
            self.epoch[e] = 0
            self._new_epoch(e)
        self.seen = {e: {} for e in ENGS}
        self.ninst = 0
        self.nwait = 0
        self.nsem = 0
        self.nalloc = 0

    def _new_epoch(self, e):
        self.epoch[e] += 1
        key = f"{e}#{self.epoch[e]}"
        self.sems[key] = self.stack.enter_context(self.nc.semaphore("s_" + key.replace("#", "_")))
        self.cnt[key] = 0
        self.cur[e] = key

    def sb(self, name, shape, dt, stack=None):
        self.nalloc += 1
        name = f"{name}_{self.nalloc}"
        t = (stack or self.stack).enter_context(self.nc.sbuf_tensor(name, list(shape), dt))
        return Buf(t, name)

    def ps(self, name, shape, dt=F32):
        t = self.stack.enter_context(self.nc.psum_tensor(name, list(shape), dt))
        b = Buf(t, name)
        b.psum = True
        return b

    def newsem(self, name=None):
        self.nsem += 1
        name = name or f"d{self.nsem}"
        s = self.stack.enter_context(self.nc.semaphore(name))
        self.sems[name] = s
        self.cnt[name] = 0
        return name

    def sbd(self, name, shape, dt, stack=None):
        b = self.sb(name, shape, dt, stack)
        b.sem = self.newsem("d_" + name)
        return b

    @staticmethod
    def _eng_of(key):
        return key.split("#")[0] if "#" in key else None

    def _need(self, e, key, val):
        if self.seen[e].get(key, 0) >= val:
            return
        ke = self._eng_of(key)
        if ke is not None:
            ep = int(key.split("#")[1])
            for k2, v2 in self.seen[e].items():
                if v2 > 0 and self._eng_of(k2) == ke and int(k2.split("#")[1]) > ep:
                    return
        self.seen[e][key] = val
        self.eng[e].wait_ge(self.sems[key], val)
        self.nwait += 1

    def deps(self, e, reads, writes):
        for b in reads:
            if b.w is not None:
                k, v = b.w
                if not (self._eng_of(k) == e and e == "pe"):
                    self._need(e, k, v)
        for b in writes:
            if b.w is not None:
                k, v = b.w
                if self._eng_of(k) != e:
                    self._need(e, k, v)
            for k, v in b.r.items():
                if self._eng_of(k) != e:
                    self._need(e, k, v)

    def op(self, e, reads, writes, fn):
        pr = [b for b in reads if b.psum]
        if pr:
            reads = [b for b in reads if not b.psum]
            writes = list(writes) + [b for b in pr if b not in writes]
        self.deps(e, reads, writes)
        ins = fn(self.eng[e])
        if self.cnt[self.cur[e]] >= SEM_LIMIT:
            self._new_epoch(e)
        key = self.cur[e]
        self.cnt[key] += 1
        ins.then_inc(self.sems[key], 1)
        v = self.cnt[key]
        for b in reads:
            for k2 in [k2 for k2 in b.r if self._eng_of(k2) == e]:
                del b.r[k2]
            b.r[key] = v
        for b in writes:
            b.w = (key, v)
            b.r = {}
        self.ninst += 1
        return ins

    def dma(self, q, semkey, out_ap, in_ap, reads, writes, **kw):
        self.deps(q, reads, writes)
        ins = self.eng[q].dma_start(out=out_ap, in_=in_ap, **kw)
        self.cnt[semkey] += 16
        assert self.cnt[semkey] <= 2000, semkey
        ins.then_inc(self.sems[semkey], 16)
        v = self.cnt[semkey]
        for b in reads:
            b.r[semkey] = v
        for b in writes:
            b.w = (semkey, v)
            b.r = {}
        self.ninst += 1
        return ins

    def barrier(self):
        for e in ENGS:
            for k, v in list(self.cnt.items()):
                ke = self._eng_of(k)
                if ke == e or v == 0:
                    continue
                if ke is not None and k != self.cur[ke]:
                    if not (self.cnt[self.cur[ke]] == 0 and int(k.split("#")[1]) == self.epoch[ke] - 1):
                        continue
                self._need(e, k, v)


def _rope_tables():
    t = np.arange(2048)
    row = (t // 64).astype(np.float32)
    col = (t % 64).astype(np.float32)
    half = 32
    freq = (np.float32(10000.0) ** (-np.arange(0, half, 2, dtype=np.float32) / np.float32(half))).astype(np.float32)
    ang_r = row[:, None] * freq[None, :]
    ang_c = col[:, None] * freq[None, :]
    ang = np.concatenate([ang_r, ang_r, ang_c, ang_c], axis=-1).astype(np.float32)
    cos = np.cos(ang).astype(np.float32)
    sin = np.sin(ang).astype(np.float32)
    sgn = np.ones(64, np.float32)
    sgn[0:16] = -1.0
    sgn[32:48] = -1.0
    sinS = sin * sgn[None, :]
    cos = cos.reshape(16, 128, 64).transpose(1, 0, 2).copy()
    sinS = sinS.reshape(16, 128, 64).transpose(1, 0, 2).copy()
    return cos, sinS


def _na_tables(rpb):
    rows = 32
    wr = 8
    r = np.arange(rows)
    row_start = np.clip(r - wr // 2, 0, rows - wr)
    col = np.arange(64)
    col_start = np.clip(col - 8, 0, 48)
    col_ok = (col[None, :] >= col_start[:, None]) & (col[None, :] < col_start[:, None] + 16)
    dc = np.clip(col[None, :] - col[:, None] + 15, 0, 30)
    classes = [0, 1, 2, 14, 15]
    blocks = {}
    tab = np.full((8, 128, 25, 128), NEGM, np.float32)
    for ci, j in enumerate(classes):
        qrows = [2 * j, 2 * j + 1]
        lo = min(row_start[q] for q in qrows)
        hi = max(row_start[q] + wr - 1 for q in qrows)
        mlist = list(range(lo // 2, hi // 2 + 1))
        assert len(mlist) <= 5
        blocks[j] = mlist
        for si, m in enumerate(mlist):
            for kr in range(2):
                krow = 2 * m + kr
                for qr in range(2):
                    qrow = qrows[qr]
                    if not (row_start[qrow] <= krow < row_start[qrow] + wr):
                        continue
                    dr = krow - qrow + 7
                    sub = rpb[:, dr, :][:, dc]
                    sub = np.where(col_ok[None], sub, np.float32(NEGM))
                    tab[:, kr * 64:(kr + 1) * 64, ci * 5 + si, qr * 64:(qr + 1) * 64] = sub.transpose(0, 2, 1)
    return tab, blocks, classes


def _na_blocks():
    _, blocks, classes = _na_tables(np.zeros((8, 15, 31), np.float32))
    return blocks, classes


def host_prepare(inp):
    f = np.float32
    shared = {}
    shared["ada_w"] = np.ascontiguousarray(inp["ada_w"], f)
    shared["ada_bT"] = np.ascontiguousarray(inp["ada_b"].reshape(2, 48, 128).transpose(2, 0, 1), f)
    shared["norm_gT"] = np.ascontiguousarray(inp["norm_g"].reshape(2, 2, 8, 128).transpose(3, 0, 1, 2), f)
    shared["w_out"] = np.ascontiguousarray(inp["w_out"], f)
    shared["ffn_up"] = np.ascontiguousarray(inp["ffn_up"], f)
    shared["ffn_down"] = np.ascontiguousarray(inp["ffn_down"], f)
    shared["conv_wT"] = np.ascontiguousarray(inp["ffn_conv_w"].reshape(2, 3, 44, 128).transpose(3, 0, 1, 2), f)
    shared["conv_bT"] = np.ascontiguousarray(inp["ffn_conv_b"].reshape(2, 44, 128).transpose(2, 0, 1), f)
    shared["even_w"] = np.ascontiguousarray(inp["even_w_in"][0], f)
    shared["odd_w"] = np.ascontiguousarray(inp["odd_w_in"][0], f)
    bc = lambda a: np.ascontiguousarray(np.broadcast_to(np.asarray(a, f).reshape(1, -1), (128, a.size)))
    shared["gate_b_bc"] = bc(inp["mlstm_gate_b"][0])
    shared["head_g_bc"] = bc(inp["mlstm_head_g"][0])
    shared["swa_g_bc"] = bc(inp["swa_qk_g"][0])
    shared["sink_bc"] = bc(inp["swa_sink"][0])
    shared["gqa_g_bc"] = bc(inp["gqa_qk_g"][0])
    shared["na_g_bc"] = bc(inp["na_qk_g"][0])
    tab, _, _ = _na_tables(np.asarray(inp["na_rpb"][0], f))
    shared["na_bias"] = tab
    ident = np.eye(128, dtype=f)
    s = np.arange(128)
    triU = (s[:, None] <= s[None, :]).astype(f)
    triL = (s[:, None] >= s[None, :]).astype(f)
    wm = np.zeros((128, 2, 128), f)
    wm[:, 0, :] = np.where(s[None, :] <= s[:, None], 0.0, NEGM)
    wm[:, 1, :] = np.where(s[:, None] <= s[None, :], 0.0, NEGM)
    shared["consts"] = np.ascontiguousarray(np.concatenate([ident, triU, triL, wm.reshape(128, 256)], axis=1))
    cos, sinS = _rope_tables()
    shared["rope"] = np.ascontiguousarray(np.stack([cos, sinS], axis=1))
    percore = []
    for b in range(8):
        cc = np.stack([inp["c"][b].reshape(8, 128).T, inp["c_ctx"].reshape(8, 128).T], axis=-1)
        percore.append({"x": np.ascontiguousarray(inp["x"][b], f), "ctx": np.ascontiguousarray(inp["ctx"][b], f),
                        "cc": np.ascontiguousarray(cc, f)})
    return shared, percore


SHARED_SHAPES = {
    "ada_w": [2, 1024, 6144], "ada_bT": [128, 2, 48], "norm_gT": [128, 2, 2, 8], "w_out": [2, 1024, 1024],
    "ffn_up": [2, 1024, 5632], "ffn_down": [2, 2816, 1024], "conv_wT": [128, 2, 3, 44], "conv_bT": [128, 2, 44],
    "even_w": [1024, 2832], "odd_w": [1024, 2304], "gate_b_bc": [128, 16], "head_g_bc": [128, 512],
    "swa_g_bc": [128, 128], "sink_bc": [128, 8], "gqa_g_bc": [128, 128], "na_g_bc": [128, 128],
    "na_bias": [8, 128, 25, 128], "consts": [128, 640], "rope": [128, 2, 16, 64],
    "x": [2048, 1024], "ctx": [256, 1024], "cc": [128, 8, 2],
}


GROUPS = [(0, 0, 256), (1, 256, 512), (2, 768, 512), (3, 1280, 512), (4, 1792, 512)]


def tok_group(i):
    return (0, i * 128) if i < 2 else (1 + (i - 2) // 4, ((i - 2) % 4) * 128)


def build_program(stage="full"):
    nc = bass.Bass("TRN2", target_bir_lowering=False)
    D = {k: nc.dram_tensor(k, shp, F32, kind="ExternalInput").ap() for k, shp in SHARED_SHAPES.items()}
    out = nc.dram_tensor("out", [2048, 1024], F32, kind="ExternalOutput").ap()
    dbg = stage != "full"
    if dbg:
        octx = nc.dram_tensor("octx", [256, 1024], F32, kind="ExternalOutput").ap()
        dbgd = nc.dram_tensor("dbgd", [128, 8192], F32, kind="ExternalOutput").ap()
    na_blocks, na_classes = _na_blocks()

    with ExitStack() as st:
        S = Sched(nc, st)
        xs = [S.sbd(f"xs{i}", [128, 1024], F32) for i in range(NT)]
        cst = S.sbd("cst", [128, 640], F32)
        cc = S.sbd("cc", [128, 8, 2], F32)
        adab = S.sbd("adab", [128, 2, 48], F32)
        ngT = S.sbd("ngT", [128, 2, 2, 8], F32)
        cw = S.sbd("cw", [128, 2, 3, 44], F32)
        cb = S.sbd("cb", [128, 2, 44], F32)
        identb = S.sb("identb", [128, 128], BF16)
        wmb = S.sb("wmb", [128, 2, 128], BF16)
        ones_f = S.sb("ones_f", [128, 128], F32)
        ones_b = S.sb("ones_b", [128, 128], BF16)
        sc = S.sb("sc", [128, 8, 2], F32)
        modT = [S.sb(f"modT{l}", [128, 48, 2], F32) for l in range(2)]
        gbc = S.sb("gbc", [128, 2, 1024], F32)
        AB = S.sb("AB", [128, 8, 2], F32)

        psT = Ring([S.ps(f"psT{i}", [128, 8, 128], BF16) for i in range(2)])
        psA = Ring([S.ps(f"psA{i}", [128, 512], F32) for i in range(2)])
        psS = Ring([S.ps(f"psS{i}", [128, 512], F32) for i in range(2)])
        psO = Ring([S.ps(f"psO{i}", [128, 512], F32) for i in range(2)])

        IDF = lambda: cst[:, 0:128]
        TRIU = lambda: cst[:, 128:256]
        TRIL = lambda: cst[:, 256:384]

        S.dma("sp", cst.sem, cst[:], D["consts"], [], [cst])
        S.dma("sp", cc.sem, cc[:], D["cc"], [], [cc])
        S.dma("sp", adab.sem, adab[:], D["ada_bT"], [], [adab])
        S.dma("sp", ngT.sem, ngT[:], D["norm_gT"], [], [ngT])
        S.dma("sp", cw.sem, cw[:], D["conv_wT"], [], [cw])
        S.dma("sp", cb.sem, cb[:], D["conv_bT"], [], [cb])
        for i in range(NT):
            src = D["ctx"][i * 128:(i + 1) * 128, :] if i < 2 else D["x"][(i - 2) * 128:(i - 1) * 128, :]
            S.dma("sp", xs[i].sem, xs[i][:], src, [], [xs[i]])
        S.op("dve", [cst], [identb], lambda e: e.tensor_copy(out=identb[:], in_=cst[:, 0:128]))
        S.op("dve", [cst], [wmb], lambda e: e.tensor_copy(out=wmb[:], in_=cst[:, 384:640].rearrange("p (a b) -> p a b", a=2)))
        S.op("dve", [], [ones_f], lambda e: e.memset(ones_f[:], 1.0))
        S.op("dve", [], [ones_b], lambda e: e.memset(ones_b[:], 1.0))
        S.op("act", [cc], [sc], lambda e: e.activation(out=sc[:], in_=cc[:], func=AF.Silu))

        dstg = S.sb("dstg", [128, 128], F32) if dbg else None
        dstate = {"col": 0, "items": []}

        def dump(name, buf, ap, n):
            if not dbg:
                return
            stg = dstg
            sem = S.newsem()
            S.op("act", [buf], [stg], lambda e: e.activation(out=stg[:, 0:n], in_=ap, func=AF.Copy))
            c0 = dstate["col"]
            S.dma("sp", sem, dbgd[:, c0:c0 + n], stg[:, 0:n], [stg], [])
            S._need("sp", sem, S.cnt[sem])
            dstate["items"].append((name, c0, n))
            dstate["col"] = c0 + n
            print("DUMP", name, c0, n, flush=True)

        def wview(wb, shape_str, **kw):
            n = 1
            for v in kw.values():
                n *= v
            return wb

        def mod_phase(l):
            with ExitStack() as ph:
                ring = Ring([S.sbd(f"adaw{l}_{i}", [128, 8, 512], BF16, ph) for i in range(3)])
                schi = S.sb(f"schi{l}", [128, 8, 2], BF16, ph)
                schf = S.sb(f"schf{l}", [128, 8, 2], F32, ph)
                sclo = S.sb(f"sclo{l}", [128, 8, 2], BF16, ph)
                S.op("dve", [sc], [schi], lambda e: e.tensor_copy(out=schi[:], in_=sc[:]))
                S.op("dve", [schi], [schf], lambda e: e.tensor_copy(out=schf[:], in_=schi[:]))
                S.op("dve", [sc, schf], [schf], lambda e: e.tensor_tensor(out=schf[:], in0=sc[:], in1=schf[:], op=ALU.subtract))
                S.op("dve", [schf], [sclo], lambda e: e.tensor_copy(out=sclo[:], in_=schf[:]))
                wbs = {}

                def ld(cg):
                    if cg >= 12:
                        return
                    wb = ring.get()
                    S.dma("pool", wb.sem, wb[:], D["ada_w"][l, :, cg * 512:(cg + 1) * 512].rearrange("(k p) n -> p k n", p=128), [], [wb])
                    wbs[cg] = wb
                ld(0)
                ld(1)
                for cg in range(12):
                    ld(cg + 2)
                    wb = wbs[cg]
                    ps = psA.get()
                    for c4 in range(4):
                        for k in range(8):
                            S.op("pe", [wb, schi], [ps], lambda e: e.matmul(ps[:, c4 * 2:c4 * 2 + 2], lhsT=wb[:, k, c4 * 128:(c4 + 1) * 128], rhs=schi[:, k, :], start=(k == 0), stop=False))
                            S.op("pe", [wb, sclo], [ps], lambda e: e.matmul(ps[:, c4 * 2:c4 * 2 + 2], lhsT=wb[:, k, c4 * 128:(c4 + 1) * 128], rhs=sclo[:, k, :], start=False, stop=(k == 7)))
                    S.op("dve", [ps, adab], [modT[l]], lambda e: e.tensor_tensor(
                        out=modT[l][:, cg * 4:(cg + 1) * 4, :], in0=ps[:, 0:8].rearrange("p (c j) -> p c j", j=2),
                        in1=adab[:, l, cg * 4:(cg + 1) * 4].unsqueeze(2).to_broadcast([128, 4, 2]), op=ALU.add))
                S.barrier()

        def mk_AB(l, which):
            scl = 8 if which == 0 else 32
            S.op("dve", [modT[l]], [AB], lambda e: e.tensor_scalar(out=AB[:], in0=modT[l][:, scl:scl + 8, :], scalar1=1.0, scalar2=None, op0=ALU.add))
            S.op("dve", [AB, ngT], [AB], lambda e: e.tensor_tensor(out=AB[:], in0=AB[:], in1=ngT[:, l, which, :].unsqueeze(2).to_broadcast([128, 8, 2]), op=ALU.mult))

        def mk_gate(l, gchunk, ph):
            hl = S.sb(f"ghl{l}_{gchunk}", [128, 8, 2], F32, ph)
            hb = S.sb(f"ghb{l}_{gchunk}", [128, 8, 2], BF16, ph)
            hf = S.sb(f"ghf{l}_{gchunk}", [128, 8, 2], F32, ph)
            lo = S.sb(f"glo{l}_{gchunk}", [128, 8, 2], F32, ph)
            lb = S.sb(f"glb{l}_{gchunk}", [128, 8, 2], BF16, ph)
            lf = S.sb(f"glf{l}_{gchunk}", [128, 8, 2], F32, ph)
            S.op("dve", [modT[l]], [hl], lambda e: e.tensor_copy(out=hl[:], in_=modT[l][:, gchunk:gchunk + 8, :]))
            S.op("dve", [hl], [hb], lambda e: e.tensor_copy(out=hb[:], in_=hl[:]))
            S.op("dve", [hb], [hf], lambda e: e.tensor_copy(out=hf[:], in_=hb[:]))
            S.op("dve", [hl, hf], [lo], lambda e: e.tensor_tensor(out=lo[:], in0=hl[:], in1=hf[:], op=ALU.subtract))
            S.op("dve", [lo], [lb], lambda e: e.tensor_copy(out=lb[:], in_=lo[:]))
            S.op("dve", [lb], [lf], lambda e: e.tensor_copy(out=lf[:], in_=lb[:]))
            dgr = Ring([S.sb(f"dg{l}_{gchunk}_{i}", [128, 2, 128], BF16, ph) for i in range(2)])
            for j in range(2):
                for half in range(2):
                    ps = psA.get()
                    for k4 in range(4):
                        kk = half * 4 + k4
                        dg = dgr.get()
                        S.op("dve", [identb, hf], [dg], lambda e: e.tensor_scalar(out=dg[:, 0, :], in0=identb[:], scalar1=hf[:, kk, j:j + 1], scalar2=None, op0=ALU.mult))
                        S.op("dve", [identb, lf], [dg], lambda e: e.tensor_scalar(out=dg[:, 1, :], in0=identb[:], scalar1=lf[:, kk, j:j + 1], scalar2=None, op0=ALU.mult))
                        S.op("pe", [ones_b, dg], [ps], lambda e: e.matmul(ps[:, k4 * 128:(k4 + 1) * 128], lhsT=ones_b[:], rhs=dg[:, 0, :], start=True, stop=False))
                        S.op("pe", [ones_b, dg], [ps], lambda e: e.matmul(ps[:, k4 * 128:(k4 + 1) * 128], lhsT=ones_b[:], rhs=dg[:, 1, :], start=False, stop=True))
                    S.op("act", [ps], [gbc], lambda e: e.activation(out=gbc[:, j, half * 512:(half + 1) * 512], in_=ps[:], func=AF.Copy))

        def rstd_of(t, n_ap, dim):
            S.op("dve", [t], [t], lambda e: e.tensor_scalar(out=n_ap(), in0=n_ap(), scalar1=1.0 / dim, scalar2=EPS, op0=ALU.mult, op1=ALU.add))
            S.op("act", [t], [t], lambda e: e.activation(out=n_ap(), in_=n_ap(), func=AF.Ln))
            S.op("act", [t], [t], lambda e: e.activation(out=n_ap(), in_=n_ap(), func=AF.Exp, scale=-0.5))

        def norm_phase(l, which, hTg, ph, tiles=range(NT)):
            mk_AB(l, which)
            sh = 0 if which == 0 else 24
            ss = S.sb(f"nss{l}{which}", [128, NT], F32, ph)
            junk = S.sb(f"njunk{l}{which}", [128, 1024], BF16, ph)
            xnr = Ring([S.sb(f"xn{l}{which}_{i}", [128, 1024], BF16, ph) for i in range(2)])
            S.op("dve", [], [ss], lambda e: e.memset(ss[:], 1.0))
            for i in tiles:
                S.op("act", [xs[i]], [junk, ss], lambda e: e.activation(out=junk[:], in_=xs[i][:], func=AF.Square, accum_out=ss[:, i:i + 1]))
            rstd_of(ss, lambda: ss[:], 1024)
            import os
            if os.environ.get("KSUB") in ("a", "c"):
                return
            for i in tiles:
                xn = xnr.get()
                S.op("dve", [xs[i], ss], [xn], lambda e: e.tensor_scalar(out=xn[:], in0=xs[i][:], scalar1=ss[:, i:i + 1], scalar2=None, op0=ALU.mult))
                pt = psT.get()
                for k in range(8):
                    S.op("pe", [xn, identb], [pt], lambda e: e.transpose(out=pt[:, k, :], in_=xn[:, k * 128:(k + 1) * 128], identity=identb[:]))
                g, off = tok_group(i)
                j = 1 if i < 2 else 0
                for k in range(8):
                    if k % 2 == 0:
                        S.op("dve", [pt, AB, modT[l]], [hTg[g]], lambda e: e.tensor_scalar(
                            out=hTg[g][:, k, off:off + 128], in0=pt[:, k, :], scalar1=AB[:, k, j:j + 1], scalar2=modT[l][:, sh + k, j:j + 1], op0=ALU.mult, op1=ALU.add))
                    else:
                        S.op("act", [pt, AB, modT[l]], [hTg[g]], lambda e: e.activation(
                            out=hTg[g][:, k, off:off + 128], in_=pt[:, k, :], func=AF.Identity, scale=AB[:, k, j:j + 1], bias=modT[l][:, sh + k, j:j + 1]))

        def wload(wb, n, src):
            dst = wb[:, 0:8 * n].rearrange("p (k n) -> p k n", k=8)
            S.dma("pool", wb.sem, dst, src, [], [wb])
            return dst

        def qk_prep(ps, ps_ap, nh, g_ap, rope_tile, out_ap, wk, rope):
            sq, ssq, qn, t1 = wk
            n = nh * 64
            v3 = lambda ap: ap.rearrange("p (h d) -> p h d", d=64)
            S.op("act", [ps], [sq], lambda e: e.activation(out=sq[:, 0:n], in_=ps_ap, func=AF.Square))
            S.op("dve", [sq], [ssq], lambda e: e.tensor_reduce(out=ssq[:, 0:nh], in_=v3(sq[:, 0:n]), axis=AX.X, op=ALU.add))
            rstd_of(ssq, lambda: ssq[:, 0:nh], 64)
            S.op("dve", [ps, ssq], [qn], lambda e: e.tensor_tensor(out=v3(qn[:, 0:n]), in0=v3(ps_ap), in1=ssq[:, 0:nh].unsqueeze(2).to_broadcast([128, nh, 64]), op=ALU.mult))
            if rope_tile is None:
                S.op("dve", [qn], [out_ap[0]], lambda e: e.tensor_tensor(out=out_ap[1], in0=v3(qn[:, 0:n]), in1=g_ap.unsqueeze(1).to_broadcast([128, nh, 64]), op=ALU.mult))
                return
            S.op("dve", [qn], [qn], lambda e: e.tensor_tensor(out=v3(qn[:, 0:n]), in0=v3(qn[:, 0:n]), in1=g_ap.unsqueeze(1).to_broadcast([128, nh, 64]), op=ALU.mult))
            cos_ap = rope[:, 0, :]
            sin_ap = rope[:, 1, :]
            S.op("dve", [qn, rope], [t1], lambda e: e.tensor_tensor(out=v3(t1[:, 0:n]), in0=v3(qn[:, 0:n]), in1=cos_ap.unsqueeze(1).to_broadcast([128, nh, 64]), op=ALU.mult))
            v5 = lambda ap: ap.rearrange("p (h x y d) -> p h x y d", x=2, y=2, d=16)
            s4 = sin_ap.rearrange("p (x y d) -> p x y d", x=2, y=2)
            for y in range(2):
                S.op("dve", [qn, rope], [sq], lambda e: e.tensor_tensor(
                    out=v5(sq[:, 0:n])[:, :, :, y, :], in0=v5(qn[:, 0:n])[:, :, :, 1 - y, :],
                    in1=s4[:, :, y, :].unsqueeze(1).to_broadcast([128, nh, 2, 16]), op=ALU.mult))
            S.op("dve", [t1, sq], [out_ap[0]], lambda e: e.tensor_tensor(out=out_ap[1], in0=v3(t1[:, 0:n]), in1=v3(sq[:, 0:n]), op=ALU.add))

        def residual(i, ps, cgi, j):
            tmp = restmp.get()
            S.op("dve", [ps, gbc], [tmp], lambda e: e.tensor_tensor(out=tmp[:], in0=ps[:], in1=gbc[:, j, cgi * 512:(cgi + 1) * 512], op=ALU.mult))
            rstate["n"] += 1
            S.op("dve", [tmp, xs[i]], [xs[i]], lambda e: e.tensor_tensor(out=xs[i][:, cgi * 512:(cgi + 1) * 512], in0=xs[i][:, cgi * 512:(cgi + 1) * 512], in1=tmp[:], op=ALU.add))

        restmp = Ring([S.sb(f"restmp{i}", [128, 512], F32) for i in range(2)])
        rstate = {"n": 0}

        def mixer0():
            l = 0
            with ExitStack() as ph:
                hTg = [S.sb("hT0_0", [128, 8, 256], BF16, ph)] + [S.sb(f"hT0_{g}", [128, 8, 512], BF16, ph) for g in range(1, 5)]
                with ExitStack() as ph2:
                    norm_phase(0, 0, hTg, ph2)
                    import os
                    if os.environ.get("KSUB") not in ("a", "b"):
                        mk_gate(0, 16, ph2)
                    S.barrier()
                if stage == "norm":
                    return
                mixTa = S.sb("mixTa", [128, 4, NT * 128], BF16, ph)
                with ExitStack() as ph2:
                    wring = Ring([S.sbd(f"w0_{i}", [128, 8 * 384], BF16, ph2) for i in range(2)])
                    gateb = S.sbd("gateb", [128, 16], F32, ph2)
                    headg = S.sbd("headg", [128, 512], F32, ph2)
                    S.dma("sp", gateb.sem, gateb[:], D["gate_b_bc"], [], [gateb])
                    S.dma("sp", headg.sem, headg[:], D["head_g_bc"], [], [headg])
                    mlstm(hTg, mixTa, gateb, headg, wring, ph2)
                    S.barrier()
                if stage == "mlstm":
                    return
                with ExitStack() as ph2:
                    gqa_attn(0, hTg, mixTa, None, ph2)
                    S.barrier()

        def mlstm_gates(hTg, gateb, wring, pg, es, eb, edec, ekw):
            G = S.sb("G", [128, NT, 16], F32, pg)
            wg = wload(wring.get(), 16, D["even_w"][:, 2048:2064].rearrange("(k p) n -> p k n", p=128))
            wgb = wring.bufs[(wring.i - 1) % len(wring.bufs)]
            for i in range(NT):
                g, off = tok_group(i)
                ps = psO.get()
                for k in range(8):
                    S.op("pe", [hTg[g], wgb], [ps], lambda e: e.matmul(ps[:, 0:16], lhsT=hTg[g][:, k, off:off + 128], rhs=wg[:, k, :], start=(k == 0), stop=(k == 7)))
                S.op("dve", [ps, gateb], [G], lambda e: e.tensor_tensor(out=G[:, i, :], in0=ps[:, 0:16], in1=gateb[:], op=ALU.add))
            E = S.sb("E", [128, 2, NT, 4], F32, pg)
            for d in range(2):
                S.op("act", [G], [E], lambda e: e.activation(out=E[:, d], in_=G[:, :, 4 + 8 * d:8 + 8 * d], func=AF.Exp, scale=-1.0))
            S.op("dve", [E], [E], lambda e: e.tensor_scalar(out=E[:], in0=E[:], scalar1=1.0, scalar2=None, op0=ALU.add))
            S.op("act", [E], [E], lambda e: e.activation(out=E[:], in_=E[:], func=AF.Ln))
            tg = S.sb("tg", [128, NT, 4], F32, pg)
            f72 = lambda ap: ap.rearrange("p t h -> p (t h)")
            trib = S.sb("trib", [128, 2, 128], BF16, pg)
            S.op("dve", [cst], [trib], lambda e: e.tensor_copy(out=trib[:], in_=cst[:, 128:384].rearrange("p (a b) -> p a b", a=2)))
            Ehi = S.sb("Ehi", [128, 2, NT, 4], BF16, pg)
            Ehf = S.sb("Ehf", [128, 2, NT, 4], F32, pg)
            Elo = S.sb("Elo", [128, 2, NT, 4], BF16, pg)
            S.op("dve", [E], [Ehi], lambda e: e.tensor_copy(out=Ehi[:], in_=E[:]))
            S.op("dve", [Ehi], [Ehf], lambda e: e.tensor_copy(out=Ehf[:], in_=Ehi[:]))
            S.op("dve", [E, Ehf], [Ehf], lambda e: e.tensor_tensor(out=Ehf[:], in0=E[:], in1=Ehf[:], op=ALU.subtract))
            S.op("dve", [Ehf], [Elo], lambda e: e.tensor_copy(out=Elo[:], in_=Ehf[:]))
            for d in range(2):
                psb = psO.get()
                S.op("pe", [trib, Ehi], [psb], lambda e: e.matmul(psb[:, 0:72], lhsT=trib[:, d, :], rhs=f72(Ehi[:, d]), start=True, stop=False))
                S.op("pe", [trib, Elo], [psb], lambda e: e.matmul(psb[:, 0:72], lhsT=trib[:, d, :], rhs=f72(Elo[:, d]), start=False, stop=True))
                S.op("pe", [ones_b, Ehi], [psb], lambda e: e.matmul(psb[:, 72:144], lhsT=ones_b[:], rhs=f72(Ehi[:, d]), start=True, stop=False))
                S.op("pe", [ones_b, Elo], [psb], lambda e: e.matmul(psb[:, 72:144], lhsT=ones_b[:], rhs=f72(Elo[:, d]), start=False, stop=True))
                S.op("dve", [psb, G], [tg], lambda e: e.tensor_tensor(out=tg[:], in0=psb[:, 0:72].rearrange("p (t h) -> p t h", h=4), in1=G[:, :, 8 * d:8 * d + 4], op=ALU.add))
                S.op("act", [tg], [es], lambda e: e.activation(out=es[:, d], in_=tg[:], func=AF.Exp))
                S.op("act", [psb], [eb], lambda e: e.activation(out=f72(eb[:, d]), in_=psb[:, 0:72], func=AF.Exp, scale=-1.0))
                S.op("act", [psb], [edec], lambda e: e.activation(out=f72(edec[:, d]), in_=psb[:, 72:144], func=AF.Exp, scale=-1.0))
                S.op("dve", [es, edec], [ekw], lambda e: e.tensor_tensor(out=ekw[:, d], in0=es[:, d], in1=edec[:, d], op=ALU.mult))

            pass
            pass
            pass
            pass
            pass

        def mlstm(hTg, mixTa, gateb, headg, wring, ph):
            es = S.sb("es", [128, 2, NT, 4], F32, ph)
            eb = S.sb("eb", [128, 2, NT, 4], F32, ph)
            edec = S.sb("edec", [128, 2, NT, 4], F32, ph)
            ekw = S.sb("ekw", [128, 2, NT, 4], F32, ph)
            with ExitStack() as pg:
                mlstm_gates(hTg, gateb, wring, pg, es, eb, edec, ekw)
                S.barrier()
            KS_ = ""
            KH_ = -1
            qT = S.sb("qTa", [128, NT * 128], BF16, ph)
            kT = S.sb("kTa", [128, NT * 128], BF16, ph)
            ktok = S.sb("ktok", [128, NT, 128], BF16, ph)
            vaug = S.sb("vaug", [128, NT, 130], BF16, ph)
            hraw = [S.sb(f"hraw{d}", [128, NT, 130], F32, ph) for d in range(2)]
            rnm = S.sb("rnm", [128, 2, NT], F32, ph)
            Cst = [S.sb(f"Cst{d}", [128, 129], F32, ph) for d in range(2)]
            Cbf3 = [[S.sb(f"Cbf{d}_{r}", [128, 130], BF16, ph) for r in range(3)] for d in range(2)]
            PTr = Ring([S.sb(f"PTm{i}", [128, 128], BF16, ph) for i in range(4)])
            kwr = Ring([S.sb(f"kwm{i}", [128, 128], BF16, ph) for i in range(4)])
            hss = S.sb("hss", [128, NT], F32, ph)
            hjunk = S.sb("hjunk", [128, 128], F32, ph)
            ogr = Ring([S.sb(f"og{i}", [128, 128], F32, ph) for i in range(2)])
            t1r = Ring([S.sb(f"mt1{i}", [128, 128], F32, ph) for i in range(2)])
            mxr = Ring([S.sb(f"mmx{i}", [128, 128], BF16, ph) for i in range(2)])
            S.op("dve", [], [vaug], lambda e: e.memset(vaug[:, :, 128:129], 1.0))
            orders = [list(range(NT)), [1, 0] + list(range(NT - 1, 1, -1))]
            KS = 128.0 ** -0.5

            for h in range(4):
                wb = wring.get()
                src = D["even_w"][:, 0:1536].rearrange("(k p) (g h n) -> p k g h n", p=128, g=3, h=4)[:, :, :, h, :]
                wq = wb[:, 0:8 * 384].rearrange("p (k g n) -> p k g n", k=8, g=3)
                for g3 in range(3):
                    S.dma("pool", wb.sem, wq[:, :, g3, :], src[:, :, g3, :], [], [wb])
                wob = wring.get()
                wo = wload(wob, 128, D["even_w"][:, 1536 + h * 128:1536 + (h + 1) * 128].rearrange("(k p) n -> p k n", p=128))
                flip = 0
                for (g, c0, n) in GROUPS:
                    for which, dst, scl in ((0, qT, 1.0), (1, kT, KS)):
                        ps = psA.get()
                        for k in range(8):
                            S.op("pe", [wb, hTg[g]], [ps], lambda e: e.matmul(ps[:, 0:n], lhsT=wq[:, k, which, :], rhs=hTg[g][:, k, 0:n], start=(k == 0), stop=(k == 7)))
                        if flip % 2 == 0:
                            S.op("act", [ps], [dst], lambda e: e.activation(out=dst[:, c0:c0 + n], in_=ps[:, 0:n], func=AF.Copy, scale=scl))
                        else:
                            S.op("dve", [ps], [dst], lambda e: e.tensor_scalar(out=dst[:, c0:c0 + n], in0=ps[:, 0:n], scalar1=scl, scalar2=None, op0=ALU.mult))
                        flip += 1
                if KS_ == "m2a" and h == KH_:
                    return
                for i in range(NT):
                    g, off = tok_group(i)
                    ps = psA.get()
                    for k in range(8):
                        S.op("pe", [wb, hTg[g]], [ps], lambda e: e.matmul(ps[:, 0:256], lhsT=hTg[g][:, k, off:off + 128], rhs=wb[:, k * 384 + 128:k * 384 + 384], start=(k == 0), stop=(k == 7)))
                    S.op("act", [ps], [ktok], lambda e: e.activation(out=ktok[:, i, :], in_=ps[:, 0:128], func=AF.Copy, scale=KS))
                    S.op("dve", [ps], [vaug], lambda e: e.tensor_copy(out=vaug[:, i, 0:128], in_=ps[:, 128:256]))
                if KS_ == "m2b" and h == KH_:
                    return
                if h == 0:
                    pass
                    pass
                    pass
                    pass
                if KS_ == "m2" and h == KH_:
                    return
                written = [False] * NT
                PTs = {}

                def emitA(step, d):
                    i = orders[d][step]
                    col = lambda a: a[:, d, i, h:h + 1]
                    cs = slice(i * 128, (i + 1) * 128)
                    pss = psS.get()
                    S.op("pe", [kT, qT], [pss], lambda e: e.matmul(pss[:, 0:128], lhsT=kT[:, cs], rhs=qT[:, cs], start=True, stop=True))
                    PT = PTr.get()
                    msk = TRIU() if d == 0 else TRIL()
                    S.op("dve", [pss, es, cst], [PT], lambda e: e.scalar_tensor_tensor(out=PT[:], in0=pss[:, 0:128], scalar=col(es), in1=msk, op0=ALU.mult, op1=ALU.mult))
                    PTs[(step, d)] = PT
                    if step < NT - 1:
                        kw = kwr.get()
                        S.op("act", [ktok, ekw], [kw], lambda e: e.activation(out=kw[:], in_=ktok[:, i, :], func=AF.Copy, scale=col(ekw)))
                        psc = psA.get()
                        S.op("pe", [kw, vaug], [psc], lambda e: e.matmul(psc[:, 0:129], lhsT=kw[:], rhs=vaug[:, i, 0:129], start=True, stop=True))
                        if step == 0:
                            S.op("dve", [psc], [Cst[d]], lambda e: e.tensor_copy(out=Cst[d][:], in_=psc[:, 0:129]))
                        else:
                            S.op("dve", [psc, Cst[d], edec], [Cst[d]], lambda e: e.scalar_tensor_tensor(out=Cst[d][:], in0=Cst[d][:], scalar=col(edec), in1=psc[:, 0:129], op0=ALU.mult, op1=ALU.add))
                        cb3 = Cbf3[d][(step + 1) % 3]
                        S.op("act", [Cst[d]], [cb3], lambda e: e.activation(out=cb3[:, 0:129], in_=Cst[d][:], func=AF.Copy))

                def emitB(step, d):
                    i = orders[d][step]
                    col = lambda a: a[:, d, i, h:h + 1]
                    cs = slice(i * 128, (i + 1) * 128)
                    PT = PTs.pop((step, d))
                    acc = psO.get()
                    if step > 0:
                        cb3 = Cbf3[d][step % 3]
                        S.op("pe", [qT, cb3], [acc], lambda e: e.matmul(acc[:, 0:129], lhsT=qT[:, cs], rhs=cb3[:, 0:129], start=True, stop=False))
                    S.op("pe", [PT, vaug], [acc], lambda e: e.matmul(acc[:, 0:129], lhsT=PT[:], rhs=vaug[:, i, 0:129], start=(step == 0), stop=True))
                    S.op("act", [acc, eb], [hraw[d]], lambda e: e.activation(out=hraw[d][:, i, 0:129], in_=acc[:, 0:129], func=AF.Copy, scale=col(eb)))

                emitA(0, 0)
                emitA(0, 1)
                for step in range(NT):
                    if step + 1 < NT:
                        emitA(step + 1, 0)
                        emitA(step + 1, 1)
                    emitB(step, 0)
                    emitB(step, 1)
                if h == 0:
                    pass
                    pass
                if KS_ == "m3" and h == KH_:
                    return
                for d in range(2):
                    S.op("act", [hraw[d]], [rnm], lambda e: e.activation(out=rnm[:, d, :], in_=hraw[d][:, :, 128], func=AF.Abs))
                S.op("dve", [rnm], [rnm], lambda e: e.tensor_scalar(out=rnm[:], in0=rnm[:], scalar1=1.0, scalar2=None, op0=ALU.max))
                S.op("dve", [rnm], [rnm], lambda e: e.reciprocal(out=rnm[:], in_=rnm[:]))
                for d in range(2):
                    S.op("dve", [hraw[d], rnm], [hraw[d]], lambda e: e.tensor_tensor(out=hraw[d][:, :, 0:128], in0=hraw[d][:, :, 0:128], in1=rnm[:, d, :].unsqueeze(2).to_broadcast([128, NT, 128]), op=ALU.mult))
                S.op("dve", [hraw[0], hraw[1]], [hraw[0]], lambda e: e.tensor_tensor(out=hraw[0][:, :, 0:128], in0=hraw[0][:, :, 0:128], in1=hraw[1][:, :, 0:128], op=ALU.add))
                S.op("dve", [], [hss], lambda e: e.memset(hss[:], 1.0))
                for i in range(NT):
                    S.op("act", [hraw[0]], [hjunk, hss], lambda e: e.activation(out=hjunk[:], in_=hraw[0][:, i, 0:128], func=AF.Square, accum_out=hss[:, i:i + 1]))
                rstd_of(hss, lambda: hss[:], 128)
                for i in range(NT):
                    g, off = tok_group(i)
                    ps = psA.get()
                    for k in range(8):
                        S.op("pe", [wob, hTg[g]], [ps], lambda e: e.matmul(ps[:, 0:128], lhsT=hTg[g][:, k, off:off + 128], rhs=wo[:, k, :], start=(k == 0), stop=(k == 7)))
                    og = ogr.get()
                    S.op("act", [ps], [og], lambda e: e.activation(out=og[:], in_=ps[:, 0:128], func=AF.Sigmoid))
                    t1 = t1r.get()
                    S.op("dve", [hraw[0], hss, headg], [t1], lambda e: e.scalar_tensor_tensor(out=t1[:], in0=hraw[0][:, i, 0:128], scalar=hss[:, i:i + 1], in1=headg[:, h * 128:(h + 1) * 128], op0=ALU.mult, op1=ALU.mult))
                    mx = mxr.get()
                    S.op("dve", [t1, og], [mx], lambda e: e.tensor_tensor(out=mx[:], in0=t1[:], in1=og[:], op=ALU.mult))
                    pt = psT.get()
                    S.op("pe", [mx, identb], [pt], lambda e: e.transpose(out=pt[:, 0, :], in_=mx[:], identity=identb[:]))
                    S.op("act", [pt], [mixTa], lambda e: e.activation(out=mixTa[:, h, i * 128:(i + 1) * 128], in_=pt[:, 0, :], func=AF.Copy))
                if KS_ == "m4" and h == KH_:
                    return

        def attn_scores_exp_pv(kv_specs, nheads_per_kv, qT, q_sl, PTr, accs, first, last):
            pass

        def gqa_attn(l, hTg, other, wring, ph):
            wname = "even_w" if l == 0 else "odd_w"
            qc0, kc0 = (2064, 2576) if l == 0 else (0, 512)
            swag = S.sbd(f"swag{l}", [128, 128], F32, ph)
            roper = Ring([S.sbd(f"rope{l}_{i}", [128, 2, 64], F32, ph) for i in range(2)])

            def get_rope(jt):
                rb = roper.get()
                S.dma("sp", rb.sem, rb[:], D["rope"][:, :, jt, :], [], [rb])
                return rb
            S.dma("sp", swag.sem, swag[:], D["swa_g_bc" if l == 0 else "gqa_g_bc"], [], [swag])
            gq = S.sb(f"gq{l}", [128, 64], F32, ph)
            S.op("dve", [swag], [gq], lambda e: e.tensor_scalar(out=gq[:], in0=swag[:, 0:64], scalar1=0.125, scalar2=None, op0=ALU.mult))
            esink = S.sb(f"esink{l}", [128, 8], F32, ph)
            if l == 0:
                sinkb = S.sbd("sinkb", [128, 8], F32, ph)
                S.dma("sp", sinkb.sem, sinkb[:], D["sink_bc"], [], [sinkb])
                S.op("act", [sinkb], [esink], lambda e: e.activation(out=esink[:], in_=sinkb[:], func=AF.Exp))
            else:
                S.op("dve", [], [esink], lambda e: e.memset(esink[:], 0.0))
            wout = S.sbd(f"wout{l}", [128, 8 * 1024], BF16, ph)
            woutv = wout[:, :].rearrange("p (k n) -> p k n", k=8)
            S.dma("pool", wout.sem, woutv, D["w_out"][l].rearrange("(k p) n -> p k n", p=128), [], [wout])
            wkb = S.sbd(f"wkv{l}", [128, 8 * 256], BF16, ph)
            wkv = wload(wkb, 256, D[wname][:, kc0:kc0 + 256].rearrange("(k p) n -> p k n", p=128))
            wqb = S.sbd(f"wqq{l}", [128, 8 * 512], BF16, ph)
            wq = wload(wqb, 512, D[wname][:, qc0:qc0 + 512].rearrange("(k p) n -> p k n", p=128))
            kTd = [S.sb(f"kTd{g}", [128, NT * 128], BF16, ph) for g in range(2)]
            vb = S.sb("vb", [128, NT, 2, 66], BF16, ph)
            S.op("dve", [], [vb], lambda e: e.memset(vb[:, :, :, 64:65], 1.0))
            wk = (S.sb("wk_sq", [128, 512], F32, ph), S.sb("wk_ss", [128, 8], F32, ph), S.sb("wk_qn", [128, 512], F32, ph), S.sb("wk_t1", [128, 512], F32, ph))
            kd = S.sb("kd", [128, 2, 2, 64], BF16, ph)
            kn = S.sb("kn", [128, 2, 64], BF16, ph)
            for i in range(NT):
                g, off = tok_group(i)
                ps = psA.get()
                for k in range(8):
                    S.op("pe", [wkb, hTg[g]], [ps], lambda e: e.matmul(ps[:, 0:256], lhsT=hTg[g][:, k, off:off + 128], rhs=wkv[:, k, :], start=(k == 0), stop=(k == 7)))
                S.op("act", [ps], [vb], lambda e: e.activation(out=vb[:, i, :, 0:64], in_=ps[:, 128:256].rearrange("p (g d) -> p g d", g=2), func=AF.Copy))
                qk_prep(ps, ps[:, 0:128], 2, swag[:, 64:128], (i - 2) if i >= 2 else None, (kn, kn[:]), wk, get_rope(i - 2) if i >= 2 else None)
                S.op("dve", [kn], [kd], lambda e: e.tensor_copy(out=kd[:], in_=kn[:, :, :].unsqueeze(2).to_broadcast([128, 2, 2, 64])))
                pt = psT.get()
                for g2 in range(2):
                    S.op("pe", [kd, identb], [pt], lambda e: e.transpose(out=pt[:, g2, :], in_=kd[:, g2].rearrange("p a d -> p (a d)"), identity=identb[:]))
                for g2 in range(2):
                    S.op("act" if g2 == 0 else "dve", [pt], [kTd[g2]],
                         (lambda e: e.activation(out=kTd[0][:, i * 128:(i + 1) * 128], in_=pt[:, 0, :], func=AF.Copy)) if g2 == 0 else
                         (lambda e: e.tensor_copy(out=kTd[1][:, i * 128:(i + 1) * 128], in_=pt[:, 1, :])))
            import os
            KS_ = os.environ.get("KSUB", "")
            if KS_ == "w1":
                return
            qb = S.sb("qb", [128, 8, 64], BF16, ph)
            qz = S.sb("qz", [128, 2, 4, 128], BF16, ph)
            S.op("dve", [], [qz], lambda e: e.memset(qz[:], 0.0))
            wmb4 = S.sb("wmb4", [128, 2, 4, 128], BF16, ph)
            S.op("dve", [wmb], [wmb4], lambda e: e.tensor_copy(out=wmb4[:], in_=wmb[:, :, :].unsqueeze(2).to_broadcast([128, 2, 4, 128])))
            PTr = Ring([S.sb(f"PTw{i}", [128, 512], BF16, ph) for i in range(2)])
            den = S.sb("wden", [128, 8], F32, ph)
            mixb = S.sb("mixb", [128, 512], BF16, ph)
            mixTb = S.sb("mixTb", [128, 4, 128], BF16, ph)
            for i in (range(NT) if l == 0 else range(2, NT)):
                g, off = tok_group(i)
                lat = i >= 2
                j = i - 2
                ps = psA.get()
                for k in range(8):
                    S.op("pe", [wqb, hTg[g]], [ps], lambda e: e.matmul(ps[:, 0:512], lhsT=hTg[g][:, k, off:off + 128], rhs=wq[:, k, :], start=(k == 0), stop=(k == 7)))
                qk_prep(ps, ps[:, 0:512], 8, gq[:], j if lat else None, (qb, qb[:]), wk, get_rope(j) if lat else None)
                pt = psT.get()
                for pr in range(4):
                    S.op("pe", [qb, identb], [pt], lambda e: e.transpose(out=pt[:, pr, :], in_=qb[:, 2 * pr:2 * pr + 2, :].rearrange("p a d -> p (a d)"), identity=identb[:]))
                S.op("act", [pt], [qz], lambda e: e.activation(out=qz[0:64, 0, :, :], in_=pt[0:64, 0:4, :], func=AF.Copy))
                S.op("dve", [pt], [qz], lambda e: e.tensor_copy(out=qz[64:128, 1, :, :], in_=pt[64:128, 0:4, :]))
                if KS_ == "w2a":
                    return
                if l == 1:
                    blocks = [(m, None) for m in range(NT)]
                elif lat:
                    blocks = [(0, None), (1, None)]
                    if j > 0:
                        blocks.append((i - 1, 0))
                    blocks.append((i, None))
                    if j < 15:
                        blocks.append((i + 1, 1))
                else:
                    blocks = [(0, None), (1, None)]
                for g2 in range(2):
                    acc = psO.get()

                    def emit_scores(m, msk):
                        pss = psS.get()
                        for half in range(2):
                            S.op("pe", [kTd[g2], qz], [pss], lambda e: e.matmul(
                                pss[:, half * 256:(half + 1) * 256], lhsT=kTd[g2][:, m * 128:(m + 1) * 128],
                                rhs=qz[:, half, 2 * g2:2 * g2 + 2, :].rearrange("p a q -> p (a q)"),
                                start=(half == 0), stop=(half == 1 and msk is None)))
                        if msk is not None:
                            S.op("pe", [identb, wmb4], [pss], lambda e: e.matmul(pss[:, 0:512], lhsT=identb[:], rhs=wmb4[:, msk, :, :].rearrange("p a q -> p (a q)"), start=False, stop=True))
                        return pss

                    nxt = emit_scores(*blocks[0])
                    for bi, (m, msk) in enumerate(blocks):
                        pss = nxt
                        if bi + 1 < len(blocks):
                            nxt = emit_scores(*blocks[bi + 1])
                        PT = PTr.get()
                        S.op("act", [pss], [PT], lambda e: e.activation(out=PT[:], in_=pss[:], func=AF.Exp))
                        for hh in range(4):
                            S.op("pe", [PT, vb], [acc], lambda e: e.matmul(acc[:, hh * 128:hh * 128 + 65], lhsT=PT[:, hh * 128:(hh + 1) * 128], rhs=vb[:, m, g2, 0:65], start=(bi == 0 and hh == 0), stop=(bi == len(blocks) - 1)))
                    if KS_ == "w2c":
                        return
                    a3 = acc[:, :].rearrange("p (h c) -> p h c", h=4)
                    S.op("dve", [acc, esink], [den], lambda e: e.tensor_tensor(out=den[:, g2 * 4:(g2 + 1) * 4].rearrange("p (b a) -> p b a", b=2), in0=a3[:, :, 64].rearrange("p (b a) -> p b a", b=2),
                                                                            in1=esink[:, g2 * 4:(g2 + 1) * 4].rearrange("p (a b) -> p b a", a=2), op=ALU.add))
                    S.op("dve", [den], [den], lambda e: e.reciprocal(out=den[:, g2 * 4:(g2 + 1) * 4], in_=den[:, g2 * 4:(g2 + 1) * 4]))
                    S.op("dve", [acc, den], [mixb], lambda e: e.tensor_tensor(
                        out=mixb[:, g2 * 256:(g2 + 1) * 256].rearrange("p (a b d) -> p b a d", a=2, b=2), in0=a3[:, :, 0:64].rearrange("p (b a) d -> p b a d", b=2),
                        in1=den[:, g2 * 4:(g2 + 1) * 4].rearrange("p (b a) -> p b a", b=2).unsqueeze(3).to_broadcast([128, 2, 2, 64]), op=ALU.mult))
                if KS_ == "w2d":
                    return
                pt2 = psT.get()
                for c in range(4):
                    S.op("pe", [mixb, identb], [pt2], lambda e: e.transpose(out=pt2[:, c, :], in_=mixb[:, c * 128:(c + 1) * 128], identity=identb[:]))
                S.op("act", [pt2], [mixTb], lambda e: e.activation(out=mixTb[:], in_=pt2[:, 0:4, :], func=AF.Copy))
                if KS_ == "w2" and i == 2:
                    return
                for cgi in range(2):
                    pso = psA.get()
                    for k in range(8):
                        if l == 0:
                            lhs = other[:, k, i * 128:(i + 1) * 128] if k < 4 else mixTb[:, k - 4, :]
                        else:
                            lhs = mixTb[:, k, :] if k < 4 else other[:, k - 4, j * 128:(j + 1) * 128]
                        S.op("pe", [other, mixTb, wout], [pso], lambda e: e.matmul(pso[:, 0:512], lhsT=lhs, rhs=woutv[:, k, cgi * 512:(cgi + 1) * 512], start=(k == 0), stop=(k == 7)))
                    residual(i, pso, cgi, 0 if lat else 1)

        def ffn_phase(l, tiles):
            tiles = list(tiles)
            with ExitStack() as ph:
                hTg = [S.sb(f"hF{l}_0", [128, 8, 256], BF16, ph)] + [S.sb(f"hF{l}_{g}", [128, 8, 512], BF16, ph) for g in range(1, 5)]
                with ExitStack() as ph2:
                    norm_phase(l, 1, hTg, ph2, tiles)
                    mk_gate(l, 40, ph2)
                    S.barrier()
                segs = [gg for gg in GROUPS if (gg[0] > 0 or 0 in tiles)]
                lo = segs[0][1]
                ranges = ([(0, 256)] if lo == 0 else []) + [(256, 2304)]
                GS = 3
                ur = Ring([S.sb(f"fu{l}_{i}", [128, 2304], F32, ph) for i in range(2)])
                yr = Ring([S.sb(f"fy{l}_{i}", [128, 2304], F32, ph) for i in range(2)])
                actT = S.sb(f"actT{l}", [128, GS, 2304], BF16, ph)
                wur = Ring([S.sbd(f"wu{l}_{i}", [128, 8 * 256], BF16, ph) for i in range(3)])
                wdr = Ring([S.sbd(f"wd{l}_{i}", [128, GS * 1024], BF16, ph) for i in range(2)])
                has_ctx = (lo == 0)
                wdraw = Ring([S.sb(f"wdraw{l}_{i}", [128, GS * 1024], BF16, ph) for i in range(1)]) if has_ctx else None
                upsrc = D["ffn_up"][l].rearrange("(k p) (g c n) -> p k g c n", p=128, g=2, c=22)
                wu_loaded = {}
                wd_loaded = {}

                def load_wu(cp):
                    if cp >= 22 or cp in wu_loaded:
                        return
                    wub = wur.get()
                    wu = wub[:, :].rearrange("p (k g n) -> p k g n", k=8, g=2)
                    for g3 in range(2):
                        S.dma("pool", wub.sem, wu[:, :, g3, :], upsrc[:, :, g3, cp, :], [], [wub])
                    wu_loaded[cp] = (wub, wu)

                def load_wd(c0):
                    if c0 >= 22 or c0 in wd_loaded:
                        return
                    ncg = min(GS, 22 - c0)
                    wdb = wdr.get()
                    wd = wdb[:, 0:ncg * 1024].rearrange("p (c n) -> p c n", c=ncg)
                    S.dma("pool", wdb.sem, wd, D["ffn_down"][l, c0 * 128:(c0 + ncg) * 128, :].rearrange("(c p) n -> p c n", p=128), [], [wdb])
                    wd_loaded[c0] = (wdb, wd)

                def scale_wd(c0):
                    ncg = min(GS, 22 - c0)
                    wdb, wd = wd_loaded[c0]
                    raw = None
                    if has_ctx:
                        rb = wdraw.get()
                        raw = rb[:, 0:ncg * 1024].rearrange("p (c n) -> p c n", c=ncg)
                        S.op("pool", [wdb], [rb], lambda e: e.tensor_copy(out=raw, in_=wd))
                        wd_loaded[c0] = (wdb, wd, rb, raw)
                    S.op("pool", [wdb, gbc], [wdb], lambda e: e.tensor_tensor(out=wd, in0=wd, in1=gbc[:, 0, :].unsqueeze(1).to_broadcast([128, ncg, 1024]), op=ALU.mult))
                    if not has_ctx:
                        wd_loaded[c0] = (wdb, wd, None, None)

                load_wu(0)
                load_wu(1)
                load_wd(0)

                def emit_up1(cp):
                    load_wu(cp + 2)
                    wub, wu = wu_loaded[cp]
                    ys = []
                    for gv in range(2):
                        ch = gv * 22 + cp
                        u = ur.get()
                        y = yr.get()
                        w0 = cw[:, l, 0, ch:ch + 1]
                        w1 = cw[:, l, 1, ch:ch + 1]
                        w2 = cw[:, l, 2, ch:ch + 1]
                        for (g, t0, n) in segs:
                            ps = psA.get()
                            for k in range(8):
                                S.op("pe", [wub, hTg[g]], [ps], lambda e: e.matmul(ps[:, 0:n], lhsT=wu[:, k, gv, :], rhs=hTg[g][:, k, 0:n], start=(k == 0), stop=(k == 7)))
                            S.op("act", [ps], [u], lambda e: e.activation(out=u[:, t0:t0 + n], in_=ps[:, 0:n], func=AF.Copy))
                        S.op("act", [u, cw, cb], [y], lambda e: e.activation(out=y[:, lo:2304], in_=u[:, lo:2304], func=AF.Identity, scale=w1, bias=cb[:, l, ch:ch + 1]))
                        for (a, b_) in ranges:
                            S.op("dve", [u, cw, y], [y], lambda e: e.scalar_tensor_tensor(out=y[:, a + 1:b_], in0=u[:, a:b_ - 1], scalar=w0, in1=y[:, a + 1:b_], op0=ALU.mult, op1=ALU.add))
                            S.op("dve", [u, cw, y], [y], lambda e: e.scalar_tensor_tensor(out=y[:, a:b_ - 1], in0=u[:, a + 1:b_], scalar=w2, in1=y[:, a:b_ - 1], op0=ALU.mult, op1=ALU.add))
                        ys.append(y)
                    S.op("act", [ys[0]], [ys[0]], lambda e: e.activation(out=ys[0][:, lo:2304], in_=ys[0][:, lo:2304], func=AF.Silu))
                    return ys

                def emit_up2(ys, ci):
                    S.op("dve", [ys[0], ys[1]], [actT], lambda e: e.tensor_tensor(out=actT[:, ci, lo:2304], in0=ys[0][:, lo:2304], in1=ys[1][:, lo:2304], op=ALU.mult))

                def emit_down(c0):
                    ncg = min(GS, 22 - c0)
                    scale_wd(c0)
                    wdb, wd, rb, raw = wd_loaded[c0]
                    for i in tiles:
                        for cgi in range(2):
                            ps = psO.get()
                            if i >= 2:
                                for ci in range(ncg):
                                    S.op("pe", [actT, wdb], [ps], lambda e: e.matmul(ps[:, 0:512], lhsT=actT[:, ci, i * 128:(i + 1) * 128], rhs=wd[:, ci, cgi * 512:(cgi + 1) * 512], start=(ci == 0), stop=(ci == ncg - 1)))
                                S.op("dve", [ps, xs[i]], [xs[i]], lambda e: e.tensor_tensor(out=xs[i][:, cgi * 512:(cgi + 1) * 512], in0=ps[:], in1=xs[i][:, cgi * 512:(cgi + 1) * 512], op=ALU.add))
                            else:
                                for ci in range(ncg):
                                    S.op("pe", [actT, rb], [ps], lambda e: e.matmul(ps[:, 0:512], lhsT=actT[:, ci, i * 128:(i + 1) * 128], rhs=raw[:, ci, cgi * 512:(cgi + 1) * 512], start=(ci == 0), stop=(ci == ncg - 1)))
                                residual(i, ps, cgi, 1)

                pending = None
                for c0 in range(0, 22, GS):
                    ncg = min(GS, 22 - c0)
                    ys0 = emit_up1(c0)
                    if pending is not None:
                        emit_down(pending)
                    load_wd(c0 + GS)
                    emit_up2(ys0, 0)
                    for ci in range(1, ncg):
                        emit_up2(emit_up1(c0 + ci), ci)
                    pending = c0
                emit_down(pending)
                S.barrier()

        def prep_batch(raw, sq, ss, T, nh, g_ap, out_buf, out_ap):
            n = T * nh
            r3 = raw[:, 0:T, :].rearrange("p t (h d) -> p (t h) d", d=64)
            s3 = sq[:, 0:T, :].rearrange("p t (h d) -> p (t h) d", d=64)
            S.op("act", [raw], [sq], lambda e: e.activation(out=sq[:, 0:T, :], in_=raw[:, 0:T, :], func=AF.Square))
            S.op("dve", [sq], [ss], lambda e: e.tensor_reduce(out=ss[:, 0:n], in_=s3, axis=AX.X, op=ALU.add))
            rstd_of(ss, lambda: ss[:, 0:n], 64)
            S.op("dve", [raw, ss], [raw], lambda e: e.tensor_tensor(out=r3, in0=r3, in1=ss[:, 0:n].unsqueeze(2).to_broadcast([128, n, 64]), op=ALU.mult))
            S.op("dve", [raw], [out_buf], lambda e: e.tensor_tensor(out=out_ap.rearrange("p t (h d) -> p (t h) d", d=64), in0=r3, in1=g_ap.unsqueeze(1).to_broadcast([128, n, 64]), op=ALU.mult))

        def na_attn(hTg, mixTd, wring, ph):
            nag = S.sbd("nag", [128, 128], F32, ph)
            S.dma("sp", nag.sem, nag[:], D["na_g_bc"], [], [nag])
            gq = S.sb("nagq", [128, 64], F32, ph)
            S.op("dve", [nag], [gq], lambda e: e.tensor_scalar(out=gq[:], in0=nag[:, 0:64], scalar1=0.125, scalar2=None, op0=ALU.mult))
            kTn = S.sb("kTn", [128, NT * 128], BF16, ph)
            vn = S.sb("vn", [128, NT, 2, 66], BF16, ph)
            qTn = S.sb("qTn", [128, 2048], BF16, ph)
            S.op("dve", [], [vn], lambda e: e.memset(vn[:, :, :, 64:65], 1.0))
            TB = 9
            raw = S.sb("nraw", [128, TB, 128], F32, ph)
            sq = S.sb("nsq", [128, TB, 128], F32, ph)
            ssb = S.sb("nss", [128, TB * 2], F32, ph)
            nrm = S.sb("nnrm", [128, TB, 128], BF16, ph)
            biasr = Ring([S.sbd(f"nbias{i}", [128, 25, 128], F32, ph) for i in range(1)])
            stmp = Ring([S.sb(f"nstmp{i}", [128, 5, 128], F32, ph) for i in range(2)])
            PTr = Ring([S.sb(f"PTn{i}", [128, 7, 128], BF16, ph) for i in range(3)])
            rdn = Ring([S.sb(f"nrd{i}", [128, 1], F32, ph) for i in range(3)])
            mixd = S.sb("mixd", [128, 16, 2, 64], BF16, ph)
            naS = Ring(psS.bufs + psA.bufs)
            wsrc = D["odd_w"][:, 768:2304].rearrange("(k p) (g h n) -> p k g h n", p=128, g=3, h=4)
            wl = {}

            def load_w(pr):
                if pr >= 4 or pr in wl:
                    return
                wb = wring.get()
                wq = wb[:, 0:8 * 384].rearrange("p (k g n) -> p k g n", k=8, g=3)
                for g3 in range(3):
                    S.dma("pool", wb.sem, wq[:, :, g3, :], wsrc[:, :, g3, pr, :], [], [wb])
                wl[pr] = (wb, wq)

            load_w(0)
            for pr in range(4):
                wb, wq = wl[pr]
                load_w(pr + 1)
                for t0 in range(0, NT, TB):
                    tl = list(range(t0, min(NT, t0 + TB)))
                    for i in tl:
                        g, off = tok_group(i)
                        ps = psA.get()
                        for k in range(8):
                            S.op("pe", [wb, hTg[g]], [ps], lambda e: e.matmul(ps[:, 0:256], lhsT=hTg[g][:, k, off:off + 128], rhs=wb[:, k * 384 + 128:k * 384 + 384], start=(k == 0), stop=(k == 7)))
                        S.op("act", [ps], [vn], lambda e: e.activation(out=vn[:, i, :, 0:64], in_=ps[:, 128:256].rearrange("p (g d) -> p g d", g=2), func=AF.Copy))
                        S.op("dve", [ps], [raw], lambda e: e.tensor_copy(out=raw[:, i - t0, :], in_=ps[:, 0:128]))
                    prep_batch(raw, sq, ssb, len(tl), 2, nag[:, 64:128], nrm, nrm[:, 0:len(tl), :])
                    for i in tl:
                        pt = psT.get()
                        S.op("pe", [nrm, identb], [pt], lambda e: e.transpose(out=pt[:, 0, :], in_=nrm[:, i - t0, :], identity=identb[:]))
                        S.op("act", [pt], [kTn], lambda e: e.activation(out=kTn[:, i * 128:(i + 1) * 128], in_=pt[:, 0, :], func=AF.Copy))
                for t0 in range(2, NT, 8):
                    tl = list(range(t0, t0 + 8))
                    for i in tl:
                        g, off = tok_group(i)
                        ps2 = psA.get()
                        for k in range(8):
                            S.op("pe", [wb, hTg[g]], [ps2], lambda e: e.matmul(ps2[:, 0:128], lhsT=hTg[g][:, k, off:off + 128], rhs=wq[:, k, 0, :], start=(k == 0), stop=(k == 7)))
                        S.op("dve", [ps2], [raw], lambda e: e.tensor_copy(out=raw[:, i - t0, :], in_=ps2[:, 0:128]))
                    prep_batch(raw, sq, ssb, 8, 2, gq[:], nrm, nrm[:, 0:8, :])
                    for i in tl:
                        j = i - 2
                        pt2 = psT.get()
                        S.op("pe", [nrm, identb], [pt2], lambda e: e.transpose(out=pt2[:, 0, :], in_=nrm[:, i - t0, :], identity=identb[:]))
                        S.op("dve", [pt2], [qTn], lambda e: e.tensor_copy(out=qTn[:, j * 128:(j + 1) * 128], in_=pt2[:, 0, :]))
                for hh in range(2):
                    head = 2 * pr + hh
                    bt = biasr.get()
                    S.dma("sp", bt.sem, bt[:], D["na_bias"][head], [], [bt])
                    prs = slice(hh * 64, (hh + 1) * 64)

                    def blocks_of(j):
                        ci = 0 if j == 0 else 1 if j == 1 else 3 if j == 14 else 4 if j == 15 else 2
                        mlist = list(range(j - 2, j + 3)) if ci == 2 else na_blocks[j]
                        return ci, len(mlist), [0, 1] + [m + 2 for m in mlist]

                    def emit_scores(j):
                        ci, nb, keyt = blocks_of(j)
                        pA = naS.get()
                        pB = naS.get()
                        for bi, kt in enumerate(keyt):
                            pp, off2 = (pA, bi) if bi < 4 else (pB, bi - 4)
                            S.op("pe", [kTn, qTn], [pp], lambda e: e.matmul(pp[:, off2 * 128:(off2 + 1) * 128], lhsT=kTn[prs, kt * 128:(kt + 1) * 128], rhs=qTn[prs, j * 128:(j + 1) * 128], start=True, stop=True))
                        return pA, pB

                    nxt = emit_scores(0)
                    for j in range(16):
                        ci, nb, keyt = blocks_of(j)
                        pA, pB = nxt
                        if j + 1 < 16:
                            nxt = emit_scores(j + 1)
                        stp = stmp.get()
                        S.op("dve", [pA, bt], [stp], lambda e: e.tensor_tensor(out=stp[:, 0:2, :], in0=pA[:, 256:512].rearrange("p (b q) -> p b q", b=2), in1=bt[:, ci * 5:ci * 5 + 2, :], op=ALU.add))
                        S.op("dve", [pB, bt], [stp], lambda e: e.tensor_tensor(out=stp[:, 2:nb, :], in0=pB[:, 0:(nb - 2) * 128].rearrange("p (b q) -> p b q", b=nb - 2), in1=bt[:, ci * 5 + 2:ci * 5 + nb, :], op=ALU.add))
                        PT = PTr.get()
                        S.op("act", [pA], [PT], lambda e: e.activation(out=PT[:, 0:2, :], in_=pA[:, 0:256].rearrange("p (b q) -> p b q", b=2), func=AF.Exp))
                        S.op("act", [stp], [PT], lambda e: e.activation(out=PT[:, 2:2 + nb, :], in_=stp[:, 0:nb, :], func=AF.Exp))
                        acc = psO.get()
                        for bi, kt in enumerate(keyt):
                            S.op("pe", [PT, vn], [acc], lambda e: e.matmul(acc[:, 0:65], lhsT=PT[:, bi, :], rhs=vn[:, kt, hh, 0:65], start=(bi == 0), stop=(bi == len(keyt) - 1)))
                        rd = rdn.get()
                        S.op("dve", [acc], [rd], lambda e: e.reciprocal(out=rd[:], in_=acc[:, 64:65]))
                        S.op("act", [acc, rd], [mixd], lambda e: e.activation(out=mixd[:, j, hh, :], in_=acc[:, 0:64], func=AF.Copy, scale=rd[:, 0:1]))
                for j in range(16):
                    pt = psT.get()
                    S.op("pe", [mixd, identb], [pt], lambda e: e.transpose(out=pt[:, 0, :], in_=mixd[:, j, :, :].rearrange("p a d -> p (a d)"), identity=identb[:]))
                    S.op("dve", [pt], [mixTd], lambda e: e.tensor_copy(out=mixTd[:, pr, j * 128:(j + 1) * 128], in_=pt[:, 0, :]))

        def mixer1():
            with ExitStack() as ph:
                hTg = [S.sb("hT1_0", [128, 8, 256], BF16, ph)] + [S.sb(f"hT1_{g}", [128, 8, 512], BF16, ph) for g in range(1, 5)]
                with ExitStack() as ph2:
                    norm_phase(1, 0, hTg, ph2)
                    mk_gate(1, 16, ph2)
                    S.barrier()
                mixTd = S.sb("mixTd", [128, 4, 2048], BF16, ph)
                with ExitStack() as ph2:
                    wring = Ring([S.sbd(f"w1_{i}", [128, 8 * 384], BF16, ph2) for i in range(2)])
                    na_attn(hTg, mixTd, wring, ph2)
                    S.barrier()
                with ExitStack() as ph2:
                    gqa_attn(1, hTg, mixTd, None, ph2)
                    S.barrier()

        mod_phase(0)
        if stage != "mod":
            mixer0()
        if stage not in ("l0mix", "mod", "norm", "mlstm"):
            ffn_phase(0, range(NT))
        if stage not in ("l0mix", "l0", "mod", "norm", "mlstm"):
            mod_phase(1)
            mixer1()
            if stage != "l1mix":
                ffn_phase(1, range(2, NT))
        osem = S.newsem("d_out")
        for i in range(2, NT):
            S.dma("sp", osem, out[(i - 2) * 128:(i - 1) * 128, :], xs[i][:], [xs[i]], [])
        if dbg:
            for i in range(2):
                S.dma("sp", osem, octx[i * 128:(i + 1) * 128, :], xs[i][:], [xs[i]], [])
        S._need("sp", osem, S.cnt[osem])
        S.barrier()
        print(f"[kernel] instructions={S.ninst} waits={S.nwait} sems={len(S.sems)}", flush=True)
    return nc


_CACHE = {}


def kernel(**inputs):
    shared, percore = host_prepare({k: np.asarray(v) for k, v in inputs.items()})
    if "nc" not in _CACHE:
        _CACHE["nc"] = build_program("full")
    nc = _CACHE["nc"]
    in_maps = []
    for b in range(8):
        m = dict(shared)
        m.update(percore[b])
        in_maps.append(m)
    res = run_bass_kernel_spmd(nc, in_maps, core_ids=list(range(8)))
    return np.stack([np.asarray(r["out"], np.float32) for r in res.results], axis=0)
```

```python
import numpy as np
from contextlib import ExitStack
import concourse.bass as bass
import concourse.mybir as mybir
from concourse.bass_utils import run_bass_kernel_spmd

F32 = mybir.dt.float32
BF16 = mybir.dt.bfloat16
AF = mybir.ActivationFunctionType
ALU = mybir.AluOpType
AX = mybir.AxisListType

ENGS = ("pe", "act", "dve", "pool", "sp")
NT = 18
EPS = 1e-6
NEGM = -30000.0


class Buf:
    __slots__ = ("t", "name", "w", "r", "sem", "psum")

    def __init__(self, t, name):
        self.t = t
        self.name = name
        self.w = None
        self.r = {}
        self.sem = None
        self.psum = False

    def __getitem__(self, idx):
        return self.t[idx]


class Ring:
    def __init__(self, bufs):
        self.bufs = bufs
        self.i = 0

    def get(self):
        b = self.bufs[self.i % len(self.bufs)]
        self.i += 1
        return b


SEM_LIMIT = 1500


class Sched:
    def __init__(self, nc, stack):
        self.nc = nc
        self.stack = stack
        self.eng = {"pe": nc.tensor, "act": nc.scalar, "dve": nc.vector,
                    "pool": nc.gpsimd, "sp": nc.sync}
        self.sems = {}
        self.cnt = {}
        self.epoch = {}
        self.cur = {}
        for e in ENGS:
            self.epoch[e] = 0
            self._new_epoch(e)
        self.seen = {e: {} for e in ENGS}
        self.ninst = 0
        self.nwait = 0
        self.nsem = 0
        self.nalloc = 0

    def _new_epoch(self, e):
        self.epoch[e] += 1
        key = f"{e}#{self.epoch[e]}"
        self.sems[key] = self.stack.enter_context(self.nc.semaphore("s_" + key.replace("#", "_")))
        self.cnt[key] = 0
        self.cur[e] = key

    def sb(self, name, shape, dt, stack=None):
        self.nalloc += 1
        name = f"{name}_{self.nalloc}"
        t = (stack or self.stack).enter_context(self.nc.sbuf_tensor(name, list(shape), dt))
        return Buf(t, name)

    def ps(self, name, shape, dt=F32):
        t = self.stack.enter_context(self.nc.psum_tensor(name, list(shape), dt))
        b = Buf(t, name)
        b.psum = True
        return b

    def newsem(self, name=None):
        self.nsem += 1
        name = name or f"d{self.nsem}"
        s = self.stack.enter_context(self.nc.semaphore(name))
        self.sems[name] = s
        self.cnt[name] = 0
        return name

    def sbd(self, name, shape, dt, stack=None):
        b = self.sb(name, shape, dt, stack)
        b.sem = self.newsem("d_" + b.name)
        return b

    @staticmethod
    def _eng_of(key):
        return key.split("#")[0] if "#" in key else None

    def _need(self, e, key, val):
        if self.seen[e].get(key, 0) >= val:
            return
        ke = self._eng_of(key)
        if ke is not None:
            ep = int(key.split("#")[1])
            for k2, v2 in self.seen[e].items():
                if v2 > 0 and self._eng_of(k2) == ke and int(k2.split("#")[1]) > ep:
                    return
        self.seen[e][key] = val
        self.eng[e].wait_ge(self.sems[key], val)
        self.nwait += 1

    def deps(self, e, reads, writes):
        for b in reads:
            if b.w is not None:
                k, v = b.w
                if not (self._eng_of(k) == e and e == "pe"):
                    self._need(e, k, v)
        for b in writes:
            if b.w is not None:
                k, v = b.w
                if self._eng_of(k) != e:
                    self._need(e, k, v)
            for k, v in b.r.items():
                if self._eng_of(k) != e:
                    self._need(e, k, v)

    def op(self, e, reads, writes, fn):
        pr = [b for b in reads if b.psum]
        if pr:
            reads = [b for b in reads if not b.psum]
            writes = list(writes) + [b for b in pr if b not in writes]
        self.deps(e, reads, writes)
        ins = fn(self.eng[e])
        if self.cnt[self.cur[e]] >= SEM_LIMIT:
            self._new_epoch(e)
        key = self.cur[e]
        self.cnt[key] += 1
        ins.then_inc(self.sems[key], 1)
        v = self.cnt[key]
        for b in reads:
            for k2 in [k2 for k2 in b.r if self._eng_of(k2) == e]:
                del b.r[k2]
            b.r[key] = v
        for b in writes:
            b.w = (key, v)
            b.r = {}
        self.ninst += 1
        return ins

    def dma(self, q, semkey, out_ap, in_ap, reads, writes, **kw):
        self.deps(q, reads, writes)
        ins = self.eng[q].dma_start(out=out_ap, in_=in_ap, **kw)
        self.cnt[semkey] += 16
        assert self.cnt[semkey] <= 2000, semkey
        ins.then_inc(self.sems[semkey], 16)
        v = self.cnt[semkey]
        for b in reads:
            b.r[semkey] = v
        for b in writes:
            b.w = (semkey, v)
            b.r = {}
        self.ninst += 1
        return ins

    def barrier(self):
        for e in ENGS:
            for k, v in list(self.cnt.items()):
                ke = self._eng_of(k)
                if ke == e or v == 0:
                    continue
                if ke is not None and k != self.cur[ke]:
                    if not (self.cnt[self.cur[ke]] == 0 and int(k.split("#")[1]) == self.epoch[ke] - 1):
                        continue
                self._need(e, k, v)


def _rope_tables():
    t = np.arange(2048)
    row = (t // 64).astype(np.float32)
    col = (t % 64).astype(np.float32)
    half = 32
    freq = (np.float32(10000.0) ** (-np.arange(0, half, 2, dtype=np.float32) / np.float32(half))).astype(np.float32)
    ang_r = row[:, None] * freq[None, :]
    ang_c = col[:, None] * freq[None, :]
    ang = np.concatenate([ang_r, ang_r, ang_c, ang_c], axis=-1).astype(np.float32)
    cos = np.cos(ang).astype(np.float32)
    sin = np.sin(ang).astype(np.float32)
    sgn = np.ones(64, np.float32)
    sgn[0:16] = -1.0
    sgn[32:48] = -1.0
    sinS = sin * sgn[None, :]
    cos = cos.reshape(16, 128, 64).transpose(1, 0, 2).copy()
    sinS = sinS.reshape(16, 128, 64).transpose(1, 0, 2).copy()
    return cos, sinS


def _na_tables(rpb):
    rows = 32
    wr = 8
    r = np.arange(rows)
    row_start = np.clip(r - wr // 2, 0, rows - wr)
    col = np.arange(64)
    col_start = np.clip(col - 8, 0, 48)
    col_ok = (col[None, :] >= col_start[:, None]) & (col[None, :] < col_start[:, None] + 16)
    dc = np.clip(col[None, :] - col[:, None] + 15, 0, 30)
    classes = [0, 1, 2, 14, 15]
    blocks = {}
    tab = np.full((8, 128, 25, 128), NEGM, np.float32)
    for ci, j in enumerate(classes):
        qrows = [2 * j, 2 * j + 1]
        lo = min(row_start[q] for q in qrows)
        hi = max(row_start[q] + wr - 1 for q in qrows)
        mlist = list(range(lo // 2, hi // 2 + 1))
        assert len(mlist) <= 5
        blocks[j] = mlist
        for si, m in enumerate(mlist):
            for kr in range(2):
                krow = 2 * m + kr
                for qr in range(2):
                    qrow = qrows[qr]
                    if not (row_start[qrow] <= krow < row_start[qrow] + wr):
                        continue
                    dr = krow - qrow + 7
                    sub = rpb[:, dr, :][:, dc]
                    sub = np.where(col_ok[None], sub, np.float32(NEGM))
                    tab[:, kr * 64:(kr + 1) * 64, ci * 5 + si, qr * 64:(qr + 1) * 64] = sub.transpose(0, 2, 1)
    return tab, blocks, classes


def _na_blocks():
    _, blocks, classes = _na_tables(np.zeros((8, 15, 31), np.float32))
    return blocks, classes


def host_prepare(inp):
    f = np.float32
    shared = {}
    shared["ada_w"] = np.ascontiguousarray(inp["ada_w"], f)
    shared["ada_bT"] = np.ascontiguousarray(inp["ada_b"].reshape(2, 48, 128).transpose(2, 0, 1), f)
    shared["norm_gT"] = np.ascontiguousarray(inp["norm_g"].reshape(2, 2, 8, 128).transpose(3, 0, 1, 2), f)
    shared["w_out"] = np.ascontiguousarray(inp["w_out"], f)
    shared["ffn_up"] = np.ascontiguousarray(inp["ffn_up"], f)
    shared["ffn_down"] = np.ascontiguousarray(inp["ffn_down"], f)
    shared["conv_wT"] = np.ascontiguousarray(inp["ffn_conv_w"].reshape(2, 3, 44, 128).transpose(3, 0, 1, 2), f)
    shared["conv_bT"] = np.ascontiguousarray(inp["ffn_conv_b"].reshape(2, 44, 128).transpose(2, 0, 1), f)
    shared["even_w"] = np.ascontiguousarray(inp["even_w_in"][0], f)
    shared["odd_w"] = np.ascontiguousarray(inp["odd_w_in"][0], f)
    bc = lambda a: np.ascontiguousarray(np.broadcast_to(np.asarray(a, f).reshape(1, -1), (128, a.size)))
    shared["gate_b_bc"] = bc(inp["mlstm_gate_b"][0])
    shared["head_g_bc"] = bc(inp["mlstm_head_g"][0])
    shared["swa_g_bc"] = bc(inp["swa_qk_g"][0])
    shared["sink_bc"] = bc(inp["swa_sink"][0])
    shared["gqa_g_bc"] = bc(inp["gqa_qk_g"][0])
    shared["na_g_bc"] = bc(inp["na_qk_g"][0])
    tab, _, _ = _na_tables(np.asarray(inp["na_rpb"][0], f))
    shared["na_bias"] = tab
    ident = np.eye(128, dtype=f)
    s = np.arange(128)
    triU = (s[:, None] <= s[None, :]).astype(f)
    triL = (s[:, None] >= s[None, :]).astype(f)
    wm = np.zeros((128, 2, 128), f)
    wm[:, 0, :] = np.where(s[None, :] <= s[:, None], 0.0, NEGM)
    wm[:, 1, :] = np.where(s[:, None] <= s[None, :], 0.0, NEGM)
    shared["consts"] = np.ascontiguousarray(np.concatenate([ident, triU, triL, wm.reshape(128, 256)], axis=1))
    cos, sinS = _rope_tables()
    shared["rope"] = np.ascontiguousarray(np.stack([cos, sinS], axis=1))
    percore = []
    for b in range(8):
        cc = np.stack([inp["c"][b].reshape(8, 128).T, inp["c_ctx"].reshape(8, 128).T], axis=-1)
        percore.append({"x": np.ascontiguousarray(inp["x"][b], f), "ctx": np.ascontiguousarray(inp["ctx"][b], f),
                        "cc": np.ascontiguousarray(cc, f)})
    return shared, percore


SHARED_SHAPES = {
    "ada_w": [2, 1024, 6144], "ada_bT": [128, 2, 48], "norm_gT": [128, 2, 2, 8], "w_out": [2, 1024, 1024],
    "ffn_up": [2, 1024, 5632], "ffn_down": [2, 2816, 1024], "conv_wT": [128, 2, 3, 44], "conv_bT": [128, 2, 44],
    "even_w": [1024, 2832], "odd_w": [1024, 2304], "gate_b_bc": [128, 16], "head_g_bc": [128, 512],
    "swa_g_bc": [128, 128], "sink_bc": [128, 8], "gqa_g_bc": [128, 128], "na_g_bc": [128, 128],
    "na_bias": [8, 128, 25, 128], "consts": [128, 640], "rope": [128, 2, 16, 64],
    "x": [2048, 1024], "ctx": [256, 1024], "cc": [128, 8, 2],
}


GROUPS = [(0, 0, 256), (1, 256, 512), (2, 768, 512), (3, 1280, 512), (4, 1792, 512)]


def tok_group(i):
    return (0, i * 128) if i < 2 else (1 + (i - 2) // 4, ((i - 2) % 4) * 128)


def build_program(stage="full"):
    nc = bass.Bass("TRN2", target_bir_lowering=False)
    D = {k: nc.dram_tensor(k, shp, F32, kind="ExternalInput").ap() for k, shp in SHARED_SHAPES.items()}
    out = nc.dram_tensor("out", [2048, 1024], F32, kind="ExternalOutput").ap()
    dbg = stage != "full"
    if dbg:
        octx = nc.dram_tensor("octx", [256, 1024], F32, kind="ExternalOutput").ap()
        dbgd = nc.dram_tensor("dbgd", [128, 8192], F32, kind="ExternalOutput").ap()
    na_blocks, na_classes = _na_blocks()

    with ExitStack() as st:
        S = Sched(nc, st)
        xs = [S.sbd(f"xs{i}", [128, 1024], F32) for i in range(NT)]
        cst = S.sbd("cst", [128, 640], F32)
        cc = S.sbd("cc", [128, 8, 2], F32)
        adab = S.sbd("adab", [128, 2, 48], F32)
        ngT = S.sbd("ngT", [128, 2, 2, 8], F32)
        cw = S.sbd("cw", [128, 2, 3, 44], F32)
        cb = S.sbd("cb", [128, 2, 44], F32)
        identb = S.sb("identb", [128, 128], BF16)
        wmb = S.sb("wmb", [128, 2, 128], BF16)
        ones_f = S.sb("ones_f", [128, 128], F32)
        ones_b = S.sb("ones_b", [128, 128], BF16)
        sc = S.sb("sc", [128, 8, 2], F32)
        modT = [S.sb(f"modT{l}", [128, 48, 2], F32) for l in range(2)]
        gbc = S.sb("gbc", [128, 2, 1024], F32)
        AB = S.sb("AB", [128, 8, 2], F32)

        psT = Ring([S.ps(f"psT{i}", [128, 8, 128], BF16) for i in range(2)])
        psA = Ring([S.ps(f"psA{i}", [128, 512], F32) for i in range(2)])
        psS = Ring([S.ps(f"psS{i}", [128, 512], F32) for i in range(2)])
        psO = Ring([S.ps(f"psO{i}", [128, 512], F32) for i in range(2)])

        IDF = lambda: cst[:, 0:128]
        TRIU = lambda: cst[:, 128:256]
        TRIL = lambda: cst[:, 256:384]

        S.dma("sp", cst.sem, cst[:], D["consts"], [], [cst])
        S.dma("sp", cc.sem, cc[:], D["cc"], [], [cc])
        S.dma("sp", adab.sem, adab[:], D["ada_bT"], [], [adab])
        S.dma("sp", ngT.sem, ngT[:], D["norm_gT"], [], [ngT])
        S.dma("sp", cw.sem, cw[:], D["conv_wT"], [], [cw])
        S.dma("sp", cb.sem, cb[:], D["conv_bT"], [], [cb])
        for i in range(NT):
            src = D["ctx"][i * 128:(i + 1) * 128, :] if i < 2 else D["x"][(i - 2) * 128:(i - 1) * 128, :]
            S.dma("sp", xs[i].sem, xs[i][:], src, [], [xs[i]])
        S.op("dve", [cst], [identb], lambda e: e.tensor_copy(out=identb[:], in_=cst[:, 0:128]))
        S.op("dve", [cst], [wmb], lambda e: e.tensor_copy(out=wmb[:], in_=cst[:, 384:640].rearrange("p (a b) -> p a b", a=2)))
        S.op("dve", [], [ones_f], lambda e: e.memset(ones_f[:], 1.0))
        S.op("dve", [], [ones_b], lambda e: e.memset(ones_b[:], 1.0))
        S.op("act", [cc], [sc], lambda e: e.activation(out=sc[:], in_=cc[:], func=AF.Silu))

        dstg = S.sb("dstg", [128, 128], F32) if dbg else None
        dstate = {"col": 0, "items": []}

        def dump(name, buf, ap, n):
            if not dbg:
                return
            stg = dstg
            sem = S.newsem()
            S.op("act", [buf], [stg], lambda e: e.activation(out=stg[:, 0:n], in_=ap, func=AF.Copy))
            c0 = dstate["col"]
            S.dma("sp", sem, dbgd[:, c0:c0 + n], stg[:, 0:n], [stg], [])
            S._need("sp", sem, S.cnt[sem])
            dstate["items"].append((name, c0, n))
            dstate["col"] = c0 + n
            print("DUMP", name, c0, n, flush=True)

        def wview(wb, shape_str, **kw):
            n = 1
            for v in kw.values():
                n *= v
            return wb

        def mod_phase(l):
            with ExitStack() as ph:
                ring = Ring([S.sbd(f"adaw{l}_{i}", [128, 8, 512], BF16, ph) for i in range(3)])
                schi = S.sb(f"schi{l}", [128, 8, 2], BF16, ph)
                schf = S.sb(f"schf{l}", [128, 8, 2], F32, ph)
                sclo = S.sb(f"sclo{l}", [128, 8, 2], BF16, ph)
                S.op("dve", [sc], [schi], lambda e: e.tensor_copy(out=schi[:], in_=sc[:]))
                S.op("dve", [schi], [schf], lambda e: e.tensor_copy(out=schf[:], in_=schi[:]))
                S.op("dve", [sc, schf], [schf], lambda e: e.tensor_tensor(out=schf[:], in0=sc[:], in1=schf[:], op=ALU.subtract))
                S.op("dve", [schf], [sclo], lambda e: e.tensor_copy(out=sclo[:], in_=schf[:]))
                wbs = {}

                def ld(cg):
                    if cg >= 12:
                        return
                    wb = ring.get()
                    S.dma("pool", wb.sem, wb[:], D["ada_w"][l, :, cg * 512:(cg + 1) * 512].rearrange("(k p) n -> p k n", p=128), [], [wb])
                    wbs[cg] = wb
                ld(0)
                ld(1)
                for cg in range(12):
                    ld(cg + 2)
                    wb = wbs[cg]
                    ps = psA.get()
                    for c4 in range(4):
                        for k in range(8):
                            S.op("pe", [wb, schi], [ps], lambda e: e.matmul(ps[:, c4 * 2:c4 * 2 + 2], lhsT=wb[:, k, c4 * 128:(c4 + 1) * 128], rhs=schi[:, k, :], start=(k == 0), stop=False))
                            S.op("pe", [wb, sclo], [ps], lambda e: e.matmul(ps[:, c4 * 2:c4 * 2 + 2], lhsT=wb[:, k, c4 * 128:(c4 + 1) * 128], rhs=sclo[:, k, :], start=False, stop=(k == 7)))
                    S.op("dve", [ps, adab], [modT[l]], lambda e: e.tensor_tensor(
                        out=modT[l][:, cg * 4:(cg + 1) * 4, :], in0=ps[:, 0:8].rearrange("p (c j) -> p c j", j=2),
                        in1=adab[:, l, cg * 4:(cg + 1) * 4].unsqueeze(2).to_broadcast([128, 4, 2]), op=ALU.add))
                S.barrier()

        def mk_AB(l, which):
            scl = 8 if which == 0 else 32
            S.op("dve", [modT[l]], [AB], lambda e: e.tensor_scalar(out=AB[:], in0=modT[l][:, scl:scl + 8, :], scalar1=1.0, scalar2=None, op0=ALU.add))
            S.op("dve", [AB, ngT], [AB], lambda e: e.tensor_tensor(out=AB[:], in0=AB[:], in1=ngT[:, l, which, :].unsqueeze(2).to_broadcast([128, 8, 2]), op=ALU.mult))

        def mk_gate(l, gchunk, ph):
            hl = S.sb(f"ghl{l}_{gchunk}", [128, 8, 2], F32, ph)
            hb = S.sb(f"ghb{l}_{gchunk}", [128, 8, 2], BF16, ph)
            hf = S.sb(f"ghf{l}_{gchunk}", [128, 8, 2], F32, ph)
            lo = S.sb(f"glo{l}_{gchunk}", [128, 8, 2], F32, ph)
            lb = S.sb(f"glb{l}_{gchunk}", [128, 8, 2], BF16, ph)
            lf = S.sb(f"glf{l}_{gchunk}", [128, 8, 2], F32, ph)
            S.op("dve", [modT[l]], [hl], lambda e: e.tensor_copy(out=hl[:], in_=modT[l][:, gchunk:gchunk + 8, :]))
            S.op("dve", [hl], [hb], lambda e: e.tensor_copy(out=hb[:], in_=hl[:]))
            S.op("dve", [hb], [hf], lambda e: e.tensor_copy(out=hf[:], in_=hb[:]))
            S.op("dve", [hl, hf], [lo], lambda e: e.tensor_tensor(out=lo[:], in0=hl[:], in1=hf[:], op=ALU.subtract))
            S.op("dve", [lo], [lb], lambda e: e.tensor_copy(out=lb[:], in_=lo[:]))
            S.op("dve", [lb], [lf], lambda e: e.tensor_copy(out=lf[:], in_=lb[:]))
            dgr = Ring([S.sb(f"dg{l}_{gchunk}_{i}", [128, 2, 128], BF16, ph) for i in range(2)])
            for j in range(2):
                for half in range(2):
                    ps = psA.get()
                    for k4 in range(4):
                        kk = half * 4 + k4
                        dg = dgr.get()
                        S.op("dve", [identb, hf], [dg], lambda e: e.tensor_scalar(out=dg[:, 0, :], in0=identb[:], scalar1=hf[:, kk, j:j + 1], scalar2=None, op0=ALU.mult))
                        S.op("dve", [identb, lf], [dg], lambda e: e.tensor_scalar(out=dg[:, 1, :], in0=identb[:], scalar1=lf[:, kk, j:j + 1], scalar2=None, op0=ALU.mult))
                        S.op("pe", [ones_b, dg], [ps], lambda e: e.matmul(ps[:, k4 * 128:(k4 + 1) * 128], lhsT=ones_b[:], rhs=dg[:, 0, :], start=True, stop=False))
                        S.op("pe", [ones_b, dg], [ps], lambda e: e.matmul(ps[:, k4 * 128:(k4 + 1) * 128], lhsT=ones_b[:], rhs=dg[:, 1, :], start=False, stop=True))
                    S.op("act", [ps], [gbc], lambda e: e.activation(out=gbc[:, j, half * 512:(half + 1) * 512], in_=ps[:], func=AF.Copy))

        def rstd_of(t, n_ap, dim):
            S.op("dve", [t], [t], lambda e: e.tensor_scalar(out=n_ap(), in0=n_ap(), scalar1=1.0 / dim, scalar2=EPS, op0=ALU.mult, op1=ALU.add))
            S.op("act", [t], [t], lambda e: e.activation(out=n_ap(), in_=n_ap(), func=AF.Ln))
            S.op("act", [t], [t], lambda e: e.activation(out=n_ap(), in_=n_ap(), func=AF.Exp, scale=-0.5))

        def norm_phase(l, which, hTg, ph, tiles=range(NT)):
            mk_AB(l, which)
            sh = 0 if which == 0 else 24
            ss = S.sb(f"nss{l}{which}", [128, NT], F32, ph)
            junk = S.sb(f"njunk{l}{which}", [128, 1024], BF16, ph)
            xnr = Ring([S.sb(f"xn{l}{which}_{i}", [128, 1024], BF16, ph) for i in range(2)])
            S.op("dve", [], [ss], lambda e: e.memset(ss[:], 1.0))
            for i in tiles:
                S.op("act", [xs[i]], [junk, ss], lambda e: e.activation(out=junk[:], in_=xs[i][:], func=AF.Square, accum_out=ss[:, i:i + 1]))
            rstd_of(ss, lambda: ss[:], 1024)
            import os
            if os.environ.get("KSUB") in ("a", "c"):
                return
            for i in tiles:
                xn = xnr.get()
                S.op("dve", [xs[i], ss], [xn], lambda e: e.tensor_scalar(out=xn[:], in0=xs[i][:], scalar1=ss[:, i:i + 1], scalar2=None, op0=ALU.mult))
                pt = psT.get()
                for k in range(8):
                    S.op("pe", [xn, identb], [pt], lambda e: e.transpose(out=pt[:, k, :], in_=xn[:, k * 128:(k + 1) * 128], identity=identb[:]))
                g, off = tok_group(i)
                j = 1 if i < 2 else 0
                for k in range(8):
                    if k % 2 == 0:
                        S.op("dve", [pt, AB, modT[l]], [hTg[g]], lambda e: e.tensor_scalar(
                            out=hTg[g][:, k, off:off + 128], in0=pt[:, k, :], scalar1=AB[:, k, j:j + 1], scalar2=modT[l][:, sh + k, j:j + 1], op0=ALU.mult, op1=ALU.add))
                    else:
                        S.op("act", [pt, AB, modT[l]], [hTg[g]], lambda e: e.activation(
                            out=hTg[g][:, k, off:off + 128], in_=pt[:, k, :], func=AF.Identity, scale=AB[:, k, j:j + 1], bias=modT[l][:, sh + k, j:j + 1]))

        def wload(wb, n, src):
            dst = wb[:, 0:8 * n].rearrange("p (k n) -> p k n", k=8)
            S.dma("pool", wb.sem, dst, src, [], [wb])
            return dst

        def qk_prep(ps, ps_ap, nh, g_ap, rope_tile, out_ap, wk, rope):
            sq, ssq, qn, t1 = wk
            n = nh * 64
            v3 = lambda ap: ap.rearrange("p (h d) -> p h d", d=64)
            S.op("act", [ps], [sq], lambda e: e.activation(out=sq[:, 0:n], in_=ps_ap, func=AF.Square))
            S.op("dve", [sq], [ssq], lambda e: e.tensor_reduce(out=ssq[:, 0:nh], in_=v3(sq[:, 0:n]), axis=AX.X, op=ALU.add))
            rstd_of(ssq, lambda: ssq[:, 0:nh], 64)
            S.op("dve", [ps, ssq], [qn], lambda e: e.tensor_tensor(out=v3(qn[:, 0:n]), in0=v3(ps_ap), in1=ssq[:, 0:nh].unsqueeze(2).to_broadcast([128, nh, 64]), op=ALU.mult))
            if rope_tile is None:
                S.op("dve", [qn], [out_ap[0]], lambda e: e.tensor_tensor(out=out_ap[1], in0=v3(qn[:, 0:n]), in1=g_ap.unsqueeze(1).to_broadcast([128, nh, 64]), op=ALU.mult))
                return
            S.op("dve", [qn], [qn], lambda e: e.tensor_tensor(out=v3(qn[:, 0:n]), in0=v3(qn[:, 0:n]), in1=g_ap.unsqueeze(1).to_broadcast([128, nh, 64]), op=ALU.mult))
            cos_ap = rope[:, 0, :]
            sin_ap = rope[:, 1, :]
            S.op("dve", [qn, rope], [t1], lambda e: e.tensor_tensor(out=v3(t1[:, 0:n]), in0=v3(qn[:, 0:n]), in1=cos_ap.unsqueeze(1).to_broadcast([128, nh, 64]), op=ALU.mult))
            v5 = lambda ap: ap.rearrange("p (h x y d) -> p h x y d", x=2, y=2, d=16)
            s4 = sin_ap.rearrange("p (x y d) -> p x y d", x=2, y=2)
            for y in range(2):
                S.op("dve", [qn, rope], [sq], lambda e: e.tensor_tensor(
                    out=v5(sq[:, 0:n])[:, :, :, y, :], in0=v5(qn[:, 0:n])[:, :, :, 1 - y, :],
                    in1=s4[:, :, y, :].unsqueeze(1).to_broadcast([128, nh, 2, 16]), op=ALU.mult))
            S.op("dve", [t1, sq], [out_ap[0]], lambda e: e.tensor_tensor(out=out_ap[1], in0=v3(t1[:, 0:n]), in1=v3(sq[:, 0:n]), op=ALU.add))

        def prep_batch(raw, sq, ss, T, nh, g_ap, out_buf, out_ap, inplace=False):
            n = T * nh
            r3 = raw[:, 0:T, :].rearrange("p t (h d) -> p (t h) d", d=64)
            s3 = sq[:, 0:T, :].rearrange("p t (h d) -> p (t h) d", d=64)
            S.op("act", [raw], [sq], lambda e: e.activation(out=sq[:, 0:T, :], in_=raw[:, 0:T, :], func=AF.Square))
            S.op("dve", [sq], [ss], lambda e: e.tensor_reduce(out=ss[:, 0:n], in_=s3, axis=AX.X, op=ALU.add))
            rstd_of(ss, lambda: ss[:, 0:n], 64)
            S.op("dve", [raw, ss], [raw], lambda e: e.tensor_tensor(out=r3, in0=r3, in1=ss[:, 0:n].unsqueeze(2).to_broadcast([128, n, 64]), op=ALU.mult))
            if inplace:
                S.op("dve", [raw], [raw], lambda e: e.tensor_tensor(out=r3, in0=r3, in1=g_ap.unsqueeze(1).to_broadcast([128, n, 64]), op=ALU.mult))
                return
            S.op("dve", [raw], [out_buf], lambda e: e.tensor_tensor(out=out_ap.rearrange("p t (h d) -> p (t h) d", d=64), in0=r3, in1=g_ap.unsqueeze(1).to_broadcast([128, n, 64]), op=ALU.mult))

        def residual(i, ps, cgi, j):
            tmp = restmp.get()
            S.op("dve", [ps, gbc], [tmp], lambda e: e.tensor_tensor(out=tmp[:], in0=ps[:], in1=gbc[:, j, cgi * 512:(cgi + 1) * 512], op=ALU.mult))
            rstate["n"] += 1
            S.op("dve", [tmp, xs[i]], [xs[i]], lambda e: e.tensor_tensor(out=xs[i][:, cgi * 512:(cgi + 1) * 512], in0=xs[i][:, cgi * 512:(cgi + 1) * 512], in1=tmp[:], op=ALU.add))

        restmp = Ring([S.sb(f"restmp{i}", [128, 512], F32) for i in range(2)])
        rstate = {"n": 0}

        def mixer0():
            l = 0
            with ExitStack() as ph:
                hTg = [S.sb("hT0_0", [128, 8, 256], BF16, ph)] + [S.sb(f"hT0_{g}", [128, 8, 512], BF16, ph) for g in range(1, 5)]
                with ExitStack() as ph2:
                    norm_phase(0, 0, hTg, ph2)
                    import os
                    if os.environ.get("KSUB") not in ("a", "b"):
                        mk_gate(0, 16, ph2)
                    S.barrier()
                if stage == "norm":
                    return
                mixTa = S.sb("mixTa", [128, 4, NT * 128], BF16, ph)
                with ExitStack() as ph2:
                    wring = Ring([S.sbd(f"w0_{i}", [128, 8 * 384], BF16, ph2) for i in range(2)])
                    gateb = S.sbd("gateb", [128, 16], F32, ph2)
                    headg = S.sbd("headg", [128, 512], F32, ph2)
                    S.dma("sp", gateb.sem, gateb[:], D["gate_b_bc"], [], [gateb])
                    S.dma("sp", headg.sem, headg[:], D["head_g_bc"], [], [headg])
                    mlstm(hTg, mixTa, gateb, headg, wring, ph2)
                    S.barrier()
                if stage == "mlstm":
                    return
                with ExitStack() as ph2:
                    gqa_attn(0, hTg, mixTa, None, ph2)
                    S.barrier()

        def mlstm_gates(hTg, gateb, wring, pg, es, eb, edec, ekw):
            G = S.sb("G", [128, NT, 16], F32, pg)
            wg = wload(wring.get(), 16, D["even_w"][:, 2048:2064].rearrange("(k p) n -> p k n", p=128))
            wgb = wring.bufs[(wring.i - 1) % len(wring.bufs)]
            for i in range(NT):
                g, off = tok_group(i)
                ps = psO.get()
                for k in range(8):
                    S.op("pe", [hTg[g], wgb], [ps], lambda e: e.matmul(ps[:, 0:16], lhsT=hTg[g][:, k, off:off + 128], rhs=wg[:, k, :], start=(k == 0), stop=(k == 7)))
                S.op("dve", [ps, gateb], [G], lambda e: e.tensor_tensor(out=G[:, i, :], in0=ps[:, 0:16], in1=gateb[:], op=ALU.add))
            E = S.sb("E", [128, 2, NT, 4], F32, pg)
            for d in range(2):
                S.op("act", [G], [E], lambda e: e.activation(out=E[:, d], in_=G[:, :, 4 + 8 * d:8 + 8 * d], func=AF.Exp, scale=-1.0))
            S.op("dve", [E], [E], lambda e: e.tensor_scalar(out=E[:], in0=E[:], scalar1=1.0, scalar2=None, op0=ALU.add))
            S.op("act", [E], [E], lambda e: e.activation(out=E[:], in_=E[:], func=AF.Ln))
            tg = S.sb("tg", [128, NT, 4], F32, pg)
            f72 = lambda ap: ap.rearrange("p t h -> p (t h)")
            trib = S.sb("trib", [128, 2, 128], BF16, pg)
            S.op("dve", [cst], [trib], lambda e: e.tensor_copy(out=trib[:], in_=cst[:, 128:384].rearrange("p (a b) -> p a b", a=2)))
            Ehi = S.sb("Ehi", [128, 2, NT, 4], BF16, pg)
            Ehf = S.sb("Ehf", [128, 2, NT, 4], F32, pg)
            Elo = S.sb("Elo", [128, 2, NT, 4], BF16, pg)
            S.op("dve", [E], [Ehi], lambda e: e.tensor_copy(out=Ehi[:], in_=E[:]))
            S.op("dve", [Ehi], [Ehf], lambda e: e.tensor_copy(out=Ehf[:], in_=Ehi[:]))
            S.op("dve", [E, Ehf], [Ehf], lambda e: e.tensor_tensor(out=Ehf[:], in0=E[:], in1=Ehf[:], op=ALU.subtract))
            S.op("dve", [Ehf], [Elo], lambda e: e.tensor_copy(out=Elo[:], in_=Ehf[:]))
            for d in range(2):
                psb = psO.get()
                S.op("pe", [trib, Ehi], [psb], lambda e: e.matmul(psb[:, 0:72], lhsT=trib[:, d, :], rhs=f72(Ehi[:, d]), start=True, stop=False))
                S.op("pe", [trib, Elo], [psb], lambda e: e.matmul(psb[:, 0:72], lhsT=trib[:, d, :], rhs=f72(Elo[:, d]), start=False, stop=True))
                S.op("pe", [ones_b, Ehi], [psb], lambda e: e.matmul(psb[:, 72:144], lhsT=ones_b[:], rhs=f72(Ehi[:, d]), start=True, stop=False))
                S.op("pe", [ones_b, Elo], [psb], lambda e: e.matmul(psb[:, 72:144], lhsT=ones_b[:], rhs=f72(Elo[:, d]), start=False, stop=True))
                S.op("dve", [psb, G], [tg], lambda e: e.tensor_tensor(out=tg[:], in0=psb[:, 0:72].rearrange("p (t h) -> p t h", h=4), in1=G[:, :, 8 * d:8 * d + 4], op=ALU.add))
                S.op("act", [tg], [es], lambda e: e.activation(out=es[:, d], in_=tg[:], func=AF.Exp))
                S.op("act", [psb], [eb], lambda e: e.activation(out=f72(eb[:, d]), in_=psb[:, 0:72], func=AF.Exp, scale=-1.0))
                S.op("act", [psb], [edec], lambda e: e.activation(out=f72(edec[:, d]), in_=psb[:, 72:144], func=AF.Exp, scale=-1.0))
                S.op("dve", [es, edec], [ekw], lambda e: e.tensor_tensor(out=ekw[:, d], in0=es[:, d], in1=edec[:, d], op=ALU.mult))

            pass
            pass
            pass
            pass
            pass

        def mlstm(hTg, mixTa, gateb, headg, wring, ph):
            es = S.sb("es", [128, 2, NT, 4], F32, ph)
            eb = S.sb("eb", [128, 2, NT, 4], F32, ph)
            edec = S.sb("edec", [128, 2, NT, 4], F32, ph)
            ekw = S.sb("ekw", [128, 2, NT, 4], F32, ph)
            with ExitStack() as pg:
                mlstm_gates(hTg, gateb, wring, pg, es, eb, edec, ekw)
                S.barrier()
            KS_ = ""
            KH_ = -1
            qT = S.sb("qTa", [128, NT * 128], BF16, ph)
            kT = S.sb("kTa", [128, NT * 128], BF16, ph)
            ktok = S.sb("ktok", [128, NT, 128], BF16, ph)
            vaug = S.sb("vaug", [128, NT, 130], BF16, ph)
            hraw = [S.sb(f"hraw{d}", [128, NT, 130], F32, ph) for d in range(2)]
            rnm = S.sb("rnm", [128, 2, NT], F32, ph)
            Cst = [S.sb(f"Cst{d}", [128, 129], F32, ph) for d in range(2)]
            Cbf3 = [[S.sb(f"Cbf{d}_{r}", [128, 130], BF16, ph) for r in range(3)] for d in range(2)]
            PTr = Ring([S.sb(f"PTm{i}", [128, 128], BF16, ph) for i in range(4)])
            kwr = Ring([S.sb(f"kwm{i}", [128, 128], BF16, ph) for i in range(4)])
            hss = S.sb("hss", [128, NT], F32, ph)
            hjunk = S.sb("hjunk", [128, 128], F32, ph)
            ogr = Ring([S.sb(f"og{i}", [128, 128], F32, ph) for i in range(2)])
            t1r = Ring([S.sb(f"mt1{i}", [128, 128], F32, ph) for i in range(2)])
            mxr = Ring([S.sb(f"mmx{i}", [128, 128], BF16, ph) for i in range(2)])
            S.op("dve", [], [vaug], lambda e: e.memset(vaug[:, :, 128:129], 1.0))
            orders = [list(range(NT)), [1, 0] + list(range(NT - 1, 1, -1))]
            KS = 128.0 ** -0.5

            for h in range(4):
                wb = wring.get()
                src = D["even_w"][:, 0:1536].rearrange("(k p) (g h n) -> p k g h n", p=128, g=3, h=4)[:, :, :, h, :]
                wq = wb[:, 0:8 * 384].rearrange("p (k g n) -> p k g n", k=8, g=3)
                for g3 in range(3):
                    S.dma("pool", wb.sem, wq[:, :, g3, :], src[:, :, g3, :], [], [wb])
                wob = wring.get()
                wo = wload(wob, 128, D["even_w"][:, 1536 + h * 128:1536 + (h + 1) * 128].rearrange("(k p) n -> p k n", p=128))
                flip = 0
                for (g, c0, n) in GROUPS:
                    for which, dst, scl in ((0, qT, 1.0), (1, kT, KS)):
                        ps = psA.get()
                        for k in range(8):
                            S.op("pe", [wb, hTg[g]], [ps], lambda e: e.matmul(ps[:, 0:n], lhsT=wq[:, k, which, :], rhs=hTg[g][:, k, 0:n], start=(k == 0), stop=(k == 7)))
                        if flip % 2 == 0:
                            S.op("act", [ps], [dst], lambda e: e.activation(out=dst[:, c0:c0 + n], in_=ps[:, 0:n], func=AF.Copy, scale=scl))
                        else:
                            S.op("dve", [ps], [dst], lambda e: e.tensor_scalar(out=dst[:, c0:c0 + n], in0=ps[:, 0:n], scalar1=scl, scalar2=None, op0=ALU.mult))
                        flip += 1
                if KS_ == "m2a" and h == KH_:
                    return
                for i in range(NT):
                    g, off = tok_group(i)
                    ps = psA.get()
                    for k in range(8):
                        S.op("pe", [wb, hTg[g]], [ps], lambda e: e.matmul(ps[:, 0:256], lhsT=hTg[g][:, k, off:off + 128], rhs=wb[:, k * 384 + 128:k * 384 + 384], start=(k == 0), stop=(k == 7)))
                    S.op("act", [ps], [ktok], lambda e: e.activation(out=ktok[:, i, :], in_=ps[:, 0:128], func=AF.Copy, scale=KS))
                    S.op("dve", [ps], [vaug], lambda e: e.tensor_copy(out=vaug[:, i, 0:128], in_=ps[:, 128:256]))
                if KS_ == "m2b" and h == KH_:
                    return
                if h == 0:
                    pass
                    pass
                    pass
                    pass
                if KS_ == "m2" and h == KH_:
                    return
                written = [False] * NT
                PTs = {}

                def emitA(step, d):
                    i = orders[d][step]
                    col = lambda a: a[:, d, i, h:h + 1]
                    cs = slice(i * 128, (i + 1) * 128)
                    pss = psS.get()
                    S.op("pe", [kT, qT], [pss], lambda e: e.matmul(pss[:, 0:128], lhsT=kT[:, cs], rhs=qT[:, cs], start=True, stop=True))
                    PT = PTr.get()
                    msk = TRIU() if d == 0 else TRIL()
                    S.op("dve", [pss, es, cst], [PT], lambda e: e.scalar_tensor_tensor(out=PT[:], in0=pss[:, 0:128], scalar=col(es), in1=msk, op0=ALU.mult, op1=ALU.mult))
                    PTs[(step, d)] = PT
                    if step < NT - 1:
                        kw = kwr.get()
                        S.op("act", [ktok, ekw], [kw], lambda e: e.activation(out=kw[:], in_=ktok[:, i, :], func=AF.Copy, scale=col(ekw)))
                        psc = psA.get()
                        S.op("pe", [kw, vaug], [psc], lambda e: e.matmul(psc[:, 0:129], lhsT=kw[:], rhs=vaug[:, i, 0:129], start=True, stop=True))
                        if step == 0:
                            S.op("dve", [psc], [Cst[d]], lambda e: e.tensor_copy(out=Cst[d][:], in_=psc[:, 0:129]))
                        else:
                            S.op("dve", [psc, Cst[d], edec], [Cst[d]], lambda e: e.scalar_tensor_tensor(out=Cst[d][:], in0=Cst[d][:], scalar=col(edec), in1=psc[:, 0:129], op0=ALU.mult, op1=ALU.add))
                        cb3 = Cbf3[d][(step + 1) % 3]
                        S.op("act", [Cst[d]], [cb3], lambda e: e.activation(out=cb3[:, 0:129], in_=Cst[d][:], func=AF.Copy))

                def emitB(step, d):
                    i = orders[d][step]
                    col = lambda a: a[:, d, i, h:h + 1]
                    cs = slice(i * 128, (i + 1) * 128)
                    PT = PTs.pop((step, d))
                    acc = psO.get()
                    if step > 0:
                        cb3 = Cbf3[d][step % 3]
                        S.op("pe", [qT, cb3], [acc], lambda e: e.matmul(acc[:, 0:129], lhsT=qT[:, cs], rhs=cb3[:, 0:129], start=True, stop=False))
                    S.op("pe", [PT, vaug], [acc], lambda e: e.matmul(acc[:, 0:129], lhsT=PT[:], rhs=vaug[:, i, 0:129], start=(step == 0), stop=True))
                    S.op("act", [acc, eb], [hraw[d]], lambda e: e.activation(out=hraw[d][:, i, 0:129], in_=acc[:, 0:129], func=AF.Copy, scale=col(eb)))

                emitA(0, 0)
                emitA(0, 1)
                for step in range(NT):
                    if step + 1 < NT:
                        emitA(step + 1, 0)
                        emitA(step + 1, 1)
                    emitB(step, 0)
                    emitB(step, 1)
                if h == 0:
                    pass
                    pass
                if KS_ == "m3" and h == KH_:
                    return
                for d in range(2):
                    S.op("act", [hraw[d]], [rnm], lambda e: e.activation(out=rnm[:, d, :], in_=hraw[d][:, :, 128], func=AF.Abs))
                S.op("dve", [rnm], [rnm], lambda e: e.tensor_scalar(out=rnm[:], in0=rnm[:], scalar1=1.0, scalar2=None, op0=ALU.max))
                S.op("dve", [rnm], [rnm], lambda e: e.reciprocal(out=rnm[:], in_=rnm[:]))
                for d in range(2):
                    S.op("dve", [hraw[d], rnm], [hraw[d]], lambda e: e.tensor_tensor(out=hraw[d][:, :, 0:128], in0=hraw[d][:, :, 0:128], in1=rnm[:, d, :].unsqueeze(2).to_broadcast([128, NT, 128]), op=ALU.mult))
                S.op("dve", [hraw[0], hraw[1]], [hraw[0]], lambda e: e.tensor_tensor(out=hraw[0][:, :, 0:128], in0=hraw[0][:, :, 0:128], in1=hraw[1][:, :, 0:128], op=ALU.add))
                S.op("dve", [], [hss], lambda e: e.memset(hss[:], 1.0))
                for i in range(NT):
                    S.op("act", [hraw[0]], [hjunk, hss], lambda e: e.activation(out=hjunk[:], in_=hraw[0][:, i, 0:128], func=AF.Square, accum_out=hss[:, i:i + 1]))
                rstd_of(hss, lambda: hss[:], 128)
                for i in range(NT):
                    g, off = tok_group(i)
                    ps = psA.get()
                    for k in range(8):
                        S.op("pe", [wob, hTg[g]], [ps], lambda e: e.matmul(ps[:, 0:128], lhsT=hTg[g][:, k, off:off + 128], rhs=wo[:, k, :], start=(k == 0), stop=(k == 7)))
                    og = ogr.get()
                    S.op("act", [ps], [og], lambda e: e.activation(out=og[:], in_=ps[:, 0:128], func=AF.Sigmoid))
                    t1 = t1r.get()
                    S.op("dve", [hraw[0], hss, headg], [t1], lambda e: e.scalar_tensor_tensor(out=t1[:], in0=hraw[0][:, i, 0:128], scalar=hss[:, i:i + 1], in1=headg[:, h * 128:(h + 1) * 128], op0=ALU.mult, op1=ALU.mult))
                    mx = mxr.get()
                    S.op("dve", [t1, og], [mx], lambda e: e.tensor_tensor(out=mx[:], in0=t1[:], in1=og[:], op=ALU.mult))
                    pt = psT.get()
                    S.op("pe", [mx, identb], [pt], lambda e: e.transpose(out=pt[:, 0, :], in_=mx[:], identity=identb[:]))
                    S.op("act", [pt], [mixTa], lambda e: e.activation(out=mixTa[:, h, i * 128:(i + 1) * 128], in_=pt[:, 0, :], func=AF.Copy))
                if KS_ == "m4" and h == KH_:
                    return

        def attn_scores_exp_pv(kv_specs, nheads_per_kv, qT, q_sl, PTr, accs, first, last):
            pass

        def gqa_attn(l, hTg, other, wring, ph):
            wname = "even_w" if l == 0 else "odd_w"
            qc0, kc0 = (2064, 2576) if l == 0 else (0, 512)
            swag = S.sbd(f"swag{l}", [128, 128], F32, ph)
            roper = Ring([S.sbd(f"rope{l}_{i}", [128, 2, 64], F32, ph) for i in range(2)])

            def get_rope(jt):
                rb = roper.get()
                S.dma("sp", rb.sem, rb[:], D["rope"][:, :, jt, :], [], [rb])
                return rb
            S.dma("sp", swag.sem, swag[:], D["swa_g_bc" if l == 0 else "gqa_g_bc"], [], [swag])
            gq = S.sb(f"gq{l}", [128, 64], F32, ph)
            S.op("dve", [swag], [gq], lambda e: e.tensor_scalar(out=gq[:], in0=swag[:, 0:64], scalar1=0.125, scalar2=None, op0=ALU.mult))
            esink = S.sb(f"esink{l}", [128, 8], F32, ph)
            if l == 0:
                sinkb = S.sbd("sinkb", [128, 8], F32, ph)
                S.dma("sp", sinkb.sem, sinkb[:], D["sink_bc"], [], [sinkb])
                S.op("act", [sinkb], [esink], lambda e: e.activation(out=esink[:], in_=sinkb[:], func=AF.Exp))
            else:
                S.op("dve", [], [esink], lambda e: e.memset(esink[:], 0.0))
            wkb = S.sbd(f"wkv{l}", [128, 8 * 256], BF16, ph)
            wkv = wload(wkb, 256, D[wname][:, kc0:kc0 + 256].rearrange("(k p) n -> p k n", p=128))
            kTd = [S.sb(f"kTd{g}", [128, NT * 128], BF16, ph) for g in range(2)]
            vb = S.sb("vb", [128, NT, 2, 66], BF16, ph)
            S.op("dve", [], [vb], lambda e: e.memset(vb[:, :, :, 64:65], 1.0))
            with ExitStack() as pk:
                TB = 9
                kraw = S.sb("kraw", [128, TB, 128], F32, pk)
                ksq = S.sb("ksq", [128, TB, 128], F32, pk)
                kt2 = S.sb("kt2", [128, TB, 128], F32, pk)
                kss = S.sb("kss", [128, TB * 2], F32, pk)
                knb = S.sb("knb", [128, TB, 128], BF16, pk)
                kd = S.sb("kd", [128, 2, 2, 64], BF16, pk)
                rtab = S.sbd("rtab", [128, 2, TB, 64], F32, pk)
                for t0 in range(0, NT, TB):
                    tl = list(range(t0, t0 + TB))
                    r0 = max(0, 2 - t0)
                    nl = TB - r0
                    j0 = t0 + r0 - 2
                    for cs_ in range(2):
                        S.dma("sp", rtab.sem, rtab[:, cs_, 0:nl, :], D["rope"][:, cs_, j0:j0 + nl, :], [], [rtab])
                    for i in tl:
                        g, off = tok_group(i)
                        ps = psA.get()
                        for k in range(8):
                            S.op("pe", [wkb, hTg[g]], [ps], lambda e: e.matmul(ps[:, 0:256], lhsT=hTg[g][:, k, off:off + 128], rhs=wkv[:, k, :], start=(k == 0), stop=(k == 7)))
                        S.op("act", [ps], [vb], lambda e: e.activation(out=vb[:, i, :, 0:64], in_=ps[:, 128:256].rearrange("p (g d) -> p g d", g=2), func=AF.Copy))
                        S.op("dve", [ps], [kraw], lambda e: e.tensor_copy(out=kraw[:, i - t0, :], in_=ps[:, 0:128]))
                    prep_batch(kraw, ksq, kss, TB, 2, swag[:, 64:128], None, None, inplace=True)
                    if r0 > 0:
                        S.op("act", [kraw], [knb], lambda e: e.activation(out=knb[:, 0:r0, :], in_=kraw[:, 0:r0, :], func=AF.Copy))
                    v4 = lambda ap: ap.rearrange("p t (h d) -> p t h d", d=64)
                    cosb = rtab[:, 0, 0:nl, :].unsqueeze(2).to_broadcast([128, nl, 2, 64])
                    S.op("dve", [kraw, rtab], [ksq], lambda e: e.tensor_tensor(out=v4(ksq[:, r0:TB, :]), in0=v4(kraw[:, r0:TB, :]), in1=cosb, op=ALU.mult))
                    v6 = lambda ap: ap.rearrange("p t (h x y d) -> p t h x y d", h=2, x=2, y=2)
                    s5 = rtab[:, 1, 0:nl, :].rearrange("p t (x y d) -> p t x y d", x=2, y=2)
                    for hh_ in range(2):
                        for y in range(2):
                            S.op("dve", [kraw, rtab], [kt2], lambda e: e.tensor_tensor(
                                out=v6(kt2[:, r0:TB, :])[:, :, hh_, :, y, :], in0=v6(kraw[:, r0:TB, :])[:, :, hh_, :, 1 - y, :],
                                in1=s5[:, :, :, y, :], op=ALU.mult))
                    S.op("dve", [ksq, kt2], [knb], lambda e: e.tensor_tensor(out=knb[:, r0:TB, :], in0=ksq[:, r0:TB, :], in1=kt2[:, r0:TB, :], op=ALU.add))
                    for i in tl:
                        S.op("dve", [knb], [kd], lambda e: e.tensor_copy(out=kd[:], in_=knb[:, i - t0, :].rearrange("p (g d) -> p g d", g=2).unsqueeze(2).to_broadcast([128, 2, 2, 64])))
                        pt = psT.get()
                        for g2 in range(2):
                            S.op("pe", [kd, identb], [pt], lambda e: e.transpose(out=pt[:, g2, :], in_=kd[:, g2].rearrange("p a d -> p (a d)"), identity=identb[:]))
                        S.op("act", [pt], [kTd[0]], lambda e: e.activation(out=kTd[0][:, i * 128:(i + 1) * 128], in_=pt[:, 0, :], func=AF.Copy))
                        S.op("act", [pt], [kTd[1]], lambda e: e.activation(out=kTd[1][:, i * 128:(i + 1) * 128], in_=pt[:, 1, :], func=AF.Copy))
                S.barrier()
            wqb = S.sbd(f"wqq{l}", [128, 8 * 512], BF16, ph)
            wq = wload(wqb, 512, D[wname][:, qc0:qc0 + 512].rearrange("(k p) n -> p k n", p=128))
            wout = S.sbd(f"wout{l}", [128, 8 * 1024], BF16, ph)
            woutv = wout[:, :].rearrange("p (k n) -> p k n", k=8)
            S.dma("pool", wout.sem, woutv, D["w_out"][l].rearrange("(k p) n -> p k n", p=128), [], [wout])
            wk = (S.sb("wk_sq", [128, 512], F32, ph), S.sb("wk_ss", [128, 8], F32, ph), S.sb("wk_qn", [128, 512], F32, ph), S.sb("wk_t1", [128, 512], F32, ph))
            import os
            KS_ = os.environ.get("KSUB", "")
            if KS_ == "w1" or (KS_ == "g1k" and l == 1):
                return
            qb = S.sb("qb", [128, 8, 64], BF16, ph)
            qz = S.sb("qz", [128, 2, 4, 128], BF16, ph)
            S.op("dve", [], [qz], lambda e: e.memset(qz[:], 0.0))
            wmb4 = S.sb("wmb4", [128, 2, 4, 128], BF16, ph)
            S.op("dve", [wmb], [wmb4], lambda e: e.tensor_copy(out=wmb4[:], in_=wmb[:, :, :].unsqueeze(2).to_broadcast([128, 2, 4, 128])))
            PTr = Ring([S.sb(f"PTw{i}", [128, 512], BF16, ph) for i in range(2)])
            den = S.sb("wden", [128, 8], F32, ph)
            mixb = S.sb("mixb", [128, 512], BF16, ph)
            mixTb = S.sb("mixTb", [128, 4, 128], BF16, ph)
            def emit_qprep(i):
                g, off = tok_group(i)
                lat = i >= 2
                j = i - 2
                ps = psA.get()
                for k in range(8):
                    S.op("pe", [wqb, hTg[g]], [ps], lambda e: e.matmul(ps[:, 0:512], lhsT=hTg[g][:, k, off:off + 128], rhs=wq[:, k, :], start=(k == 0), stop=(k == 7)))
                qk_prep(ps, ps[:, 0:512], 8, gq[:], j if lat else None, (qb, qb[:]), wk, get_rope(j) if lat else None)

            qtiles = list(range(NT) if l == 0 else range(2, NT))
            emit_qprep(qtiles[0])
            for qi, i in enumerate(qtiles):
                g, off = tok_group(i)
                lat = i >= 2
                j = i - 2
                pt = psT.get()
                for pr in range(4):
                    S.op("pe", [qb, identb], [pt], lambda e: e.transpose(out=pt[:, pr, :], in_=qb[:, 2 * pr:2 * pr + 2, :].rearrange("p a d -> p (a d)"), identity=identb[:]))
                S.op("act", [pt], [qz], lambda e: e.activation(out=qz[0:64, 0, :, :], in_=pt[0:64, 0:4, :], func=AF.Copy))
                S.op("dve", [pt], [qz], lambda e: e.tensor_copy(out=qz[64:128, 1, :, :], in_=pt[64:128, 0:4, :]))
                if qi + 1 < len(qtiles):
                    emit_qprep(qtiles[qi + 1])
                if KS_ == "w2a":
                    return
                if l == 1:
                    blocks = [(m, None) for m in range(NT)]
                elif lat:
                    blocks = [(0, None), (1, None)]
                    if j > 0:
                        blocks.append((i - 1, 0))
                    blocks.append((i, None))
                    if j < 15:
                        blocks.append((i + 1, 1))
                else:
                    blocks = [(0, None), (1, None)]
                for g2 in range(2):
                    acc = psO.get()

                    def emit_scores(m, msk):
                        pss = psS.get()
                        for half in range(2):
                            S.op("pe", [kTd[g2], qz], [pss], lambda e: e.matmul(
                                pss[:, half * 256:(half + 1) * 256], lhsT=kTd[g2][:, m * 128:(m + 1) * 128],
                                rhs=qz[:, half, 2 * g2:2 * g2 + 2, :].rearrange("p a q -> p (a q)"),
                                start=(half == 0), stop=(half == 1 and msk is None)))
                        if msk is not None:
                            S.op("pe", [identb, wmb4], [pss], lambda e: e.matmul(pss[:, 0:512], lhsT=identb[:], rhs=wmb4[:, msk, :, :].rearrange("p a q -> p (a q)"), start=False, stop=True))
                        return pss

                    nxt = emit_scores(*blocks[0])
                    for bi, (m, msk) in enumerate(blocks):
                        pss = nxt
                        if bi + 1 < len(blocks):
                            nxt = emit_scores(*blocks[bi + 1])
                        PT = PTr.get()
                        S.op("act", [pss], [PT], lambda e: e.activation(out=PT[:], in_=pss[:], func=AF.Exp))
                        for hh in range(4):
                            S.op("pe", [PT, vb], [acc], lambda e: e.matmul(acc[:, hh * 128:hh * 128 + 65], lhsT=PT[:, hh * 128:(hh + 1) * 128], rhs=vb[:, m, g2, 0:65], start=(bi == 0 and hh == 0), stop=(bi == len(blocks) - 1)))
                    if KS_ == "w2c":
                        return
                    a3 = acc[:, :].rearrange("p (h c) -> p h c", h=4)
                    S.op("dve", [acc, esink], [den], lambda e: e.tensor_tensor(out=den[:, g2 * 4:(g2 + 1) * 4].rearrange("p (b a) -> p b a", b=2), in0=a3[:, :, 64].rearrange("p (b a) -> p b a", b=2),
                                                                            in1=esink[:, g2 * 4:(g2 + 1) * 4].rearrange("p (a b) -> p b a", a=2), op=ALU.add))
                    S.op("dve", [den], [den], lambda e: e.reciprocal(out=den[:, g2 * 4:(g2 + 1) * 4], in_=den[:, g2 * 4:(g2 + 1) * 4]))
                    S.op("dve", [acc, den], [mixb], lambda e: e.tensor_tensor(
                        out=mixb[:, g2 * 256:(g2 + 1) * 256].rearrange("p (a b d) -> p b a d", a=2, b=2), in0=a3[:, :, 0:64].rearrange("p (b a) d -> p b a d", b=2),
                        in1=den[:, g2 * 4:(g2 + 1) * 4].rearrange("p (b a) -> p b a", b=2).unsqueeze(3).to_broadcast([128, 2, 2, 64]), op=ALU.mult))
                if KS_ == "w2d":
                    return
                pt2 = psT.get()
                for c in range(4):
                    S.op("pe", [mixb, identb], [pt2], lambda e: e.transpose(out=pt2[:, c, :], in_=mixb[:, c * 128:(c + 1) * 128], identity=identb[:]))
                S.op("act", [pt2], [mixTb], lambda e: e.activation(out=mixTb[:], in_=pt2[:, 0:4, :], func=AF.Copy))
                if (KS_ == "w2" and i == 2) or (KS_ == "g1q" and l == 1 and i == 3):
                    return
                for cgi in range(2):
                    pso = psA.get()
                    for k in range(8):
                        if l == 0:
                            lhs = other[:, k, i * 128:(i + 1) * 128] if k < 4 else mixTb[:, k - 4, :]
                        else:
                            lhs = mixTb[:, k, :] if k < 4 else other[:, k - 4, j * 128:(j + 1) * 128]
                        S.op("pe", [other, mixTb, wout], [pso], lambda e: e.matmul(pso[:, 0:512], lhsT=lhs, rhs=woutv[:, k, cgi * 512:(cgi + 1) * 512], start=(k == 0), stop=(k == 7)))
                    residual(i, pso, cgi, 0 if lat else 1)

        def ffn_phase(l, tiles):
            tiles = list(tiles)
            with ExitStack() as ph:
                hTg = [S.sb(f"hF{l}_0", [128, 8, 256], BF16, ph)] + [S.sb(f"hF{l}_{g}", [128, 8, 512], BF16, ph) for g in range(1, 5)]
                with ExitStack() as ph2:
                    norm_phase(l, 1, hTg, ph2, tiles)
                    mk_gate(l, 40, ph2)
                    S.barrier()
                segs = [gg for gg in GROUPS if (gg[0] > 0 or 0 in tiles)]
                lo = segs[0][1]
                ranges = ([(0, 256)] if lo == 0 else []) + [(256, 2304)]
                GS = 3
                ur = Ring([S.sb(f"fu{l}_{i}", [128, 2304], F32, ph) for i in range(2)])
                yr = Ring([S.sb(f"fy{l}_{i}", [128, 2304], F32, ph) for i in range(2)])
                actT = S.sb(f"actT{l}", [128, GS, 2304], BF16, ph)
                wur = Ring([S.sbd(f"wu{l}_{i}", [128, 8 * 256], BF16, ph) for i in range(3)])
                wdr = Ring([S.sbd(f"wd{l}_{i}", [128, GS * 1024], BF16, ph) for i in range(2)])
                has_ctx = (lo == 0)
                wdraw = Ring([S.sb(f"wdraw{l}_{i}", [128, GS * 1024], BF16, ph) for i in range(1)]) if has_ctx else None
                upsrc = D["ffn_up"][l].rearrange("(k p) (g c n) -> p k g c n", p=128, g=2, c=22)
                wu_loaded = {}
                wd_loaded = {}

                def load_wu(cp):
                    if cp >= 22 or cp in wu_loaded:
                        return
                    wub = wur.get()
                    wu = wub[:, :].rearrange("p (k g n) -> p k g n", k=8, g=2)
                    for g3 in range(2):
                        S.dma("pool", wub.sem, wu[:, :, g3, :], upsrc[:, :, g3, cp, :], [], [wub])
                    wu_loaded[cp] = (wub, wu)

                def load_wd(c0):
                    if c0 >= 22 or c0 in wd_loaded:
                        return
                    ncg = min(GS, 22 - c0)
                    wdb = wdr.get()
                    wd = wdb[:, 0:ncg * 1024].rearrange("p (c n) -> p c n", c=ncg)
                    S.dma("pool", wdb.sem, wd, D["ffn_down"][l, c0 * 128:(c0 + ncg) * 128, :].rearrange("(c p) n -> p c n", p=128), [], [wdb])
                    wd_loaded[c0] = (wdb, wd)

                def scale_wd(c0):
                    ncg = min(GS, 22 - c0)
                    wdb, wd = wd_loaded[c0]
                    raw = None
                    if has_ctx:
                        rb = wdraw.get()
                        raw = rb[:, 0:ncg * 1024].rearrange("p (c n) -> p c n", c=ncg)
                        S.op("pool", [wdb], [rb], lambda e: e.tensor_copy(out=raw, in_=wd))
                        wd_loaded[c0] = (wdb, wd, rb, raw)
                    S.op("pool", [wdb, gbc], [wdb], lambda e: e.tensor_tensor(out=wd, in0=wd, in1=gbc[:, 0, :].unsqueeze(1).to_broadcast([128, ncg, 1024]), op=ALU.mult))
                    if not has_ctx:
                        wd_loaded[c0] = (wdb, wd, None, None)

                load_wu(0)
                load_wu(1)
                load_wd(0)

                def emit_up1(cp):
                    load_wu(cp + 2)
                    wub, wu = wu_loaded[cp]
                    ys = []
                    for gv in range(2):
                        ch = gv * 22 + cp
                        u = ur.get()
                        y = yr.get()
                        w0 = cw[:, l, 0, ch:ch + 1]
                        w1 = cw[:, l, 1, ch:ch + 1]
                        w2 = cw[:, l, 2, ch:ch + 1]
                        for (g, t0, n) in segs:
                            ps = psA.get()
                            for k in range(8):
                                S.op("pe", [wub, hTg[g]], [ps], lambda e: e.matmul(ps[:, 0:n], lhsT=wu[:, k, gv, :], rhs=hTg[g][:, k, 0:n], start=(k == 0), stop=(k == 7)))
                            S.op("act", [ps], [u], lambda e: e.activation(out=u[:, t0:t0 + n], in_=ps[:, 0:n], func=AF.Copy))
                        S.op("act", [u, cw, cb], [y], lambda e: e.activation(out=y[:, lo:2304], in_=u[:, lo:2304], func=AF.Identity, scale=w1, bias=cb[:, l, ch:ch + 1]))
                        for (a, b_) in ranges:
                            S.op("dve", [u, cw, y], [y], lambda e: e.scalar_tensor_tensor(out=y[:, a + 1:b_], in0=u[:, a:b_ - 1], scalar=w0, in1=y[:, a + 1:b_], op0=ALU.mult, op1=ALU.add))
                            S.op("dve", [u, cw, y], [y], lambda e: e.scalar_tensor_tensor(out=y[:, a:b_ - 1], in0=u[:, a + 1:b_], scalar=w2, in1=y[:, a:b_ - 1], op0=ALU.mult, op1=ALU.add))
                        ys.append(y)
                    S.op("act", [ys[0]], [ys[0]], lambda e: e.activation(out=ys[0][:, lo:2304], in_=ys[0][:, lo:2304], func=AF.Silu))
                    return ys

                def emit_up2(ys, ci):
                    S.op("dve", [ys[0], ys[1]], [actT], lambda e: e.tensor_tensor(out=actT[:, ci, lo:2304], in0=ys[0][:, lo:2304], in1=ys[1][:, lo:2304], op=ALU.mult))

                def emit_down(c0):
                    ncg = min(GS, 22 - c0)
                    scale_wd(c0)
                    wdb, wd, rb, raw = wd_loaded[c0]
                    for i in tiles:
                        for cgi in range(2):
                            ps = psO.get()
                            if i >= 2:
                                for ci in range(ncg):
                                    S.op("pe", [actT, wdb], [ps], lambda e: e.matmul(ps[:, 0:512], lhsT=actT[:, ci, i * 128:(i + 1) * 128], rhs=wd[:, ci, cgi * 512:(cgi + 1) * 512], start=(ci == 0), stop=(ci == ncg - 1)))
                                S.op("dve", [ps, xs[i]], [xs[i]], lambda e: e.tensor_tensor(out=xs[i][:, cgi * 512:(cgi + 1) * 512], in0=ps[:], in1=xs[i][:, cgi * 512:(cgi + 1) * 512], op=ALU.add))
                            else:
                                for ci in range(ncg):
                                    S.op("pe", [actT, rb], [ps], lambda e: e.matmul(ps[:, 0:512], lhsT=actT[:, ci, i * 128:(i + 1) * 128], rhs=raw[:, ci, cgi * 512:(cgi + 1) * 512], start=(ci == 0), stop=(ci == ncg - 1)))
                                residual(i, ps, cgi, 1)

                pending = None
                for c0 in range(0, 22, GS):
                    ncg = min(GS, 22 - c0)
                    ys0 = emit_up1(c0)
                    if pending is not None:
                        emit_down(pending)
                    load_wd(c0 + GS)
                    emit_up2(ys0, 0)
                    for ci in range(1, ncg):
                        emit_up2(emit_up1(c0 + ci), ci)
                    pending = c0
                emit_down(pending)
                S.barrier()

        def na_attn(hTg, mixTd, wring, ph):
            nag = S.sbd("nag", [128, 128], F32, ph)
            S.dma("sp", nag.sem, nag[:], D["na_g_bc"], [], [nag])
            gq = S.sb("nagq", [128, 64], F32, ph)
            S.op("dve", [nag], [gq], lambda e: e.tensor_scalar(out=gq[:], in0=nag[:, 0:64], scalar1=0.125, scalar2=None, op0=ALU.mult))
            kTn = S.sb("kTn", [128, NT * 128], BF16, ph)
            vn = S.sb("vn", [128, NT, 2, 66], BF16, ph)
            qTn = S.sb("qTn", [128, 2048], BF16, ph)
            S.op("dve", [], [vn], lambda e: e.memset(vn[:, :, :, 64:65], 1.0))
            TB = 9
            raw = S.sb("nraw", [128, TB, 128], F32, ph)
            sq = S.sb("nsq", [128, TB, 128], F32, ph)
            ssb = S.sb("nss", [128, TB * 2], F32, ph)
            nrm = S.sb("nnrm", [128, TB, 128], BF16, ph)
            biasr = Ring([S.sbd(f"nbias{i}", [128, 25, 128], F32, ph) for i in range(1)])
            stmp = Ring([S.sb(f"nstmp{i}", [128, 5, 128], F32, ph) for i in range(2)])
            PTr = Ring([S.sb(f"PTn{i}", [128, 7, 128], BF16, ph) for i in range(3)])
            rdn = Ring([S.sb(f"nrd{i}", [128, 1], F32, ph) for i in range(3)])
            mixd = S.sb("mixd", [128, 16, 2, 64], BF16, ph)
            naS = Ring(psS.bufs + psA.bufs)
            wsrc = D["odd_w"][:, 768:2304].rearrange("(k p) (g h n) -> p k g h n", p=128, g=3, h=4)
            wl = {}

            def load_w(pr):
                if pr >= 4 or pr in wl:
                    return
                wb = wring.get()
                wq = wb[:, 0:8 * 384].rearrange("p (k g n) -> p k g n", k=8, g=3)
                for g3 in range(3):
                    S.dma("pool", wb.sem, wq[:, :, g3, :], wsrc[:, :, g3, pr, :], [], [wb])
                wl[pr] = (wb, wq)

            load_w(0)
            for pr in range(4):
                wb, wq = wl[pr]
                load_w(pr + 1)
                for t0 in range(0, NT, TB):
                    tl = list(range(t0, min(NT, t0 + TB)))
                    for i in tl:
                        g, off = tok_group(i)
                        ps = psA.get()
                        for k in range(8):
                            S.op("pe", [wb, hTg[g]], [ps], lambda e: e.matmul(ps[:, 0:256], lhsT=hTg[g][:, k, off:off + 128], rhs=wb[:, k * 384 + 128:k * 384 + 384], start=(k == 0), stop=(k == 7)))
                        S.op("act", [ps], [vn], lambda e: e.activation(out=vn[:, i, :, 0:64], in_=ps[:, 128:256].rearrange("p (g d) -> p g d", g=2), func=AF.Copy))
                        S.op("dve", [ps], [raw], lambda e: e.tensor_copy(out=raw[:, i - t0, :], in_=ps[:, 0:128]))
                    prep_batch(raw, sq, ssb, len(tl), 2, nag[:, 64:128], nrm, nrm[:, 0:len(tl), :])
                    for i in tl:
                        pt = psT.get()
                        S.op("pe", [nrm, identb], [pt], lambda e: e.transpose(out=pt[:, 0, :], in_=nrm[:, i - t0, :], identity=identb[:]))
                        S.op("act", [pt], [kTn], lambda e: e.activation(out=kTn[:, i * 128:(i + 1) * 128], in_=pt[:, 0, :], func=AF.Copy))
                for t0 in range(2, NT, 8):
                    tl = list(range(t0, t0 + 8))
                    for i in tl:
                        g, off = tok_group(i)
                        ps2 = psA.get()
                        for k in range(8):
                            S.op("pe", [wb, hTg[g]], [ps2], lambda e: e.matmul(ps2[:, 0:128], lhsT=hTg[g][:, k, off:off + 128], rhs=wq[:, k, 0, :], start=(k == 0), stop=(k == 7)))
                        S.op("dve", [ps2], [raw], lambda e: e.tensor_copy(out=raw[:, i - t0, :], in_=ps2[:, 0:128]))
                    prep_batch(raw, sq, ssb, 8, 2, gq[:], nrm, nrm[:, 0:8, :])
                    for i in tl:
                        j = i - 2
                        pt2 = psT.get()
                        S.op("pe", [nrm, identb], [pt2], lambda e: e.transpose(out=pt2[:, 0, :], in_=nrm[:, i - t0, :], identity=identb[:]))
                        S.op("dve", [pt2], [qTn], lambda e: e.tensor_copy(out=qTn[:, j * 128:(j + 1) * 128], in_=pt2[:, 0, :]))
                for hh in range(2):
                    head = 2 * pr + hh
                    bt = biasr.get()
                    S.dma("sp", bt.sem, bt[:], D["na_bias"][head], [], [bt])
                    prs = slice(hh * 64, (hh + 1) * 64)

                    def blocks_of(j):
                        ci = 0 if j == 0 else 1 if j == 1 else 3 if j == 14 else 4 if j == 15 else 2
                        mlist = list(range(j - 2, j + 3)) if ci == 2 else na_blocks[j]
                        return ci, len(mlist), [0, 1] + [m + 2 for m in mlist]

                    def emit_scores(j):
                        ci, nb, keyt = blocks_of(j)
                        pA = naS.get()
                        pB = naS.get()
                        for bi, kt in enumerate(keyt):
                            pp, off2 = (pA, bi) if bi < 4 else (pB, bi - 4)
                            S.op("pe", [kTn, qTn], [pp], lambda e: e.matmul(pp[:, off2 * 128:(off2 + 1) * 128], lhsT=kTn[prs, kt * 128:(kt + 1) * 128], rhs=qTn[prs, j * 128:(j + 1) * 128], start=True, stop=True))
                        return pA, pB

                    nxt = emit_scores(0)
                    for j in range(16):
                        ci, nb, keyt = blocks_of(j)
                        pA, pB = nxt
                        if j + 1 < 16:
                            nxt = emit_scores(j + 1)
                        stp = stmp.get()
                        S.op("dve", [pA, bt], [stp], lambda e: e.tensor_tensor(out=stp[:, 0:2, :], in0=pA[:, 256:512].rearrange("p (b q) -> p b q", b=2), in1=bt[:, ci * 5:ci * 5 + 2, :], op=ALU.add))
                        S.op("dve", [pB, bt], [stp], lambda e: e.tensor_tensor(out=stp[:, 2:nb, :], in0=pB[:, 0:(nb - 2) * 128].rearrange("p (b q) -> p b q", b=nb - 2), in1=bt[:, ci * 5 + 2:ci * 5 + nb, :], op=ALU.add))
                        PT = PTr.get()
                        S.op("act", [pA], [PT], lambda e: e.activation(out=PT[:, 0:2, :], in_=pA[:, 0:256].rearrange("p (b q) -> p b q", b=2), func=AF.Exp))
                        S.op("act", [stp], [PT], lambda e: e.activation(out=PT[:, 2:2 + nb, :], in_=stp[:, 0:nb, :], func=AF.Exp))
                        acc = psO.get()
                        for bi, kt in enumerate(keyt):
                            S.op("pe", [PT, vn], [acc], lambda e: e.matmul(acc[:, 0:65], lhsT=PT[:, bi, :], rhs=vn[:, kt, hh, 0:65], start=(bi == 0), stop=(bi == len(keyt) - 1)))
                        rd = rdn.get()
                        S.op("dve", [acc], [rd], lambda e: e.reciprocal(out=rd[:], in_=acc[:, 64:65]))
                        S.op("act", [acc, rd], [mixd], lambda e: e.activation(out=mixd[:, j, hh, :], in_=acc[:, 0:64], func=AF.Copy, scale=rd[:, 0:1]))
                for j in range(16):
                    pt = psT.get()
                    S.op("pe", [mixd, identb], [pt], lambda e: e.transpose(out=pt[:, 0, :], in_=mixd[:, j, :, :].rearrange("p a d -> p (a d)"), identity=identb[:]))
                    S.op("dve", [pt], [mixTd], lambda e: e.tensor_copy(out=mixTd[:, pr, j * 128:(j + 1) * 128], in_=pt[:, 0, :]))

        def mixer1():
            with ExitStack() as ph:
                hTg = [S.sb("hT1_0", [128, 8, 256], BF16, ph)] + [S.sb(f"hT1_{g}", [128, 8, 512], BF16, ph) for g in range(1, 5)]
                with ExitStack() as ph2:
                    norm_phase(1, 0, hTg, ph2)
                    mk_gate(1, 16, ph2)
                    S.barrier()
                mixTd = S.sb("mixTd", [128, 4, 2048], BF16, ph)
                with ExitStack() as ph2:
                    wring = Ring([S.sbd(f"w1_{i}", [128, 8 * 384], BF16, ph2) for i in range(2)])
                    na_attn(hTg, mixTd, wring, ph2)
                    S.barrier()
                import os
                if os.environ.get("KSUB") == "nogqa1":
                    return
                with ExitStack() as ph2:
                    gqa_attn(1, hTg, mixTd, None, ph2)
                    S.barrier()

        mod_phase(0)
        if stage != "mod":
            mixer0()
        if stage not in ("l0mix", "mod", "norm", "mlstm"):
            ffn_phase(0, range(NT))
        if stage not in ("l0mix", "l0", "mod", "norm", "mlstm"):
            mod_phase(1)
            mixer1()
            if stage != "l1mix":
                ffn_phase(1, range(2, NT))
        osem = S.newsem("d_out")
        for i in range(2, NT):
            S.dma("sp", osem, out[(i - 2) * 128:(i - 1) * 128, :], xs[i][:], [xs[i]], [])
        if dbg:
            for i in range(2):
                S.dma("sp", osem, octx[i * 128:(i + 1) * 128, :], xs[i][:], [xs[i]], [])
        S._need("sp", osem, S.cnt[osem])
        S.barrier()
        print(f"[kernel] instructions={S.ninst} waits={S.nwait} sems={len(S.sems)}", flush=True)
    return nc


_CACHE = {}


def kernel(**inputs):
    shared, percore = host_prepare({k: np.asarray(v) for k, v in inputs.items()})
    if "nc" not in _CACHE:
        _CACHE["nc"] = build_program("full")
    nc = _CACHE["nc"]
    in_maps = []
    for b in range(8):
        m = dict(shared)
        m.update(percore[b])
        in_maps.append(m)
    res = run_bass_kernel_spmd(nc, in_maps, core_ids=list(range(8)))
    return np.stack([np.asarray(r["out"], np.float32) for r in res.results], axis=0)
```

```python
import numpy as np
from contextlib import ExitStack
import concourse.bass as bass
import concourse.mybir as mybir
from concourse.bass_utils import run_bass_kernel_spmd

F32 = mybir.dt.float32
BF16 = mybir.dt.bfloat16
AF = mybir.ActivationFunctionType
ALU = mybir.AluOpType
AX = mybir.AxisListType

ENGS = ("pe", "act", "dve", "pool", "sp")
NT = 18
EPS = 1e-6
NEGM = -30000.0


class Buf:
    __slots__ = ("t", "name", "w", "r", "sem", "psum")

    def __init__(self, t, name):
        self.t = t
        self.name = name
        self.w = None
        self.r = {}
        self.sem = None
        self.psum = False

    def __getitem__(self, idx):
        return self.t[idx]


class Ring:
    def __init__(self, bufs):
        self.bufs = bufs
        self.i = 0

    def get(self):
        b = self.bufs[self.i % len(self.bufs)]
        self.i += 1
        return b


SEM_LIMIT = 1500


class Sched:
    def __init__(self, nc, stack):
        self.nc = nc
        self.stack = stack
        self.eng = {"pe": nc.tensor, "act": nc.scalar, "dve": nc.vector,
                    "pool": nc.gpsimd, "sp": nc.sync}
        self.sems = {}
        self.cnt = {}
        self.epoch = {}
        self.cur = {}
        for e in ENGS:
            self.epoch[e] = 0
            self._new_epoch(e)
        self.seen = {e: {} for e in ENGS}
        self.ninst = 0
        self.nwait = 0
        self.nsem = 0
        self.nalloc = 0

    def _new_epoch(self, e):
        self.epoch[e] += 1
        key = f"{e}#{self.epoch[e]}"
        self.sems[key] = self.stack.enter_context(self.nc.semaphore("s_" + key.replace("#", "_")))
        self.cnt[key] = 0
        self.cur[e] = key

    def sb(self, name, shape, dt, stack=None):
        self.nalloc += 1
        name = f"{name}_{self.nalloc}"
        t = (stack or self.stack).enter_context(self.nc.sbuf_tensor(name, list(shape), dt))
        return Buf(t, name)

    def ps(self, name, shape, dt=F32):
        t = self.stack.enter_context(self.nc.psum_tensor(name, list(shape), dt))
        b = Buf(t, name)
        b.psum = True
        return b

    def newsem(self, name=None):
        self.nsem += 1
        name = name or f"d{self.nsem}"
        s = self.stack.enter_context(self.nc.semaphore(name))
        self.sems[name] = s
        self.cnt[name] = 0
        return name

    def sbd(self, name, shape, dt, stack=None):
        b = self.sb(name, shape, dt, stack)
        b.sem = self.newsem("d_" + b.name)
        return b

    @staticmethod
    def _eng_of(key):
        return key.split("#")[0] if "#" in key else None

    def _need(self, e, key, val):
        if self.seen[e].get(key, 0) >= val:
            return
        ke = self._eng_of(key)
        if ke is not None:
            ep = int(key.split("#")[1])
            for k2, v2 in self.seen[e].items():
                if v2 > 0 and self._eng_of(k2) == ke and int(k2.split("#")[1]) > ep:
                    return
        self.seen[e][key] = val
        self.eng[e].wait_ge(self.sems[key], val)
        self.nwait += 1

    def deps(self, e, reads, writes):
        for b in reads:
            if b.w is not None:
                k, v = b.w
                if not (self._eng_of(k) == e and e == "pe"):
                    self._need(e, k, v)
        for b in writes:
            if b.w is not None:
                k, v = b.w
                if self._eng_of(k) != e:
                    self._need(e, k, v)
            for k, v in b.r.items():
                if self._eng_of(k) != e:
                    self._need(e, k, v)

    def op(self, e, reads, writes, fn):
        pr = [b for b in reads if b.psum]
        if pr:
            reads = [b for b in reads if not b.psum]
            writes = list(writes) + [b for b in pr if b not in writes]
        self.deps(e, reads, writes)
        ins = fn(self.eng[e])
        if self.cnt[self.cur[e]] >= SEM_LIMIT:
            self._new_epoch(e)
        key = self.cur[e]
        self.cnt[key] += 1
        ins.then_inc(self.sems[key], 1)
        v = self.cnt[key]
        for b in reads:
            for k2 in [k2 for k2 in b.r if self._eng_of(k2) == e]:
                del b.r[k2]
            b.r[key] = v
        for b in writes:
            b.w = (key, v)
            b.r = {}
        self.ninst += 1
        return ins

    def dma(self, q, semkey, out_ap, in_ap, reads, writes, **kw):
        self.deps(q, reads, writes)
        ins = self.eng[q].dma_start(out=out_ap, in_=in_ap, **kw)
        self.cnt[semkey] += 16
        assert self.cnt[semkey] <= 2000, semkey
        ins.then_inc(self.sems[semkey], 16)
        v = self.cnt[semkey]
        for b in reads:
            b.r[semkey] = v
        for b in writes:
            b.w = (semkey, v)
            b.r = {}
        self.ninst += 1
        return ins

    def barrier(self):
        for e in ENGS:
            for k, v in list(self.cnt.items()):
                ke = self._eng_of(k)
                if ke == e or v == 0:
                    continue
                if ke is not None and k != self.cur[ke]:
                    if not (self.cnt[self.cur[ke]] == 0 and int(k.split("#")[1]) == self.epoch[ke] - 1):
                        continue
                self._need(e, k, v)


def _rope_tables():
    t = np.arange(2048)
    row = (t // 64).astype(np.float32)
    col = (t % 64).astype(np.float32)
    half = 32
    freq = (np.float32(10000.0) ** (-np.arange(0, half, 2, dtype=np.float32) / np.float32(half))).astype(np.float32)
    ang_r = row[:, None] * freq[None, :]
    ang_c = col[:, None] * freq[None, :]
    ang = np.concatenate([ang_r, ang_r, ang_c, ang_c], axis=-1).astype(np.float32)
    cos = np.cos(ang).astype(np.float32)
    sin = np.sin(ang).astype(np.float32)
    sgn = np.ones(64, np.float32)
    sgn[0:16] = -1.0
    sgn[32:48] = -1.0
    sinS = sin * sgn[None, :]
    cos = cos.reshape(16, 128, 64).transpose(1, 0, 2).copy()
    sinS = sinS.reshape(16, 128, 64).transpose(1, 0, 2).copy()
    return cos, sinS


def _na_tables(rpb):
    rows = 32
    wr = 8
    r = np.arange(rows)
    row_start = np.clip(r - wr // 2, 0, rows - wr)
    col = np.arange(64)
    col_start = np.clip(col - 8, 0, 48)
    col_ok = (col[None, :] >= col_start[:, None]) & (col[None, :] < col_start[:, None] + 16)
    dc = np.clip(col[None, :] - col[:, None] + 15, 0, 30)
    classes = [0, 1, 2, 14, 15]
    blocks = {}
    tab = np.full((8, 128, 25, 128), NEGM, np.float32)
    for ci, j in enumerate(classes):
        qrows = [2 * j, 2 * j + 1]
        lo = min(row_start[q] for q in qrows)
        hi = max(row_start[q] + wr - 1 for q in qrows)
        mlist = list(range(lo // 2, hi // 2 + 1))
        assert len(mlist) <= 5
        blocks[j] = mlist
        for si, m in enumerate(mlist):
            for kr in range(2):
                krow = 2 * m + kr
                for qr in range(2):
                    qrow = qrows[qr]
                    if not (row_start[qrow] <= krow < row_start[qrow] + wr):
                        continue
                    dr = krow - qrow + 7
                    sub = rpb[:, dr, :][:, dc]
                    sub = np.where(col_ok[None], sub, np.float32(NEGM))
                    tab[:, kr * 64:(kr + 1) * 64, ci * 5 + si, qr * 64:(qr + 1) * 64] = sub.transpose(0, 2, 1)
    return tab, blocks, classes


def _na_blocks():
    _, blocks, classes = _na_tables(np.zeros((8, 15, 31), np.float32))
    return blocks, classes


def host_prepare(inp):
    f = np.float32
    shared = {}
    shared["ada_w"] = np.ascontiguousarray(inp["ada_w"], f)
    shared["ada_bT"] = np.ascontiguousarray(inp["ada_b"].reshape(2, 48, 128).transpose(2, 0, 1), f)
    shared["norm_gT"] = np.ascontiguousarray(inp["norm_g"].reshape(2, 2, 8, 128).transpose(3, 0, 1, 2), f)
    shared["w_out"] = np.ascontiguousarray(inp["w_out"], f)
    shared["ffn_up"] = np.ascontiguousarray(inp["ffn_up"], f)
    shared["ffn_down"] = np.ascontiguousarray(inp["ffn_down"], f)
    shared["conv_wT"] = np.ascontiguousarray(inp["ffn_conv_w"].reshape(2, 3, 44, 128).transpose(3, 0, 1, 2), f)
    shared["conv_bT"] = np.ascontiguousarray(inp["ffn_conv_b"].reshape(2, 44, 128).transpose(2, 0, 1), f)
    shared["even_w"] = np.ascontiguousarray(inp["even_w_in"][0], f)
    shared["odd_w"] = np.ascontiguousarray(inp["odd_w_in"][0], f)
    bc = lambda a: np.ascontiguousarray(np.broadcast_to(np.asarray(a, f).reshape(1, -1), (128, a.size)))
    shared["gate_b_bc"] = bc(inp["mlstm_gate_b"][0])
    shared["head_g_bc"] = bc(inp["mlstm_head_g"][0])
    shared["swa_g_bc"] = bc(inp["swa_qk_g"][0])
    shared["sink_bc"] = bc(inp["swa_sink"][0])
    shared["gqa_g_bc"] = bc(inp["gqa_qk_g"][0])
    shared["na_g_bc"] = bc(inp["na_qk_g"][0])
    tab, _, _ = _na_tables(np.asarray(inp["na_rpb"][0], f))
    shared["na_bias"] = tab
    ident = np.eye(128, dtype=f)
    s = np.arange(128)
    triU = (s[:, None] <= s[None, :]).astype(f)
    triL = (s[:, None] >= s[None, :]).astype(f)
    wm = np.zeros((128, 2, 128), f)
    wm[:, 0, :] = np.where(s[None, :] <= s[:, None], 0.0, NEGM)
    wm[:, 1, :] = np.where(s[:, None] <= s[None, :], 0.0, NEGM)
    shared["consts"] = np.ascontiguousarray(np.concatenate([ident, triU, triL, wm.reshape(128, 256)], axis=1))
    cos, sinS = _rope_tables()
    shared["rope"] = np.ascontiguousarray(np.stack([cos, sinS], axis=1))
    percore = []
    for b in range(8):
        cc = np.stack([inp["c"][b].reshape(8, 128).T, inp["c_ctx"].reshape(8, 128).T], axis=-1)
        percore.append({"x": np.ascontiguousarray(inp["x"][b], f), "ctx": np.ascontiguousarray(inp["ctx"][b], f),
                        "cc": np.ascontiguousarray(cc, f)})
    return shared, percore


SHARED_SHAPES = {
    "ada_w": [2, 1024, 6144], "ada_bT": [128, 2, 48], "norm_gT": [128, 2, 2, 8], "w_out": [2, 1024, 1024],
    "ffn_up": [2, 1024, 5632], "ffn_down": [2, 2816, 1024], "conv_wT": [128, 2, 3, 44], "conv_bT": [128, 2, 44],
    "even_w": [1024, 2832], "odd_w": [1024, 2304], "gate_b_bc": [128, 16], "head_g_bc": [128, 512],
    "swa_g_bc": [128, 128], "sink_bc": [128, 8], "gqa_g_bc": [128, 128], "na_g_bc": [128, 128],
    "na_bias": [8, 128, 25, 128], "consts": [128, 640], "rope": [128, 2, 16, 64],
    "x": [2048, 1024], "ctx": [256, 1024], "cc": [128, 8, 2],
}


GROUPS = [(0, 0, 256), (1, 256, 512), (2, 768, 512), (3, 1280, 512), (4, 1792, 512)]


def tok_group(i):
    return (0, i * 128) if i < 2 else (1 + (i - 2) // 4, ((i - 2) % 4) * 128)


def build_program(stage="full"):
    nc = bass.Bass("TRN2", target_bir_lowering=False)
    D = {k: nc.dram_tensor(k, shp, F32, kind="ExternalInput").ap() for k, shp in SHARED_SHAPES.items()}
    out = nc.dram_tensor("out", [2048, 1024], F32, kind="ExternalOutput").ap()
    dbg = stage != "full"
    if dbg:
        octx = nc.dram_tensor("octx", [256, 1024], F32, kind="ExternalOutput").ap()
        dbgd = nc.dram_tensor("dbgd", [128, 8192], F32, kind="ExternalOutput").ap()
    na_blocks, na_classes = _na_blocks()

    with ExitStack() as st:
        S = Sched(nc, st)
        xs = [S.sbd(f"xs{i}", [128, 1024], F32) for i in range(NT)]
        cst = S.sbd("cst", [128, 640], F32)
        cc = S.sbd("cc", [128, 8, 2], F32)
        adab = S.sbd("adab", [128, 2, 48], F32)
        ngT = S.sbd("ngT", [128, 2, 2, 8], F32)
        cw = S.sbd("cw", [128, 2, 3, 44], F32)
        cb = S.sbd("cb", [128, 2, 44], F32)
        identb = S.sb("identb", [128, 128], BF16)
        wmb = S.sb("wmb", [128, 2, 128], BF16)
        ones_f = S.sb("ones_f", [128, 128], F32)
        ones_b = S.sb("ones_b", [128, 128], BF16)
        sc = S.sb("sc", [128, 8, 2], F32)
        modT = [S.sb(f"modT{l}", [128, 48, 2], F32) for l in range(2)]
        gbc = S.sb("gbc", [128, 2, 1024], F32)
        AB = S.sb("AB", [128, 8, 2], F32)

        psT = Ring([S.ps(f"psT{i}", [128, 8, 128], BF16) for i in range(2)])
        psA = Ring([S.ps(f"psA{i}", [128, 512], F32) for i in range(2)])
        psS = Ring([S.ps(f"psS{i}", [128, 512], F32) for i in range(2)])
        psO = Ring([S.ps(f"psO{i}", [128, 512], F32) for i in range(2)])

        IDF = lambda: cst[:, 0:128]
        TRIU = lambda: cst[:, 128:256]
        TRIL = lambda: cst[:, 256:384]

        S.dma("sp", cst.sem, cst[:], D["consts"], [], [cst])
        S.dma("sp", cc.sem, cc[:], D["cc"], [], [cc])
        S.dma("sp", adab.sem, adab[:], D["ada_bT"], [], [adab])
        S.dma("sp", ngT.sem, ngT[:], D["norm_gT"], [], [ngT])
        S.dma("sp", cw.sem, cw[:], D["conv_wT"], [], [cw])
        S.dma("sp", cb.sem, cb[:], D["conv_bT"], [], [cb])
        for i in range(NT):
            src = D["ctx"][i * 128:(i + 1) * 128, :] if i < 2 else D["x"][(i - 2) * 128:(i - 1) * 128, :]
            S.dma("sp", xs[i].sem, xs[i][:], src, [], [xs[i]])
        S.op("dve", [cst], [identb], lambda e: e.tensor_copy(out=identb[:], in_=cst[:, 0:128]))
        S.op("dve", [cst], [wmb], lambda e: e.tensor_copy(out=wmb[:], in_=cst[:, 384:640].rearrange("p (a b) -> p a b", a=2)))
        S.op("dve", [], [ones_f], lambda e: e.memset(ones_f[:], 1.0))
        S.op("dve", [], [ones_b], lambda e: e.memset(ones_b[:], 1.0))
        S.op("act", [cc], [sc], lambda e: e.activation(out=sc[:], in_=cc[:], func=AF.Silu))

        dstg = S.sb("dstg", [128, 128], F32) if dbg else None
        dstate = {"col": 0, "items": []}

        def dump(name, buf, ap, n):
            if not dbg:
                return
            stg = dstg
            sem = S.newsem()
            S.op("act", [buf], [stg], lambda e: e.activation(out=stg[:, 0:n], in_=ap, func=AF.Copy))
            c0 = dstate["col"]
            S.dma("sp", sem, dbgd[:, c0:c0 + n], stg[:, 0:n], [stg], [])
            S._need("sp", sem, S.cnt[sem])
            dstate["items"].append((name, c0, n))
            dstate["col"] = c0 + n
            print("DUMP", name, c0, n, flush=True)

        def wview(wb, shape_str, **kw):
            n = 1
            for v in kw.values():
                n *= v
            return wb

        def mod_phase(l):
            with ExitStack() as ph:
                ring = Ring([S.sbd(f"adaw{l}_{i}", [128, 8, 512], BF16, ph) for i in range(3)])
                schi = S.sb(f"schi{l}", [128, 8, 2], BF16, ph)
                schf = S.sb(f"schf{l}", [128, 8, 2], F32, ph)
                sclo = S.sb(f"sclo{l}", [128, 8, 2], BF16, ph)
                S.op("dve", [sc], [schi], lambda e: e.tensor_copy(out=schi[:], in_=sc[:]))
                S.op("dve", [schi], [schf], lambda e: e.tensor_copy(out=schf[:], in_=schi[:]))
                S.op("dve", [sc, schf], [schf], lambda e: e.tensor_tensor(out=schf[:], in0=sc[:], in1=schf[:], op=ALU.subtract))
                S.op("dve", [schf], [sclo], lambda e: e.tensor_copy(out=sclo[:], in_=schf[:]))
                wbs = {}

                def ld(cg):
                    if cg >= 12:
                        return
                    wb = ring.get()
                    S.dma("pool", wb.sem, wb[:], D["ada_w"][l, :, cg * 512:(cg + 1) * 512].rearrange("(k p) n -> p k n", p=128), [], [wb])
                    wbs[cg] = wb
                ld(0)
                ld(1)
                for cg in range(12):
                    ld(cg + 2)
                    wb = wbs[cg]
                    ps = psA.get()
                    for c4 in range(4):
                        for k in range(8):
                            S.op("pe", [wb, schi], [ps], lambda e: e.matmul(ps[:, c4 * 2:c4 * 2 + 2], lhsT=wb[:, k, c4 * 128:(c4 + 1) * 128], rhs=schi[:, k, :], start=(k == 0), stop=False))
                            S.op("pe", [wb, sclo], [ps], lambda e: e.matmul(ps[:, c4 * 2:c4 * 2 + 2], lhsT=wb[:, k, c4 * 128:(c4 + 1) * 128], rhs=sclo[:, k, :], start=False, stop=(k == 7)))
                    S.op("dve", [ps, adab], [modT[l]], lambda e: e.tensor_tensor(
                        out=modT[l][:, cg * 4:(cg + 1) * 4, :], in0=ps[:, 0:8].rearrange("p (c j) -> p c j", j=2),
                        in1=adab[:, l, cg * 4:(cg + 1) * 4].unsqueeze(2).to_broadcast([128, 4, 2]), op=ALU.add))
                S.barrier()

        def mk_AB(l, which):
            scl = 8 if which == 0 else 32
            S.op("dve", [modT[l]], [AB], lambda e: e.tensor_scalar(out=AB[:], in0=modT[l][:, scl:scl + 8, :], scalar1=1.0, scalar2=None, op0=ALU.add))
            S.op("dve", [AB, ngT], [AB], lambda e: e.tensor_tensor(out=AB[:], in0=AB[:], in1=ngT[:, l, which, :].unsqueeze(2).to_broadcast([128, 8, 2]), op=ALU.mult))

        def mk_gate(l, gchunk, ph):
            hl = S.sb(f"ghl{l}_{gchunk}", [128, 8, 2], F32, ph)
            hb = S.sb(f"ghb{l}_{gchunk}", [128, 8, 2], BF16, ph)
            hf = S.sb(f"ghf{l}_{gchunk}", [128, 8, 2], F32, ph)
            lo = S.sb(f"glo{l}_{gchunk}", [128, 8, 2], F32, ph)
            lb = S.sb(f"glb{l}_{gchunk}", [128, 8, 2], BF16, ph)
            lf = S.sb(f"glf{l}_{gchunk}", [128, 8, 2], F32, ph)
            S.op("dve", [modT[l]], [hl], lambda e: e.tensor_copy(out=hl[:], in_=modT[l][:, gchunk:gchunk + 8, :]))
            S.op("dve", [hl], [hb], lambda e: e.tensor_copy(out=hb[:], in_=hl[:]))
            S.op("dve", [hb], [hf], lambda e: e.tensor_copy(out=hf[:], in_=hb[:]))
            S.op("dve", [hl, hf], [lo], lambda e: e.tensor_tensor(out=lo[:], in0=hl[:], in1=hf[:], op=ALU.subtract))
            S.op("dve", [lo], [lb], lambda e: e.tensor_copy(out=lb[:], in_=lo[:]))
            S.op("dve", [lb], [lf], lambda e: e.tensor_copy(out=lf[:], in_=lb[:]))
            dgr = Ring([S.sb(f"dg{l}_{gchunk}_{i}", [128, 2, 128], BF16, ph) for i in range(2)])
            for j in range(2):
                for half in range(2):
                    ps = psA.get()
                    for k4 in range(4):
                        kk = half * 4 + k4
                        dg = dgr.get()
                        S.op("dve", [identb, hf], [dg], lambda e: e.tensor_scalar(out=dg[:, 0, :], in0=identb[:], scalar1=hf[:, kk, j:j + 1], scalar2=None, op0=ALU.mult))
                        S.op("dve", [identb, lf], [dg], lambda e: e.tensor_scalar(out=dg[:, 1, :], in0=identb[:], scalar1=lf[:, kk, j:j + 1], scalar2=None, op0=ALU.mult))
                        S.op("pe", [ones_b, dg], [ps], lambda e: e.matmul(ps[:, k4 * 128:(k4 + 1) * 128], lhsT=ones_b[:], rhs=dg[:, 0, :], start=True, stop=False))
                        S.op("pe", [ones_b, dg], [ps], lambda e: e.matmul(ps[:, k4 * 128:(k4 + 1) * 128], lhsT=ones_b[:], rhs=dg[:, 1, :], start=False, stop=True))
                    S.op("act", [ps], [gbc], lambda e: e.activation(out=gbc[:, j, half * 512:(half + 1) * 512], in_=ps[:], func=AF.Copy))

        def rstd_of(t, n_ap, dim):
            S.op("dve", [t], [t], lambda e: e.tensor_scalar(out=n_ap(), in0=n_ap(), scalar1=1.0 / dim, scalar2=EPS, op0=ALU.mult, op1=ALU.add))
            S.op("act", [t], [t], lambda e: e.activation(out=n_ap(), in_=n_ap(), func=AF.Ln))
            S.op("act", [t], [t], lambda e: e.activation(out=n_ap(), in_=n_ap(), func=AF.Exp, scale=-0.5))

        def norm_phase(l, which, hTg, ph, tiles=range(NT)):
            mk_AB(l, which)
            sh = 0 if which == 0 else 24
            ss = S.sb(f"nss{l}{which}", [128, NT], F32, ph)
            junk = S.sb(f"njunk{l}{which}", [128, 1024], BF16, ph)
            xnr = Ring([S.sb(f"xn{l}{which}_{i}", [128, 1024], BF16, ph) for i in range(2)])
            S.op("dve", [], [ss], lambda e: e.memset(ss[:], 1.0))
            for i in tiles:
                S.op("act", [xs[i]], [junk, ss], lambda e: e.activation(out=junk[:], in_=xs[i][:], func=AF.Square, accum_out=ss[:, i:i + 1]))
            rstd_of(ss, lambda: ss[:], 1024)
            import os
            if os.environ.get("KSUB") in ("a", "c"):
                return
            for i in tiles:
                xn = xnr.get()
                S.op("dve", [xs[i], ss], [xn], lambda e: e.tensor_scalar(out=xn[:], in0=xs[i][:], scalar1=ss[:, i:i + 1], scalar2=None, op0=ALU.mult))
                pt = psT.get()
                for k in range(8):
                    S.op("pe", [xn, identb], [pt], lambda e: e.transpose(out=pt[:, k, :], in_=xn[:, k * 128:(k + 1) * 128], identity=identb[:]))
                g, off = tok_group(i)
                j = 1 if i < 2 else 0
                for k in range(8):
                    if k % 2 == 0:
                        S.op("dve", [pt, AB, modT[l]], [hTg[g]], lambda e: e.tensor_scalar(
                            out=hTg[g][:, k, off:off + 128], in0=pt[:, k, :], scalar1=AB[:, k, j:j + 1], scalar2=modT[l][:, sh + k, j:j + 1], op0=ALU.mult, op1=ALU.add))
                    else:
                        S.op("act", [pt, AB, modT[l]], [hTg[g]], lambda e: e.activation(
                            out=hTg[g][:, k, off:off + 128], in_=pt[:, k, :], func=AF.Identity, scale=AB[:, k, j:j + 1], bias=modT[l][:, sh + k, j:j + 1]))

        def wload(wb, n, src):
            dst = wb[:, 0:8 * n].rearrange("p (k n) -> p k n", k=8)
            S.dma("pool", wb.sem, dst, src, [], [wb])
            return dst

        def qk_prep(ps, ps_ap, nh, g_ap, rope_tile, out_ap, wk, rope):
            sq, ssq, qn, t1 = wk
            n = nh * 64
            v3 = lambda ap: ap.rearrange("p (h d) -> p h d", d=64)
            S.op("act", [ps], [sq], lambda e: e.activation(out=sq[:, 0:n], in_=ps_ap, func=AF.Square))
            S.op("dve", [sq], [ssq], lambda e: e.tensor_reduce(out=ssq[:, 0:nh], in_=v3(sq[:, 0:n]), axis=AX.X, op=ALU.add))
            rstd_of(ssq, lambda: ssq[:, 0:nh], 64)
            S.op("dve", [ps, ssq], [qn], lambda e: e.tensor_tensor(out=v3(qn[:, 0:n]), in0=v3(ps_ap), in1=ssq[:, 0:nh].unsqueeze(2).to_broadcast([128, nh, 64]), op=ALU.mult))
            if rope_tile is None:
                S.op("dve", [qn], [out_ap[0]], lambda e: e.tensor_tensor(out=out_ap[1], in0=v3(qn[:, 0:n]), in1=g_ap.unsqueeze(1).to_broadcast([128, nh, 64]), op=ALU.mult))
                return
            S.op("dve", [qn], [qn], lambda e: e.tensor_tensor(out=v3(qn[:, 0:n]), in0=v3(qn[:, 0:n]), in1=g_ap.unsqueeze(1).to_broadcast([128, nh, 64]), op=ALU.mult))
            cos_ap = rope[:, 0, :]
            sin_ap = rope[:, 1, :]
            S.op("dve", [qn, rope], [t1], lambda e: e.tensor_tensor(out=v3(t1[:, 0:n]), in0=v3(qn[:, 0:n]), in1=cos_ap.unsqueeze(1).to_broadcast([128, nh, 64]), op=ALU.mult))
            v5 = lambda ap: ap.rearrange("p (h x y d) -> p h x y d", x=2, y=2, d=16)
            s4 = sin_ap.rearrange("p (x y d) -> p x y d", x=2, y=2)
            for y in range(2):
                S.op("dve", [qn, rope], [sq], lambda e: e.tensor_tensor(
                    out=v5(sq[:, 0:n])[:, :, :, y, :], in0=v5(qn[:, 0:n])[:, :, :, 1 - y, :],
                    in1=s4[:, :, y, :].unsqueeze(1).to_broadcast([128, nh, 2, 16]), op=ALU.mult))
            S.op("dve", [t1, sq], [out_ap[0]], lambda e: e.tensor_tensor(out=out_ap[1], in0=v3(t1[:, 0:n]), in1=v3(sq[:, 0:n]), op=ALU.add))

        def prep_batch(raw, sq, ss, T, nh, g_ap, out_buf, out_ap, inplace=False):
            n = T * nh
            r3 = raw[:, 0:T, :].rearrange("p t (h d) -> p (t h) d", d=64)
            s3 = sq[:, 0:T, :].rearrange("p t (h d) -> p (t h) d", d=64)
            S.op("act", [raw], [sq], lambda e: e.activation(out=sq[:, 0:T, :], in_=raw[:, 0:T, :], func=AF.Square))
            S.op("dve", [sq], [ss], lambda e: e.tensor_reduce(out=ss[:, 0:n], in_=s3, axis=AX.X, op=ALU.add))
            rstd_of(ss, lambda: ss[:, 0:n], 64)
            S.op("dve", [raw, ss], [raw], lambda e: e.tensor_tensor(out=r3, in0=r3, in1=ss[:, 0:n].unsqueeze(2).to_broadcast([128, n, 64]), op=ALU.mult))
            if inplace:
                S.op("dve", [raw], [raw], lambda e: e.tensor_tensor(out=r3, in0=r3, in1=g_ap.unsqueeze(1).to_broadcast([128, n, 64]), op=ALU.mult))
                return
            S.op("dve", [raw], [out_buf], lambda e: e.tensor_tensor(out=out_ap.rearrange("p t (h d) -> p (t h) d", d=64), in0=r3, in1=g_ap.unsqueeze(1).to_broadcast([128, n, 64]), op=ALU.mult))

        def residual(i, ps, cgi, j):
            tmp = restmp.get()
            S.op("dve", [ps, gbc], [tmp], lambda e: e.tensor_tensor(out=tmp[:], in0=ps[:], in1=gbc[:, j, cgi * 512:(cgi + 1) * 512], op=ALU.mult))
            rstate["n"] += 1
            S.op("dve", [tmp, xs[i]], [xs[i]], lambda e: e.tensor_tensor(out=xs[i][:, cgi * 512:(cgi + 1) * 512], in0=xs[i][:, cgi * 512:(cgi + 1) * 512], in1=tmp[:], op=ALU.add))

        restmp = Ring([S.sb(f"restmp{i}", [128, 512], F32) for i in range(2)])
        rstate = {"n": 0}

        def mixer0():
            l = 0
            with ExitStack() as ph:
                hTg = [S.sb("hT0_0", [128, 8, 256], BF16, ph)] + [S.sb(f"hT0_{g}", [128, 8, 512], BF16, ph) for g in range(1, 5)]
                with ExitStack() as ph2:
                    norm_phase(0, 0, hTg, ph2)
                    import os
                    if os.environ.get("KSUB") not in ("a", "b"):
                        mk_gate(0, 16, ph2)
                    S.barrier()
                if stage == "norm":
                    return
                mixTa = S.sb("mixTa", [128, 4, NT * 128], BF16, ph)
                with ExitStack() as ph2:
                    wring = Ring([S.sbd(f"w0_{i}", [128, 8 * 384], BF16, ph2) for i in range(2)])
                    gateb = S.sbd("gateb", [128, 16], F32, ph2)
                    headg = S.sbd("headg", [128, 512], F32, ph2)
                    S.dma("sp", gateb.sem, gateb[:], D["gate_b_bc"], [], [gateb])
                    S.dma("sp", headg.sem, headg[:], D["head_g_bc"], [], [headg])
                    mlstm(hTg, mixTa, gateb, headg, wring, ph2)
                    S.barrier()
                if stage == "mlstm":
                    return
                with ExitStack() as ph2:
                    gqa_attn(0, hTg, mixTa, None, ph2)
                    S.barrier()

        def mlstm_gates(hTg, gateb, wring, pg, es, eb, edec, ekw):
            G = S.sb("G", [128, NT, 16], F32, pg)
            wg = wload(wring.get(), 16, D["even_w"][:, 2048:2064].rearrange("(k p) n -> p k n", p=128))
            wgb = wring.bufs[(wring.i - 1) % len(wring.bufs)]
            for i in range(NT):
                g, off = tok_group(i)
                ps = psO.get()
                for k in range(8):
                    S.op("pe", [hTg[g], wgb], [ps], lambda e: e.matmul(ps[:, 0:16], lhsT=hTg[g][:, k, off:off + 128], rhs=wg[:, k, :], start=(k == 0), stop=(k == 7)))
                S.op("dve", [ps, gateb], [G], lambda e: e.tensor_tensor(out=G[:, i, :], in0=ps[:, 0:16], in1=gateb[:], op=ALU.add))
            E = S.sb("E", [128, 2, NT, 4], F32, pg)
            for d in range(2):
                S.op("act", [G], [E], lambda e: e.activation(out=E[:, d], in_=G[:, :, 4 + 8 * d:8 + 8 * d], func=AF.Exp, scale=-1.0))
            S.op("dve", [E], [E], lambda e: e.tensor_scalar(out=E[:], in0=E[:], scalar1=1.0, scalar2=None, op0=ALU.add))
            S.op("act", [E], [E], lambda e: e.activation(out=E[:], in_=E[:], func=AF.Ln))
            tg = S.sb("tg", [128, NT, 4], F32, pg)
            f72 = lambda ap: ap.rearrange("p t h -> p (t h)")
            trib = S.sb("trib", [128, 2, 128], BF16, pg)
            S.op("dve", [cst], [trib], lambda e: e.tensor_copy(out=trib[:], in_=cst[:, 128:384].rearrange("p (a b) -> p a b", a=2)))
            Ehi = S.sb("Ehi", [128, 2, NT, 4], BF16, pg)
            Ehf = S.sb("Ehf", [128, 2, NT, 4], F32, pg)
            Elo = S.sb("Elo", [128, 2, NT, 4], BF16, pg)
            S.op("dve", [E], [Ehi], lambda e: e.tensor_copy(out=Ehi[:], in_=E[:]))
            S.op("dve", [Ehi], [Ehf], lambda e: e.tensor_copy(out=Ehf[:], in_=Ehi[:]))
            S.op("dve", [E, Ehf], [Ehf], lambda e: e.tensor_tensor(out=Ehf[:], in0=E[:], in1=Ehf[:], op=ALU.subtract))
            S.op("dve", [Ehf], [Elo], lambda e: e.tensor_copy(out=Elo[:], in_=Ehf[:]))
            for d in range(2):
                psb = psO.get()
                S.op("pe", [trib, Ehi], [psb], lambda e: e.matmul(psb[:, 0:72], lhsT=trib[:, d, :], rhs=f72(Ehi[:, d]), start=True, stop=False))
                S.op("pe", [trib, Elo], [psb], lambda e: e.matmul(psb[:, 0:72], lhsT=trib[:, d, :], rhs=f72(Elo[:, d]), start=False, stop=True))
                S.op("pe", [ones_b, Ehi], [psb], lambda e: e.matmul(psb[:, 72:144], lhsT=ones_b[:], rhs=f72(Ehi[:, d]), start=True, stop=False))
                S.op("pe", [ones_b, Elo], [psb], lambda e: e.matmul(psb[:, 72:144], lhsT=ones_b[:], rhs=f72(Elo[:, d]), start=False, stop=True))
                S.op("dve", [psb, G], [tg], lambda e: e.tensor_tensor(out=tg[:], in0=psb[:, 0:72].rearrange("p (t h) -> p t h", h=4), in1=G[:, :, 8 * d:8 * d + 4], op=ALU.add))
                S.op("act", [tg], [es], lambda e: e.activation(out=es[:, d], in_=tg[:], func=AF.Exp))
                S.op("act", [psb], [eb], lambda e: e.activation(out=f72(eb[:, d]), in_=psb[:, 0:72], func=AF.Exp, scale=-1.0))
                S.op("act", [psb], [edec], lambda e: e.activation(out=f72(edec[:, d]), in_=psb[:, 72:144], func=AF.Exp, scale=-1.0))
                S.op("dve", [es, edec], [ekw], lambda e: e.tensor_tensor(out=ekw[:, d], in0=es[:, d], in1=edec[:, d], op=ALU.mult))

            pass
            pass
            pass
            pass
            pass

        def mlstm(hTg, mixTa, gateb, headg, wring, ph):
            es = S.sb("es", [128, 2, NT, 4], F32, ph)
            eb = S.sb("eb", [128, 2, NT, 4], F32, ph)
            edec = S.sb("edec", [128, 2, NT, 4], F32, ph)
            ekw = S.sb("ekw", [128, 2, NT, 4], F32, ph)
            with ExitStack() as pg:
                mlstm_gates(hTg, gateb, wring, pg, es, eb, edec, ekw)
                S.barrier()
            KS_ = ""
            KH_ = -1
            qT = S.sb("qTa", [128, NT * 128], BF16, ph)
            kT = S.sb("kTa", [128, NT * 128], BF16, ph)
            ktok = S.sb("ktok", [128, NT, 128], BF16, ph)
            vaug = S.sb("vaug", [128, NT, 130], BF16, ph)
            hraw = [S.sb(f"hraw{d}", [128, NT, 130], F32, ph) for d in range(2)]
            rnm = S.sb("rnm", [128, 2, NT], F32, ph)
            Cst = [S.sb(f"Cst{d}", [128, 129], F32, ph) for d in range(2)]
            Cbf3 = [[S.sb(f"Cbf{d}_{r}", [128, 130], BF16, ph) for r in range(3)] for d in range(2)]
            PTr = Ring([S.sb(f"PTm{i}", [128, 128], BF16, ph) for i in range(4)])
            kwr = Ring([S.sb(f"kwm{i}", [128, 128], BF16, ph) for i in range(4)])
            hss = S.sb("hss", [128, NT], F32, ph)
            hjunk = S.sb("hjunk", [128, 128], F32, ph)
            ogr = Ring([S.sb(f"og{i}", [128, 128], F32, ph) for i in range(2)])
            t1r = Ring([S.sb(f"mt1{i}", [128, 128], F32, ph) for i in range(2)])
            mxr = Ring([S.sb(f"mmx{i}", [128, 128], BF16, ph) for i in range(2)])
            S.op("dve", [], [vaug], lambda e: e.memset(vaug[:, :, 128:129], 1.0))
            orders = [list(range(NT)), [1, 0] + list(range(NT - 1, 1, -1))]
            KS = 128.0 ** -0.5

            for h in range(4):
                wb = wring.get()
                src = D["even_w"][:, 0:1536].rearrange("(k p) (g h n) -> p k g h n", p=128, g=3, h=4)[:, :, :, h, :]
                wq = wb[:, 0:8 * 384].rearrange("p (k g n) -> p k g n", k=8, g=3)
                for g3 in range(3):
                    S.dma("pool", wb.sem, wq[:, :, g3, :], src[:, :, g3, :], [], [wb])
                wob = wring.get()
                wo = wload(wob, 128, D["even_w"][:, 1536 + h * 128:1536 + (h + 1) * 128].rearrange("(k p) n -> p k n", p=128))
                flip = 0
                for (g, c0, n) in GROUPS:
                    for which, dst, scl in ((0, qT, 1.0), (1, kT, KS)):
                        ps = psA.get()
                        for k in range(8):
                            S.op("pe", [wb, hTg[g]], [ps], lambda e: e.matmul(ps[:, 0:n], lhsT=wq[:, k, which, :], rhs=hTg[g][:, k, 0:n], start=(k == 0), stop=(k == 7)))
                        if flip % 2 == 0:
                            S.op("act", [ps], [dst], lambda e: e.activation(out=dst[:, c0:c0 + n], in_=ps[:, 0:n], func=AF.Copy, scale=scl))
                        else:
                            S.op("dve", [ps], [dst], lambda e: e.tensor_scalar(out=dst[:, c0:c0 + n], in0=ps[:, 0:n], scalar1=scl, scalar2=None, op0=ALU.mult))
                        flip += 1
                if KS_ == "m2a" and h == KH_:
                    return
                for i in range(NT):
                    g, off = tok_group(i)
                    ps = psA.get()
                    for k in range(8):
                        S.op("pe", [wb, hTg[g]], [ps], lambda e: e.matmul(ps[:, 0:256], lhsT=hTg[g][:, k, off:off + 128], rhs=wb[:, k * 384 + 128:k * 384 + 384], start=(k == 0), stop=(k == 7)))
                    S.op("act", [ps], [ktok], lambda e: e.activation(out=ktok[:, i, :], in_=ps[:, 0:128], func=AF.Copy, scale=KS))
                    S.op("dve", [ps], [vaug], lambda e: e.tensor_copy(out=vaug[:, i, 0:128], in_=ps[:, 128:256]))
                if KS_ == "m2b" and h == KH_:
                    return
                if h == 0:
                    pass
                    pass
                    pass
                    pass
                if KS_ == "m2" and h == KH_:
                    return
                written = [False] * NT
                PTs = {}

                def emitA2(step):
                    ii = [orders[d][step] for d in range(2)]
                    col = lambda a, d: a[:, d, ii[d], h:h + 1]
                    css = [slice(i * 128, (i + 1) * 128) for i in ii]
                    pss2, kws, pscs = [], [], []
                    for d in range(2):
                        pss = psS.get()
                        S.op("pe", [kT, qT], [pss], lambda e: e.matmul(pss[:, 0:128], lhsT=kT[:, css[d]], rhs=qT[:, css[d]], start=True, stop=True))
                        pss2.append(pss)
                    if step < NT - 1:
                        for d in range(2):
                            kw = kwr.get()
                            S.op("act", [ktok, ekw], [kw], lambda e: e.activation(out=kw[:], in_=ktok[:, ii[d], :], func=AF.Copy, scale=col(ekw, d)))
                            kws.append(kw)
                        for d in range(2):
                            psc = psA.get()
                            S.op("pe", [kws[d], vaug], [psc], lambda e: e.matmul(psc[:, 0:129], lhsT=kws[d][:], rhs=vaug[:, ii[d], 0:129], start=True, stop=True))
                            pscs.append(psc)
                    for d in range(2):
                        PT = PTr.get()
                        msk = TRIU() if d == 0 else TRIL()
                        S.op("dve", [pss2[d], es, cst], [PT], lambda e: e.scalar_tensor_tensor(out=PT[:], in0=pss2[d][:, 0:128], scalar=col(es, d), in1=msk, op0=ALU.mult, op1=ALU.mult))
                        PTs[(step, d)] = PT
                    if step < NT - 1:
                        for d in range(2):
                            psc = pscs[d]
                            if step == 0:
                                S.op("dve", [psc], [Cst[d]], lambda e: e.tensor_copy(out=Cst[d][:], in_=psc[:, 0:129]))
                            else:
                                S.op("dve", [psc, Cst[d], edec], [Cst[d]], lambda e: e.scalar_tensor_tensor(out=Cst[d][:], in0=Cst[d][:], scalar=col(edec, d), in1=psc[:, 0:129], op0=ALU.mult, op1=ALU.add))
                            cb3 = Cbf3[d][(step + 1) % 3]
                            S.op("dve", [Cst[d]], [cb3], lambda e: e.tensor_copy(out=cb3[:, 0:129], in_=Cst[d][:]))

                def emitB(step, d):
                    i = orders[d][step]
                    col = lambda a: a[:, d, i, h:h + 1]
                    cs = slice(i * 128, (i + 1) * 128)
                    PT = PTs.pop((step, d))
                    acc = psO.get()
                    if step > 0:
                        cb3 = Cbf3[d][step % 3]
                        S.op("pe", [qT, cb3], [acc], lambda e: e.matmul(acc[:, 0:129], lhsT=qT[:, cs], rhs=cb3[:, 0:129], start=True, stop=False))
                    S.op("pe", [PT, vaug], [acc], lambda e: e.matmul(acc[:, 0:129], lhsT=PT[:], rhs=vaug[:, i, 0:129], start=(step == 0), stop=True))
                    S.op("act", [acc, eb], [hraw[d]], lambda e: e.activation(out=hraw[d][:, i, 0:129], in_=acc[:, 0:129], func=AF.Copy, scale=col(eb)))

                emitA2(0)
                for step in range(NT):
                    if step + 1 < NT:
                        emitA2(step + 1)
                    emitB(step, 0)
                    emitB(step, 1)
                if h == 0:
                    pass
                    pass
                if KS_ == "m3" and h == KH_:
                    return
                for d in range(2):
                    S.op("act", [hraw[d]], [rnm], lambda e: e.activation(out=rnm[:, d, :], in_=hraw[d][:, :, 128], func=AF.Abs))
                S.op("dve", [rnm], [rnm], lambda e: e.tensor_scalar(out=rnm[:], in0=rnm[:], scalar1=1.0, scalar2=None, op0=ALU.max))
                S.op("dve", [rnm], [rnm], lambda e: e.reciprocal(out=rnm[:], in_=rnm[:]))
                for d in range(2):
                    S.op("dve", [hraw[d], rnm], [hraw[d]], lambda e: e.tensor_tensor(out=hraw[d][:, :, 0:128], in0=hraw[d][:, :, 0:128], in1=rnm[:, d, :].unsqueeze(2).to_broadcast([128, NT, 128]), op=ALU.mult))
                S.op("dve", [hraw[0], hraw[1]], [hraw[0]], lambda e: e.tensor_tensor(out=hraw[0][:, :, 0:128], in0=hraw[0][:, :, 0:128], in1=hraw[1][:, :, 0:128], op=ALU.add))
                S.op("dve", [], [hss], lambda e: e.memset(hss[:], 1.0))
                for i in range(NT):
                    S.op("act", [hraw[0]], [hjunk, hss], lambda e: e.activation(out=hjunk[:], in_=hraw[0][:, i, 0:128], func=AF.Square, accum_out=hss[:, i:i + 1]))
                rstd_of(hss, lambda: hss[:], 128)

                def out_stage1(i):
                    g, off = tok_group(i)
                    ps = psA.get()
                    for k in range(8):
                        S.op("pe", [wob, hTg[g]], [ps], lambda e: e.matmul(ps[:, 0:128], lhsT=hTg[g][:, k, off:off + 128], rhs=wo[:, k, :], start=(k == 0), stop=(k == 7)))
                    og = ogr.get()
                    S.op("act", [ps], [og], lambda e: e.activation(out=og[:], in_=ps[:, 0:128], func=AF.Sigmoid))
                    return og

                def out_stage2(i, og):
                    t1 = t1r.get()
                    S.op("dve", [hraw[0], hss, headg], [t1], lambda e: e.scalar_tensor_tensor(out=t1[:], in0=hraw[0][:, i, 0:128], scalar=hss[:, i:i + 1], in1=headg[:, h * 128:(h + 1) * 128], op0=ALU.mult, op1=ALU.mult))
                    mx = mxr.get()
                    S.op("dve", [t1, og], [mx], lambda e: e.tensor_tensor(out=mx[:], in0=t1[:], in1=og[:], op=ALU.mult))
                    pt = psT.get()
                    S.op("pe", [mx, identb], [pt], lambda e: e.transpose(out=pt[:, 0, :], in_=mx[:], identity=identb[:]))
                    S.op("act", [pt], [mixTa], lambda e: e.activation(out=mixTa[:, h, i * 128:(i + 1) * 128], in_=pt[:, 0, :], func=AF.Copy))

                ogn = out_stage1(0)
                for i in range(NT):
                    ogc = ogn
                    if i + 1 < NT:
                        ogn = out_stage1(i + 1)
                    out_stage2(i, ogc)
                if KS_ == "m4" and h == KH_:
                    return

        def attn_scores_exp_pv(kv_specs, nheads_per_kv, qT, q_sl, PTr, accs, first, last):
            pass

        def gqa_attn(l, hTg, other, wring, ph):
            wname = "even_w" if l == 0 else "odd_w"
            qc0, kc0 = (2064, 2576) if l == 0 else (0, 512)
            swag = S.sbd(f"swag{l}", [128, 128], F32, ph)
            roper = Ring([S.sbd(f"rope{l}_{i}", [128, 2, 64], F32, ph) for i in range(2)])

            def get_rope(jt):
                rb = roper.get()
                S.dma("sp", rb.sem, rb[:], D["rope"][:, :, jt, :], [], [rb])
                return rb
            S.dma("sp", swag.sem, swag[:], D["swa_g_bc" if l == 0 else "gqa_g_bc"], [], [swag])
            gq = S.sb(f"gq{l}", [128, 64], F32, ph)
            S.op("dve", [swag], [gq], lambda e: e.tensor_scalar(out=gq[:], in0=swag[:, 0:64], scalar1=0.125, scalar2=None, op0=ALU.mult))
            esink = S.sb(f"esink{l}", [128, 8], F32, ph)
            if l == 0:
                sinkb = S.sbd("sinkb", [128, 8], F32, ph)
                S.dma("sp", sinkb.sem, sinkb[:], D["sink_bc"], [], [sinkb])
                S.op("act", [sinkb], [esink], lambda e: e.activation(out=esink[:], in_=sinkb[:], func=AF.Exp))
            else:
                S.op("dve", [], [esink], lambda e: e.memset(esink[:], 0.0))
            wkb = S.sbd(f"wkv{l}", [128, 8 * 256], BF16, ph)
            wkv = wload(wkb, 256, D[wname][:, kc0:kc0 + 256].rearrange("(k p) n -> p k n", p=128))
            kTd = [S.sb(f"kTd{g}", [128, NT * 128], BF16, ph) for g in range(2)]
            vb = S.sb("vb", [128, NT, 2, 66], BF16, ph)
            S.op("dve", [], [vb], lambda e: e.memset(vb[:, :, :, 64:65], 1.0))
            with ExitStack() as pk:
                TB = 9
                kraw = S.sb("kraw", [128, TB, 128], F32, pk)
                ksq = S.sb("ksq", [128, TB, 128], F32, pk)
                kt2 = S.sb("kt2", [128, TB, 128], F32, pk)
                kss = S.sb("kss", [128, TB * 2], F32, pk)
                knb = S.sb("knb", [128, TB, 128], BF16, pk)
                kd = S.sb("kd", [128, 2, 2, 64], BF16, pk)
                rtab = S.sbd("rtab", [128, 2, TB, 64], F32, pk)
                for t0 in range(0, NT, TB):
                    tl = list(range(t0, t0 + TB))
                    r0 = max(0, 2 - t0)
                    nl = TB - r0
                    j0 = t0 + r0 - 2
                    for cs_ in range(2):
                        S.dma("sp", rtab.sem, rtab[:, cs_, 0:nl, :], D["rope"][:, cs_, j0:j0 + nl, :], [], [rtab])
                    for i in tl:
                        g, off = tok_group(i)
                        ps = psA.get()
                        for k in range(8):
                            S.op("pe", [wkb, hTg[g]], [ps], lambda e: e.matmul(ps[:, 0:256], lhsT=hTg[g][:, k, off:off + 128], rhs=wkv[:, k, :], start=(k == 0), stop=(k == 7)))
                        S.op("act", [ps], [vb], lambda e: e.activation(out=vb[:, i, :, 0:64], in_=ps[:, 128:256].rearrange("p (g d) -> p g d", g=2), func=AF.Copy))
                        S.op("dve", [ps], [kraw], lambda e: e.tensor_copy(out=kraw[:, i - t0, :], in_=ps[:, 0:128]))
                    prep_batch(kraw, ksq, kss, TB, 2, swag[:, 64:128], None, None, inplace=True)
                    if r0 > 0:
                        S.op("act", [kraw], [knb], lambda e: e.activation(out=knb[:, 0:r0, :], in_=kraw[:, 0:r0, :], func=AF.Copy))
                    v4 = lambda ap: ap.rearrange("p t (h d) -> p t h d", d=64)
                    cosb = rtab[:, 0, 0:nl, :].unsqueeze(2).to_broadcast([128, nl, 2, 64])
                    S.op("dve", [kraw, rtab], [ksq], lambda e: e.tensor_tensor(out=v4(ksq[:, r0:TB, :]), in0=v4(kraw[:, r0:TB, :]), in1=cosb, op=ALU.mult))
                    v6 = lambda ap: ap.rearrange("p t (h x y d) -> p t h x y d", h=2, x=2, y=2)
                    s5 = rtab[:, 1, 0:nl, :].rearrange("p t (x y d) -> p t x y d", x=2, y=2)
                    for hh_ in range(2):
                        for y in range(2):
                            S.op("dve", [kraw, rtab], [kt2], lambda e: e.tensor_tensor(
                                out=v6(kt2[:, r0:TB, :])[:, :, hh_, :, y, :], in0=v6(kraw[:, r0:TB, :])[:, :, hh_, :, 1 - y, :],
                                in1=s5[:, :, :, y, :], op=ALU.mult))
                    S.op("dve", [ksq, kt2], [knb], lambda e: e.tensor_tensor(out=knb[:, r0:TB, :], in0=ksq[:, r0:TB, :], in1=kt2[:, r0:TB, :], op=ALU.add))
                    for i in tl:
                        S.op("dve", [knb], [kd], lambda e: e.tensor_copy(out=kd[:], in_=knb[:, i - t0, :].rearrange("p (g d) -> p g d", g=2).unsqueeze(2).to_broadcast([128, 2, 2, 64])))
                        pt = psT.get()
                        for g2 in range(2):
                            S.op("pe", [kd, identb], [pt], lambda e: e.transpose(out=pt[:, g2, :], in_=kd[:, g2].rearrange("p a d -> p (a d)"), identity=identb[:]))
                        S.op("act", [pt], [kTd[0]], lambda e: e.activation(out=kTd[0][:, i * 128:(i + 1) * 128], in_=pt[:, 0, :], func=AF.Copy))
                        S.op("act", [pt], [kTd[1]], lambda e: e.activation(out=kTd[1][:, i * 128:(i + 1) * 128], in_=pt[:, 1, :], func=AF.Copy))
                S.barrier()
            wqb = S.sbd(f"wqq{l}", [128, 8 * 512], BF16, ph)
            wq = wload(wqb, 512, D[wname][:, qc0:qc0 + 512].rearrange("(k p) n -> p k n", p=128))
            wout = S.sbd(f"wout{l}", [128, 8 * 1024], BF16, ph)
            woutv = wout[:, :].rearrange("p (k n) -> p k n", k=8)
            S.dma("pool", wout.sem, woutv, D["w_out"][l].rearrange("(k p) n -> p k n", p=128), [], [wout])
            wk = (S.sb("wk_sq", [128, 512], F32, ph), S.sb("wk_ss", [128, 8], F32, ph), S.sb("wk_qn", [128, 512], F32, ph), S.sb("wk_t1", [128, 512], F32, ph))
            import os
            KS_ = os.environ.get("KSUB", "")
            if KS_ == "w1" or (KS_ == "g1k" and l == 1):
                return
            qb = S.sb("qb", [128, 8, 64], BF16, ph)
            qz = S.sb("qz", [128, 2, 4, 128], BF16, ph)
            S.op("dve", [], [qz], lambda e: e.memset(qz[:], 0.0))
            wmb4 = S.sb("wmb4", [128, 2, 4, 128], BF16, ph)
            S.op("dve", [wmb], [wmb4], lambda e: e.tensor_copy(out=wmb4[:], in_=wmb[:, :, :].unsqueeze(2).to_broadcast([128, 2, 4, 128])))
            PTr = Ring([S.sb(f"PTw{i}", [128, 512], BF16, ph) for i in range(2)])
            den = S.sb("wden", [128, 8], F32, ph)
            mixb = S.sb("mixb", [128, 512], BF16, ph)
            mixTb = S.sb("mixTb", [128, 4, 128], BF16, ph)
            def emit_qprep(i):
                g, off = tok_group(i)
                lat = i >= 2
                j = i - 2
                ps = psA.get()
                for k in range(8):
                    S.op("pe", [wqb, hTg[g]], [ps], lambda e: e.matmul(ps[:, 0:512], lhsT=hTg[g][:, k, off:off + 128], rhs=wq[:, k, :], start=(k == 0), stop=(k == 7)))
                qk_prep(ps, ps[:, 0:512], 8, gq[:], j if lat else None, (qb, qb[:]), wk, get_rope(j) if lat else None)

            qtiles = list(range(NT) if l == 0 else range(2, NT))
            emit_qprep(qtiles[0])
            for qi, i in enumerate(qtiles):
                g, off = tok_group(i)
                lat = i >= 2
                j = i - 2
                pt = psT.get()
                for pr in range(4):
                    S.op("pe", [qb, identb], [pt], lambda e: e.transpose(out=pt[:, pr, :], in_=qb[:, 2 * pr:2 * pr + 2, :].rearrange("p a d -> p (a d)"), identity=identb[:]))
                S.op("act", [pt], [qz], lambda e: e.activation(out=qz[0:64, 0, :, :], in_=pt[0:64, 0:4, :], func=AF.Copy))
                S.op("dve", [pt], [qz], lambda e: e.tensor_copy(out=qz[64:128, 1, :, :], in_=pt[64:128, 0:4, :]))
                if qi + 1 < len(qtiles):
                    emit_qprep(qtiles[qi + 1])
                if KS_ == "w2a":
                    return
                if l == 1:
                    blocks = [(m, None) for m in range(NT)]
                elif lat:
                    blocks = [(0, None), (1, None)]
                    if j > 0:
                        blocks.append((i - 1, 0))
                    blocks.append((i, None))
                    if j < 15:
                        blocks.append((i + 1, 1))
                else:
                    blocks = [(0, None), (1, None)]
                for g2 in range(2):
                    acc = psO.get()

                    def emit_scores(m, msk):
                        pss = psS.get()
                        for half in range(2):
                            S.op("pe", [kTd[g2], qz], [pss], lambda e: e.matmul(
                                pss[:, half * 256:(half + 1) * 256], lhsT=kTd[g2][:, m * 128:(m + 1) * 128],
                                rhs=qz[:, half, 2 * g2:2 * g2 + 2, :].rearrange("p a q -> p (a q)"),
                                start=(half == 0), stop=(half == 1 and msk is None)))
                        if msk is not None:
                            S.op("pe", [identb, wmb4], [pss], lambda e: e.matmul(pss[:, 0:512], lhsT=identb[:], rhs=wmb4[:, msk, :, :].rearrange("p a q -> p (a q)"), start=False, stop=True))
                        return pss

                    nxt = emit_scores(*blocks[0])
                    for bi, (m, msk) in enumerate(blocks):
                        pss = nxt
                        if bi + 1 < len(blocks):
                            nxt = emit_scores(*blocks[bi + 1])
                        PT = PTr.get()
                        S.op("act", [pss], [PT], lambda e: e.activation(out=PT[:], in_=pss[:], func=AF.Exp))
                        for hh in range(4):
                            S.op("pe", [PT, vb], [acc], lambda e: e.matmul(acc[:, hh * 128:hh * 128 + 65], lhsT=PT[:, hh * 128:(hh + 1) * 128], rhs=vb[:, m, g2, 0:65], start=(bi == 0 and hh == 0), stop=(bi == len(blocks) - 1)))
                    if KS_ == "w2c":
                        return
                    a3 = acc[:, :].rearrange("p (h c) -> p h c", h=4)
                    S.op("dve", [acc, esink], [den], lambda e: e.tensor_tensor(out=den[:, g2 * 4:(g2 + 1) * 4].rearrange("p (b a) -> p b a", b=2), in0=a3[:, :, 64].rearrange("p (b a) -> p b a", b=2),
                                                                            in1=esink[:, g2 * 4:(g2 + 1) * 4].rearrange("p (a b) -> p b a", a=2), op=ALU.add))
                    S.op("dve", [den], [den], lambda e: e.reciprocal(out=den[:, g2 * 4:(g2 + 1) * 4], in_=den[:, g2 * 4:(g2 + 1) * 4]))
                    S.op("dve", [acc, den], [mixb], lambda e: e.tensor_tensor(
                        out=mixb[:, g2 * 256:(g2 + 1) * 256].rearrange("p (a b d) -> p b a d", a=2, b=2), in0=a3[:, :, 0:64].rearrange("p (b a) d -> p b a d", b=2),
                        in1=den[:, g2 * 4:(g2 + 1) * 4].rearrange("p (b a) -> p b a", b=2).unsqueeze(3).to_broadcast([128, 2, 2, 64]), op=ALU.mult))
                if KS_ == "w2d":
                    return
                pt2 = psT.get()
                for c in range(4):
                    S.op("pe", [mixb, identb], [pt2], lambda e: e.transpose(out=pt2[:, c, :], in_=mixb[:, c * 128:(c + 1) * 128], identity=identb[:]))
                S.op("act", [pt2], [mixTb], lambda e: e.activation(out=mixTb[:], in_=pt2[:, 0:4, :], func=AF.Copy))
                if (KS_ == "w2" and i == 2) or (KS_ == "g1q" and l == 1 and i == 3):
                    return
                for cgi in range(2):
                    pso = psA.get()
                    for k in range(8):
                        if l == 0:
                            lhs = other[:, k, i * 128:(i + 1) * 128] if k < 4 else mixTb[:, k - 4, :]
                        else:
                            lhs = mixTb[:, k, :] if k < 4 else other[:, k - 4, j * 128:(j + 1) * 128]
                        S.op("pe", [other, mixTb, wout], [pso], lambda e: e.matmul(pso[:, 0:512], lhsT=lhs, rhs=woutv[:, k, cgi * 512:(cgi + 1) * 512], start=(k == 0), stop=(k == 7)))
                    residual(i, pso, cgi, 0 if lat else 1)

        def ffn_phase(l, tiles):
            tiles = list(tiles)
            with ExitStack() as ph:
                hTg = [S.sb(f"hF{l}_0", [128, 8, 256], BF16, ph)] + [S.sb(f"hF{l}_{g}", [128, 8, 512], BF16, ph) for g in range(1, 5)]
                with ExitStack() as ph2:
                    norm_phase(l, 1, hTg, ph2, tiles)
                    mk_gate(l, 40, ph2)
                    S.barrier()
                segs = [gg for gg in GROUPS if (gg[0] > 0 or 0 in tiles)]
                lo = segs[0][1]
                ranges = ([(0, 256)] if lo == 0 else []) + [(256, 2304)]
                GS = 3
                ur = Ring([S.sb(f"fu{l}_{i}", [128, 2304], F32, ph) for i in range(2)])
                yr = Ring([S.sb(f"fy{l}_{i}", [128, 2304], F32, ph) for i in range(2)])
                actT = S.sb(f"actT{l}", [128, GS, 2304], BF16, ph)
                wur = Ring([S.sbd(f"wu{l}_{i}", [128, 8 * 256], BF16, ph) for i in range(3)])
                wdr = Ring([S.sbd(f"wd{l}_{i}", [128, GS * 1024], BF16, ph) for i in range(2)])
                has_ctx = (lo == 0)
                wdraw = Ring([S.sb(f"wdraw{l}_{i}", [128, GS * 1024], BF16, ph) for i in range(1)]) if has_ctx else None
                upsrc = D["ffn_up"][l].rearrange("(k p) (g c n) -> p k g c n", p=128, g=2, c=22)
                wu_loaded = {}
                wd_loaded = {}

                def load_wu(cp):
                    if cp >= 22 or cp in wu_loaded:
                        return
                    wub = wur.get()
                    wu = wub[:, :].rearrange("p (k g n) -> p k g n", k=8, g=2)
                    for g3 in range(2):
                        S.dma("pool", wub.sem, wu[:, :, g3, :], upsrc[:, :, g3, cp, :], [], [wub])
                    wu_loaded[cp] = (wub, wu)

                def load_wd(c0):
                    if c0 >= 22 or c0 in wd_loaded:
                        return
                    ncg = min(GS, 22 - c0)
                    wdb = wdr.get()
                    wd = wdb[:, 0:ncg * 1024].rearrange("p (c n) -> p c n", c=ncg)
                    S.dma("pool", wdb.sem, wd, D["ffn_down"][l, c0 * 128:(c0 + ncg) * 128, :].rearrange("(c p) n -> p c n", p=128), [], [wdb])
                    wd_loaded[c0] = (wdb, wd)

                def scale_wd(c0):
                    ncg = min(GS, 22 - c0)
                    wdb, wd = wd_loaded[c0]
                    raw = None
                    if has_ctx:
                        rb = wdraw.get()
                        raw = rb[:, 0:ncg * 1024].rearrange("p (c n) -> p c n", c=ncg)
                        S.op("pool", [wdb], [rb], lambda e: e.tensor_copy(out=raw, in_=wd))
                        wd_loaded[c0] = (wdb, wd, rb, raw)
                    S.op("pool", [wdb, gbc], [wdb], lambda e: e.tensor_tensor(out=wd, in0=wd, in1=gbc[:, 0, :].unsqueeze(1).to_broadcast([128, ncg, 1024]), op=ALU.mult))
                    if not has_ctx:
                        wd_loaded[c0] = (wdb, wd, None, None)

                load_wu(0)
                load_wu(1)
                load_wd(0)

                def emit_up1(cp):
                    load_wu(cp + 2)
                    wub, wu = wu_loaded[cp]
                    ys = []
                    for gv in range(2):
                        ch = gv * 22 + cp
                        u = ur.get()
                        y = yr.get()
                        w0 = cw[:, l, 0, ch:ch + 1]
                        w1 = cw[:, l, 1, ch:ch + 1]
                        w2 = cw[:, l, 2, ch:ch + 1]
                        for (g, t0, n) in segs:
                            ps = psA.get()
                            for k in range(8):
                                S.op("pe", [wub, hTg[g]], [ps], lambda e: e.matmul(ps[:, 0:n], lhsT=wu[:, k, gv, :], rhs=hTg[g][:, k, 0:n], start=(k == 0), stop=(k == 7)))
                            S.op("act", [ps], [u], lambda e: e.activation(out=u[:, t0:t0 + n], in_=ps[:, 0:n], func=AF.Copy))
                        S.op("act", [u, cw, cb], [y], lambda e: e.activation(out=y[:, lo:2304], in_=u[:, lo:2304], func=AF.Identity, scale=w1, bias=cb[:, l, ch:ch + 1]))
                        for (a, b_) in ranges:
                            S.op("dve", [u, cw, y], [y], lambda e: e.scalar_tensor_tensor(out=y[:, a + 1:b_], in0=u[:, a:b_ - 1], scalar=w0, in1=y[:, a + 1:b_], op0=ALU.mult, op1=ALU.add))
                            S.op("dve", [u, cw, y], [y], lambda e: e.scalar_tensor_tensor(out=y[:, a:b_ - 1], in0=u[:, a + 1:b_], scalar=w2, in1=y[:, a:b_ - 1], op0=ALU.mult, op1=ALU.add))
                        ys.append(y)
                    S.op("act", [ys[0]], [ys[0]], lambda e: e.activation(out=ys[0][:, lo:2304], in_=ys[0][:, lo:2304], func=AF.Silu))
                    return ys

                def emit_up2(ys, ci):
                    S.op("dve", [ys[0], ys[1]], [actT], lambda e: e.tensor_tensor(out=actT[:, ci, lo:2304], in0=ys[0][:, lo:2304], in1=ys[1][:, lo:2304], op=ALU.mult))

                def emit_down(c0):
                    ncg = min(GS, 22 - c0)
                    scale_wd(c0)
                    wdb, wd, rb, raw = wd_loaded[c0]
                    for i in tiles:
                        for cgi in range(2):
                            ps = psO.get()
                            if i >= 2:
                                for ci in range(ncg):
                                    S.op("pe", [actT, wdb], [ps], lambda e: e.matmul(ps[:, 0:512], lhsT=actT[:, ci, i * 128:(i + 1) * 128], rhs=wd[:, ci, cgi * 512:(cgi + 1) * 512], start=(ci == 0), stop=(ci == ncg - 1)))
                                S.op("dve", [ps, xs[i]], [xs[i]], lambda e: e.tensor_tensor(out=xs[i][:, cgi * 512:(cgi + 1) * 512], in0=ps[:], in1=xs[i][:, cgi * 512:(cgi + 1) * 512], op=ALU.add))
                            else:
                                for ci in range(ncg):
                                    S.op("pe", [actT, rb], [ps], lambda e: e.matmul(ps[:, 0:512], lhsT=actT[:, ci, i * 128:(i + 1) * 128], rhs=raw[:, ci, cgi * 512:(cgi + 1) * 512], start=(ci == 0), stop=(ci == ncg - 1)))
                                residual(i, ps, cgi, 1)

                pending = None
                for c0 in range(0, 22, GS):
                    ncg = min(GS, 22 - c0)
                    ys0 = emit_up1(c0)
                    if pending is not None:
                        emit_down(pending)
                    load_wd(c0 + GS)
                    emit_up2(ys0, 0)
                    for ci in range(1, ncg):
                        emit_up2(emit_up1(c0 + ci), ci)
                    pending = c0
                emit_down(pending)
                S.barrier()

        def na_attn(hTg, mixTd, wring, ph):
            nag = S.sbd("nag", [128, 128], F32, ph)
            S.dma("sp", nag.sem, nag[:], D["na_g_bc"], [], [nag])
            gq = S.sb("nagq", [128, 64], F32, ph)
            S.op("dve", [nag], [gq], lambda e: e.tensor_scalar(out=gq[:], in0=nag[:, 0:64], scalar1=0.125, scalar2=None, op0=ALU.mult))
            kTn = S.sb("kTn", [128, NT * 128], BF16, ph)
            vn = S.sb("vn", [128, NT, 2, 66], BF16, ph)
            qTn = S.sb("qTn", [128, 2048], BF16, ph)
            S.op("dve", [], [vn], lambda e: e.memset(vn[:, :, :, 64:65], 1.0))
            TB = 9
            raw = S.sb("nraw", [128, TB, 128], F32, ph)
            sq = S.sb("nsq", [128, TB, 128], F32, ph)
            ssb = S.sb("nss", [128, TB * 2], F32, ph)
            nrm = S.sb("nnrm", [128, TB, 128], BF16, ph)
            biasr = Ring([S.sbd(f"nbias{i}", [128, 25, 128], F32, ph) for i in range(1)])
            stmp = Ring([S.sb(f"nstmp{i}", [128, 5, 128], F32, ph) for i in range(2)])
            PTr = Ring([S.sb(f"PTn{i}", [128, 7, 128], BF16, ph) for i in range(3)])
            rdn = Ring([S.sb(f"nrd{i}", [128, 1], F32, ph) for i in range(3)])
            mixd = S.sb("mixd", [128, 16, 2, 64], BF16, ph)
            naS = Ring(psS.bufs + psA.bufs)
            wsrc = D["odd_w"][:, 768:2304].rearrange("(k p) (g h n) -> p k g h n", p=128, g=3, h=4)
            wl = {}

            def load_w(pr):
                if pr >= 4 or pr in wl:
                    return
                wb = wring.get()
                wq = wb[:, 0:8 * 384].rearrange("p (k g n) -> p k g n", k=8, g=3)
                for g3 in range(3):
                    S.dma("pool", wb.sem, wq[:, :, g3, :], wsrc[:, :, g3, pr, :], [], [wb])
                wl[pr] = (wb, wq)

            load_w(0)
            for pr in range(4):
                wb, wq = wl[pr]
                load_w(pr + 1)
                for t0 in range(0, NT, TB):
                    tl = list(range(t0, min(NT, t0 + TB)))
                    for i in tl:
                        g, off = tok_group(i)
                        ps = psA.get()
                        for k in range(8):
                            S.op("pe", [wb, hTg[g]], [ps], lambda e: e.matmul(ps[:, 0:256], lhsT=hTg[g][:, k, off:off + 128], rhs=wb[:, k * 384 + 128:k * 384 + 384], start=(k == 0), stop=(k == 7)))
                        S.op("act", [ps], [vn], lambda e: e.activation(out=vn[:, i, :, 0:64], in_=ps[:, 128:256].rearrange("p (g d) -> p g d", g=2), func=AF.Copy))
                        S.op("dve", [ps], [raw], lambda e: e.tensor_copy(out=raw[:, i - t0, :], in_=ps[:, 0:128]))
                    prep_batch(raw, sq, ssb, len(tl), 2, nag[:, 64:128], nrm, nrm[:, 0:len(tl), :])
                    for i in tl:
                        pt = psT.get()
                        S.op("pe", [nrm, identb], [pt], lambda e: e.transpose(out=pt[:, 0, :], in_=nrm[:, i - t0, :], identity=identb[:]))
                        S.op("act", [pt], [kTn], lambda e: e.activation(out=kTn[:, i * 128:(i + 1) * 128], in_=pt[:, 0, :], func=AF.Copy))
                for t0 in range(2, NT, 8):
                    tl = list(range(t0, t0 + 8))
                    for i in tl:
                        g, off = tok_group(i)
                        ps2 = psA.get()
                        for k in range(8):
                            S.op("pe", [wb, hTg[g]], [ps2], lambda e: e.matmul(ps2[:, 0:128], lhsT=hTg[g][:, k, off:off + 128], rhs=wq[:, k, 0, :], start=(k == 0), stop=(k == 7)))
                        S.op("dve", [ps2], [raw], lambda e: e.tensor_copy(out=raw[:, i - t0, :], in_=ps2[:, 0:128]))
                    prep_batch(raw, sq, ssb, 8, 2, gq[:], nrm, nrm[:, 0:8, :])
                    for i in tl:
                        j = i - 2
                        pt2 = psT.get()
                        S.op("pe", [nrm, identb], [pt2], lambda e: e.transpose(out=pt2[:, 0, :], in_=nrm[:, i - t0, :], identity=identb[:]))
                        S.op("dve", [pt2], [qTn], lambda e: e.tensor_copy(out=qTn[:, j * 128:(j + 1) * 128], in_=pt2[:, 0, :]))
                for hh in range(2):
                    head = 2 * pr + hh
                    bt = biasr.get()
                    S.dma("sp", bt.sem, bt[:], D["na_bias"][head], [], [bt])
                    prs = slice(hh * 64, (hh + 1) * 64)

                    def blocks_of(j):
                        ci = 0 if j == 0 else 1 if j == 1 else 3 if j == 14 else 4 if j == 15 else 2
                        mlist = list(range(j - 2, j + 3)) if ci == 2 else na_blocks[j]
                        return ci, len(mlist), [0, 1] + [m + 2 for m in mlist]

                    def emit_scores(j):
                        ci, nb, keyt = blocks_of(j)
                        pA = naS.get()
                        pB = naS.get()
                        for bi, kt in enumerate(keyt):
                            pp, off2 = (pA, bi) if bi < 4 else (pB, bi - 4)
                            S.op("pe", [kTn, qTn], [pp], lambda e: e.matmul(pp[:, off2 * 128:(off2 + 1) * 128], lhsT=kTn[prs, kt * 128:(kt + 1) * 128], rhs=qTn[prs, j * 128:(j + 1) * 128], start=True, stop=True))
                        return pA, pB

                    def emit_norm(acc_, j_):
                        rd = rdn.get()
                        S.op("dve", [acc_], [rd], lambda e: e.reciprocal(out=rd[:], in_=acc_[:, 64:65]))
                        S.op("act", [acc_, rd], [mixd], lambda e: e.activation(out=mixd[:, j_, hh, :], in_=acc_[:, 0:64], func=AF.Copy, scale=rd[:, 0:1]))

                    pend_norm = None
                    nxt = emit_scores(0)
                    for j in range(16):
                        ci, nb, keyt = blocks_of(j)
                        pA, pB = nxt
                        if j + 1 < 16:
                            nxt = emit_scores(j + 1)
                        stp = stmp.get()
                        S.op("dve", [pA, bt], [stp], lambda e: e.tensor_tensor(out=stp[:, 0:2, :], in0=pA[:, 256:512].rearrange("p (b q) -> p b q", b=2), in1=bt[:, ci * 5:ci * 5 + 2, :], op=ALU.add))
                        S.op("dve", [pB, bt], [stp], lambda e: e.tensor_tensor(out=stp[:, 2:nb, :], in0=pB[:, 0:(nb - 2) * 128].rearrange("p (b q) -> p b q", b=nb - 2), in1=bt[:, ci * 5 + 2:ci * 5 + nb, :], op=ALU.add))
                        PT = PTr.get()
                        S.op("act", [pA], [PT], lambda e: e.activation(out=PT[:, 0:2, :], in_=pA[:, 0:256].rearrange("p (b q) -> p b q", b=2), func=AF.Exp))
                        S.op("act", [stp], [PT], lambda e: e.activation(out=PT[:, 2:2 + nb, :], in_=stp[:, 0:nb, :], func=AF.Exp))
                        acc = psO.get()
                        for bi, kt in enumerate(keyt):
                            S.op("pe", [PT, vn], [acc], lambda e: e.matmul(acc[:, 0:65], lhsT=PT[:, bi, :], rhs=vn[:, kt, hh, 0:65], start=(bi == 0), stop=(bi == len(keyt) - 1)))
                        if pend_norm is not None:
                            emit_norm(*pend_norm)
                        pend_norm = (acc, j)
                    emit_norm(*pend_norm)
                    pend_norm = None
                for j in range(16):
                    pt = psT.get()
                    S.op("pe", [mixd, identb], [pt], lambda e: e.transpose(out=pt[:, 0, :], in_=mixd[:, j, :, :].rearrange("p a d -> p (a d)"), identity=identb[:]))
                    S.op("dve", [pt], [mixTd], lambda e: e.tensor_copy(out=mixTd[:, pr, j * 128:(j + 1) * 128], in_=pt[:, 0, :]))

        def mixer1():
            with ExitStack() as ph:
                hTg = [S.sb("hT1_0", [128, 8, 256], BF16, ph)] + [S.sb(f"hT1_{g}", [128, 8, 512], BF16, ph) for g in range(1, 5)]
                with ExitStack() as ph2:
                    norm_phase(1, 0, hTg, ph2)
                    mk_gate(1, 16, ph2)
                    S.barrier()
                mixTd = S.sb("mixTd", [128, 4, 2048], BF16, ph)
                with ExitStack() as ph2:
                    wring = Ring([S.sbd(f"w1_{i}", [128, 8 * 384], BF16, ph2) for i in range(2)])
                    na_attn(hTg, mixTd, wring, ph2)
                    S.barrier()
                import os
                if os.environ.get("KSUB") == "nogqa1":
                    return
                with ExitStack() as ph2:
                    gqa_attn(1, hTg, mixTd, None, ph2)
                    S.barrier()

        mod_phase(0)
        if stage != "mod":
            mixer0()
        if stage not in ("l0mix", "mod", "norm", "mlstm"):
            ffn_phase(0, range(NT))
        if stage not in ("l0mix", "l0", "mod", "norm", "mlstm"):
            mod_phase(1)
            mixer1()
            if stage != "l1mix":
                ffn_phase(1, range(2, NT))
        osem = S.newsem("d_out")
        for i in range(2, NT):
            S.dma("sp", osem, out[(i - 2) * 128:(i - 1) * 128, :], xs[i][:], [xs[i]], [])
        if dbg:
            for i in range(2):
                S.dma("sp", osem, octx[i * 128:(i + 1) * 128, :], xs[i][:], [xs[i]], [])
        S._need("sp", osem, S.cnt[osem])
        S.barrier()
        print(f"[kernel] instructions={S.ninst} waits={S.nwait} sems={len(S.sems)}", flush=True)
    return nc


_CACHE = {}


def kernel(**inputs):
    shared, percore = host_prepare({k: np.asarray(v) for k, v in inputs.items()})
    if "nc" not in _CACHE:
        _CACHE["nc"] = build_program("full")
    nc = _CACHE["nc"]
    in_maps = []
    for b in range(8):
        m = dict(shared)
        m.update(percore[b])
        in_maps.append(m)
    res = run_bass_kernel_spmd(nc, in_maps, core_ids=list(range(8)))
    return np.stack([np.asarray(r["out"], np.float32) for r in res.results], axis=0)
```

```python
import numpy as np
from contextlib import ExitStack
import concourse.bass as bass
import concourse.mybir as mybir
from concourse.bass_utils import run_bass_kernel_spmd

F32 = mybir.dt.float32
BF16 = mybir.dt.bfloat16
AF = mybir.ActivationFunctionType
ALU = mybir.AluOpType
AX = mybir.AxisListType

ENGS = ("pe", "act", "dve", "pool", "sp")
NT = 18
EPS = 1e-6
NEGM = -30000.0


class Buf:
    __slots__ = ("t", "name", "w", "r", "sem", "psum")

    def __init__(self, t, name):
        self.t = t
        self.name = name
        self.w = None
        self.r = {}
        self.sem = None
        self.psum = False

    def __getitem__(self, idx):
        return self.t[idx]


class Ring:
    def __init__(self, bufs):
        self.bufs = bufs
        self.i = 0

    def get(self):
        b = self.bufs[self.i % len(self.bufs)]
        self.i += 1
        return b


SEM_LIMIT = 1500


class Sched:
    def __init__(self, nc, stack):
        self.nc = nc
        self.stack = stack
        self.eng = {"pe": nc.tensor, "act": nc.scalar, "dve": nc.vector,
                    "pool": nc.gpsimd, "sp": nc.sync}
        self.sems = {}
        self.cnt = {}
        self.epoch = {}
        self.cur = {}
        for e in ENGS:
            self.epoch[e] = 0
            self._new_epoch(e)
        self.seen = {e: {} for e in ENGS}
        self.ninst = 0
        self.nwait = 0
        self.nsem = 0
        self.nalloc = 0

    def _new_epoch(self, e):
        self.epoch[e] += 1
        key = f"{e}#{self.epoch[e]}"
        self.sems[key] = self.stack.enter_context(self.nc.semaphore("s_" + key.replace("#", "_")))
        self.cnt[key] = 0
        self.cur[e] = key

    def sb(self, name, shape, dt, stack=None):
        self.nalloc += 1
        name = f"{name}_{self.nalloc}"
        t = (stack or self.stack).enter_context(self.nc.sbuf_tensor(name, list(shape), dt))
        return Buf(t, name)

    def ps(self, name, shape, dt=F32):
        t = self.stack.enter_context(self.nc.psum_tensor(name, list(shape), dt))
        b = Buf(t, name)
        b.psum = True
        return b

    def newsem(self, name=None):
        self.nsem += 1
        name = name or f"d{self.nsem}"
        s = self.stack.enter_context(self.nc.semaphore(name))
        self.sems[name] = s
        self.cnt[name] = 0
        return name

    def sbd(self, name, shape, dt, stack=None):
        b = self.sb(name, shape, dt, stack)
        b.sem = self.newsem("d_" + b.name)
        return b

    @staticmethod
    def _eng_of(key):
        return key.split("#")[0] if "#" in key else None

    def _need(self, e, key, val):
        if self.seen[e].get(key, 0) >= val:
            return
        ke = self._eng_of(key)
        if ke is not None:
            ep = int(key.split("#")[1])
            for k2, v2 in self.seen[e].items():
                if v2 > 0 and self._eng_of(k2) == ke and int(k2.split("#")[1]) > ep:
                    return
        self.seen[e][key] = val
        self.eng[e].wait_ge(self.sems[key], val)
        self.nwait += 1

    def deps(self, e, reads, writes):
        for b in reads:
            if b.w is not None:
                k, v = b.w
                if not (self._eng_of(k) == e and e == "pe"):
                    self._need(e, k, v)
        for b in writes:
            if b.w is not None:
                k, v = b.w
                if self._eng_of(k) != e:
                    self._need(e, k, v)
            for k, v in b.r.items():
                if self._eng_of(k) != e:
                    self._need(e, k, v)

    def op(self, e, reads, writes, fn):
        pr = [b for b in reads if b.psum]
        if pr:
            reads = [b for b in reads if not b.psum]
            writes = list(writes) + [b for b in pr if b not in writes]
        self.deps(e, reads, writes)
        ins = fn(self.eng[e])
        if self.cnt[self.cur[e]] >= SEM_LIMIT:
            self._new_epoch(e)
        key = self.cur[e]
        self.cnt[key] += 1
        ins.then_inc(self.sems[key], 1)
        v = self.cnt[key]
        for b in reads:
            for k2 in [k2 for k2 in b.r if self._eng_of(k2) == e]:
                del b.r[k2]
            b.r[key] = v
        for b in writes:
            b.w = (key, v)
            b.r = {}
        self.ninst += 1
        return ins

    def dma(self, q, semkey, out_ap, in_ap, reads, writes, **kw):
        self.deps(q, reads, writes)
        ins = self.eng[q].dma_start(out=out_ap, in_=in_ap, **kw)
        self.cnt[semkey] += 16
        assert self.cnt[semkey] <= 2000, semkey
        ins.then_inc(self.sems[semkey], 16)
        v = self.cnt[semkey]
        for b in reads:
            b.r[semkey] = v
        for b in writes:
            b.w = (semkey, v)
            b.r = {}
        self.ninst += 1
        return ins

    def barrier(self):
        for e in ENGS:
            for k, v in list(self.cnt.items()):
                ke = self._eng_of(k)
                if ke == e or v == 0:
                    continue
                if ke is not None and k != self.cur[ke]:
                    if not (self.cnt[self.cur[ke]] == 0 and int(k.split("#")[1]) == self.epoch[ke] - 1):
                        continue
                self._need(e, k, v)


def _rope_tables():
    t = np.arange(2048)
    row = (t // 64).astype(np.float32)
    col = (t % 64).astype(np.float32)
    half = 32
    freq = (np.float32(10000.0) ** (-np.arange(0, half, 2, dtype=np.float32) / np.float32(half))).astype(np.float32)
    ang_r = row[:, None] * freq[None, :]
    ang_c = col[:, None] * freq[None, :]
    ang = np.concatenate([ang_r, ang_r, ang_c, ang_c], axis=-1).astype(np.float32)
    cos = np.cos(ang).astype(np.float32)
    sin = np.sin(ang).astype(np.float32)
    sgn = np.ones(64, np.float32)
    sgn[0:16] = -1.0
    sgn[32:48] = -1.0
    sinS = sin * sgn[None, :]
    cos = cos.reshape(16, 128, 64).transpose(1, 0, 2).copy()
    sinS = sinS.reshape(16, 128, 64).transpose(1, 0, 2).copy()
    return cos, sinS


def _na_tables(rpb):
    rows = 32
    wr = 8
    r = np.arange(rows)
    row_start = np.clip(r - wr // 2, 0, rows - wr)
    col = np.arange(64)
    col_start = np.clip(col - 8, 0, 48)
    col_ok = (col[None, :] >= col_start[:, None]) & (col[None, :] < col_start[:, None] + 16)
    dc = np.clip(col[None, :] - col[:, None] + 15, 0, 30)
    classes = [0, 1, 2, 14, 15]
    blocks = {}
    tab = np.full((8, 128, 25, 128), NEGM, np.float32)
    for ci, j in enumerate(classes):
        qrows = [2 * j, 2 * j + 1]
        lo = min(row_start[q] for q in qrows)
        hi = max(row_start[q] + wr - 1 for q in qrows)
        mlist = list(range(lo // 2, hi // 2 + 1))
        assert len(mlist) <= 5
        blocks[j] = mlist
        for si, m in enumerate(mlist):
            for kr in range(2):
                krow = 2 * m + kr
                for qr in range(2):
                    qrow = qrows[qr]
                    if not (row_start[qrow] <= krow < row_start[qrow] + wr):
                        continue
                    dr = krow - qrow + 7
                    sub = rpb[:, dr, :][:, dc]
                    sub = np.where(col_ok[None], sub, np.float32(NEGM))
                    tab[:, kr * 64:(kr + 1) * 64, ci * 5 + si, qr * 64:(qr + 1) * 64] = sub.transpose(0, 2, 1)
    return tab, blocks, classes


def _na_blocks():
    _, blocks, classes = _na_tables(np.zeros((8, 15, 31), np.float32))
    return blocks, classes


def host_prepare(inp):
    f = np.float32
    shared = {}
    shared["ada_w"] = np.ascontiguousarray(inp["ada_w"], f)
    shared["ada_bT"] = np.ascontiguousarray(inp["ada_b"].reshape(2, 48, 128).transpose(2, 0, 1), f)
    shared["norm_gT"] = np.ascontiguousarray(inp["norm_g"].reshape(2, 2, 8, 128).transpose(3, 0, 1, 2), f)
    shared["w_out"] = np.ascontiguousarray(inp["w_out"], f)
    shared["ffn_up"] = np.ascontiguousarray(inp["ffn_up"], f)
    shared["ffn_down"] = np.ascontiguousarray(inp["ffn_down"], f)
    shared["conv_wT"] = np.ascontiguousarray(inp["ffn_conv_w"].reshape(2, 3, 44, 128).transpose(3, 0, 1, 2), f)
    shared["conv_bT"] = np.ascontiguousarray(inp["ffn_conv_b"].reshape(2, 44, 128).transpose(2, 0, 1), f)
    shared["even_w"] = np.ascontiguousarray(inp["even_w_in"][0], f)
    shared["odd_w"] = np.ascontiguousarray(inp["odd_w_in"][0], f)
    bc = lambda a: np.ascontiguousarray(np.broadcast_to(np.asarray(a, f).reshape(1, -1), (128, a.size)))
    shared["gate_b_bc"] = bc(inp["mlstm_gate_b"][0])
    shared["head_g_bc"] = bc(inp["mlstm_head_g"][0])
    shared["swa_g_bc"] = bc(inp["swa_qk_g"][0])
    shared["sink_bc"] = bc(inp["swa_sink"][0])
    shared["gqa_g_bc"] = bc(inp["gqa_qk_g"][0])
    shared["na_g_bc"] = bc(inp["na_qk_g"][0])
    tab, _, _ = _na_tables(np.asarray(inp["na_rpb"][0], f))
    shared["na_bias"] = tab
    ident = np.eye(128, dtype=f)
    s = np.arange(128)
    triU = (s[:, None] <= s[None, :]).astype(f)
    triL = (s[:, None] >= s[None, :]).astype(f)
    wm = np.zeros((128, 2, 128), f)
    wm[:, 0, :] = np.where(s[None, :] <= s[:, None], 0.0, NEGM)
    wm[:, 1, :] = np.where(s[:, None] <= s[None, :], 0.0, NEGM)
    shared["consts"] = np.ascontiguousarray(np.concatenate([ident, triU, triL, wm.reshape(128, 256)], axis=1))
    cos, sinS = _rope_tables()
    shared["rope"] = np.ascontiguousarray(np.stack([cos, sinS], axis=1))
    percore = []
    for b in range(8):
        cc = np.stack([inp["c"][b].reshape(8, 128).T, inp["c_ctx"].reshape(8, 128).T], axis=-1)
        percore.append({"x": np.ascontiguousarray(inp["x"][b], f), "ctx": np.ascontiguousarray(inp["ctx"][b], f),
                        "cc": np.ascontiguousarray(cc, f)})
    return shared, percore


SHARED_SHAPES = {
    "ada_w": [2, 1024, 6144], "ada_bT": [128, 2, 48], "norm_gT": [128, 2, 2, 8], "w_out": [2, 1024, 1024],
    "ffn_up": [2, 1024, 5632], "ffn_down": [2, 2816, 1024], "conv_wT": [128, 2, 3, 44], "conv_bT": [128, 2, 44],
    "even_w": [1024, 2832], "odd_w": [1024, 2304], "gate_b_bc": [128, 16], "head_g_bc": [128, 512],
    "swa_g_bc": [128, 128], "sink_bc": [128, 8], "gqa_g_bc": [128, 128], "na_g_bc": [128, 128],
    "na_bias": [8, 128, 25, 128], "consts": [128, 640], "rope": [128, 2, 16, 64],
    "x": [2048, 1024], "ctx": [256, 1024], "cc": [128, 8, 2],
}


GROUPS = [(0, 0, 256), (1, 256, 512), (2, 768, 512), (3, 1280, 512), (4, 1792, 512)]


def tok_group(i):
    return (0, i * 128) if i < 2 else (1 + (i - 2) // 4, ((i - 2) % 4) * 128)


def build_program(stage="full"):
    nc = bass.Bass("TRN2", target_bir_lowering=False)
    D = {k: nc.dram_tensor(k, shp, F32, kind="ExternalInput").ap() for k, shp in SHARED_SHAPES.items()}
    out = nc.dram_tensor("out", [2048, 1024], F32, kind="ExternalOutput").ap()
    dbg = stage != "full"
    if dbg:
        octx = nc.dram_tensor("octx", [256, 1024], F32, kind="ExternalOutput").ap()
        dbgd = nc.dram_tensor("dbgd", [128, 8192], F32, kind="ExternalOutput").ap()
    na_blocks, na_classes = _na_blocks()

    with ExitStack() as st:
        S = Sched(nc, st)
        xs = [S.sbd(f"xs{i}", [128, 1024], F32) for i in range(NT)]
        cst = S.sbd("cst", [128, 640], F32)
        cc = S.sbd("cc", [128, 8, 2], F32)
        adab = S.sbd("adab", [128, 2, 48], F32)
        ngT = S.sbd("ngT", [128, 2, 2, 8], F32)
        cw = S.sbd("cw", [128, 2, 3, 44], F32)
        cb = S.sbd("cb", [128, 2, 44], F32)
        identb = S.sb("identb", [128, 128], BF16)
        wmb = S.sb("wmb", [128, 2, 128], BF16)
        ones_f = S.sb("ones_f", [128, 128], F32)
        ones_b = S.sb("ones_b", [128, 128], BF16)
        sc = S.sb("sc", [128, 8, 2], F32)
        modT = [S.sb(f"modT{l}", [128, 48, 2], F32) for l in range(2)]
        gbc = S.sb("gbc", [128, 2, 1024], F32)
        AB = S.sb("AB", [128, 8, 2], F32)

        psT = Ring([S.ps(f"psT{i}", [128, 8, 128], BF16) for i in range(2)])
        psA = Ring([S.ps(f"psA{i}", [128, 512], F32) for i in range(2)])
        psS = Ring([S.ps(f"psS{i}", [128, 512], F32) for i in range(2)])
        psO = Ring([S.ps(f"psO{i}", [128, 512], F32) for i in range(2)])

        IDF = lambda: cst[:, 0:128]
        TRIU = lambda: cst[:, 128:256]
        TRIL = lambda: cst[:, 256:384]

        S.dma("sp", cst.sem, cst[:], D["consts"], [], [cst])
        S.dma("sp", cc.sem, cc[:], D["cc"], [], [cc])
        S.dma("sp", adab.sem, adab[:], D["ada_bT"], [], [adab])
        S.dma("sp", ngT.sem, ngT[:], D["norm_gT"], [], [ngT])
        S.dma("sp", cw.sem, cw[:], D["conv_wT"], [], [cw])
        S.dma("sp", cb.sem, cb[:], D["conv_bT"], [], [cb])
        for i in range(NT):
            src = D["ctx"][i * 128:(i + 1) * 128, :] if i < 2 else D["x"][(i - 2) * 128:(i - 1) * 128, :]
            S.dma("sp", xs[i].sem, xs[i][:], src, [], [xs[i]])
        S.op("dve", [cst], [identb], lambda e: e.tensor_copy(out=identb[:], in_=cst[:, 0:128]))
        S.op("dve", [cst], [wmb], lambda e: e.tensor_copy(out=wmb[:], in_=cst[:, 384:640].rearrange("p (a b) -> p a b", a=2)))
        S.op("dve", [], [ones_f], lambda e: e.memset(ones_f[:], 1.0))
        S.op("dve", [], [ones_b], lambda e: e.memset(ones_b[:], 1.0))
        S.op("act", [cc], [sc], lambda e: e.activation(out=sc[:], in_=cc[:], func=AF.Silu))

        dstg = S.sb("dstg", [128, 128], F32) if dbg else None
        dstate = {"col": 0, "items": []}

        def dump(name, buf, ap, n):
            if not dbg:
                return
            stg = dstg
            sem = S.newsem()
            S.op("act", [buf], [stg], lambda e: e.activation(out=stg[:, 0:n], in_=ap, func=AF.Copy))
            c0 = dstate["col"]
            S.dma("sp", sem, dbgd[:, c0:c0 + n], stg[:, 0:n], [stg], [])
            S._need("sp", sem, S.cnt[sem])
            dstate["items"].append((name, c0, n))
            dstate["col"] = c0 + n
            print("DUMP", name, c0, n, flush=True)

        def wview(wb, shape_str, **kw):
            n = 1
            for v in kw.values():
                n *= v
            return wb

        def mod_phase(l):
            with ExitStack() as ph:
                ring = Ring([S.sbd(f"adaw{l}_{i}", [128, 8, 512], BF16, ph) for i in range(3)])
                schi = S.sb(f"schi{l}", [128, 8, 2], BF16, ph)
                schf = S.sb(f"schf{l}", [128, 8, 2], F32, ph)
                sclo = S.sb(f"sclo{l}", [128, 8, 2], BF16, ph)
                S.op("dve", [sc], [schi], lambda e: e.tensor_copy(out=schi[:], in_=sc[:]))
                S.op("dve", [schi], [schf], lambda e: e.tensor_copy(out=schf[:], in_=schi[:]))
                S.op("dve", [sc, schf], [schf], lambda e: e.tensor_tensor(out=schf[:], in0=sc[:], in1=schf[:], op=ALU.subtract))
                S.op("dve", [schf], [sclo], lambda e: e.tensor_copy(out=sclo[:], in_=schf[:]))
                wbs = {}

                def ld(cg):
                    if cg >= 12:
                        return
                    wb = ring.get()
                    S.dma("pool", wb.sem, wb[:], D["ada_w"][l, :, cg * 512:(cg + 1) * 512].rearrange("(k p) n -> p k n", p=128), [], [wb])
                    wbs[cg] = wb
                ld(0)
                ld(1)
                for cg in range(12):
                    ld(cg + 2)
                    wb = wbs[cg]
                    ps = psA.get()
                    for c4 in range(4):
                        for k in range(8):
                            S.op("pe", [wb, schi], [ps], lambda e: e.matmul(ps[:, c4 * 2:c4 * 2 + 2], lhsT=wb[:, k, c4 * 128:(c4 + 1) * 128], rhs=schi[:, k, :], start=(k == 0), stop=False))
                            S.op("pe", [wb, sclo], [ps], lambda e: e.matmul(ps[:, c4 * 2:c4 * 2 + 2], lhsT=wb[:, k, c4 * 128:(c4 + 1) * 128], rhs=sclo[:, k, :], start=False, stop=(k == 7)))
                    S.op("dve", [ps, adab], [modT[l]], lambda e: e.tensor_tensor(
                        out=modT[l][:, cg * 4:(cg + 1) * 4, :], in0=ps[:, 0:8].rearrange("p (c j) -> p c j", j=2),
                        in1=adab[:, l, cg * 4:(cg + 1) * 4].unsqueeze(2).to_broadcast([128, 4, 2]), op=ALU.add))
                S.barrier()

        def mk_AB(l, which):
            scl = 8 if which == 0 else 32
            S.op("dve", [modT[l]], [AB], lambda e: e.tensor_scalar(out=AB[:], in0=modT[l][:, scl:scl + 8, :], scalar1=1.0, scalar2=None, op0=ALU.add))
            S.op("dve", [AB, ngT], [AB], lambda e: e.tensor_tensor(out=AB[:], in0=AB[:], in1=ngT[:, l, which, :].unsqueeze(2).to_broadcast([128, 8, 2]), op=ALU.mult))

        def mk_gate(l, gchunk, ph):
            hl = S.sb(f"ghl{l}_{gchunk}", [128, 8, 2], F32, ph)
            hb = S.sb(f"ghb{l}_{gchunk}", [128, 8, 2], BF16, ph)
            hf = S.sb(f"ghf{l}_{gchunk}", [128, 8, 2], F32, ph)
            lo = S.sb(f"glo{l}_{gchunk}", [128, 8, 2], F32, ph)
            lb = S.sb(f"glb{l}_{gchunk}", [128, 8, 2], BF16, ph)
            lf = S.sb(f"glf{l}_{gchunk}", [128, 8, 2], F32, ph)
            S.op("dve", [modT[l]], [hl], lambda e: e.tensor_copy(out=hl[:], in_=modT[l][:, gchunk:gchunk + 8, :]))
            S.op("dve", [hl], [hb], lambda e: e.tensor_copy(out=hb[:], in_=hl[:]))
            S.op("dve", [hb], [hf], lambda e: e.tensor_copy(out=hf[:], in_=hb[:]))
            S.op("dve", [hl, hf], [lo], lambda e: e.tensor_tensor(out=lo[:], in0=hl[:], in1=hf[:], op=ALU.subtract))
            S.op("dve", [lo], [lb], lambda e: e.tensor_copy(out=lb[:], in_=lo[:]))
            S.op("dve", [lb], [lf], lambda e: e.tensor_copy(out=lf[:], in_=lb[:]))
            dgr = Ring([S.sb(f"dg{l}_{gchunk}_{i}", [128, 2, 128], BF16, ph) for i in range(2)])
            for j in range(2):
                for half in range(2):
                    ps = psA.get()
                    for k4 in range(4):
                        kk = half * 4 + k4
                        dg = dgr.get()
                        S.op("dve", [identb, hf], [dg], lambda e: e.tensor_scalar(out=dg[:, 0, :], in0=identb[:], scalar1=hf[:, kk, j:j + 1], scalar2=None, op0=ALU.mult))
                        S.op("dve", [identb, lf], [dg], lambda e: e.tensor_scalar(out=dg[:, 1, :], in0=identb[:], scalar1=lf[:, kk, j:j + 1], scalar2=None, op0=ALU.mult))
                        S.op("pe", [ones_b, dg], [ps], lambda e: e.matmul(ps[:, k4 * 128:(k4 + 1) * 128], lhsT=ones_b[:], rhs=dg[:, 0, :], start=True, stop=False))
                        S.op("pe", [ones_b, dg], [ps], lambda e: e.matmul(ps[:, k4 * 128:(k4 + 1) * 128], lhsT=ones_b[:], rhs=dg[:, 1, :], start=False, stop=True))
                    S.op("act", [ps], [gbc], lambda e: e.activation(out=gbc[:, j, half * 512:(half + 1) * 512], in_=ps[:], func=AF.Copy))

        def rstd_of(t, n_ap, dim):
            S.op("dve", [t], [t], lambda e: e.tensor_scalar(out=n_ap(), in0=n_ap(), scalar1=1.0 / dim, scalar2=EPS, op0=ALU.mult, op1=ALU.add))
            S.op("act", [t], [t], lambda e: e.activation(out=n_ap(), in_=n_ap(), func=AF.Ln))
            S.op("act", [t], [t], lambda e: e.activation(out=n_ap(), in_=n_ap(), func=AF.Exp, scale=-0.5))

        def norm_phase(l, which, hTg, ph, tiles=range(NT)):
            mk_AB(l, which)
            sh = 0 if which == 0 else 24
            ss = S.sb(f"nss{l}{which}", [128, NT], F32, ph)
            junk = S.sb(f"njunk{l}{which}", [128, 1024], BF16, ph)
            xnr = Ring([S.sb(f"xn{l}{which}_{i}", [128, 1024], BF16, ph) for i in range(2)])
            S.op("dve", [], [ss], lambda e: e.memset(ss[:], 1.0))
            for i in tiles:
                S.op("act", [xs[i]], [junk, ss], lambda e: e.activation(out=junk[:], in_=xs[i][:], func=AF.Square, accum_out=ss[:, i:i + 1]))
            rstd_of(ss, lambda: ss[:], 1024)
            import os
            if os.environ.get("KSUB") in ("a", "c"):
                return
            tl_ = list(tiles)

            def stage_xn(i):
                xn = xnr.get()
                S.op("dve", [xs[i], ss], [xn], lambda e: e.tensor_scalar(out=xn[:], in0=xs[i][:], scalar1=ss[:, i:i + 1], scalar2=None, op0=ALU.mult))
                pt = psT.get()
                for k in range(8):
                    S.op("pe", [xn, identb], [pt], lambda e: e.transpose(out=pt[:, k, :], in_=xn[:, k * 128:(k + 1) * 128], identity=identb[:]))
                return pt

            def stage_evac(i, pt):
                g, off = tok_group(i)
                j = 1 if i < 2 else 0
                for k in range(8):
                    if k % 2 == 0:
                        S.op("dve", [pt, AB, modT[l]], [hTg[g]], lambda e: e.tensor_scalar(
                            out=hTg[g][:, k, off:off + 128], in0=pt[:, k, :], scalar1=AB[:, k, j:j + 1], scalar2=modT[l][:, sh + k, j:j + 1], op0=ALU.mult, op1=ALU.add))
                    else:
                        S.op("act", [pt, AB, modT[l]], [hTg[g]], lambda e: e.activation(
                            out=hTg[g][:, k, off:off + 128], in_=pt[:, k, :], func=AF.Identity, scale=AB[:, k, j:j + 1], bias=modT[l][:, sh + k, j:j + 1]))

            ptn = stage_xn(tl_[0])
            for n_, i in enumerate(tl_):
                ptc = ptn
                if n_ + 1 < len(tl_):
                    ptn = stage_xn(tl_[n_ + 1])
                stage_evac(i, ptc)

        def wload(wb, n, src):
            dst = wb[:, 0:8 * n].rearrange("p (k n) -> p k n", k=8)
            S.dma("pool", wb.sem, dst, src, [], [wb])
            return dst

        def qk_prep(ps, ps_ap, nh, g_ap, rope_tile, out_ap, wk, rope):
            sq, ssq, qn, t1 = wk
            n = nh * 64
            v3 = lambda ap: ap.rearrange("p (h d) -> p h d", d=64)
            S.op("act", [ps], [sq], lambda e: e.activation(out=sq[:, 0:n], in_=ps_ap, func=AF.Square))
            S.op("dve", [sq], [ssq], lambda e: e.tensor_reduce(out=ssq[:, 0:nh], in_=v3(sq[:, 0:n]), axis=AX.X, op=ALU.add))
            rstd_of(ssq, lambda: ssq[:, 0:nh], 64)
            S.op("dve", [ps, ssq], [qn], lambda e: e.tensor_tensor(out=v3(qn[:, 0:n]), in0=v3(ps_ap), in1=ssq[:, 0:nh].unsqueeze(2).to_broadcast([128, nh, 64]), op=ALU.mult))
            if rope_tile is None:
                S.op("dve", [qn], [out_ap[0]], lambda e: e.tensor_tensor(out=out_ap[1], in0=v3(qn[:, 0:n]), in1=g_ap.unsqueeze(1).to_broadcast([128, nh, 64]), op=ALU.mult))
                return
            S.op("dve", [qn], [qn], lambda e: e.tensor_tensor(out=v3(qn[:, 0:n]), in0=v3(qn[:, 0:n]), in1=g_ap.unsqueeze(1).to_broadcast([128, nh, 64]), op=ALU.mult))
            cos_ap = rope[:, 0, :]
            sin_ap = rope[:, 1, :]
            S.op("dve", [qn, rope], [t1], lambda e: e.tensor_tensor(out=v3(t1[:, 0:n]), in0=v3(qn[:, 0:n]), in1=cos_ap.unsqueeze(1).to_broadcast([128, nh, 64]), op=ALU.mult))
            v5 = lambda ap: ap.rearrange("p (h x y d) -> p h x y d", x=2, y=2, d=16)
            s4 = sin_ap.rearrange("p (x y d) -> p x y d", x=2, y=2)
            for y in range(2):
                S.op("dve", [qn, rope], [sq], lambda e: e.tensor_tensor(
                    out=v5(sq[:, 0:n])[:, :, :, y, :], in0=v5(qn[:, 0:n])[:, :, :, 1 - y, :],
                    in1=s4[:, :, y, :].unsqueeze(1).to_broadcast([128, nh, 2, 16]), op=ALU.mult))
            S.op("dve", [t1, sq], [out_ap[0]], lambda e: e.tensor_tensor(out=out_ap[1], in0=v3(t1[:, 0:n]), in1=v3(sq[:, 0:n]), op=ALU.add))

        def prep_batch(raw, sq, ss, T, nh, g_ap, out_buf, out_ap, inplace=False):
            n = T * nh
            r3 = raw[:, 0:T, :].rearrange("p t (h d) -> p (t h) d", d=64)
            s3 = sq[:, 0:T, :].rearrange("p t (h d) -> p (t h) d", d=64)
            S.op("act", [raw], [sq], lambda e: e.activation(out=sq[:, 0:T, :], in_=raw[:, 0:T, :], func=AF.Square))
            S.op("dve", [sq], [ss], lambda e: e.tensor_reduce(out=ss[:, 0:n], in_=s3, axis=AX.X, op=ALU.add))
            rstd_of(ss, lambda: ss[:, 0:n], 64)
            S.op("dve", [raw, ss], [raw], lambda e: e.tensor_tensor(out=r3, in0=r3, in1=ss[:, 0:n].unsqueeze(2).to_broadcast([128, n, 64]), op=ALU.mult))
            if inplace:
                S.op("dve", [raw], [raw], lambda e: e.tensor_tensor(out=r3, in0=r3, in1=g_ap.unsqueeze(1).to_broadcast([128, n, 64]), op=ALU.mult))
                return
            S.op("dve", [raw], [out_buf], lambda e: e.tensor_tensor(out=out_ap.rearrange("p t (h d) -> p (t h) d", d=64), in0=r3, in1=g_ap.unsqueeze(1).to_broadcast([128, n, 64]), op=ALU.mult))

        def residual(i, ps, cgi, j):
            tmp = restmp.get()
            S.op("dve", [ps, gbc], [tmp], lambda e: e.tensor_tensor(out=tmp[:], in0=ps[:], in1=gbc[:, j, cgi * 512:(cgi + 1) * 512], op=ALU.mult))
            rstate["n"] += 1
            S.op("dve", [tmp, xs[i]], [xs[i]], lambda e: e.tensor_tensor(out=xs[i][:, cgi * 512:(cgi + 1) * 512], in0=xs[i][:, cgi * 512:(cgi + 1) * 512], in1=tmp[:], op=ALU.add))

        restmp = Ring([S.sb(f"restmp{i}", [128, 512], F32) for i in range(1)])
        rstate = {"n": 0}

        def mixer0():
            l = 0
            with ExitStack() as ph:
                hTg = [S.sb("hT0_0", [128, 8, 256], BF16, ph)] + [S.sb(f"hT0_{g}", [128, 8, 512], BF16, ph) for g in range(1, 5)]
                with ExitStack() as ph2:
                    norm_phase(0, 0, hTg, ph2)
                    import os
                    if os.environ.get("KSUB") not in ("a", "b"):
                        mk_gate(0, 16, ph2)
                    S.barrier()
                if stage == "norm":
                    return
                mixTa = S.sb("mixTa", [128, 4, NT * 128], BF16, ph)
                with ExitStack() as ph2:
                    wring = Ring([S.sbd(f"w0_{i}", [128, 8 * 384], BF16, ph2) for i in range(2)])
                    gateb = S.sbd("gateb", [128, 16], F32, ph2)
                    headg = S.sbd("headg", [128, 512], F32, ph2)
                    S.dma("sp", gateb.sem, gateb[:], D["gate_b_bc"], [], [gateb])
                    S.dma("sp", headg.sem, headg[:], D["head_g_bc"], [], [headg])
                    mlstm(hTg, mixTa, gateb, headg, wring, ph2)
                    S.barrier()
                if stage == "mlstm":
                    return
                with ExitStack() as ph2:
                    gqa_attn(0, hTg, mixTa, None, ph2)
                    S.barrier()

        def mlstm_gates(hTg, gateb, wring, pg, es, eb, edec, ekw):
            G = S.sb("G", [128, NT, 16], F32, pg)
            wgb = S.sbd("wgates", [128, 8 * 16], BF16, pg)
            wg = wload(wgb, 16, D["even_w"][:, 2048:2064].rearrange("(k p) n -> p k n", p=128))
            for i in range(NT):
                g, off = tok_group(i)
                ps = psO.get()
                for k in range(8):
                    S.op("pe", [hTg[g], wgb], [ps], lambda e: e.matmul(ps[:, 0:16], lhsT=hTg[g][:, k, off:off + 128], rhs=wg[:, k, :], start=(k == 0), stop=(k == 7)))
                S.op("dve", [ps, gateb], [G], lambda e: e.tensor_tensor(out=G[:, i, :], in0=ps[:, 0:16], in1=gateb[:], op=ALU.add))
            E = S.sb("E", [128, 2, NT, 4], F32, pg)
            for d in range(2):
                S.op("act", [G], [E], lambda e: e.activation(out=E[:, d], in_=G[:, :, 4 + 8 * d:8 + 8 * d], func=AF.Exp, scale=-1.0))
            S.op("dve", [E], [E], lambda e: e.tensor_scalar(out=E[:], in0=E[:], scalar1=1.0, scalar2=None, op0=ALU.add))
            S.op("act", [E], [E], lambda e: e.activation(out=E[:], in_=E[:], func=AF.Ln))
            tg = S.sb("tg", [128, NT, 4], F32, pg)
            f72 = lambda ap: ap.rearrange("p t h -> p (t h)")
            trib = S.sb("trib", [128, 2, 128], BF16, pg)
            S.op("dve", [cst], [trib], lambda e: e.tensor_copy(out=trib[:], in_=cst[:, 128:384].rearrange("p (a b) -> p a b", a=2)))
            Ehi = S.sb("Ehi", [128, 2, NT, 4], BF16, pg)
            Ehf = S.sb("Ehf", [128, 2, NT, 4], F32, pg)
            Elo = S.sb("Elo", [128, 2, NT, 4], BF16, pg)
            S.op("dve", [E], [Ehi], lambda e: e.tensor_copy(out=Ehi[:], in_=E[:]))
            S.op("dve", [Ehi], [Ehf], lambda e: e.tensor_copy(out=Ehf[:], in_=Ehi[:]))
            S.op("dve", [E, Ehf], [Ehf], lambda e: e.tensor_tensor(out=Ehf[:], in0=E[:], in1=Ehf[:], op=ALU.subtract))
            S.op("dve", [Ehf], [Elo], lambda e: e.tensor_copy(out=Elo[:], in_=Ehf[:]))
            for d in range(2):
                psb = psO.get()
                S.op("pe", [trib, Ehi], [psb], lambda e: e.matmul(psb[:, 0:72], lhsT=trib[:, d, :], rhs=f72(Ehi[:, d]), start=True, stop=False))
                S.op("pe", [trib, Elo], [psb], lambda e: e.matmul(psb[:, 0:72], lhsT=trib[:, d, :], rhs=f72(Elo[:, d]), start=False, stop=True))
                S.op("pe", [ones_b, Ehi], [psb], lambda e: e.matmul(psb[:, 72:144], lhsT=ones_b[:], rhs=f72(Ehi[:, d]), start=True, stop=False))
                S.op("pe", [ones_b, Elo], [psb], lambda e: e.matmul(psb[:, 72:144], lhsT=ones_b[:], rhs=f72(Elo[:, d]), start=False, stop=True))
                S.op("dve", [psb, G], [tg], lambda e: e.tensor_tensor(out=tg[:], in0=psb[:, 0:72].rearrange("p (t h) -> p t h", h=4), in1=G[:, :, 8 * d:8 * d + 4], op=ALU.add))
                S.op("act", [tg], [es], lambda e: e.activation(out=es[:, d], in_=tg[:], func=AF.Exp))
                S.op("act", [psb], [eb], lambda e: e.activation(out=f72(eb[:, d]), in_=psb[:, 0:72], func=AF.Exp, scale=-1.0))
                S.op("act", [psb], [edec], lambda e: e.activation(out=f72(edec[:, d]), in_=psb[:, 72:144], func=AF.Exp, scale=-1.0))
                S.op("dve", [es, edec], [ekw], lambda e: e.tensor_tensor(out=ekw[:, d], in0=es[:, d], in1=edec[:, d], op=ALU.mult))

            pass
            pass
            pass
            pass
            pass

        def mlstm(hTg, mixTa, gateb, headg, wring, ph):
            es = S.sb("es", [128, 2, NT, 4], F32, ph)
            eb = S.sb("eb", [128, 2, NT, 4], F32, ph)
            edec = S.sb("edec", [128, 2, NT, 4], F32, ph)
            ekw = S.sb("ekw", [128, 2, NT, 4], F32, ph)
            with ExitStack() as pg:
                mlstm_gates(hTg, gateb, wring, pg, es, eb, edec, ekw)
                S.barrier()
            KS_ = ""
            KH_ = -1
            qT = S.sb("qTa", [128, NT * 128], BF16, ph)
            kT = S.sb("kTa", [128, NT * 128], BF16, ph)
            ktok = S.sb("ktok", [128, NT, 128], BF16, ph)
            vaug = S.sb("vaug", [128, NT, 130], BF16, ph)
            hraw = [S.sb(f"hraw{d}", [128, NT, 130], F32, ph) for d in range(2)]
            rnm = S.sb("rnm", [128, 2, NT], F32, ph)
            Cst = [S.sb(f"Cst{d}", [128, 129], F32, ph) for d in range(2)]
            Cbf3 = [[S.sb(f"Cbf{d}_{r}", [128, 130], BF16, ph) for r in range(3)] for d in range(2)]
            PTr = Ring([S.sb(f"PTm{i}", [128, 128], BF16, ph) for i in range(4)])
            kwr = Ring([S.sb(f"kwm{i}", [128, 128], BF16, ph) for i in range(2)])
            hss = S.sb("hss", [128, NT], F32, ph)
            hjunk = S.sb("hjunk", [128, 128], BF16, ph)
            ogr = Ring([S.sb(f"og{i}", [128, 128], F32, ph) for i in range(2)])
            t1r = Ring([S.sb(f"mt1{i}", [128, 128], F32, ph) for i in range(2)])
            mxr = Ring([S.sb(f"mmx{i}", [128, 128], BF16, ph) for i in range(2)])
            S.op("dve", [], [vaug], lambda e: e.memset(vaug[:, :, 128:129], 1.0))
            orders = [list(range(NT)), [1, 0] + list(range(NT - 1, 1, -1))]
            KS = 128.0 ** -0.5

            def load_qkv(hd):
                wb_ = wring.get()
                src = D["even_w"][:, 0:1536].rearrange("(k p) (g h n) -> p k g h n", p=128, g=3, h=4)[:, :, :, hd, :]
                wq_ = wb_[:, 0:8 * 384].rearrange("p (k g n) -> p k g n", k=8, g=3)
                for g3 in range(3):
                    S.dma("pool", wb_.sem, wq_[:, :, g3, :], src[:, :, g3, :], [], [wb_])
                return wb_, wq_

            woring = Ring([S.sbd(f"wo_{i}", [128, 8 * 128], BF16, ph) for i in range(2)])
            nxt_w = load_qkv(0)
            for h in range(4):
                wb, wq = nxt_w
                wob = woring.get()
                wo = wload(wob, 128, D["even_w"][:, 1536 + h * 128:1536 + (h + 1) * 128].rearrange("(k p) n -> p k n", p=128))
                if h + 1 < 4:
                    nxt_w = load_qkv(h + 1)
                flip = 0
                for (g, c0, n) in GROUPS:
                    for which, dst, scl in ((0, qT, 1.0), (1, kT, KS)):
                        ps = psA.get()
                        for k in range(8):
                            S.op("pe", [wb, hTg[g]], [ps], lambda e: e.matmul(ps[:, 0:n], lhsT=wq[:, k, which, :], rhs=hTg[g][:, k, 0:n], start=(k == 0), stop=(k == 7)))
                        if flip % 2 == 0:
                            S.op("act", [ps], [dst], lambda e: e.activation(out=dst[:, c0:c0 + n], in_=ps[:, 0:n], func=AF.Copy, scale=scl))
                        else:
                            S.op("dve", [ps], [dst], lambda e: e.tensor_scalar(out=dst[:, c0:c0 + n], in0=ps[:, 0:n], scalar1=scl, scalar2=None, op0=ALU.mult))
                        flip += 1
                if KS_ == "m2a" and h == KH_:
                    return
                for i in range(NT):
                    g, off = tok_group(i)
                    ps = psA.get()
                    for k in range(8):
                        S.op("pe", [wb, hTg[g]], [ps], lambda e: e.matmul(ps[:, 0:256], lhsT=hTg[g][:, k, off:off + 128], rhs=wb[:, k * 384 + 128:k * 384 + 384], start=(k == 0), stop=(k == 7)))
                    S.op("act", [ps], [ktok], lambda e: e.activation(out=ktok[:, i, :], in_=ps[:, 0:128], func=AF.Copy, scale=KS))
                    S.op("dve", [ps], [vaug], lambda e: e.tensor_copy(out=vaug[:, i, 0:128], in_=ps[:, 128:256]))
                if KS_ == "m2b" and h == KH_:
                    return
                if h == 0:
                    pass
                    pass
                    pass
                    pass
                if KS_ == "m2" and h == KH_:
                    return
                written = [False] * NT
                PTs = {}

                def emitA2(step):
                    ii = [orders[d][step] for d in range(2)]
                    col = lambda a, d: a[:, d, ii[d], h:h + 1]
                    css = [slice(i * 128, (i + 1) * 128) for i in ii]
                    pss2, kws, pscs = [], [], []
                    for d in range(2):
                        pss = psS.get()
                        S.op("pe", [kT, qT], [pss], lambda e: e.matmul(pss[:, 0:128], lhsT=kT[:, css[d]], rhs=qT[:, css[d]], start=True, stop=True))
                        pss2.append(pss)
                    if step < NT - 1:
                        for d in range(2):
                            kw = kwr.get()
                            S.op("act", [ktok, ekw], [kw], lambda e: e.activation(out=kw[:], in_=ktok[:, ii[d], :], func=AF.Copy, scale=col(ekw, d)))
                            kws.append(kw)
                        for d in range(2):
                            psc = psA.get()
                            S.op("pe", [kws[d], vaug], [psc], lambda e: e.matmul(psc[:, 0:129], lhsT=kws[d][:], rhs=vaug[:, ii[d], 0:129], start=True, stop=True))
                            pscs.append(psc)
                    for d in range(2):
                        PT = PTr.get()
                        msk = TRIU() if d == 0 else TRIL()
                        S.op("dve", [pss2[d], es, cst], [PT], lambda e: e.scalar_tensor_tensor(out=PT[:], in0=pss2[d][:, 0:128], scalar=col(es, d), in1=msk, op0=ALU.mult, op1=ALU.mult))
                        PTs[(step, d)] = PT
                    if step < NT - 1:
                        for d in range(2):
                            psc = pscs[d]
                            if step == 0:
                                S.op("dve", [psc], [Cst[d]], lambda e: e.tensor_copy(out=Cst[d][:], in_=psc[:, 0:129]))
                            else:
                                S.op("dve", [psc, Cst[d], edec], [Cst[d]], lambda e: e.scalar_tensor_tensor(out=Cst[d][:], in0=Cst[d][:], scalar=col(edec, d), in1=psc[:, 0:129], op0=ALU.mult, op1=ALU.add))
                            cb3 = Cbf3[d][(step + 1) % 3]
                            S.op("dve", [Cst[d]], [cb3], lambda e: e.tensor_copy(out=cb3[:, 0:129], in_=Cst[d][:]))

                def emitB(step, d):
                    i = orders[d][step]
                    col = lambda a: a[:, d, i, h:h + 1]
                    cs = slice(i * 128, (i + 1) * 128)
                    PT = PTs.pop((step, d))
                    acc = psO.get()
                    if step > 0:
                        cb3 = Cbf3[d][step % 3]
                        S.op("pe", [qT, cb3], [acc], lambda e: e.matmul(acc[:, 0:129], lhsT=qT[:, cs], rhs=cb3[:, 0:129], start=True, stop=False))
                    S.op("pe", [PT, vaug], [acc], lambda e: e.matmul(acc[:, 0:129], lhsT=PT[:], rhs=vaug[:, i, 0:129], start=(step == 0), stop=True))
                    S.op("act", [acc, eb], [hraw[d]], lambda e: e.activation(out=hraw[d][:, i, 0:129], in_=acc[:, 0:129], func=AF.Copy, scale=col(eb)))

                emitA2(0)
                for step in range(NT):
                    if step + 1 < NT:
                        emitA2(step + 1)
                    emitB(step, 0)
                    emitB(step, 1)
                if h == 0:
                    pass
                    pass
                if KS_ == "m3" and h == KH_:
                    return
                for d in range(2):
                    S.op("act", [hraw[d]], [rnm], lambda e: e.activation(out=rnm[:, d, :], in_=hraw[d][:, :, 128], func=AF.Abs))
                S.op("dve", [rnm], [rnm], lambda e: e.tensor_scalar(out=rnm[:], in0=rnm[:], scalar1=1.0, scalar2=None, op0=ALU.max))
                S.op("dve", [rnm], [rnm], lambda e: e.reciprocal(out=rnm[:], in_=rnm[:]))
                for d in range(2):
                    S.op("dve", [hraw[d], rnm], [hraw[d]], lambda e: e.tensor_tensor(out=hraw[d][:, :, 0:128], in0=hraw[d][:, :, 0:128], in1=rnm[:, d, :].unsqueeze(2).to_broadcast([128, NT, 128]), op=ALU.mult))
                S.op("dve", [hraw[0], hraw[1]], [hraw[0]], lambda e: e.tensor_tensor(out=hraw[0][:, :, 0:128], in0=hraw[0][:, :, 0:128], in1=hraw[1][:, :, 0:128], op=ALU.add))
                S.op("dve", [], [hss], lambda e: e.memset(hss[:], 1.0))
                for i in range(NT):
                    S.op("act", [hraw[0]], [hjunk, hss], lambda e: e.activation(out=hjunk[:], in_=hraw[0][:, i, 0:128], func=AF.Square, accum_out=hss[:, i:i + 1]))
                rstd_of(hss, lambda: hss[:], 128)

                def out_stage1(i):
                    g, off = tok_group(i)
                    ps = psA.get()
                    for k in range(8):
                        S.op("pe", [wob, hTg[g]], [ps], lambda e: e.matmul(ps[:, 0:128], lhsT=hTg[g][:, k, off:off + 128], rhs=wo[:, k, :], start=(k == 0), stop=(k == 7)))
                    og = ogr.get()
                    S.op("act", [ps], [og], lambda e: e.activation(out=og[:], in_=ps[:, 0:128], func=AF.Sigmoid))
                    return og

                def out_stage2(i, og):
                    t1 = t1r.get()
                    S.op("dve", [hraw[0], hss, headg], [t1], lambda e: e.scalar_tensor_tensor(out=t1[:], in0=hraw[0][:, i, 0:128], scalar=hss[:, i:i + 1], in1=headg[:, h * 128:(h + 1) * 128], op0=ALU.mult, op1=ALU.mult))
                    mx = mxr.get()
                    S.op("dve", [t1, og], [mx], lambda e: e.tensor_tensor(out=mx[:], in0=t1[:], in1=og[:], op=ALU.mult))
                    pt = psT.get()
                    S.op("pe", [mx, identb], [pt], lambda e: e.transpose(out=pt[:, 0, :], in_=mx[:], identity=identb[:]))
                    S.op("act", [pt], [mixTa], lambda e: e.activation(out=mixTa[:, h, i * 128:(i + 1) * 128], in_=pt[:, 0, :], func=AF.Copy))

                ogn = out_stage1(0)
                for i in range(NT):
                    ogc = ogn
                    if i + 1 < NT:
                        ogn = out_stage1(i + 1)
                    out_stage2(i, ogc)
                if KS_ == "m4" and h == KH_:
                    return

        def attn_scores_exp_pv(kv_specs, nheads_per_kv, qT, q_sl, PTr, accs, first, last):
            pass

        def gqa_attn(l, hTg, other, wring, ph):
            wname = "even_w" if l == 0 else "odd_w"
            qc0, kc0 = (2064, 2576) if l == 0 else (0, 512)
            swag = S.sbd(f"swag{l}", [128, 128], F32, ph)
            roper = Ring([S.sbd(f"rope{l}_{i}", [128, 2, 64], F32, ph) for i in range(2)])

            def get_rope(jt):
                rb = roper.get()
                S.dma("sp", rb.sem, rb[:], D["rope"][:, :, jt, :], [], [rb])
                return rb
            S.dma("sp", swag.sem, swag[:], D["swa_g_bc" if l == 0 else "gqa_g_bc"], [], [swag])
            gq = S.sb(f"gq{l}", [128, 64], F32, ph)
            S.op("dve", [swag], [gq], lambda e: e.tensor_scalar(out=gq[:], in0=swag[:, 0:64], scalar1=0.125, scalar2=None, op0=ALU.mult))
            esink = S.sb(f"esink{l}", [128, 8], F32, ph)
            if l == 0:
                sinkb = S.sbd("sinkb", [128, 8], F32, ph)
                S.dma("sp", sinkb.sem, sinkb[:], D["sink_bc"], [], [sinkb])
                S.op("act", [sinkb], [esink], lambda e: e.activation(out=esink[:], in_=sinkb[:], func=AF.Exp))
            else:
                S.op("dve", [], [esink], lambda e: e.memset(esink[:], 0.0))
            wkb = S.sbd(f"wkv{l}", [128, 8 * 256], BF16, ph)
            wkv = wload(wkb, 256, D[wname][:, kc0:kc0 + 256].rearrange("(k p) n -> p k n", p=128))
            kTd = [S.sb(f"kTd{g}", [128, NT * 128], BF16, ph) for g in range(2)]
            vb = S.sb("vb", [128, NT, 2, 66], BF16, ph)
            S.op("dve", [], [vb], lambda e: e.memset(vb[:, :, :, 64:65], 1.0))
            with ExitStack() as pk:
                TB = 9
                kraw = S.sb("kraw", [128, TB, 128], F32, pk)
                ksq = S.sb("ksq", [128, TB, 128], F32, pk)
                kt2 = S.sb("kt2", [128, TB, 128], F32, pk)
                kss = S.sb("kss", [128, TB * 2], F32, pk)
                knb = S.sb("knb", [128, TB, 128], BF16, pk)
                kd = S.sb("kd", [128, 2, 2, 64], BF16, pk)
                rtab = S.sbd("rtab", [128, 2, TB, 64], F32, pk)
                for t0 in range(0, NT, TB):
                    tl = list(range(t0, t0 + TB))
                    r0 = max(0, 2 - t0)
                    nl = TB - r0
                    j0 = t0 + r0 - 2
                    for cs_ in range(2):
                        S.dma("sp", rtab.sem, rtab[:, cs_, 0:nl, :], D["rope"][:, cs_, j0:j0 + nl, :], [], [rtab])
                    for i in tl:
                        g, off = tok_group(i)
                        ps = psA.get()
                        for k in range(8):
                            S.op("pe", [wkb, hTg[g]], [ps], lambda e: e.matmul(ps[:, 0:256], lhsT=hTg[g][:, k, off:off + 128], rhs=wkv[:, k, :], start=(k == 0), stop=(k == 7)))
                        S.op("act", [ps], [vb], lambda e: e.activation(out=vb[:, i, :, 0:64], in_=ps[:, 128:256].rearrange("p (g d) -> p g d", g=2), func=AF.Copy))
                        S.op("dve", [ps], [kraw], lambda e: e.tensor_copy(out=kraw[:, i - t0, :], in_=ps[:, 0:128]))
                    prep_batch(kraw, ksq, kss, TB, 2, swag[:, 64:128], None, None, inplace=True)
                    if r0 > 0:
                        S.op("act", [kraw], [knb], lambda e: e.activation(out=knb[:, 0:r0, :], in_=kraw[:, 0:r0, :], func=AF.Copy))
                    v4 = lambda ap: ap.rearrange("p t (h d) -> p t h d", d=64)
                    cosb = rtab[:, 0, 0:nl, :].unsqueeze(2).to_broadcast([128, nl, 2, 64])
                    S.op("dve", [kraw, rtab], [ksq], lambda e: e.tensor_tensor(out=v4(ksq[:, r0:TB, :]), in0=v4(kraw[:, r0:TB, :]), in1=cosb, op=ALU.mult))
                    v6 = lambda ap: ap.rearrange("p t (h x y d) -> p t h x y d", h=2, x=2, y=2)
                    s5 = rtab[:, 1, 0:nl, :].rearrange("p t (x y d) -> p t x y d", x=2, y=2)
                    for hh_ in range(2):
                        for y in range(2):
                            S.op("dve", [kraw, rtab], [kt2], lambda e: e.tensor_tensor(
                                out=v6(kt2[:, r0:TB, :])[:, :, hh_, :, y, :], in0=v6(kraw[:, r0:TB, :])[:, :, hh_, :, 1 - y, :],
                                in1=s5[:, :, :, y, :], op=ALU.mult))
                    S.op("dve", [ksq, kt2], [knb], lambda e: e.tensor_tensor(out=knb[:, r0:TB, :], in0=ksq[:, r0:TB, :], in1=kt2[:, r0:TB, :], op=ALU.add))
                    for i in tl:
                        S.op("dve", [knb], [kd], lambda e: e.tensor_copy(out=kd[:], in_=knb[:, i - t0, :].rearrange("p (g d) -> p g d", g=2).unsqueeze(2).to_broadcast([128, 2, 2, 64])))
                        pt = psT.get()
                        for g2 in range(2):
                            S.op("pe", [kd, identb], [pt], lambda e: e.transpose(out=pt[:, g2, :], in_=kd[:, g2].rearrange("p a d -> p (a d)"), identity=identb[:]))
                        S.op("act", [pt], [kTd[0]], lambda e: e.activation(out=kTd[0][:, i * 128:(i + 1) * 128], in_=pt[:, 0, :], func=AF.Copy))
                        S.op("act", [pt], [kTd[1]], lambda e: e.activation(out=kTd[1][:, i * 128:(i + 1) * 128], in_=pt[:, 1, :], func=AF.Copy))
                S.barrier()
            wqb = S.sbd(f"wqq{l}", [128, 8 * 512], BF16, ph)
            wq = wload(wqb, 512, D[wname][:, qc0:qc0 + 512].rearrange("(k p) n -> p k n", p=128))
            wout = S.sbd(f"wout{l}", [128, 8 * 1024], BF16, ph)
            woutv = wout[:, :].rearrange("p (k n) -> p k n", k=8)
            S.dma("pool", wout.sem, woutv, D["w_out"][l].rearrange("(k p) n -> p k n", p=128), [], [wout])
            wk = (S.sb("wk_sq", [128, 512], F32, ph), S.sb("wk_ss", [128, 8], F32, ph), S.sb("wk_qn", [128, 512], F32, ph), S.sb("wk_t1", [128, 512], F32, ph))
            import os
            KS_ = os.environ.get("KSUB", "")
            if KS_ == "w1" or (KS_ == "g1k" and l == 1):
                return
            qb = S.sb("qb", [128, 8, 64], BF16, ph)
            qz = S.sb("qz", [128, 2, 4, 128], BF16, ph)
            S.op("dve", [], [qz], lambda e: e.memset(qz[:], 0.0))
            wmb4 = S.sb("wmb4", [128, 2, 4, 128], BF16, ph)
            S.op("dve", [wmb], [wmb4], lambda e: e.tensor_copy(out=wmb4[:], in_=wmb[:, :, :].unsqueeze(2).to_broadcast([128, 2, 4, 128])))
            PTr = Ring([S.sb(f"PTw{i}", [128, 512], BF16, ph) for i in range(2)])
            den = S.sb("wden", [128, 8], F32, ph)
            mixb = S.sb("mixb", [128, 512], BF16, ph)
            mixTb = S.sb("mixTb", [128, 4, 128], BF16, ph)
            def emit_qprep(i):
                g, off = tok_group(i)
                lat = i >= 2
                j = i - 2
                ps = psA.get()
                for k in range(8):
                    S.op("pe", [wqb, hTg[g]], [ps], lambda e: e.matmul(ps[:, 0:512], lhsT=hTg[g][:, k, off:off + 128], rhs=wq[:, k, :], start=(k == 0), stop=(k == 7)))
                qk_prep(ps, ps[:, 0:512], 8, gq[:], j if lat else None, (qb, qb[:]), wk, get_rope(j) if lat else None)

            qtiles = list(range(NT) if l == 0 else range(2, NT))
            emit_qprep(qtiles[0])
            for qi, i in enumerate(qtiles):
                g, off = tok_group(i)
                lat = i >= 2
                j = i - 2
                pt = psT.get()
                for pr in range(4):
                    S.op("pe", [qb, identb], [pt], lambda e: e.transpose(out=pt[:, pr, :], in_=qb[:, 2 * pr:2 * pr + 2, :].rearrange("p a d -> p (a d)"), identity=identb[:]))
                S.op("act", [pt], [qz], lambda e: e.activation(out=qz[0:64, 0, :, :], in_=pt[0:64, 0:4, :], func=AF.Copy))
                S.op("dve", [pt], [qz], lambda e: e.tensor_copy(out=qz[64:128, 1, :, :], in_=pt[64:128, 0:4, :]))
                if qi + 1 < len(qtiles):
                    emit_qprep(qtiles[qi + 1])
                if KS_ == "w2a":
                    return
                if l == 1:
                    blocks = [(m, None) for m in range(NT)]
                elif lat:
                    blocks = [(0, None), (1, None)]
                    if j > 0:
                        blocks.append((i - 1, 0))
                    blocks.append((i, None))
                    if j < 15:
                        blocks.append((i + 1, 1))
                else:
                    blocks = [(0, None), (1, None)]
                for g2 in range(2):
                    acc = psO.get()

                    def emit_scores(m, msk):
                        pss = psS.get()
                        for half in range(2):
                            S.op("pe", [kTd[g2], qz], [pss], lambda e: e.matmul(
                                pss[:, half * 256:(half + 1) * 256], lhsT=kTd[g2][:, m * 128:(m + 1) * 128],
                                rhs=qz[:, half, 2 * g2:2 * g2 + 2, :].rearrange("p a q -> p (a q)"),
                                start=(half == 0), stop=(half == 1 and msk is None)))
                        if msk is not None:
                            S.op("pe", [identb, wmb4], [pss], lambda e: e.matmul(pss[:, 0:512], lhsT=identb[:], rhs=wmb4[:, msk, :, :].rearrange("p a q -> p (a q)"), start=False, stop=True))
                        return pss

                    nxt = emit_scores(*blocks[0])
                    for bi, (m, msk) in enumerate(blocks):
                        pss = nxt
                        if bi + 1 < len(blocks):
                            nxt = emit_scores(*blocks[bi + 1])
                        PT = PTr.get()
                        S.op("act", [pss], [PT], lambda e: e.activation(out=PT[:], in_=pss[:], func=AF.Exp))
                        for hh in range(4):
                            S.op("pe", [PT, vb], [acc], lambda e: e.matmul(acc[:, hh * 128:hh * 128 + 65], lhsT=PT[:, hh * 128:(hh + 1) * 128], rhs=vb[:, m, g2, 0:65], start=(bi == 0 and hh == 0), stop=(bi == len(blocks) - 1)))
                    if KS_ == "w2c":
                        return
                    a3 = acc[:, :].rearrange("p (h c) -> p h c", h=4)
                    S.op("dve", [acc, esink], [den], lambda e: e.tensor_tensor(out=den[:, g2 * 4:(g2 + 1) * 4].rearrange("p (b a) -> p b a", b=2), in0=a3[:, :, 64].rearrange("p (b a) -> p b a", b=2),
                                                                            in1=esink[:, g2 * 4:(g2 + 1) * 4].rearrange("p (a b) -> p b a", a=2), op=ALU.add))
                    S.op("dve", [den], [den], lambda e: e.reciprocal(out=den[:, g2 * 4:(g2 + 1) * 4], in_=den[:, g2 * 4:(g2 + 1) * 4]))
                    S.op("dve", [acc, den], [mixb], lambda e: e.tensor_tensor(
                        out=mixb[:, g2 * 256:(g2 + 1) * 256].rearrange("p (a b d) -> p b a d", a=2, b=2), in0=a3[:, :, 0:64].rearrange("p (b a) d -> p b a d", b=2),
                        in1=den[:, g2 * 4:(g2 + 1) * 4].rearrange("p (b a) -> p b a", b=2).unsqueeze(3).to_broadcast([128, 2, 2, 64]), op=ALU.mult))
                if KS_ == "w2d":
                    return
                pt2 = psT.get()
                for c in range(4):
                    S.op("pe", [mixb, identb], [pt2], lambda e: e.transpose(out=pt2[:, c, :], in_=mixb[:, c * 128:(c + 1) * 128], identity=identb[:]))
                S.op("act", [pt2], [mixTb], lambda e: e.activation(out=mixTb[:], in_=pt2[:, 0:4, :], func=AF.Copy))
                if (KS_ == "w2" and i == 2) or (KS_ == "g1q" and l == 1 and i == 3):
                    return
                for cgi in range(2):
                    pso = psA.get()
                    for k in range(8):
                        if l == 0:
                            lhs = other[:, k, i * 128:(i + 1) * 128] if k < 4 else mixTb[:, k - 4, :]
                        else:
                            lhs = mixTb[:, k, :] if k < 4 else other[:, k - 4, j * 128:(j + 1) * 128]
                        S.op("pe", [other, mixTb, wout], [pso], lambda e: e.matmul(pso[:, 0:512], lhsT=lhs, rhs=woutv[:, k, cgi * 512:(cgi + 1) * 512], start=(k == 0), stop=(k == 7)))
                    residual(i, pso, cgi, 0 if lat else 1)

        def ffn_phase(l, tiles):
            tiles = list(tiles)
            with ExitStack() as ph:
                hTg = [S.sb(f"hF{l}_0", [128, 8, 256], BF16, ph)] + [S.sb(f"hF{l}_{g}", [128, 8, 512], BF16, ph) for g in range(1, 5)]
                with ExitStack() as ph2:
                    norm_phase(l, 1, hTg, ph2, tiles)
                    mk_gate(l, 40, ph2)
                    S.barrier()
                segs = [gg for gg in GROUPS if (gg[0] > 0 or 0 in tiles)]
                lo = segs[0][1]
                ranges = ([(0, 256)] if lo == 0 else []) + [(256, 2304)]
                GS = 3
                ur = Ring([S.sb(f"fu{l}_{i}", [128, 2304], F32, ph) for i in range(2)])
                yr = Ring([S.sb(f"fy{l}_{i}", [128, 2304], F32, ph) for i in range(2)])
                actT = S.sb(f"actT{l}", [128, GS, 2304], BF16, ph)
                wur = Ring([S.sbd(f"wu{l}_{i}", [128, 8 * 256], BF16, ph) for i in range(3)])
                wdr = Ring([S.sbd(f"wd{l}_{i}", [128, GS * 1024], BF16, ph) for i in range(2)])
                has_ctx = (lo == 0)
                wdraw = Ring([S.sb(f"wdraw{l}_{i}", [128, GS * 1024], BF16, ph) for i in range(1)]) if has_ctx else None
                upsrc = D["ffn_up"][l].rearrange("(k p) (g c n) -> p k g c n", p=128, g=2, c=22)
                wu_loaded = {}
                wd_loaded = {}

                def load_wu(cp):
                    if cp >= 22 or cp in wu_loaded:
                        return
                    wub = wur.get()
                    wu = wub[:, :].rearrange("p (k g n) -> p k g n", k=8, g=2)
                    for g3 in range(2):
                        S.dma("pool", wub.sem, wu[:, :, g3, :], upsrc[:, :, g3, cp, :], [], [wub])
                    wu_loaded[cp] = (wub, wu)

                def load_wd(c0):
                    if c0 >= 22 or c0 in wd_loaded:
                        return
                    ncg = min(GS, 22 - c0)
                    wdb = wdr.get()
                    wd = wdb[:, 0:ncg * 1024].rearrange("p (c n) -> p c n", c=ncg)
                    S.dma("pool", wdb.sem, wd, D["ffn_down"][l, c0 * 128:(c0 + ncg) * 128, :].rearrange("(c p) n -> p c n", p=128), [], [wdb])
                    wd_loaded[c0] = (wdb, wd)

                def scale_wd(c0):
                    ncg = min(GS, 22 - c0)
                    wdb, wd = wd_loaded[c0]
                    raw = None
                    if has_ctx:
                        rb = wdraw.get()
                        raw = rb[:, 0:ncg * 1024].rearrange("p (c n) -> p c n", c=ncg)
                        S.op("pool", [wdb], [rb], lambda e: e.tensor_copy(out=raw, in_=wd))
                        wd_loaded[c0] = (wdb, wd, rb, raw)
                    S.op("pool", [wdb, gbc], [wdb], lambda e: e.tensor_tensor(out=wd, in0=wd, in1=gbc[:, 0, :].unsqueeze(1).to_broadcast([128, ncg, 1024]), op=ALU.mult))
                    if not has_ctx:
                        wd_loaded[c0] = (wdb, wd, None, None)

                load_wu(0)
                load_wu(1)
                load_wd(0)

                def emit_up1(cp):
                    load_wu(cp + 2)
                    wub, wu = wu_loaded[cp]
                    ys = []
                    for gv in range(2):
                        ch = gv * 22 + cp
                        u = ur.get()
                        y = yr.get()
                        w0 = cw[:, l, 0, ch:ch + 1]
                        w1 = cw[:, l, 1, ch:ch + 1]
                        w2 = cw[:, l, 2, ch:ch + 1]
                        for (g, t0, n) in segs:
                            ps = psA.get()
                            for k in range(8):
                                S.op("pe", [wub, hTg[g]], [ps], lambda e: e.matmul(ps[:, 0:n], lhsT=wu[:, k, gv, :], rhs=hTg[g][:, k, 0:n], start=(k == 0), stop=(k == 7)))
                            S.op("act", [ps], [u], lambda e: e.activation(out=u[:, t0:t0 + n], in_=ps[:, 0:n], func=AF.Copy))
                        S.op("act", [u, cw, cb], [y], lambda e: e.activation(out=y[:, lo:2304], in_=u[:, lo:2304], func=AF.Identity, scale=w1, bias=cb[:, l, ch:ch + 1]))
                        for (a, b_) in ranges:
                            S.op("dve", [u, cw, y], [y], lambda e: e.scalar_tensor_tensor(out=y[:, a + 1:b_], in0=u[:, a:b_ - 1], scalar=w0, in1=y[:, a + 1:b_], op0=ALU.mult, op1=ALU.add))
                            S.op("dve", [u, cw, y], [y], lambda e: e.scalar_tensor_tensor(out=y[:, a:b_ - 1], in0=u[:, a + 1:b_], scalar=w2, in1=y[:, a:b_ - 1], op0=ALU.mult, op1=ALU.add))
                        ys.append(y)
                    S.op("act", [ys[0]], [ys[0]], lambda e: e.activation(out=ys[0][:, lo:2304], in_=ys[0][:, lo:2304], func=AF.Silu))
                    return ys

                def emit_up2(ys, ci):
                    S.op("dve", [ys[0], ys[1]], [actT], lambda e: e.tensor_tensor(out=actT[:, ci, lo:2304], in0=ys[0][:, lo:2304], in1=ys[1][:, lo:2304], op=ALU.mult))

                def emit_down(c0):
                    ncg = min(GS, 22 - c0)
                    scale_wd(c0)
                    wdb, wd, rb, raw = wd_loaded[c0]
                    for i in tiles:
                        for cgi in range(2):
                            ps = psO.get()
                            if i >= 2:
                                for ci in range(ncg):
                                    S.op("pe", [actT, wdb], [ps], lambda e: e.matmul(ps[:, 0:512], lhsT=actT[:, ci, i * 128:(i + 1) * 128], rhs=wd[:, ci, cgi * 512:(cgi + 1) * 512], start=(ci == 0), stop=(ci == ncg - 1)))
                                S.op("dve", [ps, xs[i]], [xs[i]], lambda e: e.tensor_tensor(out=xs[i][:, cgi * 512:(cgi + 1) * 512], in0=ps[:], in1=xs[i][:, cgi * 512:(cgi + 1) * 512], op=ALU.add))
                            else:
                                for ci in range(ncg):
                                    S.op("pe", [actT, rb], [ps], lambda e: e.matmul(ps[:, 0:512], lhsT=actT[:, ci, i * 128:(i + 1) * 128], rhs=raw[:, ci, cgi * 512:(cgi + 1) * 512], start=(ci == 0), stop=(ci == ncg - 1)))
                                residual(i, ps, cgi, 1)

                pending = None
                for c0 in range(0, 22, GS):
                    ncg = min(GS, 22 - c0)
                    ys0 = emit_up1(c0)
                    if pending is not None:
                        emit_down(pending)
                    load_wd(c0 + GS)
                    emit_up2(ys0, 0)
                    for ci in range(1, ncg):
                        emit_up2(emit_up1(c0 + ci), ci)
                    pending = c0
                emit_down(pending)
                S.barrier()

        def na_attn(hTg, mixTd, wring, ph):
            nag = S.sbd("nag", [128, 128], F32, ph)
            S.dma("sp", nag.sem, nag[:], D["na_g_bc"], [], [nag])
            gq = S.sb("nagq", [128, 64], F32, ph)
            S.op("dve", [nag], [gq], lambda e: e.tensor_scalar(out=gq[:], in0=nag[:, 0:64], scalar1=0.125, scalar2=None, op0=ALU.mult))
            kTn = S.sb("kTn", [128, NT * 128], BF16, ph)
            vn = S.sb("vn", [128, NT, 2, 66], BF16, ph)
            qTn = S.sb("qTn", [128, 2048], BF16, ph)
            S.op("dve", [], [vn], lambda e: e.memset(vn[:, :, :, 64:65], 1.0))
            TB = 9
            raw = S.sb("nraw", [128, TB, 128], F32, ph)
            sq = S.sb("nsq", [128, TB, 128], F32, ph)
            ssb = S.sb("nss", [128, TB * 2], F32, ph)
            nrm = S.sb("nnrm", [128, TB, 128], BF16, ph)
            biasr = Ring([S.sbd(f"nbias{i}", [128, 25, 128], F32, ph) for i in range(1)])
            stmp = Ring([S.sb(f"nstmp{i}", [128, 5, 128], F32, ph) for i in range(2)])
            PTr = Ring([S.sb(f"PTn{i}", [128, 7, 128], BF16, ph) for i in range(3)])
            rdn = Ring([S.sb(f"nrd{i}", [128, 1], F32, ph) for i in range(3)])
            mixd = S.sb("mixd", [128, 16, 2, 64], BF16, ph)
            naS = Ring(psS.bufs + psA.bufs)
            wsrc = D["odd_w"][:, 768:2304].rearrange("(k p) (g h n) -> p k g h n", p=128, g=3, h=4)
            wl = {}

            def load_w(pr):
                if pr >= 4 or pr in wl:
                    return
                wb = wring.get()
                wq = wb[:, 0:8 * 384].rearrange("p (k g n) -> p k g n", k=8, g=3)
                for g3 in range(3):
                    S.dma("pool", wb.sem, wq[:, :, g3, :], wsrc[:, :, g3, pr, :], [], [wb])
                wl[pr] = (wb, wq)

            load_w(0)
            for pr in range(4):
                wb, wq = wl[pr]
                load_w(pr + 1)
                for t0 in range(0, NT, TB):
                    tl = list(range(t0, min(NT, t0 + TB)))
                    for i in tl:
                        g, off = tok_group(i)
                        ps = psA.get()
                        for k in range(8):
                            S.op("pe", [wb, hTg[g]], [ps], lambda e: e.matmul(ps[:, 0:256], lhsT=hTg[g][:, k, off:off + 128], rhs=wb[:, k * 384 + 128:k * 384 + 384], start=(k == 0), stop=(k == 7)))
                        S.op("act", [ps], [vn], lambda e: e.activation(out=vn[:, i, :, 0:64], in_=ps[:, 128:256].rearrange("p (g d) -> p g d", g=2), func=AF.Copy))
                        S.op("dve", [ps], [raw], lambda e: e.tensor_copy(out=raw[:, i - t0, :], in_=ps[:, 0:128]))
                    prep_batch(raw, sq, ssb, len(tl), 2, nag[:, 64:128], nrm, nrm[:, 0:len(tl), :])
                    for i in tl:
                        pt = psT.get()
                        S.op("pe", [nrm, identb], [pt], lambda e: e.transpose(out=pt[:, 0, :], in_=nrm[:, i - t0, :], identity=identb[:]))
                        S.op("act", [pt], [kTn], lambda e: e.activation(out=kTn[:, i * 128:(i + 1) * 128], in_=pt[:, 0, :], func=AF.Copy))
                for t0 in range(2, NT, 8):
                    tl = list(range(t0, t0 + 8))
                    for i in tl:
                        g, off = tok_group(i)
                        ps2 = psA.get()
                        for k in range(8):
                            S.op("pe", [wb, hTg[g]], [ps2], lambda e: e.matmul(ps2[:, 0:128], lhsT=hTg[g][:, k, off:off + 128], rhs=wq[:, k, 0, :], start=(k == 0), stop=(k == 7)))
                        S.op("dve", [ps2], [raw], lambda e: e.tensor_copy(out=raw[:, i - t0, :], in_=ps2[:, 0:128]))
                    prep_batch(raw, sq, ssb, 8, 2, gq[:], nrm, nrm[:, 0:8, :])
                    for i in tl:
                        j = i - 2
                        pt2 = psT.get()
                        S.op("pe", [nrm, identb], [pt2], lambda e: e.transpose(out=pt2[:, 0, :], in_=nrm[:, i - t0, :], identity=identb[:]))
                        S.op("dve", [pt2], [qTn], lambda e: e.tensor_copy(out=qTn[:, j * 128:(j + 1) * 128], in_=pt2[:, 0, :]))
                for hh in range(2):
                    head = 2 * pr + hh
                    bt = biasr.get()
                    S.dma("sp", bt.sem, bt[:], D["na_bias"][head], [], [bt])
                    prs = slice(hh * 64, (hh + 1) * 64)

                    def blocks_of(j):
                        ci = 0 if j == 0 else 1 if j == 1 else 3 if j == 14 else 4 if j == 15 else 2
                        mlist = list(range(j - 2, j + 3)) if ci == 2 else na_blocks[j]
                        return ci, len(mlist), [0, 1] + [m + 2 for m in mlist]

                    def emit_scores(j):
                        ci, nb, keyt = blocks_of(j)
                        pA = naS.get()
                        pB = naS.get()
                        for bi, kt in enumerate(keyt):
                            pp, off2 = (pA, bi) if bi < 4 else (pB, bi - 4)
                            S.op("pe", [kTn, qTn], [pp], lambda e: e.matmul(pp[:, off2 * 128:(off2 + 1) * 128], lhsT=kTn[prs, kt * 128:(kt + 1) * 128], rhs=qTn[prs, j * 128:(j + 1) * 128], start=True, stop=True))
                        return pA, pB

                    def emit_norm(acc_, j_):
                        rd = rdn.get()
                        S.op("dve", [acc_], [rd], lambda e: e.reciprocal(out=rd[:], in_=acc_[:, 64:65]))
                        S.op("act", [acc_, rd], [mixd], lambda e: e.activation(out=mixd[:, j_, hh, :], in_=acc_[:, 0:64], func=AF.Copy, scale=rd[:, 0:1]))

                    pend_norm = None
                    nxt = emit_scores(0)
                    for j in range(16):
                        ci, nb, keyt = blocks_of(j)
                        pA, pB = nxt
                        if j + 1 < 16:
                            nxt = emit_scores(j + 1)
                        stp = stmp.get()
                        S.op("dve", [pA, bt], [stp], lambda e: e.tensor_tensor(out=stp[:, 0:2, :], in0=pA[:, 256:512].rearrange("p (b q) -> p b q", b=2), in1=bt[:, ci * 5:ci * 5 + 2, :], op=ALU.add))
                        S.op("dve", [pB, bt], [stp], lambda e: e.tensor_tensor(out=stp[:, 2:nb, :], in0=pB[:, 0:(nb - 2) * 128].rearrange("p (b q) -> p b q", b=nb - 2), in1=bt[:, ci * 5 + 2:ci * 5 + nb, :], op=ALU.add))
                        PT = PTr.get()
                        S.op("act", [pA], [PT], lambda e: e.activation(out=PT[:, 0:2, :], in_=pA[:, 0:256].rearrange("p (b q) -> p b q", b=2), func=AF.Exp))
                        S.op("act", [stp], [PT], lambda e: e.activation(out=PT[:, 2:2 + nb, :], in_=stp[:, 0:nb, :], func=AF.Exp))
                        acc = psO.get()
                        for bi, kt in enumerate(keyt):
                            S.op("pe", [PT, vn], [acc], lambda e: e.matmul(acc[:, 0:65], lhsT=PT[:, bi, :], rhs=vn[:, kt, hh, 0:65], start=(bi == 0), stop=(bi == len(keyt) - 1)))
                        if pend_norm is not None:
                            emit_norm(*pend_norm)
                        pend_norm = (acc, j)
                    emit_norm(*pend_norm)
                    pend_norm = None
                for j in range(16):
                    pt = psT.get()
                    S.op("pe", [mixd, identb], [pt], lambda e: e.transpose(out=pt[:, 0, :], in_=mixd[:, j, :, :].rearrange("p a d -> p (a d)"), identity=identb[:]))
                    S.op("dve", [pt], [mixTd], lambda e: e.tensor_copy(out=mixTd[:, pr, j * 128:(j + 1) * 128], in_=pt[:, 0, :]))

        def mixer1():
            with ExitStack() as ph:
                hTg = [S.sb("hT1_0", [128, 8, 256], BF16, ph)] + [S.sb(f"hT1_{g}", [128, 8, 512], BF16, ph) for g in range(1, 5)]
                with ExitStack() as ph2:
                    norm_phase(1, 0, hTg, ph2)
                    mk_gate(1, 16, ph2)
                    S.barrier()
                mixTd = S.sb("mixTd", [128, 4, 2048], BF16, ph)
                with ExitStack() as ph2:
                    wring = Ring([S.sbd(f"w1_{i}", [128, 8 * 384], BF16, ph2) for i in range(2)])
                    na_attn(hTg, mixTd, wring, ph2)
                    S.barrier()
                import os
                if os.environ.get("KSUB") == "nogqa1":
                    return
                with ExitStack() as ph2:
                    gqa_attn(1, hTg, mixTd, None, ph2)
                    S.barrier()

        mod_phase(0)
        if stage != "mod":
            mixer0()
        if stage not in ("l0mix", "mod", "norm", "mlstm"):
            ffn_phase(0, range(NT))
        if stage not in ("l0mix", "l0", "mod", "norm", "mlstm"):
            mod_phase(1)
            mixer1()
            if stage != "l1mix":
                ffn_phase(1, range(2, NT))
        osem = S.newsem("d_out")
        for i in range(2, NT):
            S.dma("sp", osem, out[(i - 2) * 128:(i - 1) * 128, :], xs[i][:], [xs[i]], [])
        if dbg:
            for i in range(2):
                S.dma("sp", osem, octx[i * 128:(i + 1) * 128, :], xs[i][:], [xs[i]], [])
        S._need("sp", osem, S.cnt[osem])
        S.barrier()
        print(f"[kernel] instructions={S.ninst} waits={S.nwait} sems={len(S.sems)}", flush=True)
    return nc


_CACHE = {}


def kernel(**inputs):
    shared, percore = host_prepare({k: np.asarray(v) for k, v in inputs.items()})
    if "nc" not in _CACHE:
        _CACHE["nc"] = build_program("full")
    nc = _CACHE["nc"]
    in_maps = []
    for b in range(8):
        m = dict(shared)
        m.update(percore[b])
        in_maps.append(m)
    res = run_bass_kernel_spmd(nc, in_maps, core_ids=list(range(8)))
    return np.stack([np.asarray(r["out"], np.float32) for r in res.results], axis=0)
```

```python
import numpy as np
from contextlib import ExitStack
import concourse.bass as bass
import concourse.mybir as mybir
from concourse.bass_utils import run_bass_kernel_spmd

F32 = mybir.dt.float32
BF16 = mybir.dt.bfloat16
AF = mybir.ActivationFunctionType
ALU = mybir.AluOpType
AX = mybir.AxisListType

ENGS = ("pe", "act", "dve", "pool", "sp")
NT = 18
EPS = 1e-6
NEGM = -30000.0


class Buf:
    __slots__ = ("t", "name", "w", "r", "sem", "psum")

    def __init__(self, t, name):
        self.t = t
        self.name = name
        self.w = None
        self.r = {}
        self.sem = None
        self.psum = False

    def __getitem__(self, idx):
        return self.t[idx]


class Ring:
    def __init__(self, bufs):
        self.bufs = bufs
        self.i = 0

    def get(self):
        b = self.bufs[self.i % len(self.bufs)]
        self.i += 1
        return b


SEM_LIMIT = 1500


class Sched:
    def __init__(self, nc, stack):
        self.nc = nc
        self.stack = stack
        self.eng = {"pe": nc.tensor, "act": nc.scalar, "dve": nc.vector,
                    "pool": nc.gpsimd, "sp": nc.sync}
        self.sems = {}
        self.cnt = {}
        self.epoch = {}
        self.cur = {}
        for e in ENGS:
            self.epoch[e] = 0
            self._new_epoch(e)
        self.seen = {e: {} for e in ENGS}
        self.ninst = 0
        self.nwait = 0
        self.nsem = 0
        self.nalloc = 0

    def _new_epoch(self, e):
        self.epoch[e] += 1
        key = f"{e}#{self.epoch[e]}"
        self.sems[key] = self.stack.enter_context(self.nc.semaphore("s_" + key.replace("#", "_")))
        self.cnt[key] = 0
        self.cur[e] = key

    def sb(self, name, shape, dt, stack=None):
        self.nalloc += 1
        name = f"{name}_{self.nalloc}"
        t = (stack or self.stack).enter_context(self.nc.sbuf_tensor(name, list(shape), dt))
        return Buf(t, name)

    def ps(self, name, shape, dt=F32):
        t = self.stack.enter_context(self.nc.psum_tensor(name, list(shape), dt))
        b = Buf(t, name)
        b.psum = True
        return b

    def newsem(self, name=None):
        self.nsem += 1
        name = name or f"d{self.nsem}"
        s = self.stack.enter_context(self.nc.semaphore(name))
        self.sems[name] = s
        self.cnt[name] = 0
        return name

    def sbd(self, name, shape, dt, stack=None):
        b = self.sb(name, shape, dt, stack)
        b.sem = self.newsem("d_" + b.name)
        return b

    @staticmethod
    def _eng_of(key):
        return key.split("#")[0] if "#" in key else None

    def _need(self, e, key, val):
        if self.seen[e].get(key, 0) >= val:
            return
        ke = self._eng_of(key)
        if ke is not None:
            ep = int(key.split("#")[1])
            for k2, v2 in self.seen[e].items():
                if v2 > 0 and self._eng_of(k2) == ke and int(k2.split("#")[1]) > ep:
                    return
        self.seen[e][key] = val
        self.eng[e].wait_ge(self.sems[key], val)
        self.nwait += 1

    def deps(self, e, reads, writes):
        for b in reads:
            if b.w is not None:
                k, v = b.w
                if not (self._eng_of(k) == e and e == "pe"):
                    self._need(e, k, v)
        for b in writes:
            if b.w is not None:
                k, v = b.w
                if self._eng_of(k) != e:
                    self._need(e, k, v)
            for k, v in b.r.items():
                if self._eng_of(k) != e:
                    self._need(e, k, v)

    def op(self, e, reads, writes, fn):
        pr = [b for b in reads if b.psum]
        if pr:
            reads = [b for b in reads if not b.psum]
            writes = list(writes) + [b for b in pr if b not in writes]
        self.deps(e, reads, writes)
        ins = fn(self.eng[e])
        if self.cnt[self.cur[e]] >= SEM_LIMIT:
            self._new_epoch(e)
        key = self.cur[e]
        self.cnt[key] += 1
        ins.then_inc(self.sems[key], 1)
        v = self.cnt[key]
        for b in reads:
            for k2 in [k2 for k2 in b.r if self._eng_of(k2) == e]:
                del b.r[k2]
            b.r[key] = v
        for b in writes:
            b.w = (key, v)
            b.r = {}
        self.ninst += 1
        return ins

    def dma(self, q, semkey, out_ap, in_ap, reads, writes, **kw):
        self.deps(q, reads, writes)
        ins = self.eng[q].dma_start(out=out_ap, in_=in_ap, **kw)
        self.cnt[semkey] += 16
        assert self.cnt[semkey] <= 2000, semkey
        ins.then_inc(self.sems[semkey], 16)
        v = self.cnt[semkey]
        for b in reads:
            b.r[semkey] = v
        for b in writes:
            b.w = (semkey, v)
            b.r = {}
        self.ninst += 1
        return ins

    def barrier(self):
        for e in ENGS:
            for k, v in list(self.cnt.items()):
                ke = self._eng_of(k)
                if ke == e or v == 0:
                    continue
                if ke is not None and k != self.cur[ke]:
                    if not (self.cnt[self.cur[ke]] == 0 and int(k.split("#")[1]) == self.epoch[ke] - 1):
                        continue
                self._need(e, k, v)


def _rope_tables():
    t = np.arange(2048)
    row = (t // 64).astype(np.float32)
    col = (t % 64).astype(np.float32)
    half = 32
    freq = (np.float32(10000.0) ** (-np.arange(0, half, 2, dtype=np.float32) / np.float32(half))).astype(np.float32)
    ang_r = row[:, None] * freq[None, :]
    ang_c = col[:, None] * freq[None, :]
    ang = np.concatenate([ang_r, ang_r, ang_c, ang_c], axis=-1).astype(np.float32)
    cos = np.cos(ang).astype(np.float32)
    sin = np.sin(ang).astype(np.float32)
    sgn = np.ones(64, np.float32)
    sgn[0:16] = -1.0
    sgn[32:48] = -1.0
    sinS = sin * sgn[None, :]
    cos = cos.reshape(16, 128, 64).transpose(1, 0, 2).copy()
    sinS = sinS.reshape(16, 128, 64).transpose(1, 0, 2).copy()
    return cos, sinS


def _na_tables(rpb):
    rows = 32
    wr = 8
    r = np.arange(rows)
    row_start = np.clip(r - wr // 2, 0, rows - wr)
    col = np.arange(64)
    col_start = np.clip(col - 8, 0, 48)
    col_ok = (col[None, :] >= col_start[:, None]) & (col[None, :] < col_start[:, None] + 16)
    dc = np.clip(col[None, :] - col[:, None] + 15, 0, 30)
    classes = [0, 1, 2, 14, 15]
    blocks = {}
    tab = np.full((8, 128, 25, 128), NEGM, np.float32)
    for ci, j in enumerate(classes):
        qrows = [2 * j, 2 * j + 1]
        lo = min(row_start[q] for q in qrows)
        hi = max(row_start[q] + wr - 1 for q in qrows)
        mlist = list(range(lo // 2, hi // 2 + 1))
        assert len(mlist) <= 5
        blocks[j] = mlist
        for si, m in enumerate(mlist):
            for kr in range(2):
                krow = 2 * m + kr
                for qr in range(2):
                    qrow = qrows[qr]
                    if not (row_start[qrow] <= krow < row_start[qrow] + wr):
                        continue
                    dr = krow - qrow + 7
                    sub = rpb[:, dr, :][:, dc]
                    sub = np.where(col_ok[None], sub, np.float32(NEGM))
                    tab[:, kr * 64:(kr + 1) * 64, ci * 5 + si, qr * 64:(qr + 1) * 64] = sub.transpose(0, 2, 1)
    return tab, blocks, classes


def _na_blocks():
    _, blocks, classes = _na_tables(np.zeros((8, 15, 31), np.float32))
    return blocks, classes


def host_prepare(inp):
    f = np.float32
    shared = {}
    shared["ada_w"] = np.ascontiguousarray(inp["ada_w"], f)
    shared["ada_bT"] = np.ascontiguousarray(inp["ada_b"].reshape(2, 48, 128).transpose(2, 0, 1), f)
    shared["norm_gT"] = np.ascontiguousarray(inp["norm_g"].reshape(2, 2, 8, 128).transpose(3, 0, 1, 2), f)
    shared["w_out"] = np.ascontiguousarray(inp["w_out"], f)
    shared["ffn_up"] = np.ascontiguousarray(inp["ffn_up"], f)
    shared["ffn_down"] = np.ascontiguousarray(inp["ffn_down"], f)
    shared["conv_wT"] = np.ascontiguousarray(inp["ffn_conv_w"].reshape(2, 3, 44, 128).transpose(3, 0, 1, 2), f)
    shared["conv_bT"] = np.ascontiguousarray(inp["ffn_conv_b"].reshape(2, 44, 128).transpose(2, 0, 1), f)
    shared["even_w"] = np.ascontiguousarray(inp["even_w_in"][0], f)
    shared["odd_w"] = np.ascontiguousarray(inp["odd_w_in"][0], f)
    bc = lambda a: np.ascontiguousarray(np.broadcast_to(np.asarray(a, f).reshape(1, -1), (128, a.size)))
    shared["gate_b_bc"] = bc(inp["mlstm_gate_b"][0])
    shared["head_g_bc"] = bc(inp["mlstm_head_g"][0])
    shared["swa_g_bc"] = bc(inp["swa_qk_g"][0])
    shared["sink_bc"] = bc(inp["swa_sink"][0])
    shared["gqa_g_bc"] = bc(inp["gqa_qk_g"][0])
    shared["na_g_bc"] = bc(inp["na_qk_g"][0])
    tab, _, _ = _na_tables(np.asarray(inp["na_rpb"][0], f))
    shared["na_bias"] = tab
    ident = np.eye(128, dtype=f)
    s = np.arange(128)
    triU = (s[:, None] <= s[None, :]).astype(f)
    triL = (s[:, None] >= s[None, :]).astype(f)
    wm = np.zeros((128, 2, 128), f)
    wm[:, 0, :] = np.where(s[None, :] <= s[:, None], 0.0, NEGM)
    wm[:, 1, :] = np.where(s[:, None] <= s[None, :], 0.0, NEGM)
    shared["consts"] = np.ascontiguousarray(np.concatenate([ident, triU, triL, wm.reshape(128, 256)], axis=1))
    cos, sinS = _rope_tables()
    shared["rope"] = np.ascontiguousarray(np.stack([cos, sinS], axis=1))
    percore = []
    for b in range(8):
        cc = np.stack([inp["c"][b].reshape(8, 128).T, inp["c_ctx"].reshape(8, 128).T], axis=-1)
        percore.append({"x": np.ascontiguousarray(inp["x"][b], f), "ctx": np.ascontiguousarray(inp["ctx"][b], f),
                        "cc": np.ascontiguousarray(cc, f)})
    return shared, percore


SHARED_SHAPES = {
    "ada_w": [2, 1024, 6144], "ada_bT": [128, 2, 48], "norm_gT": [128, 2, 2, 8], "w_out": [2, 1024, 1024],
    "ffn_up": [2, 1024, 5632], "ffn_down": [2, 2816, 1024], "conv_wT": [128, 2, 3, 44], "conv_bT": [128, 2, 44],
    "even_w": [1024, 2832], "odd_w": [1024, 2304], "gate_b_bc": [128, 16], "head_g_bc": [128, 512],
    "swa_g_bc": [128, 128], "sink_bc": [128, 8], "gqa_g_bc": [128, 128], "na_g_bc": [128, 128],
    "na_bias": [8, 128, 25, 128], "consts": [128, 640], "rope": [128, 2, 16, 64],
    "x": [2048, 1024], "ctx": [256, 1024], "cc": [128, 8, 2],
}


GROUPS = [(0, 0, 256), (1, 256, 512), (2, 768, 512), (3, 1280, 512), (4, 1792, 512)]


def tok_group(i):
    return (0, i * 128) if i < 2 else (1 + (i - 2) // 4, ((i - 2) % 4) * 128)


def build_program(stage="full"):
    nc = bass.Bass("TRN2", target_bir_lowering=False)
    D = {k: nc.dram_tensor(k, shp, F32, kind="ExternalInput").ap() for k, shp in SHARED_SHAPES.items()}
    out = nc.dram_tensor("out", [2048, 1024], F32, kind="ExternalOutput").ap()
    dbg = stage != "full"
    if dbg:
        octx = nc.dram_tensor("octx", [256, 1024], F32, kind="ExternalOutput").ap()
        dbgd = nc.dram_tensor("dbgd", [128, 8192], F32, kind="ExternalOutput").ap()
    na_blocks, na_classes = _na_blocks()

    with ExitStack() as st:
        S = Sched(nc, st)
        xs = [S.sbd(f"xs{i}", [128, 1024], F32) for i in range(NT)]
        cst = S.sbd("cst", [128, 640], F32)
        cc = S.sbd("cc", [128, 8, 2], F32)
        adab = S.sbd("adab", [128, 2, 48], F32)
        ngT = S.sbd("ngT", [128, 2, 2, 8], F32)
        cw = S.sbd("cw", [128, 2, 3, 44], F32)
        cb = S.sbd("cb", [128, 2, 44], F32)
        identb = S.sb("identb", [128, 128], BF16)
        wmb = S.sb("wmb", [128, 2, 128], BF16)
        ones_f = S.sb("ones_f", [128, 128], F32)
        ones_b = S.sb("ones_b", [128, 128], BF16)
        sc = S.sb("sc", [128, 8, 2], F32)
        modT = [S.sb(f"modT{l}", [128, 48, 2], F32) for l in range(2)]
        gbc = S.sb("gbc", [128, 2, 1024], F32)
        AB = S.sb("AB", [128, 8, 2], F32)

        psT = Ring([S.ps(f"psT{i}", [128, 8, 128], BF16) for i in range(2)])
        psA = Ring([S.ps(f"psA{i}", [128, 512], F32) for i in range(2)])
        psS = Ring([S.ps(f"psS{i}", [128, 512], F32) for i in range(2)])
        psO = Ring([S.ps(f"psO{i}", [128, 512], F32) for i in range(2)])

        IDF = lambda: cst[:, 0:128]
        TRIU = lambda: cst[:, 128:256]
        TRIL = lambda: cst[:, 256:384]

        S.dma("sp", cst.sem, cst[:], D["consts"], [], [cst])
        S.dma("sp", cc.sem, cc[:], D["cc"], [], [cc])
        S.dma("sp", adab.sem, adab[:], D["ada_bT"], [], [adab])
        S.dma("sp", ngT.sem, ngT[:], D["norm_gT"], [], [ngT])
        S.dma("sp", cw.sem, cw[:], D["conv_wT"], [], [cw])
        S.dma("sp", cb.sem, cb[:], D["conv_bT"], [], [cb])
        for i in range(NT):
            src = D["ctx"][i * 128:(i + 1) * 128, :] if i < 2 else D["x"][(i - 2) * 128:(i - 1) * 128, :]
            S.dma("sp", xs[i].sem, xs[i][:], src, [], [xs[i]])
        S.op("dve", [cst], [identb], lambda e: e.tensor_copy(out=identb[:], in_=cst[:, 0:128]))
        S.op("dve", [cst], [wmb], lambda e: e.tensor_copy(out=wmb[:], in_=cst[:, 384:640].rearrange("p (a b) -> p a b", a=2)))
        S.op("dve", [], [ones_f], lambda e: e.memset(ones_f[:], 1.0))
        S.op("dve", [], [ones_b], lambda e: e.memset(ones_b[:], 1.0))
        S.op("act", [cc], [sc], lambda e: e.activation(out=sc[:], in_=cc[:], func=AF.Silu))

        dstg = S.sb("dstg", [128, 128], F32) if dbg else None
        dstate = {"col": 0, "items": []}

        def dump(name, buf, ap, n):
            if not dbg:
                return
            stg = dstg
            sem = S.newsem()
            S.op("act", [buf], [stg], lambda e: e.activation(out=stg[:, 0:n], in_=ap, func=AF.Copy))
            c0 = dstate["col"]
            S.dma("sp", sem, dbgd[:, c0:c0 + n], stg[:, 0:n], [stg], [])
            S._need("sp", sem, S.cnt[sem])
            dstate["items"].append((name, c0, n))
            dstate["col"] = c0 + n
            print("DUMP", name, c0, n, flush=True)

        def wview(wb, shape_str, **kw):
            n = 1
            for v in kw.values():
                n *= v
            return wb

        def mod_phase(l):
            with ExitStack() as ph:
                ring = Ring([S.sbd(f"adaw{l}_{i}", [128, 8, 512], BF16, ph) for i in range(3)])
                schi = S.sb(f"schi{l}", [128, 8, 2], BF16, ph)
                schf = S.sb(f"schf{l}", [128, 8, 2], F32, ph)
                sclo = S.sb(f"sclo{l}", [128, 8, 2], BF16, ph)
                S.op("dve", [sc], [schi], lambda e: e.tensor_copy(out=schi[:], in_=sc[:]))
                S.op("dve", [schi], [schf], lambda e: e.tensor_copy(out=schf[:], in_=schi[:]))
                S.op("dve", [sc, schf], [schf], lambda e: e.tensor_tensor(out=schf[:], in0=sc[:], in1=schf[:], op=ALU.subtract))
                S.op("dve", [schf], [sclo], lambda e: e.tensor_copy(out=sclo[:], in_=schf[:]))
                wbs = {}

                def ld(cg):
                    if cg >= 12:
                        return
                    wb = ring.get()
                    S.dma("pool", wb.sem, wb[:], D["ada_w"][l, :, cg * 512:(cg + 1) * 512].rearrange("(k p) n -> p k n", p=128), [], [wb])
                    wbs[cg] = wb
                ld(0)
                ld(1)
                for cg in range(12):
                    ld(cg + 2)
                    wb = wbs[cg]
                    ps = psA.get()
                    for c4 in range(4):
                        for k in range(8):
                            S.op("pe", [wb, schi], [ps], lambda e: e.matmul(ps[:, c4 * 2:c4 * 2 + 2], lhsT=wb[:, k, c4 * 128:(c4 + 1) * 128], rhs=schi[:, k, :], start=(k == 0), stop=False))
                            S.op("pe", [wb, sclo], [ps], lambda e: e.matmul(ps[:, c4 * 2:c4 * 2 + 2], lhsT=wb[:, k, c4 * 128:(c4 + 1) * 128], rhs=sclo[:, k, :], start=False, stop=(k == 7)))
                    S.op("dve", [ps, adab], [modT[l]], lambda e: e.tensor_tensor(
                        out=modT[l][:, cg * 4:(cg + 1) * 4, :], in0=ps[:, 0:8].rearrange("p (c j) -> p c j", j=2),
                        in1=adab[:, l, cg * 4:(cg + 1) * 4].unsqueeze(2).to_broadcast([128, 4, 2]), op=ALU.add))
                S.barrier()

        def mk_AB(l, which):
            scl = 8 if which == 0 else 32
            S.op("dve", [modT[l]], [AB], lambda e: e.tensor_scalar(out=AB[:], in0=modT[l][:, scl:scl + 8, :], scalar1=1.0, scalar2=None, op0=ALU.add))
            S.op("dve", [AB, ngT], [AB], lambda e: e.tensor_tensor(out=AB[:], in0=AB[:], in1=ngT[:, l, which, :].unsqueeze(2).to_broadcast([128, 8, 2]), op=ALU.mult))

        def mk_gate(l, gchunk, ph):
            hl = S.sb(f"ghl{l}_{gchunk}", [128, 8, 2], F32, ph)
            hb = S.sb(f"ghb{l}_{gchunk}", [128, 8, 2], BF16, ph)
            hf = S.sb(f"ghf{l}_{gchunk}", [128, 8, 2], F32, ph)
            lo = S.sb(f"glo{l}_{gchunk}", [128, 8, 2], F32, ph)
            lb = S.sb(f"glb{l}_{gchunk}", [128, 8, 2], BF16, ph)
            lf = S.sb(f"glf{l}_{gchunk}", [128, 8, 2], F32, ph)
            S.op("dve", [modT[l]], [hl], lambda e: e.tensor_copy(out=hl[:], in_=modT[l][:, gchunk:gchunk + 8, :]))
            S.op("dve", [hl], [hb], lambda e: e.tensor_copy(out=hb[:], in_=hl[:]))
            S.op("dve", [hb], [hf], lambda e: e.tensor_copy(out=hf[:], in_=hb[:]))
            S.op("dve", [hl, hf], [lo], lambda e: e.tensor_tensor(out=lo[:], in0=hl[:], in1=hf[:], op=ALU.subtract))
            S.op("dve", [lo], [lb], lambda e: e.tensor_copy(out=lb[:], in_=lo[:]))
            S.op("dve", [lb], [lf], lambda e: e.tensor_copy(out=lf[:], in_=lb[:]))
            dgr = Ring([S.sb(f"dg{l}_{gchunk}_{i}", [128, 2, 128], BF16, ph) for i in range(2)])
            for j in range(2):
                for half in range(2):
                    ps = psA.get()
                    for k4 in range(4):
                        kk = half * 4 + k4
                        dg = dgr.get()
                        S.op("dve", [identb, hf], [dg], lambda e: e.tensor_scalar(out=dg[:, 0, :], in0=identb[:], scalar1=hf[:, kk, j:j + 1], scalar2=None, op0=ALU.mult))
                        S.op("dve", [identb, lf], [dg], lambda e: e.tensor_scalar(out=dg[:, 1, :], in0=identb[:], scalar1=lf[:, kk, j:j + 1], scalar2=None, op0=ALU.mult))
                        S.op("pe", [ones_b, dg], [ps], lambda e: e.matmul(ps[:, k4 * 128:(k4 + 1) * 128], lhsT=ones_b[:], rhs=dg[:, 0, :], start=True, stop=False))
                        S.op("pe", [ones_b, dg], [ps], lambda e: e.matmul(ps[:, k4 * 128:(k4 + 1) * 128], lhsT=ones_b[:], rhs=dg[:, 1, :], start=False, stop=True))
                    S.op("act", [ps], [gbc], lambda e: e.activation(out=gbc[:, j, half * 512:(half + 1) * 512], in_=ps[:], func=AF.Copy))

        def rstd_of(t, n_ap, dim):
            S.op("dve", [t], [t], lambda e: e.tensor_scalar(out=n_ap(), in0=n_ap(), scalar1=1.0 / dim, scalar2=EPS, op0=ALU.mult, op1=ALU.add))
            S.op("act", [t], [t], lambda e: e.activation(out=n_ap(), in_=n_ap(), func=AF.Ln))
            S.op("act", [t], [t], lambda e: e.activation(out=n_ap(), in_=n_ap(), func=AF.Exp, scale=-0.5))

        def norm_phase(l, which, hTg, ph, tiles=range(NT)):
            mk_AB(l, which)
            sh = 0 if which == 0 else 24
            ss = S.sb(f"nss{l}{which}", [128, NT], F32, ph)
            junk = S.sb(f"njunk{l}{which}", [128, 1024], BF16, ph)
            xnr = Ring([S.sb(f"xn{l}{which}_{i}", [128, 1024], BF16, ph) for i in range(2)])
            S.op("dve", [], [ss], lambda e: e.memset(ss[:], 1.0))
            for i in tiles:
                S.op("act", [xs[i]], [junk, ss], lambda e: e.activation(out=junk[:], in_=xs[i][:], func=AF.Square, accum_out=ss[:, i:i + 1]))
            rstd_of(ss, lambda: ss[:], 1024)
            import os
            if os.environ.get("KSUB") in ("a", "c"):
                return
            tl_ = list(tiles)

            def stage_xn(i):
                xn = xnr.get()
                S.op("dve", [xs[i], ss], [xn], lambda e: e.tensor_scalar(out=xn[:], in0=xs[i][:], scalar1=ss[:, i:i + 1], scalar2=None, op0=ALU.mult))
                pt = psT.get()
                for k in range(8):
                    S.op("pe", [xn, identb], [pt], lambda e: e.transpose(out=pt[:, k, :], in_=xn[:, k * 128:(k + 1) * 128], identity=identb[:]))
                return pt

            def stage_evac(i, pt):
                g, off = tok_group(i)
                j = 1 if i < 2 else 0
                for k in range(8):
                    if k % 2 == 0:
                        S.op("dve", [pt, AB, modT[l]], [hTg[g]], lambda e: e.tensor_scalar(
                            out=hTg[g][:, k, off:off + 128], in0=pt[:, k, :], scalar1=AB[:, k, j:j + 1], scalar2=modT[l][:, sh + k, j:j + 1], op0=ALU.mult, op1=ALU.add))
                    else:
                        S.op("act", [pt, AB, modT[l]], [hTg[g]], lambda e: e.activation(
                            out=hTg[g][:, k, off:off + 128], in_=pt[:, k, :], func=AF.Identity, scale=AB[:, k, j:j + 1], bias=modT[l][:, sh + k, j:j + 1]))

            ptn = stage_xn(tl_[0])
            for n_, i in enumerate(tl_):
                ptc = ptn
                if n_ + 1 < len(tl_):
                    ptn = stage_xn(tl_[n_ + 1])
                stage_evac(i, ptc)

        def wload(wb, n, src):
            dst = wb[:, 0:8 * n].rearrange("p (k n) -> p k n", k=8)
            S.dma("pool", wb.sem, dst, src, [], [wb])
            return dst

        def qk_prep(ps, ps_ap, nh, g_ap, rope_tile, out_ap, wk, rope):
            sq, ssq, qn, t1 = wk
            n = nh * 64
            v3 = lambda ap: ap.rearrange("p (h d) -> p h d", d=64)
            S.op("act", [ps], [sq], lambda e: e.activation(out=sq[:, 0:n], in_=ps_ap, func=AF.Square))
            S.op("dve", [sq], [ssq], lambda e: e.tensor_reduce(out=ssq[:, 0:nh], in_=v3(sq[:, 0:n]), axis=AX.X, op=ALU.add))
            rstd_of(ssq, lambda: ssq[:, 0:nh], 64)
            S.op("dve", [ps, ssq], [qn], lambda e: e.tensor_tensor(out=v3(qn[:, 0:n]), in0=v3(ps_ap), in1=ssq[:, 0:nh].unsqueeze(2).to_broadcast([128, nh, 64]), op=ALU.mult))
            if rope_tile is None:
                S.op("dve", [qn], [out_ap[0]], lambda e: e.tensor_tensor(out=out_ap[1], in0=v3(qn[:, 0:n]), in1=g_ap.unsqueeze(1).to_broadcast([128, nh, 64]), op=ALU.mult))
                return
            S.op("dve", [qn], [qn], lambda e: e.tensor_tensor(out=v3(qn[:, 0:n]), in0=v3(qn[:, 0:n]), in1=g_ap.unsqueeze(1).to_broadcast([128, nh, 64]), op=ALU.mult))
            cos_ap = rope[:, 0, :]
            sin_ap = rope[:, 1, :]
            S.op("dve", [qn, rope], [t1], lambda e: e.tensor_tensor(out=v3(t1[:, 0:n]), in0=v3(qn[:, 0:n]), in1=cos_ap.unsqueeze(1).to_broadcast([128, nh, 64]), op=ALU.mult))
            v5 = lambda ap: ap.rearrange("p (h x y d) -> p h x y d", x=2, y=2, d=16)
            s4 = sin_ap.rearrange("p (x y d) -> p x y d", x=2, y=2)
            for y in range(2):
                S.op("dve", [qn, rope], [sq], lambda e: e.tensor_tensor(
                    out=v5(sq[:, 0:n])[:, :, :, y, :], in0=v5(qn[:, 0:n])[:, :, :, 1 - y, :],
                    in1=s4[:, :, y, :].unsqueeze(1).to_broadcast([128, nh, 2, 16]), op=ALU.mult))
            S.op("dve", [t1, sq], [out_ap[0]], lambda e: e.tensor_tensor(out=out_ap[1], in0=v3(t1[:, 0:n]), in1=v3(sq[:, 0:n]), op=ALU.add))

        def prep_batch(raw, sq, ss, T, nh, g_ap, out_buf, out_ap, inplace=False):
            n = T * nh
            r3 = raw[:, 0:T, :].rearrange("p t (h d) -> p (t h) d", d=64)
            s3 = sq[:, 0:T, :].rearrange("p t (h d) -> p (t h) d", d=64)
            S.op("act", [raw], [sq], lambda e: e.activation(out=sq[:, 0:T, :], in_=raw[:, 0:T, :], func=AF.Square))
            S.op("dve", [sq], [ss], lambda e: e.tensor_reduce(out=ss[:, 0:n], in_=s3, axis=AX.X, op=ALU.add))
            rstd_of(ss, lambda: ss[:, 0:n], 64)
            S.op("dve", [raw, ss], [raw], lambda e: e.tensor_tensor(out=r3, in0=r3, in1=ss[:, 0:n].unsqueeze(2).to_broadcast([128, n, 64]), op=ALU.mult))
            if inplace:
                S.op("dve", [raw], [raw], lambda e: e.tensor_tensor(out=r3, in0=r3, in1=g_ap.unsqueeze(1).to_broadcast([128, n, 64]), op=ALU.mult))
                return
            S.op("dve", [raw], [out_buf], lambda e: e.tensor_tensor(out=out_ap.rearrange("p t (h d) -> p (t h) d", d=64), in0=r3, in1=g_ap.unsqueeze(1).to_broadcast([128, n, 64]), op=ALU.mult))

        def residual(i, ps, cgi, j):
            tmp = restmp.get()
            S.op("dve", [ps, gbc], [tmp], lambda e: e.tensor_tensor(out=tmp[:], in0=ps[:], in1=gbc[:, j, cgi * 512:(cgi + 1) * 512], op=ALU.mult))
            rstate["n"] += 1
            S.op("dve", [tmp, xs[i]], [xs[i]], lambda e: e.tensor_tensor(out=xs[i][:, cgi * 512:(cgi + 1) * 512], in0=xs[i][:, cgi * 512:(cgi + 1) * 512], in1=tmp[:], op=ALU.add))

        restmp = Ring([S.sb(f"restmp{i}", [128, 512], F32) for i in range(1)])
        rstate = {"n": 0}

        def mixer0():
            l = 0
            with ExitStack() as ph:
                hTg = [S.sb("hT0_0", [128, 8, 256], BF16, ph)] + [S.sb(f"hT0_{g}", [128, 8, 512], BF16, ph) for g in range(1, 5)]
                with ExitStack() as ph2:
                    norm_phase(0, 0, hTg, ph2)
                    import os
                    if os.environ.get("KSUB") not in ("a", "b"):
                        mk_gate(0, 16, ph2)
                    S.barrier()
                if stage == "norm":
                    return
                mixTa = S.sb("mixTa", [128, 4, NT * 128], BF16, ph)
                with ExitStack() as ph2:
                    wring = Ring([S.sbd(f"w0_{i}", [128, 8 * 384], BF16, ph2) for i in range(2)])
                    gateb = S.sbd("gateb", [128, 16], F32, ph2)
                    headg = S.sbd("headg", [128, 512], F32, ph2)
                    S.dma("sp", gateb.sem, gateb[:], D["gate_b_bc"], [], [gateb])
                    S.dma("sp", headg.sem, headg[:], D["head_g_bc"], [], [headg])
                    mlstm(hTg, mixTa, gateb, headg, wring, ph2)
                    S.barrier()
                if stage == "mlstm":
                    return
                with ExitStack() as ph2:
                    gqa_attn(0, hTg, mixTa, None, ph2)
                    S.barrier()

        def mlstm_gates(hTg, gateb, wring, pg, es, eb, edec, ekw):
            G = S.sb("G", [128, NT, 16], F32, pg)
            wgb = S.sbd("wgates", [128, 8 * 16], BF16, pg)
            wg = wload(wgb, 16, D["even_w"][:, 2048:2064].rearrange("(k p) n -> p k n", p=128))
            for i in range(NT):
                g, off = tok_group(i)
                ps = psO.get()
                for k in range(8):
                    S.op("pe", [hTg[g], wgb], [ps], lambda e: e.matmul(ps[:, 0:16], lhsT=hTg[g][:, k, off:off + 128], rhs=wg[:, k, :], start=(k == 0), stop=(k == 7)))
                S.op("dve", [ps, gateb], [G], lambda e: e.tensor_tensor(out=G[:, i, :], in0=ps[:, 0:16], in1=gateb[:], op=ALU.add))
            E = S.sb("E", [128, 2, NT, 4], F32, pg)
            for d in range(2):
                S.op("act", [G], [E], lambda e: e.activation(out=E[:, d], in_=G[:, :, 4 + 8 * d:8 + 8 * d], func=AF.Exp, scale=-1.0))
            S.op("dve", [E], [E], lambda e: e.tensor_scalar(out=E[:], in0=E[:], scalar1=1.0, scalar2=None, op0=ALU.add))
            S.op("act", [E], [E], lambda e: e.activation(out=E[:], in_=E[:], func=AF.Ln))
            tg = S.sb("tg", [128, NT, 4], F32, pg)
            f72 = lambda ap: ap.rearrange("p t h -> p (t h)")
            trib = S.sb("trib", [128, 2, 128], BF16, pg)
            S.op("dve", [cst], [trib], lambda e: e.tensor_copy(out=trib[:], in_=cst[:, 128:384].rearrange("p (a b) -> p a b", a=2)))
            Ehi = S.sb("Ehi", [128, 2, NT, 4], BF16, pg)
            Ehf = S.sb("Ehf", [128, 2, NT, 4], F32, pg)
            Elo = S.sb("Elo", [128, 2, NT, 4], BF16, pg)
            S.op("dve", [E], [Ehi], lambda e: e.tensor_copy(out=Ehi[:], in_=E[:]))
            S.op("dve", [Ehi], [Ehf], lambda e: e.tensor_copy(out=Ehf[:], in_=Ehi[:]))
            S.op("dve", [E, Ehf], [Ehf], lambda e: e.tensor_tensor(out=Ehf[:], in0=E[:], in1=Ehf[:], op=ALU.subtract))
            S.op("dve", [Ehf], [Elo], lambda e: e.tensor_copy(out=Elo[:], in_=Ehf[:]))
            for d in range(2):
                psb = psO.get()
                S.op("pe", [trib, Ehi], [psb], lambda e: e.matmul(psb[:, 0:72], lhsT=trib[:, d, :], rhs=f72(Ehi[:, d]), start=True, stop=False))
                S.op("pe", [trib, Elo], [psb], lambda e: e.matmul(psb[:, 0:72], lhsT=trib[:, d, :], rhs=f72(Elo[:, d]), start=False, stop=True))
                S.op("pe", [ones_b, Ehi], [psb], lambda e: e.matmul(psb[:, 72:144], lhsT=ones_b[:], rhs=f72(Ehi[:, d]), start=True, stop=False))
                S.op("pe", [ones_b, Elo], [psb], lambda e: e.matmul(psb[:, 72:144], lhsT=ones_b[:], rhs=f72(Elo[:, d]), start=False, stop=True))
                S.op("dve", [psb, G], [tg], lambda e: e.tensor_tensor(out=tg[:], in0=psb[:, 0:72].rearrange("p (t h) -> p t h", h=4), in1=G[:, :, 8 * d:8 * d + 4], op=ALU.add))
                S.op("act", [tg], [es], lambda e: e.activation(out=es[:, d], in_=tg[:], func=AF.Exp))
                S.op("act", [psb], [eb], lambda e: e.activation(out=f72(eb[:, d]), in_=psb[:, 0:72], func=AF.Exp, scale=-1.0))
                S.op("act", [psb], [edec], lambda e: e.activation(out=f72(edec[:, d]), in_=psb[:, 72:144], func=AF.Exp, scale=-1.0))
                S.op("dve", [es, edec], [ekw], lambda e: e.tensor_tensor(out=ekw[:, d], in0=es[:, d], in1=edec[:, d], op=ALU.mult))

            pass
            pass
            pass
            pass
            pass

        def mlstm(hTg, mixTa, gateb, headg, wring, ph):
            es = S.sb("es", [128, 2, NT, 4], F32, ph)
            eb = S.sb("eb", [128, 2, NT, 4], F32, ph)
            edec = S.sb("edec", [128, 2, NT, 4], F32, ph)
            ekw = S.sb("ekw", [128, 2, NT, 4], F32, ph)
            with ExitStack() as pg:
                mlstm_gates(hTg, gateb, wring, pg, es, eb, edec, ekw)
                S.barrier()
            KS_ = ""
            KH_ = -1
            qT = S.sb("qTa", [128, NT * 128], BF16, ph)
            kT = S.sb("kTa", [128, NT * 128], BF16, ph)
            ktok = S.sb("ktok", [128, NT, 128], BF16, ph)
            vaug = S.sb("vaug", [128, NT, 130], BF16, ph)
            hraw = [S.sb(f"hraw{d}", [128, NT, 130], F32, ph) for d in range(2)]
            rnm = S.sb("rnm", [128, 2, NT], F32, ph)
            Cst = [S.sb(f"Cst{d}", [128, 129], F32, ph) for d in range(2)]
            Cbf3 = [[S.sb(f"Cbf{d}_{r}", [128, 130], BF16, ph) for r in range(3)] for d in range(2)]
            PTr = Ring([S.sb(f"PTm{i}", [128, 128], BF16, ph) for i in range(4)])
            kwr = Ring([S.sb(f"kwm{i}", [128, 128], BF16, ph) for i in range(2)])
            hss = S.sb("hss", [128, NT], F32, ph)
            hjunk = S.sb("hjunk", [128, 128], BF16, ph)
            ogr = Ring([S.sb(f"og{i}", [128, 128], F32, ph) for i in range(2)])
            t1r = Ring([S.sb(f"mt1{i}", [128, 128], F32, ph) for i in range(2)])
            mxr = Ring([S.sb(f"mmx{i}", [128, 128], BF16, ph) for i in range(2)])
            S.op("dve", [], [vaug], lambda e: e.memset(vaug[:, :, 128:129], 1.0))
            orders = [list(range(NT)), [1, 0] + list(range(NT - 1, 1, -1))]
            KS = 128.0 ** -0.5

            def load_qkv(hd):
                wb_ = wring.get()
                src = D["even_w"][:, 0:1536].rearrange("(k p) (g h n) -> p k g h n", p=128, g=3, h=4)[:, :, :, hd, :]
                wq_ = wb_[:, 0:8 * 384].rearrange("p (k g n) -> p k g n", k=8, g=3)
                for g3 in range(3):
                    S.dma("pool", wb_.sem, wq_[:, :, g3, :], src[:, :, g3, :], [], [wb_])
                return wb_, wq_

            woring = Ring([S.sbd(f"wo_{i}", [128, 8 * 128], BF16, ph) for i in range(2)])
            nxt_w = load_qkv(0)
            for h in range(4):
                wb, wq = nxt_w
                wob = woring.get()
                wo = wload(wob, 128, D["even_w"][:, 1536 + h * 128:1536 + (h + 1) * 128].rearrange("(k p) n -> p k n", p=128))
                if h + 1 < 4:
                    nxt_w = load_qkv(h + 1)
                flip = 0
                for (g, c0, n) in GROUPS:
                    for which, dst, scl in ((0, qT, 1.0), (1, kT, KS)):
                        ps = psA.get()
                        for k in range(8):
                            S.op("pe", [wb, hTg[g]], [ps], lambda e: e.matmul(ps[:, 0:n], lhsT=wq[:, k, which, :], rhs=hTg[g][:, k, 0:n], start=(k == 0), stop=(k == 7)))
                        if flip % 2 == 0:
                            S.op("act", [ps], [dst], lambda e: e.activation(out=dst[:, c0:c0 + n], in_=ps[:, 0:n], func=AF.Copy, scale=scl))
                        else:
                            S.op("dve", [ps], [dst], lambda e: e.tensor_scalar(out=dst[:, c0:c0 + n], in0=ps[:, 0:n], scalar1=scl, scalar2=None, op0=ALU.mult))
                        flip += 1
                if KS_ == "m2a" and h == KH_:
                    return
                for i in range(NT):
                    g, off = tok_group(i)
                    ps = psA.get()
                    for k in range(8):
                        S.op("pe", [wb, hTg[g]], [ps], lambda e: e.matmul(ps[:, 0:256], lhsT=hTg[g][:, k, off:off + 128], rhs=wb[:, k * 384 + 128:k * 384 + 384], start=(k == 0), stop=(k == 7)))
                    S.op("act", [ps], [ktok], lambda e: e.activation(out=ktok[:, i, :], in_=ps[:, 0:128], func=AF.Copy, scale=KS))
                    S.op("dve", [ps], [vaug], lambda e: e.tensor_copy(out=vaug[:, i, 0:128], in_=ps[:, 128:256]))
                if KS_ == "m2b" and h == KH_:
                    return
                if h == 0:
                    pass
                    pass
                    pass
                    pass
                if KS_ == "m2" and h == KH_:
                    return
                written = [False] * NT
                PTs = {}

                def emitA2(step):
                    ii = [orders[d][step] for d in range(2)]
                    col = lambda a, d: a[:, d, ii[d], h:h + 1]
                    css = [slice(i * 128, (i + 1) * 128) for i in ii]
                    pss2, kws, pscs = [], [], []
                    for d in range(2):
                        pss = psS.get()
                        S.op("pe", [kT, qT], [pss], lambda e: e.matmul(pss[:, 0:128], lhsT=kT[:, css[d]], rhs=qT[:, css[d]], start=True, stop=True))
                        pss2.append(pss)
                    if step < NT - 1:
                        for d in range(2):
                            kw = kwr.get()
                            S.op("act", [ktok, ekw], [kw], lambda e: e.activation(out=kw[:], in_=ktok[:, ii[d], :], func=AF.Copy, scale=col(ekw, d)))
                            kws.append(kw)
                        for d in range(2):
                            psc = psA.get()
                            S.op("pe", [kws[d], vaug], [psc], lambda e: e.matmul(psc[:, 0:129], lhsT=kws[d][:], rhs=vaug[:, ii[d], 0:129], start=True, stop=True))
                            pscs.append(psc)
                    for d in range(2):
                        PT = PTr.get()
                        msk = TRIU() if d == 0 else TRIL()
                        S.op("dve", [pss2[d], es, cst], [PT], lambda e: e.scalar_tensor_tensor(out=PT[:], in0=pss2[d][:, 0:128], scalar=col(es, d), in1=msk, op0=ALU.mult, op1=ALU.mult))
                        PTs[(step, d)] = PT
                    if step < NT - 1:
                        for d in range(2):
                            psc = pscs[d]
                            if step == 0:
                                S.op("dve", [psc], [Cst[d]], lambda e: e.tensor_copy(out=Cst[d][:], in_=psc[:, 0:129]))
                            else:
                                S.op("dve", [psc, Cst[d], edec], [Cst[d]], lambda e: e.scalar_tensor_tensor(out=Cst[d][:], in0=Cst[d][:], scalar=col(edec, d), in1=psc[:, 0:129], op0=ALU.mult, op1=ALU.add))
                            cb3 = Cbf3[d][(step + 1) % 3]
                            S.op("dve", [Cst[d]], [cb3], lambda e: e.tensor_copy(out=cb3[:, 0:129], in_=Cst[d][:]))

                def emitB(step, d):
                    i = orders[d][step]
                    col = lambda a: a[:, d, i, h:h + 1]
                    cs = slice(i * 128, (i + 1) * 128)
                    PT = PTs.pop((step, d))
                    acc = psO.get()
                    if step > 0:
                        cb3 = Cbf3[d][step % 3]
                        S.op("pe", [qT, cb3], [acc], lambda e: e.matmul(acc[:, 0:129], lhsT=qT[:, cs], rhs=cb3[:, 0:129], start=True, stop=False))
                    S.op("pe", [PT, vaug], [acc], lambda e: e.matmul(acc[:, 0:129], lhsT=PT[:], rhs=vaug[:, i, 0:129], start=(step == 0), stop=True))
                    S.op("act", [acc, eb], [hraw[d]], lambda e: e.activation(out=hraw[d][:, i, 0:129], in_=acc[:, 0:129], func=AF.Copy, scale=col(eb)))

                emitA2(0)
                for step in range(NT):
                    if step + 1 < NT:
                        emitA2(step + 1)
                    emitB(step, 0)
                    emitB(step, 1)
                if h == 0:
                    pass
                    pass
                if KS_ == "m3" and h == KH_:
                    return
                for d in range(2):
                    S.op("act", [hraw[d]], [rnm], lambda e: e.activation(out=rnm[:, d, :], in_=hraw[d][:, :, 128], func=AF.Abs))
                S.op("dve", [rnm], [rnm], lambda e: e.tensor_scalar(out=rnm[:], in0=rnm[:], scalar1=1.0, scalar2=None, op0=ALU.max))
                S.op("dve", [rnm], [rnm], lambda e: e.reciprocal(out=rnm[:], in_=rnm[:]))
                for d in range(2):
                    S.op("dve", [hraw[d], rnm], [hraw[d]], lambda e: e.tensor_tensor(out=hraw[d][:, :, 0:128], in0=hraw[d][:, :, 0:128], in1=rnm[:, d, :].unsqueeze(2).to_broadcast([128, NT, 128]), op=ALU.mult))
                S.op("dve", [hraw[0], hraw[1]], [hraw[0]], lambda e: e.tensor_tensor(out=hraw[0][:, :, 0:128], in0=hraw[0][:, :, 0:128], in1=hraw[1][:, :, 0:128], op=ALU.add))
                S.op("dve", [], [hss], lambda e: e.memset(hss[:], 1.0))
                for i in range(NT):
                    S.op("act", [hraw[0]], [hjunk, hss], lambda e: e.activation(out=hjunk[:], in_=hraw[0][:, i, 0:128], func=AF.Square, accum_out=hss[:, i:i + 1]))
                rstd_of(hss, lambda: hss[:], 128)

                def out_stage1(i):
                    g, off = tok_group(i)
                    ps = psA.get()
                    for k in range(8):
                        S.op("pe", [wob, hTg[g]], [ps], lambda e: e.matmul(ps[:, 0:128], lhsT=hTg[g][:, k, off:off + 128], rhs=wo[:, k, :], start=(k == 0), stop=(k == 7)))
                    og = ogr.get()
                    S.op("act", [ps], [og], lambda e: e.activation(out=og[:], in_=ps[:, 0:128], func=AF.Sigmoid))
                    return og

                def out_stage2(i, og):
                    t1 = t1r.get()
                    S.op("dve", [hraw[0], hss, headg], [t1], lambda e: e.scalar_tensor_tensor(out=t1[:], in0=hraw[0][:, i, 0:128], scalar=hss[:, i:i + 1], in1=headg[:, h * 128:(h + 1) * 128], op0=ALU.mult, op1=ALU.mult))
                    mx = mxr.get()
                    S.op("dve", [t1, og], [mx], lambda e: e.tensor_tensor(out=mx[:], in0=t1[:], in1=og[:], op=ALU.mult))
                    pt = psT.get()
                    S.op("pe", [mx, identb], [pt], lambda e: e.transpose(out=pt[:, 0, :], in_=mx[:], identity=identb[:]))
                    S.op("act", [pt], [mixTa], lambda e: e.activation(out=mixTa[:, h, i * 128:(i + 1) * 128], in_=pt[:, 0, :], func=AF.Copy))

                ogn = out_stage1(0)
                for i in range(NT):
                    ogc = ogn
                    if i + 1 < NT:
                        ogn = out_stage1(i + 1)
                    out_stage2(i, ogc)
                if KS_ == "m4" and h == KH_:
                    return

        def attn_scores_exp_pv(kv_specs, nheads_per_kv, qT, q_sl, PTr, accs, first, last):
            pass

        def gqa_attn(l, hTg, other, wring, ph):
            wname = "even_w" if l == 0 else "odd_w"
            qc0, kc0 = (2064, 2576) if l == 0 else (0, 512)
            swag = S.sbd(f"swag{l}", [128, 128], F32, ph)
            roper = Ring([S.sbd(f"rope{l}_{i}", [128, 2, 64], F32, ph) for i in range(2)])

            def get_rope(jt):
                rb = roper.get()
                S.dma("sp", rb.sem, rb[:], D["rope"][:, :, jt, :], [], [rb])
                return rb
            S.dma("sp", swag.sem, swag[:], D["swa_g_bc" if l == 0 else "gqa_g_bc"], [], [swag])
            gq = S.sb(f"gq{l}", [128, 64], F32, ph)
            S.op("dve", [swag], [gq], lambda e: e.tensor_scalar(out=gq[:], in0=swag[:, 0:64], scalar1=0.125, scalar2=None, op0=ALU.mult))
            esink = S.sb(f"esink{l}", [128, 8], F32, ph)
            if l == 0:
                sinkb = S.sbd("sinkb", [128, 8], F32, ph)
                S.dma("sp", sinkb.sem, sinkb[:], D["sink_bc"], [], [sinkb])
                S.op("act", [sinkb], [esink], lambda e: e.activation(out=esink[:], in_=sinkb[:], func=AF.Exp))
            else:
                S.op("dve", [], [esink], lambda e: e.memset(esink[:], 0.0))
            wkb = S.sbd(f"wkv{l}", [128, 8 * 256], BF16, ph)
            wkv = wload(wkb, 256, D[wname][:, kc0:kc0 + 256].rearrange("(k p) n -> p k n", p=128))
            kTd = [S.sb(f"kTd{g}", [128, NT * 128], BF16, ph) for g in range(2)]
            vb = S.sb("vb", [128, NT, 2, 66], BF16, ph)
            S.op("dve", [], [vb], lambda e: e.memset(vb[:, :, :, 64:65], 1.0))
            with ExitStack() as pk:
                TB = 9
                kraw = S.sb("kraw", [128, TB, 128], F32, pk)
                ksq = S.sb("ksq", [128, TB, 128], F32, pk)
                kt2 = S.sb("kt2", [128, TB, 128], F32, pk)
                kss = S.sb("kss", [128, TB * 2], F32, pk)
                knb = S.sb("knb", [128, TB, 128], BF16, pk)
                kd = S.sb("kd", [128, 2, 2, 64], BF16, pk)
                rtab = S.sbd("rtab", [128, 2, TB, 64], F32, pk)
                for t0 in range(0, NT, TB):
                    tl = list(range(t0, t0 + TB))
                    r0 = max(0, 2 - t0)
                    nl = TB - r0
                    j0 = t0 + r0 - 2
                    for cs_ in range(2):
                        S.dma("sp", rtab.sem, rtab[:, cs_, 0:nl, :], D["rope"][:, cs_, j0:j0 + nl, :], [], [rtab])
                    for i in tl:
                        g, off = tok_group(i)
                        ps = psA.get()
                        for k in range(8):
                            S.op("pe", [wkb, hTg[g]], [ps], lambda e: e.matmul(ps[:, 0:256], lhsT=hTg[g][:, k, off:off + 128], rhs=wkv[:, k, :], start=(k == 0), stop=(k == 7)))
                        S.op("act", [ps], [vb], lambda e: e.activation(out=vb[:, i, :, 0:64], in_=ps[:, 128:256].rearrange("p (g d) -> p g d", g=2), func=AF.Copy))
                        S.op("dve", [ps], [kraw], lambda e: e.tensor_copy(out=kraw[:, i - t0, :], in_=ps[:, 0:128]))
                    prep_batch(kraw, ksq, kss, TB, 2, swag[:, 64:128], None, None, inplace=True)
                    if r0 > 0:
                        S.op("act", [kraw], [knb], lambda e: e.activation(out=knb[:, 0:r0, :], in_=kraw[:, 0:r0, :], func=AF.Copy))
                    v4 = lambda ap: ap.rearrange("p t (h d) -> p t h d", d=64)
                    cosb = rtab[:, 0, 0:nl, :].unsqueeze(2).to_broadcast([128, nl, 2, 64])
                    S.op("dve", [kraw, rtab], [ksq], lambda e: e.tensor_tensor(out=v4(ksq[:, r0:TB, :]), in0=v4(kraw[:, r0:TB, :]), in1=cosb, op=ALU.mult))
                    v6 = lambda ap: ap.rearrange("p t (h x y d) -> p t h x y d", h=2, x=2, y=2)
                    s5 = rtab[:, 1, 0:nl, :].rearrange("p t (x y d) -> p t x y d", x=2, y=2)
                    for hh_ in range(2):
                        for y in range(2):
                            S.op("dve", [kraw, rtab], [kt2], lambda e: e.tensor_tensor(
                                out=v6(kt2[:, r0:TB, :])[:, :, hh_, :, y, :], in0=v6(kraw[:, r0:TB, :])[:, :, hh_, :, 1 - y, :],
                                in1=s5[:, :, :, y, :], op=ALU.mult))
                    S.op("dve", [ksq, kt2], [knb], lambda e: e.tensor_tensor(out=knb[:, r0:TB, :], in0=ksq[:, r0:TB, :], in1=kt2[:, r0:TB, :], op=ALU.add))
                    for i in tl:
                        S.op("dve", [knb], [kd], lambda e: e.tensor_copy(out=kd[:], in_=knb[:, i - t0, :].rearrange("p (g d) -> p g d", g=2).unsqueeze(2).to_broadcast([128, 2, 2, 64])))
                        pt = psT.get()
                        for g2 in range(2):
                            S.op("pe", [kd, identb], [pt], lambda e: e.transpose(out=pt[:, g2, :], in_=kd[:, g2].rearrange("p a d -> p (a d)"), identity=identb[:]))
                        S.op("act", [pt], [kTd[0]], lambda e: e.activation(out=kTd[0][:, i * 128:(i + 1) * 128], in_=pt[:, 0, :], func=AF.Copy))
                        S.op("act", [pt], [kTd[1]], lambda e: e.activation(out=kTd[1][:, i * 128:(i + 1) * 128], in_=pt[:, 1, :], func=AF.Copy))
                S.barrier()
            wqb = S.sbd(f"wqq{l}", [128, 8 * 512], BF16, ph)
            wq = wload(wqb, 512, D[wname][:, qc0:qc0 + 512].rearrange("(k p) n -> p k n", p=128))
            wout = S.sbd(f"wout{l}", [128, 8 * 1024], BF16, ph)
            woutv = wout[:, :].rearrange("p (k n) -> p k n", k=8)
            S.dma("pool", wout.sem, woutv, D["w_out"][l].rearrange("(k p) n -> p k n", p=128), [], [wout])
            wk = (S.sb("wk_sq", [128, 512], F32, ph), S.sb("wk_ss", [128, 8], F32, ph), S.sb("wk_qn", [128, 512], F32, ph), S.sb("wk_t1", [128, 512], F32, ph))
            import os
            KS_ = os.environ.get("KSUB", "")
            if KS_ == "w1" or (KS_ == "g1k" and l == 1):
                return
            qb = S.sb("qb", [128, 8, 64], BF16, ph)
            qz = S.sb("qz", [128, 2, 4, 128], BF16, ph)
            S.op("dve", [], [qz], lambda e: e.memset(qz[:], 0.0))
            wmb4 = S.sb("wmb4", [128, 2, 4, 128], BF16, ph)
            S.op("dve", [wmb], [wmb4], lambda e: e.tensor_copy(out=wmb4[:], in_=wmb[:, :, :].unsqueeze(2).to_broadcast([128, 2, 4, 128])))
            PTr = Ring([S.sb(f"PTw{i}", [128, 512], BF16, ph) for i in range(2)])
            den = S.sb("wden", [128, 8], F32, ph)
            mixb = S.sb("mixb", [128, 512], BF16, ph)
            mixTb = S.sb("mixTb", [128, 4, 128], BF16, ph)
            def emit_qprep(i):
                g, off = tok_group(i)
                lat = i >= 2
                j = i - 2
                ps = psA.get()
                for k in range(8):
                    S.op("pe", [wqb, hTg[g]], [ps], lambda e: e.matmul(ps[:, 0:512], lhsT=hTg[g][:, k, off:off + 128], rhs=wq[:, k, :], start=(k == 0), stop=(k == 7)))
                qk_prep(ps, ps[:, 0:512], 8, gq[:], j if lat else None, (qb, qb[:]), wk, get_rope(j) if lat else None)

            qtiles = list(range(NT) if l == 0 else range(2, NT))
            emit_qprep(qtiles[0])
            for qi, i in enumerate(qtiles):
                g, off = tok_group(i)
                lat = i >= 2
                j = i - 2
                pt = psT.get()
                for pr in range(4):
                    S.op("pe", [qb, identb], [pt], lambda e: e.transpose(out=pt[:, pr, :], in_=qb[:, 2 * pr:2 * pr + 2, :].rearrange("p a d -> p (a d)"), identity=identb[:]))
                S.op("act", [pt], [qz], lambda e: e.activation(out=qz[0:64, 0, :, :], in_=pt[0:64, 0:4, :], func=AF.Copy))
                S.op("dve", [pt], [qz], lambda e: e.tensor_copy(out=qz[64:128, 1, :, :], in_=pt[64:128, 0:4, :]))
                if qi + 1 < len(qtiles):
                    emit_qprep(qtiles[qi + 1])
                if KS_ == "w2a":
                    return
                if l == 1:
                    blocks = [(m, None) for m in range(NT)]
                elif lat:
                    blocks = [(0, None), (1, None)]
                    if j > 0:
                        blocks.append((i - 1, 0))
                    blocks.append((i, None))
                    if j < 15:
                        blocks.append((i + 1, 1))
                else:
                    blocks = [(0, None), (1, None)]
                for g2 in range(2):
                    acc = psO.get()

                    def emit_scores(m, msk):
                        pss = psS.get()
                        S.op("pe", [kTd[g2], qz], [pss], lambda e: e.matmul(
                            pss[:, 0:512].rearrange("p (h c) -> p h c", h=2), lhsT=kTd[g2][:, m * 128:(m + 1) * 128],
                            rhs=qz[:, :, 2 * g2:2 * g2 + 2, :].rearrange("p h a q -> p h (a q)"),
                            start=True, stop=(msk is None)))
                        if msk is not None:
                            S.op("pe", [identb, wmb4], [pss], lambda e: e.matmul(pss[:, 0:512], lhsT=identb[:], rhs=wmb4[:, msk, :, :].rearrange("p a q -> p (a q)"), start=False, stop=True))
                        return pss

                    nxt = emit_scores(*blocks[0])
                    for bi, (m, msk) in enumerate(blocks):
                        pss = nxt
                        if bi + 1 < len(blocks):
                            nxt = emit_scores(*blocks[bi + 1])
                        PT = PTr.get()
                        S.op("act", [pss], [PT], lambda e: e.activation(out=PT[:], in_=pss[:], func=AF.Exp))
                        for hh in range(4):
                            S.op("pe", [PT, vb], [acc], lambda e: e.matmul(acc[:, hh * 128:hh * 128 + 65], lhsT=PT[:, hh * 128:(hh + 1) * 128], rhs=vb[:, m, g2, 0:65], start=(bi == 0 and hh == 0), stop=(bi == len(blocks) - 1)))
                    if KS_ == "w2c":
                        return
                    a3 = acc[:, :].rearrange("p (h c) -> p h c", h=4)
                    S.op("dve", [acc, esink], [den], lambda e: e.tensor_tensor(out=den[:, g2 * 4:(g2 + 1) * 4].rearrange("p (b a) -> p b a", b=2), in0=a3[:, :, 64].rearrange("p (b a) -> p b a", b=2),
                                                                            in1=esink[:, g2 * 4:(g2 + 1) * 4].rearrange("p (a b) -> p b a", a=2), op=ALU.add))
                    S.op("dve", [den], [den], lambda e: e.reciprocal(out=den[:, g2 * 4:(g2 + 1) * 4], in_=den[:, g2 * 4:(g2 + 1) * 4]))
                    S.op("dve", [acc, den], [mixb], lambda e: e.tensor_tensor(
                        out=mixb[:, g2 * 256:(g2 + 1) * 256].rearrange("p (a b d) -> p b a d", a=2, b=2), in0=a3[:, :, 0:64].rearrange("p (b a) d -> p b a d", b=2),
                        in1=den[:, g2 * 4:(g2 + 1) * 4].rearrange("p (b a) -> p b a", b=2).unsqueeze(3).to_broadcast([128, 2, 2, 64]), op=ALU.mult))
                if KS_ == "w2d":
                    return
                pt2 = psT.get()
                for c in range(4):
                    S.op("pe", [mixb, identb], [pt2], lambda e: e.transpose(out=pt2[:, c, :], in_=mixb[:, c * 128:(c + 1) * 128], identity=identb[:]))
                S.op("act", [pt2], [mixTb], lambda e: e.activation(out=mixTb[:], in_=pt2[:, 0:4, :], func=AF.Copy))
                if (KS_ == "w2" and i == 2) or (KS_ == "g1q" and l == 1 and i == 3):
                    return
                for cgi in range(2):
                    pso = psA.get()
                    for k in range(8):
                        if l == 0:
                            lhs = other[:, k, i * 128:(i + 1) * 128] if k < 4 else mixTb[:, k - 4, :]
                        else:
                            lhs = mixTb[:, k, :] if k < 4 else other[:, k - 4, j * 128:(j + 1) * 128]
                        S.op("pe", [other, mixTb, wout], [pso], lambda e: e.matmul(pso[:, 0:512], lhsT=lhs, rhs=woutv[:, k, cgi * 512:(cgi + 1) * 512], start=(k == 0), stop=(k == 7)))
                    residual(i, pso, cgi, 0 if lat else 1)

        def ffn_phase(l, tiles):
            tiles = list(tiles)
            with ExitStack() as ph:
                hTg = [S.sb(f"hF{l}_0", [128, 8, 256], BF16, ph)] + [S.sb(f"hF{l}_{g}", [128, 8, 512], BF16, ph) for g in range(1, 5)]
                with ExitStack() as ph2:
                    norm_phase(l, 1, hTg, ph2, tiles)
                    mk_gate(l, 40, ph2)
                    S.barrier()
                segs = [gg for gg in GROUPS if (gg[0] > 0 or 0 in tiles)]
                lo = segs[0][1]
                ranges = ([(0, 256)] if lo == 0 else []) + [(256, 2304)]
                GS = 3
                ur = Ring([S.sb(f"fu{l}_{i}", [128, 2304], F32, ph) for i in range(2)])
                yr = Ring([S.sb(f"fy{l}_{i}", [128, 2304], F32, ph) for i in range(2)])
                actT = S.sb(f"actT{l}", [128, GS, 2304], BF16, ph)
                wur = Ring([S.sbd(f"wu{l}_{i}", [128, 8 * 256], BF16, ph) for i in range(3)])
                wdr = Ring([S.sbd(f"wd{l}_{i}", [128, GS * 1024], BF16, ph) for i in range(2)])
                has_ctx = (lo == 0)
                wdraw = Ring([S.sb(f"wdraw{l}_{i}", [128, GS * 1024], BF16, ph) for i in range(1)]) if has_ctx else None
                upsrc = D["ffn_up"][l].rearrange("(k p) (g c n) -> p k g c n", p=128, g=2, c=22)
                wu_loaded = {}
                wd_loaded = {}

                def load_wu(cp):
                    if cp >= 22 or cp in wu_loaded:
                        return
                    wub = wur.get()
                    wu = wub[:, :].rearrange("p (k g n) -> p k g n", k=8, g=2)
                    for g3 in range(2):
                        S.dma("pool", wub.sem, wu[:, :, g3, :], upsrc[:, :, g3, cp, :], [], [wub])
                    wu_loaded[cp] = (wub, wu)

                def load_wd(c0):
                    if c0 >= 22 or c0 in wd_loaded:
                        return
                    ncg = min(GS, 22 - c0)
                    wdb = wdr.get()
                    wd = wdb[:, 0:ncg * 1024].rearrange("p (c n) -> p c n", c=ncg)
                    S.dma("pool", wdb.sem, wd, D["ffn_down"][l, c0 * 128:(c0 + ncg) * 128, :].rearrange("(c p) n -> p c n", p=128), [], [wdb])
                    wd_loaded[c0] = (wdb, wd)

                def scale_wd(c0):
                    ncg = min(GS, 22 - c0)
                    wdb, wd = wd_loaded[c0]
                    raw = None
                    if has_ctx:
                        rb = wdraw.get()
                        raw = rb[:, 0:ncg * 1024].rearrange("p (c n) -> p c n", c=ncg)
                        S.op("pool", [wdb], [rb], lambda e: e.tensor_copy(out=raw, in_=wd))
                        wd_loaded[c0] = (wdb, wd, rb, raw)
                    S.op("pool", [wdb, gbc], [wdb], lambda e: e.tensor_tensor(out=wd, in0=wd, in1=gbc[:, 0, :].unsqueeze(1).to_broadcast([128, ncg, 1024]), op=ALU.mult))
                    if not has_ctx:
                        wd_loaded[c0] = (wdb, wd, None, None)

                load_wu(0)
                load_wu(1)
                load_wd(0)

                def emit_up1(cp):
                    load_wu(cp + 2)
                    wub, wu = wu_loaded[cp]
                    ys = []
                    for gv in range(2):
                        ch = gv * 22 + cp
                        u = ur.get()
                        y = yr.get()
                        w0 = cw[:, l, 0, ch:ch + 1]
                        w1 = cw[:, l, 1, ch:ch + 1]
                        w2 = cw[:, l, 2, ch:ch + 1]
                        for (g, t0, n) in segs:
                            ps = psA.get()
                            for k in range(8):
                                S.op("pe", [wub, hTg[g]], [ps], lambda e: e.matmul(ps[:, 0:n], lhsT=wu[:, k, gv, :], rhs=hTg[g][:, k, 0:n], start=(k == 0), stop=(k == 7)))
                            S.op("act", [ps], [u], lambda e: e.activation(out=u[:, t0:t0 + n], in_=ps[:, 0:n], func=AF.Copy))
                        S.op("act", [u, cw, cb], [y], lambda e: e.activation(out=y[:, lo:2304], in_=u[:, lo:2304], func=AF.Identity, scale=w1, bias=cb[:, l, ch:ch + 1]))
                        for (a, b_) in ranges:
                            S.op("dve", [u, cw, y], [y], lambda e: e.scalar_tensor_tensor(out=y[:, a + 1:b_], in0=u[:, a:b_ - 1], scalar=w0, in1=y[:, a + 1:b_], op0=ALU.mult, op1=ALU.add))
                            S.op("dve", [u, cw, y], [y], lambda e: e.scalar_tensor_tensor(out=y[:, a:b_ - 1], in0=u[:, a + 1:b_], scalar=w2, in1=y[:, a:b_ - 1], op0=ALU.mult, op1=ALU.add))
                        ys.append(y)
                    S.op("act", [ys[0]], [ys[0]], lambda e: e.activation(out=ys[0][:, lo:2304], in_=ys[0][:, lo:2304], func=AF.Silu))
                    return ys

                def emit_up2(ys, ci):
                    S.op("dve", [ys[0], ys[1]], [actT], lambda e: e.tensor_tensor(out=actT[:, ci, lo:2304], in0=ys[0][:, lo:2304], in1=ys[1][:, lo:2304], op=ALU.mult))

                def emit_down(c0):
                    ncg = min(GS, 22 - c0)
                    scale_wd(c0)
                    wdb, wd, rb, raw = wd_loaded[c0]
                    for i in tiles:
                        for cgi in range(2):
                            ps = psO.get()
                            if i >= 2:
                                for ci in range(ncg):
                                    S.op("pe", [actT, wdb], [ps], lambda e: e.matmul(ps[:, 0:512], lhsT=actT[:, ci, i * 128:(i + 1) * 128], rhs=wd[:, ci, cgi * 512:(cgi + 1) * 512], start=(ci == 0), stop=(ci == ncg - 1)))
                                S.op("dve", [ps, xs[i]], [xs[i]], lambda e: e.tensor_tensor(out=xs[i][:, cgi * 512:(cgi + 1) * 512], in0=ps[:], in1=xs[i][:, cgi * 512:(cgi + 1) * 512], op=ALU.add))
                            else:
                                for ci in range(ncg):
                                    S.op("pe", [actT, rb], [ps], lambda e: e.matmul(ps[:, 0:512], lhsT=actT[:, ci, i * 128:(i + 1) * 128], rhs=raw[:, ci, cgi * 512:(cgi + 1) * 512], start=(ci == 0), stop=(ci == ncg - 1)))
                                residual(i, ps, cgi, 1)

                pending = None
                for c0 in range(0, 22, GS):
                    ncg = min(GS, 22 - c0)
                    ys0 = emit_up1(c0)
                    if pending is not None:
                        emit_down(pending)
                    load_wd(c0 + GS)
                    emit_up2(ys0, 0)
                    for ci in range(1, ncg):
                        emit_up2(emit_up1(c0 + ci), ci)
                    pending = c0
                emit_down(pending)
                S.barrier()

        def na_attn(hTg, mixTd, wring, ph):
            nag = S.sbd("nag", [128, 128], F32, ph)
            S.dma("sp", nag.sem, nag[:], D["na_g_bc"], [], [nag])
            gq = S.sb("nagq", [128, 64], F32, ph)
            S.op("dve", [nag], [gq], lambda e: e.tensor_scalar(out=gq[:], in0=nag[:, 0:64], scalar1=0.125, scalar2=None, op0=ALU.mult))
            kTn = S.sb("kTn", [128, NT * 128], BF16, ph)
            vn = S.sb("vn", [128, NT, 2, 66], BF16, ph)
            qTn = S.sb("qTn", [128, 2048], BF16, ph)
            S.op("dve", [], [vn], lambda e: e.memset(vn[:, :, :, 64:65], 1.0))
            TB = 9
            raw = S.sb("nraw", [128, TB, 128], F32, ph)
            sq = S.sb("nsq", [128, TB, 128], F32, ph)
            ssb = S.sb("nss", [128, TB * 2], F32, ph)
            nrm = S.sb("nnrm", [128, TB, 128], BF16, ph)
            biasr = Ring([S.sbd(f"nbias{i}", [128, 25, 128], F32, ph) for i in range(1)])
            stmp = Ring([S.sb(f"nstmp{i}", [128, 5, 128], F32, ph) for i in range(2)])
            PTr = Ring([S.sb(f"PTn{i}", [128, 7, 128], BF16, ph) for i in range(3)])
            rdn = Ring([S.sb(f"nrd{i}", [128, 1], F32, ph) for i in range(3)])
            mixd = S.sb("mixd", [128, 16, 2, 64], BF16, ph)
            naS = Ring(psS.bufs + psA.bufs)
            wsrc = D["odd_w"][:, 768:2304].rearrange("(k p) (g h n) -> p k g h n", p=128, g=3, h=4)
            wl = {}

            def load_w(pr):
                if pr >= 4 or pr in wl:
                    return
                wb = wring.get()
                wq = wb[:, 0:8 * 384].rearrange("p (k g n) -> p k g n", k=8, g=3)
                for g3 in range(3):
                    S.dma("pool", wb.sem, wq[:, :, g3, :], wsrc[:, :, g3, pr, :], [], [wb])
                wl[pr] = (wb, wq)

            load_w(0)
            for pr in range(4):
                wb, wq = wl[pr]
                load_w(pr + 1)
                for t0 in range(0, NT, TB):
                    tl = list(range(t0, min(NT, t0 + TB)))
                    for i in tl:
                        g, off = tok_group(i)
                        ps = psA.get()
                        for k in range(8):
                            S.op("pe", [wb, hTg[g]], [ps], lambda e: e.matmul(ps[:, 0:256], lhsT=hTg[g][:, k, off:off + 128], rhs=wb[:, k * 384 + 128:k * 384 + 384], start=(k == 0), stop=(k == 7)))
                        S.op("act", [ps], [vn], lambda e: e.activation(out=vn[:, i, :, 0:64], in_=ps[:, 128:256].rearrange("p (g d) -> p g d", g=2), func=AF.Copy))
                        S.op("dve", [ps], [raw], lambda e: e.tensor_copy(out=raw[:, i - t0, :], in_=ps[:, 0:128]))
                    prep_batch(raw, sq, ssb, len(tl), 2, nag[:, 64:128], nrm, nrm[:, 0:len(tl), :])
                    for i in tl:
                        pt = psT.get()
                        S.op("pe", [nrm, identb], [pt], lambda e: e.transpose(out=pt[:, 0, :], in_=nrm[:, i - t0, :], identity=identb[:]))
                        S.op("act", [pt], [kTn], lambda e: e.activation(out=kTn[:, i * 128:(i + 1) * 128], in_=pt[:, 0, :], func=AF.Copy))
                for t0 in range(2, NT, 8):
                    tl = list(range(t0, t0 + 8))
                    for i in tl:
                        g, off = tok_group(i)
                        ps2 = psA.get()
                        for k in range(8):
                            S.op("pe", [wb, hTg[g]], [ps2], lambda e: e.matmul(ps2[:, 0:128], lhsT=hTg[g][:, k, off:off + 128], rhs=wq[:, k, 0, :], start=(k == 0), stop=(k == 7)))
                        S.op("dve", [ps2], [raw], lambda e: e.tensor_copy(out=raw[:, i - t0, :], in_=ps2[:, 0:128]))
                    prep_batch(raw, sq, ssb, 8, 2, gq[:], nrm, nrm[:, 0:8, :])
                    for i in tl:
                        j = i - 2
                        pt2 = psT.get()
                        S.op("pe", [nrm, identb], [pt2], lambda e: e.transpose(out=pt2[:, 0, :], in_=nrm[:, i - t0, :], identity=identb[:]))
                        S.op("dve", [pt2], [qTn], lambda e: e.tensor_copy(out=qTn[:, j * 128:(j + 1) * 128], in_=pt2[:, 0, :]))
                for hh in range(2):
                    head = 2 * pr + hh
                    bt = biasr.get()
                    S.dma("sp", bt.sem, bt[:], D["na_bias"][head], [], [bt])
                    prs = slice(hh * 64, (hh + 1) * 64)

                    def blocks_of(j):
                        ci = 0 if j == 0 else 1 if j == 1 else 3 if j == 14 else 4 if j == 15 else 2
                        mlist = list(range(j - 2, j + 3)) if ci == 2 else na_blocks[j]
                        return ci, len(mlist), [0, 1] + [m + 2 for m in mlist]

                    def emit_scores(j):
                        ci, nb, keyt = blocks_of(j)
                        pA = naS.get()
                        pB = naS.get()
                        for bi, kt in enumerate(keyt):
                            pp, off2 = (pA, bi) if bi < 4 else (pB, bi - 4)
                            S.op("pe", [kTn, qTn], [pp], lambda e: e.matmul(pp[:, off2 * 128:(off2 + 1) * 128], lhsT=kTn[prs, kt * 128:(kt + 1) * 128], rhs=qTn[prs, j * 128:(j + 1) * 128], start=True, stop=True))
                        return pA, pB

                    def emit_norm(acc_, j_):
                        rd = rdn.get()
                        S.op("dve", [acc_], [rd], lambda e: e.reciprocal(out=rd[:], in_=acc_[:, 64:65]))
                        S.op("act", [acc_, rd], [mixd], lambda e: e.activation(out=mixd[:, j_, hh, :], in_=acc_[:, 0:64], func=AF.Copy, scale=rd[:, 0:1]))

                    pend_norm = None
                    nxt = emit_scores(0)
                    for j in range(16):
                        ci, nb, keyt = blocks_of(j)
                        pA, pB = nxt
                        if j + 1 < 16:
                            nxt = emit_scores(j + 1)
                        stp = stmp.get()
                        S.op("dve", [pA, bt], [stp], lambda e: e.tensor_tensor(out=stp[:, 0:2, :], in0=pA[:, 256:512].rearrange("p (b q) -> p b q", b=2), in1=bt[:, ci * 5:ci * 5 + 2, :], op=ALU.add))
                        S.op("dve", [pB, bt], [stp], lambda e: e.tensor_tensor(out=stp[:, 2:nb, :], in0=pB[:, 0:(nb - 2) * 128].rearrange("p (b q) -> p b q", b=nb - 2), in1=bt[:, ci * 5 + 2:ci * 5 + nb, :], op=ALU.add))
                        PT = PTr.get()
                        S.op("act", [pA], [PT], lambda e: e.activation(out=PT[:, 0:2, :], in_=pA[:, 0:256].rearrange("p (b q) -> p b q", b=2), func=AF.Exp))
                        S.op("act", [stp], [PT], lambda e: e.activation(out=PT[:, 2:2 + nb, :], in_=stp[:, 0:nb, :], func=AF.Exp))
                        acc = psO.get()
                        for bi, kt in enumerate(keyt):
                            S.op("pe", [PT, vn], [acc], lambda e: e.matmul(acc[:, 0:65], lhsT=PT[:, bi, :], rhs=vn[:, kt, hh, 0:65], start=(bi == 0), stop=(bi == len(keyt) - 1)))
                        if pend_norm is not None:
                            emit_norm(*pend_norm)
                        pend_norm = (acc, j)
                    emit_norm(*pend_norm)
                    pend_norm = None
                for j in range(16):
                    pt = psT.get()
                    S.op("pe", [mixd, identb], [pt], lambda e: e.transpose(out=pt[:, 0, :], in_=mixd[:, j, :, :].rearrange("p a d -> p (a d)"), identity=identb[:]))
                    S.op("dve", [pt], [mixTd], lambda e: e.tensor_copy(out=mixTd[:, pr, j * 128:(j + 1) * 128], in_=pt[:, 0, :]))

        def mixer1():
            with ExitStack() as ph:
                hTg = [S.sb("hT1_0", [128, 8, 256], BF16, ph)] + [S.sb(f"hT1_{g}", [128, 8, 512], BF16, ph) for g in range(1, 5)]
                with ExitStack() as ph2:
                    norm_phase(1, 0, hTg, ph2)
                    mk_gate(1, 16, ph2)
                    S.barrier()
                mixTd = S.sb("mixTd", [128, 4, 2048], BF16, ph)
                with ExitStack() as ph2:
                    wring = Ring([S.sbd(f"w1_{i}", [128, 8 * 384], BF16, ph2) for i in range(2)])
                    na_attn(hTg, mixTd, wring, ph2)
                    S.barrier()
                import os
                if os.environ.get("KSUB") == "nogqa1":
                    return
                with ExitStack() as ph2:
                    gqa_attn(1, hTg, mixTd, None, ph2)
                    S.barrier()

        mod_phase(0)
        if stage != "mod":
            mixer0()
        if stage not in ("l0mix", "mod", "norm", "mlstm"):
            ffn_phase(0, range(NT))
        if stage not in ("l0mix", "l0", "mod", "norm", "mlstm"):
            mod_phase(1)
            mixer1()
            if stage != "l1mix":
                ffn_phase(1, range(2, NT))
        osem = S.newsem("d_out")
        for i in range(2, NT):
            S.dma("sp", osem, out[(i - 2) * 128:(i - 1) * 128, :], xs[i][:], [xs[i]], [])
        if dbg:
            for i in range(2):
                S.dma("sp", osem, octx[i * 128:(i + 1) * 128, :], xs[i][:], [xs[i]], [])
        S._need("sp", osem, S.cnt[osem])
        S.barrier()
        print(f"[kernel] instructions={S.ninst} waits={S.nwait} sems={len(S.sems)}", flush=True)
    return nc


_CACHE = {}


def kernel(**inputs):
    shared, percore = host_prepare({k: np.asarray(v) for k, v in inputs.items()})
    if "nc" not in _CACHE:
        _CACHE["nc"] = build_program("full")
    nc = _CACHE["nc"]
    in_maps = []
    for b in range(8):
        m = dict(shared)
        m.update(percore[b])
        in_maps.append(m)
    res = run_bass_kernel_spmd(nc, in_maps, core_ids=list(range(8)))
    return np.stack([np.asarray(r["out"], np.float32) for r in res.results], axis=0)
```

```python
import numpy as np
from contextlib import ExitStack
import concourse.bass as bass
import concourse.mybir as mybir
from concourse.bass_utils import run_bass_kernel_spmd

F32 = mybir.dt.float32
BF16 = mybir.dt.bfloat16
AF = mybir.ActivationFunctionType
ALU = mybir.AluOpType
AX = mybir.AxisListType

ENGS = ("pe", "act", "dve", "pool", "sp")
NT = 18
EPS = 1e-6
NEGM = -30000.0


class Buf:
    __slots__ = ("t", "name", "w", "r", "sem", "psum")

    def __init__(self, t, name):
        self.t = t
        self.name = name
        self.w = None
        self.r = {}
        self.sem = None
        self.psum = False

    def __getitem__(self, idx):
        return self.t[idx]


class Ring:
    def __init__(self, bufs):
        self.bufs = bufs
        self.i = 0

    def get(self):
        b = self.bufs[self.i % len(self.bufs)]
        self.i += 1
        return b


SEM_LIMIT = 1500


class Sched:
    def __init__(self, nc, stack):
        self.nc = nc
        self.stack = stack
        self.eng = {"pe": nc.tensor, "act": nc.scalar, "dve": nc.vector,
                    "pool": nc.gpsimd, "sp": nc.sync}
        self.sems = {}
        self.cnt = {}
        self.epoch = {}
        self.cur = {}
        for e in ENGS:
            self.epoch[e] = 0
            self._new_epoch(e)
        self.seen = {e: {} for e in ENGS}
        self.ninst = 0
        self.nwait = 0
        self.nsem = 0
        self.nalloc = 0

    def _new_epoch(self, e):
        self.epoch[e] += 1
        key = f"{e}#{self.epoch[e]}"
        self.sems[key] = self.stack.enter_context(self.nc.semaphore("s_" + key.replace("#", "_")))
        self.cnt[key] = 0
        self.cur[e] = key

    def sb(self, name, shape, dt, stack=None):
        self.nalloc += 1
        name = f"{name}_{self.nalloc}"
        t = (stack or self.stack).enter_context(self.nc.sbuf_tensor(name, list(shape), dt))
        return Buf(t, name)

    def ps(self, name, shape, dt=F32):
        t = self.stack.enter_context(self.nc.psum_tensor(name, list(shape), dt))
        b = Buf(t, name)
        b.psum = True
        return b

    def newsem(self, name=None):
        self.nsem += 1
        name = name or f"d{self.nsem}"
        s = self.stack.enter_context(self.nc.semaphore(name))
        self.sems[name] = s
        self.cnt[name] = 0
        return name

    def sbd(self, name, shape, dt, stack=None):
        b = self.sb(name, shape, dt, stack)
        b.sem = self.newsem("d_" + b.name)
        return b

    @staticmethod
    def _eng_of(key):
        return key.split("#")[0] if "#" in key else None

    def _need(self, e, key, val):
        if self.seen[e].get(key, 0) >= val:
            return
        ke = self._eng_of(key)
        if ke is not None:
            ep = int(key.split("#")[1])
            for k2, v2 in self.seen[e].items():
                if v2 > 0 and self._eng_of(k2) == ke and int(k2.split("#")[1]) > ep:
                    return
        self.seen[e][key] = val
        self.eng[e].wait_ge(self.sems[key], val)
        self.nwait += 1

    def deps(self, e, reads, writes):
        for b in reads:
            if b.w is not None:
                k, v = b.w
                if not (self._eng_of(k) == e and e == "pe"):
                    self._need(e, k, v)
        for b in writes:
            if b.w is not None:
                k, v = b.w
                if self._eng_of(k) != e:
                    self._need(e, k, v)
            for k, v in b.r.items():
                if self._eng_of(k) != e:
                    self._need(e, k, v)

    def op(self, e, reads, writes, fn):
        pr = [b for b in reads if b.psum]
        if pr:
            reads = [b for b in reads if not b.psum]
            writes = list(writes) + [b for b in pr if b not in writes]
        self.deps(e, reads, writes)
        ins = fn(self.eng[e])
        if self.cnt[self.cur[e]] >= SEM_LIMIT:
            self._new_epoch(e)
        key = self.cur[e]
        self.cnt[key] += 1
        ins.then_inc(self.sems[key], 1)
        v = self.cnt[key]
        for b in reads:
            for k2 in [k2 for k2 in b.r if self._eng_of(k2) == e]:
                del b.r[k2]
            b.r[key] = v
        for b in writes:
            b.w = (key, v)
            b.r = {}
        self.ninst += 1
        return ins

    def dma(self, q, semkey, out_ap, in_ap, reads, writes, **kw):
        self.deps(q, reads, writes)
        ins = self.eng[q].dma_start(out=out_ap, in_=in_ap, **kw)
        self.cnt[semkey] += 16
        assert self.cnt[semkey] <= 2000, semkey
        ins.then_inc(self.sems[semkey], 16)
        v = self.cnt[semkey]
        for b in reads:
            b.r[semkey] = v
        for b in writes:
            b.w = (semkey, v)
            b.r = {}
        self.ninst += 1
        return ins

    def barrier(self):
        for e in ENGS:
            for k, v in list(self.cnt.items()):
                ke = self._eng_of(k)
                if ke == e or v == 0:
                    continue
                if ke is not None and k != self.cur[ke]:
                    if not (self.cnt[self.cur[ke]] == 0 and int(k.split("#")[1]) == self.epoch[ke] - 1):
                        continue
                self._need(e, k, v)


def _rope_tables():
    t = np.arange(2048)
    row = (t // 64).astype(np.float32)
    col = (t % 64).astype(np.float32)
    half = 32
    freq = (np.float32(10000.0) ** (-np.arange(0, half, 2, dtype=np.float32) / np.float32(half))).astype(np.float32)
    ang_r = row[:, None] * freq[None, :]
    ang_c = col[:, None] * freq[None, :]
    ang = np.concatenate([ang_r, ang_r, ang_c, ang_c], axis=-1).astype(np.float32)
    cos = np.cos(ang).astype(np.float32)
    sin = np.sin(ang).astype(np.float32)
    sgn = np.ones(64, np.float32)
    sgn[0:16] = -1.0
    sgn[32:48] = -1.0
    sinS = sin * sgn[None, :]
    cos = cos.reshape(16, 128, 64).transpose(1, 0, 2).copy()
    sinS = sinS.reshape(16, 128, 64).transpose(1, 0, 2).copy()
    return cos, sinS


def _na_tables(rpb):
    rows = 32
    wr = 8
    r = np.arange(rows)
    row_start = np.clip(r - wr // 2, 0, rows - wr)
    col = np.arange(64)
    col_start = np.clip(col - 8, 0, 48)
    col_ok = (col[None, :] >= col_start[:, None]) & (col[None, :] < col_start[:, None] + 16)
    dc = np.clip(col[None, :] - col[:, None] + 15, 0, 30)
    classes = [0, 1, 2, 14, 15]
    blocks = {}
    tab = np.full((8, 128, 25, 128), NEGM, np.float32)
    for ci, j in enumerate(classes):
        qrows = [2 * j, 2 * j + 1]
        lo = min(row_start[q] for q in qrows)
        hi = max(row_start[q] + wr - 1 for q in qrows)
        mlist = list(range(lo // 2, hi // 2 + 1))
        assert len(mlist) <= 5
        blocks[j] = mlist
        for si, m in enumerate(mlist):
            for kr in range(2):
                krow = 2 * m + kr
                for qr in range(2):
                    qrow = qrows[qr]
                    if not (row_start[qrow] <= krow < row_start[qrow] + wr):
                        continue
                    dr = krow - qrow + 7
                    sub = rpb[:, dr, :][:, dc]
                    sub = np.where(col_ok[None], sub, np.float32(NEGM))
                    tab[:, kr * 64:(kr + 1) * 64, ci * 5 + si, qr * 64:(qr + 1) * 64] = sub.transpose(0, 2, 1)
    return tab, blocks, classes


def _na_blocks():
    _, blocks, classes = _na_tables(np.zeros((8, 15, 31), np.float32))
    return blocks, classes


def host_prepare(inp):
    f = np.float32
    shared = {}
    shared["ada_w"] = np.ascontiguousarray(inp["ada_w"], f)
    shared["ada_bT"] = np.ascontiguousarray(inp["ada_b"].reshape(2, 48, 128).transpose(2, 0, 1), f)
    shared["norm_gT"] = np.ascontiguousarray(inp["norm_g"].reshape(2, 2, 8, 128).transpose(3, 0, 1, 2), f)
    shared["w_out"] = np.ascontiguousarray(inp["w_out"], f)
    shared["ffn_up"] = np.ascontiguousarray(inp["ffn_up"], f)
    shared["ffn_down"] = np.ascontiguousarray(inp["ffn_down"], f)
    shared["conv_wT"] = np.ascontiguousarray(inp["ffn_conv_w"].reshape(2, 3, 44, 128).transpose(3, 0, 1, 2), f)
    shared["conv_bT"] = np.ascontiguousarray(inp["ffn_conv_b"].reshape(2, 44, 128).transpose(2, 0, 1), f)
    shared["even_w"] = np.ascontiguousarray(inp["even_w_in"][0], f)
    shared["odd_w"] = np.ascontiguousarray(inp["odd_w_in"][0], f)
    bc = lambda a: np.ascontiguousarray(np.broadcast_to(np.asarray(a, f).reshape(1, -1), (128, a.size)))
    shared["gate_b_bc"] = bc(inp["mlstm_gate_b"][0])
    shared["head_g_bc"] = bc(inp["mlstm_head_g"][0])
    shared["swa_g_bc"] = bc(inp["swa_qk_g"][0])
    shared["sink_bc"] = bc(inp["swa_sink"][0])
    shared["gqa_g_bc"] = bc(inp["gqa_qk_g"][0])
    shared["na_g_bc"] = bc(inp["na_qk_g"][0])
    tab, _, _ = _na_tables(np.asarray(inp["na_rpb"][0], f))
    shared["na_bias"] = tab
    ident = np.eye(128, dtype=f)
    s = np.arange(128)
    triU = (s[:, None] <= s[None, :]).astype(f)
    triL = (s[:, None] >= s[None, :]).astype(f)
    wm = np.zeros((128, 2, 128), f)
    wm[:, 0, :] = np.where(s[None, :] <= s[:, None], 0.0, NEGM)
    wm[:, 1, :] = np.where(s[:, None] <= s[None, :], 0.0, NEGM)
    shared["consts"] = np.ascontiguousarray(np.concatenate([ident, triU, triL, wm.reshape(128, 256)], axis=1))
    cos, sinS = _rope_tables()
    shared["rope"] = np.ascontiguousarray(np.stack([cos, sinS], axis=1))
    percore = []
    for b in range(8):
        cc = np.stack([inp["c"][b].reshape(8, 128).T, inp["c_ctx"].reshape(8, 128).T], axis=-1)
        percore.append({"x": np.ascontiguousarray(inp["x"][b], f), "ctx": np.ascontiguousarray(inp["ctx"][b], f),
                        "cc": np.ascontiguousarray(cc, f)})
    return shared, percore


SHARED_SHAPES = {
    "ada_w": [2, 1024, 6144], "ada_bT": [128, 2, 48], "norm_gT": [128, 2, 2, 8], "w_out": [2, 1024, 1024],
    "ffn_up": [2, 1024, 5632], "ffn_down": [2, 2816, 1024], "conv_wT": [128, 2, 3, 44], "conv_bT": [128, 2, 44],
    "even_w": [1024, 2832], "odd_w": [1024, 2304], "gate_b_bc": [128, 16], "head_g_bc": [128, 512],
    "swa_g_bc": [128, 128], "sink_bc": [128, 8], "gqa_g_bc": [128, 128], "na_g_bc": [128, 128],
    "na_bias": [8, 128, 25, 128], "consts": [128, 640], "rope": [128, 2, 16, 64],
    "x": [2048, 1024], "ctx": [256, 1024], "cc": [128, 8, 2],
}


GROUPS = [(0, 0, 256), (1, 256, 512), (2, 768, 512), (3, 1280, 512), (4, 1792, 512)]


def tok_group(i):
    return (0, i * 128) if i < 2 else (1 + (i - 2) // 4, ((i - 2) % 4) * 128)


def build_program(stage="full"):
    nc = bass.Bass("TRN2", target_bir_lowering=False)
    D = {k: nc.dram_tensor(k, shp, F32, kind="ExternalInput").ap() for k, shp in SHARED_SHAPES.items()}
    out = nc.dram_tensor("out", [2048, 1024], F32, kind="ExternalOutput").ap()
    dbg = stage != "full"
    if dbg:
        octx = nc.dram_tensor("octx", [256, 1024], F32, kind="ExternalOutput").ap()
        dbgd = nc.dram_tensor("dbgd", [128, 8192], F32, kind="ExternalOutput").ap()
    na_blocks, na_classes = _na_blocks()

    with ExitStack() as st:
        S = Sched(nc, st)
        xs = [S.sbd(f"xs{i}", [128, 1024], F32) for i in range(NT)]
        cst = S.sbd("cst", [128, 640], F32)
        cc = S.sbd("cc", [128, 8, 2], F32)
        adab = S.sbd("adab", [128, 2, 48], F32)
        ngT = S.sbd("ngT", [128, 2, 2, 8], F32)
        cw = S.sbd("cw", [128, 2, 3, 44], F32)
        cb = S.sbd("cb", [128, 2, 44], F32)
        identb = S.sb("identb", [128, 128], BF16)
        wmb = S.sb("wmb", [128, 2, 128], BF16)
        ones_f = S.sb("ones_f", [128, 128], F32)
        ones_b = S.sb("ones_b", [128, 128], BF16)
        sc = S.sb("sc", [128, 8, 2], F32)
        modT = [S.sb(f"modT{l}", [128, 48, 2], F32) for l in range(2)]
        gbc = S.sb("gbc", [128, 2, 1024], F32)
        AB = S.sb("AB", [128, 8, 2], F32)

        psT = Ring([S.ps(f"psT{i}", [128, 8, 128], BF16) for i in range(2)])
        psA = Ring([S.ps(f"psA{i}", [128, 512], F32) for i in range(2)])
        psS = Ring([S.ps(f"psS{i}", [128, 512], F32) for i in range(2)])
        psO = Ring([S.ps(f"psO{i}", [128, 512], F32) for i in range(2)])

        IDF = lambda: cst[:, 0:128]
        TRIU = lambda: cst[:, 128:256]
        TRIL = lambda: cst[:, 256:384]

        S.dma("sp", cst.sem, cst[:], D["consts"], [], [cst])
        S.dma("sp", cc.sem, cc[:], D["cc"], [], [cc])
        S.dma("sp", adab.sem, adab[:], D["ada_bT"], [], [adab])
        S.dma("sp", ngT.sem, ngT[:], D["norm_gT"], [], [ngT])
        S.dma("sp", cw.sem, cw[:], D["conv_wT"], [], [cw])
        S.dma("sp", cb.sem, cb[:], D["conv_bT"], [], [cb])
        for i in range(NT):
            src = D["ctx"][i * 128:(i + 1) * 128, :] if i < 2 else D["x"][(i - 2) * 128:(i - 1) * 128, :]
            S.dma("sp", xs[i].sem, xs[i][:], src, [], [xs[i]])
        S.op("dve", [cst], [identb], lambda e: e.tensor_copy(out=identb[:], in_=cst[:, 0:128]))
        S.op("dve", [cst], [wmb], lambda e: e.tensor_copy(out=wmb[:], in_=cst[:, 384:640].rearrange("p (a b) -> p a b", a=2)))
        S.op("dve", [], [ones_f], lambda e: e.memset(ones_f[:], 1.0))
        S.op("dve", [], [ones_b], lambda e: e.memset(ones_b[:], 1.0))
        S.op("act", [cc], [sc], lambda e: e.activation(out=sc[:], in_=cc[:], func=AF.Silu))

        dstg = S.sb("dstg", [128, 128], F32) if dbg else None
        dstate = {"col": 0, "items": []}

        def dump(name, buf, ap, n):
            if not dbg:
                return
            stg = dstg
            sem = S.newsem()
            S.op("act", [buf], [stg], lambda e: e.activation(out=stg[:, 0:n], in_=ap, func=AF.Copy))
            c0 = dstate["col"]
            S.dma("sp", sem, dbgd[:, c0:c0 + n], stg[:, 0:n], [stg], [])
            S._need("sp", sem, S.cnt[sem])
            dstate["items"].append((name, c0, n))
            dstate["col"] = c0 + n
            print("DUMP", name, c0, n, flush=True)

        def wview(wb, shape_str, **kw):
            n = 1
            for v in kw.values():
                n *= v
            return wb

        def mod_phase(l):
            with ExitStack() as ph:
                ring = Ring([S.sbd(f"adaw{l}_{i}", [128, 8, 512], BF16, ph) for i in range(3)])
                schi = S.sb(f"schi{l}", [128, 8, 2], BF16, ph)
                schf = S.sb(f"schf{l}", [128, 8, 2], F32, ph)
                sclo = S.sb(f"sclo{l}", [128, 8, 2], BF16, ph)
                S.op("dve", [sc], [schi], lambda e: e.tensor_copy(out=schi[:], in_=sc[:]))
                S.op("dve", [schi], [schf], lambda e: e.tensor_copy(out=schf[:], in_=schi[:]))
                S.op("dve", [sc, schf], [schf], lambda e: e.tensor_tensor(out=schf[:], in0=sc[:], in1=schf[:], op=ALU.subtract))
                S.op("dve", [schf], [sclo], lambda e: e.tensor_copy(out=sclo[:], in_=schf[:]))
                wbs = {}

                def ld(cg):
                    if cg >= 12:
                        return
                    wb = ring.get()
                    S.dma("pool", wb.sem, wb[:], D["ada_w"][l, :, cg * 512:(cg + 1) * 512].rearrange("(k p) n -> p k n", p=128), [], [wb])
                    wbs[cg] = wb
                ld(0)
                ld(1)
                for cg in range(12):
                    ld(cg + 2)
                    wb = wbs[cg]
                    ps = psA.get()
                    for c4 in range(4):
                        for k in range(8):
                            S.op("pe", [wb, schi], [ps], lambda e: e.matmul(ps[:, c4 * 2:c4 * 2 + 2], lhsT=wb[:, k, c4 * 128:(c4 + 1) * 128], rhs=schi[:, k, :], start=(k == 0), stop=False))
                            S.op("pe", [wb, sclo], [ps], lambda e: e.matmul(ps[:, c4 * 2:c4 * 2 + 2], lhsT=wb[:, k, c4 * 128:(c4 + 1) * 128], rhs=sclo[:, k, :], start=False, stop=(k == 7)))
                    S.op("dve", [ps, adab], [modT[l]], lambda e: e.tensor_tensor(
                        out=modT[l][:, cg * 4:(cg + 1) * 4, :], in0=ps[:, 0:8].rearrange("p (c j) -> p c j", j=2),
                        in1=adab[:, l, cg * 4:(cg + 1) * 4].unsqueeze(2).to_broadcast([128, 4, 2]), op=ALU.add))
                S.barrier()

        def mk_AB(l, which):
            scl = 8 if which == 0 else 32
            S.op("dve", [modT[l]], [AB], lambda e: e.tensor_scalar(out=AB[:], in0=modT[l][:, scl:scl + 8, :], scalar1=1.0, scalar2=None, op0=ALU.add))
            S.op("dve", [AB, ngT], [AB], lambda e: e.tensor_tensor(out=AB[:], in0=AB[:], in1=ngT[:, l, which, :].unsqueeze(2).to_broadcast([128, 8, 2]), op=ALU.mult))

        def mk_gate(l, gchunk, ph):
            hl = S.sb(f"ghl{l}_{gchunk}", [128, 8, 2], F32, ph)
            hb = S.sb(f"ghb{l}_{gchunk}", [128, 8, 2], BF16, ph)
            hf = S.sb(f"ghf{l}_{gchunk}", [128, 8, 2], F32, ph)
            lo = S.sb(f"glo{l}_{gchunk}", [128, 8, 2], F32, ph)
            lb = S.sb(f"glb{l}_{gchunk}", [128, 8, 2], BF16, ph)
            lf = S.sb(f"glf{l}_{gchunk}", [128, 8, 2], F32, ph)
            S.op("dve", [modT[l]], [hl], lambda e: e.tensor_copy(out=hl[:], in_=modT[l][:, gchunk:gchunk + 8, :]))
            S.op("dve", [hl], [hb], lambda e: e.tensor_copy(out=hb[:], in_=hl[:]))
            S.op("dve", [hb], [hf], lambda e: e.tensor_copy(out=hf[:], in_=hb[:]))
            S.op("dve", [hl, hf], [lo], lambda e: e.tensor_tensor(out=lo[:], in0=hl[:], in1=hf[:], op=ALU.subtract))
            S.op("dve", [lo], [lb], lambda e: e.tensor_copy(out=lb[:], in_=lo[:]))
            S.op("dve", [lb], [lf], lambda e: e.tensor_copy(out=lf[:], in_=lb[:]))
            dgr = Ring([S.sb(f"dg{l}_{gchunk}_{i}", [128, 2, 128], BF16, ph) for i in range(2)])
            for j in range(2):
                for half in range(2):
                    ps = psA.get()
                    for k4 in range(4):
                        kk = half * 4 + k4
                        dg = dgr.get()
                        S.op("dve", [identb, hf], [dg], lambda e: e.tensor_scalar(out=dg[:, 0, :], in0=identb[:], scalar1=hf[:, kk, j:j + 1], scalar2=None, op0=ALU.mult))
                        S.op("dve", [identb, lf], [dg], lambda e: e.tensor_scalar(out=dg[:, 1, :], in0=identb[:], scalar1=lf[:, kk, j:j + 1], scalar2=None, op0=ALU.mult))
                        S.op("pe", [ones_b, dg], [ps], lambda e: e.matmul(ps[:, k4 * 128:(k4 + 1) * 128], lhsT=ones_b[:], rhs=dg[:, 0, :], start=True, stop=False))
                        S.op("pe", [ones_b, dg], [ps], lambda e: e.matmul(ps[:, k4 * 128:(k4 + 1) * 128], lhsT=ones_b[:], rhs=dg[:, 1, :], start=False, stop=True))
                    S.op("act", [ps], [gbc], lambda e: e.activation(out=gbc[:, j, half * 512:(half + 1) * 512], in_=ps[:], func=AF.Copy))

        def rstd_of(t, n_ap, dim):
            S.op("dve", [t], [t], lambda e: e.tensor_scalar(out=n_ap(), in0=n_ap(), scalar1=1.0 / dim, scalar2=EPS, op0=ALU.mult, op1=ALU.add))
            S.op("act", [t], [t], lambda e: e.activation(out=n_ap(), in_=n_ap(), func=AF.Ln))
            S.op("act", [t], [t], lambda e: e.activation(out=n_ap(), in_=n_ap(), func=AF.Exp, scale=-0.5))

        def norm_phase(l, which, hTg, ph, tiles=range(NT)):
            mk_AB(l, which)
            sh = 0 if which == 0 else 24
            ss = S.sb(f"nss{l}{which}", [128, NT], F32, ph)
            junk = S.sb(f"njunk{l}{which}", [128, 1024], BF16, ph)
            xnr = Ring([S.sb(f"xn{l}{which}_{i}", [128, 1024], BF16, ph) for i in range(2)])
            S.op("dve", [], [ss], lambda e: e.memset(ss[:], 1.0))
            for i in tiles:
                S.op("act", [xs[i]], [junk, ss], lambda e: e.activation(out=junk[:], in_=xs[i][:], func=AF.Square, accum_out=ss[:, i:i + 1]))
            rstd_of(ss, lambda: ss[:], 1024)
            import os
            if os.environ.get("KSUB") in ("a", "c"):
                return
            tl_ = list(tiles)

            def stage_xn(i):
                xn = xnr.get()
                S.op("dve", [xs[i], ss], [xn], lambda e: e.tensor_scalar(out=xn[:], in0=xs[i][:], scalar1=ss[:, i:i + 1], scalar2=None, op0=ALU.mult))
                pt = psT.get()
                for k in range(8):
                    S.op("pe", [xn, identb], [pt], lambda e: e.transpose(out=pt[:, k, :], in_=xn[:, k * 128:(k + 1) * 128], identity=identb[:]))
                return pt

            def stage_evac(i, pt):
                g, off = tok_group(i)
                j = 1 if i < 2 else 0
                for k in range(8):
                    if k % 2 == 0:
                        S.op("dve", [pt, AB, modT[l]], [hTg[g]], lambda e: e.tensor_scalar(
                            out=hTg[g][:, k, off:off + 128], in0=pt[:, k, :], scalar1=AB[:, k, j:j + 1], scalar2=modT[l][:, sh + k, j:j + 1], op0=ALU.mult, op1=ALU.add))
                    else:
                        S.op("act", [pt, AB, modT[l]], [hTg[g]], lambda e: e.activation(
                            out=hTg[g][:, k, off:off + 128], in_=pt[:, k, :], func=AF.Identity, scale=AB[:, k, j:j + 1], bias=modT[l][:, sh + k, j:j + 1]))

            ptn = stage_xn(tl_[0])
            for n_, i in enumerate(tl_):
                ptc = ptn
                if n_ + 1 < len(tl_):
                    ptn = stage_xn(tl_[n_ + 1])
                stage_evac(i, ptc)

        def wload(wb, n, src):
            dst = wb[:, 0:8 * n].rearrange("p (k n) -> p k n", k=8)
            S.dma("pool", wb.sem, dst, src, [], [wb])
            return dst

        def qk_prep(ps, ps_ap, nh, g_ap, rope_tile, out_ap, wk, rope):
            sq, ssq, qn, t1 = wk
            n = nh * 64
            v3 = lambda ap: ap.rearrange("p (h d) -> p h d", d=64)
            S.op("act", [ps], [sq], lambda e: e.activation(out=sq[:, 0:n], in_=ps_ap, func=AF.Square))
            S.op("dve", [sq], [ssq], lambda e: e.tensor_reduce(out=ssq[:, 0:nh], in_=v3(sq[:, 0:n]), axis=AX.X, op=ALU.add))
            rstd_of(ssq, lambda: ssq[:, 0:nh], 64)
            S.op("dve", [ps, ssq], [qn], lambda e: e.tensor_tensor(out=v3(qn[:, 0:n]), in0=v3(ps_ap), in1=ssq[:, 0:nh].unsqueeze(2).to_broadcast([128, nh, 64]), op=ALU.mult))
            if rope_tile is None:
                S.op("dve", [qn], [out_ap[0]], lambda e: e.tensor_tensor(out=out_ap[1], in0=v3(qn[:, 0:n]), in1=g_ap.unsqueeze(1).to_broadcast([128, nh, 64]), op=ALU.mult))
                return
            S.op("dve", [qn], [qn], lambda e: e.tensor_tensor(out=v3(qn[:, 0:n]), in0=v3(qn[:, 0:n]), in1=g_ap.unsqueeze(1).to_broadcast([128, nh, 64]), op=ALU.mult))
            cos_ap = rope[:, 0, :]
            sin_ap = rope[:, 1, :]
            S.op("dve", [qn, rope], [t1], lambda e: e.tensor_tensor(out=v3(t1[:, 0:n]), in0=v3(qn[:, 0:n]), in1=cos_ap.unsqueeze(1).to_broadcast([128, nh, 64]), op=ALU.mult))
            v5 = lambda ap: ap.rearrange("p (h x y d) -> p h x y d", x=2, y=2, d=16)
            s4 = sin_ap.rearrange("p (x y d) -> p x y d", x=2, y=2)
            for y in range(2):
                S.op("dve", [qn, rope], [sq], lambda e: e.tensor_tensor(
                    out=v5(sq[:, 0:n])[:, :, :, y, :], in0=v5(qn[:, 0:n])[:, :, :, 1 - y, :],
                    in1=s4[:, :, y, :].unsqueeze(1).to_broadcast([128, nh, 2, 16]), op=ALU.mult))
            S.op("dve", [t1, sq], [out_ap[0]], lambda e: e.tensor_tensor(out=out_ap[1], in0=v3(t1[:, 0:n]), in1=v3(sq[:, 0:n]), op=ALU.add))

        def prep_batch(raw, sq, ss, T, nh, g_ap, out_buf, out_ap, inplace=False):
            n = T * nh
            r3 = raw[:, 0:T, :].rearrange("p t (h d) -> p (t h) d", d=64)
            s3 = sq[:, 0:T, :].rearrange("p t (h d) -> p (t h) d", d=64)
            S.op("act", [raw], [sq], lambda e: e.activation(out=sq[:, 0:T, :], in_=raw[:, 0:T, :], func=AF.Square))
            S.op("dve", [sq], [ss], lambda e: e.tensor_reduce(out=ss[:, 0:n], in_=s3, axis=AX.X, op=ALU.add))
            rstd_of(ss, lambda: ss[:, 0:n], 64)
            S.op("dve", [raw, ss], [raw], lambda e: e.tensor_tensor(out=r3, in0=r3, in1=ss[:, 0:n].unsqueeze(2).to_broadcast([128, n, 64]), op=ALU.mult))
            if inplace:
                S.op("dve", [raw], [raw], lambda e: e.tensor_tensor(out=r3, in0=r3, in1=g_ap.unsqueeze(1).to_broadcast([128, n, 64]), op=ALU.mult))
                return
            S.op("dve", [raw], [out_buf], lambda e: e.tensor_tensor(out=out_ap.rearrange("p t (h d) -> p (t h) d", d=64), in0=r3, in1=g_ap.unsqueeze(1).to_broadcast([128, n, 64]), op=ALU.mult))

        def residual(i, ps, cgi, j):
            tmp = restmp.get()
            S.op("dve", [ps, gbc], [tmp], lambda e: e.tensor_tensor(out=tmp[:], in0=ps[:], in1=gbc[:, j, cgi * 512:(cgi + 1) * 512], op=ALU.mult))
            rstate["n"] += 1
            S.op("dve", [tmp, xs[i]], [xs[i]], lambda e: e.tensor_tensor(out=xs[i][:, cgi * 512:(cgi + 1) * 512], in0=xs[i][:, cgi * 512:(cgi + 1) * 512], in1=tmp[:], op=ALU.add))

        restmp = Ring([S.sb(f"restmp{i}", [128, 512], F32) for i in range(1)])
        rstate = {"n": 0}

        def mixer0():
            l = 0
            with ExitStack() as ph:
                hTg = [S.sb("hT0_0", [128, 8, 256], BF16, ph)] + [S.sb(f"hT0_{g}", [128, 8, 512], BF16, ph) for g in range(1, 5)]
                with ExitStack() as ph2:
                    norm_phase(0, 0, hTg, ph2)
                    import os
                    if os.environ.get("KSUB") not in ("a", "b"):
                        mk_gate(0, 16, ph2)
                    S.barrier()
                if stage == "norm":
                    return
                mixTa = S.sb("mixTa", [128, 4, NT * 128], BF16, ph)
                with ExitStack() as ph2:
                    wring = Ring([S.sbd(f"w0_{i}", [128, 8 * 384], BF16, ph2) for i in range(2)])
                    gateb = S.sbd("gateb", [128, 16], F32, ph2)
                    headg = S.sbd("headg", [128, 512], F32, ph2)
                    S.dma("sp", gateb.sem, gateb[:], D["gate_b_bc"], [], [gateb])
                    S.dma("sp", headg.sem, headg[:], D["head_g_bc"], [], [headg])
                    mlstm(hTg, mixTa, gateb, headg, wring, ph2)
                    S.barrier()
                if stage == "mlstm":
                    return
                with ExitStack() as ph2:
                    gqa_attn(0, hTg, mixTa, None, ph2)
                    S.barrier()

        def mlstm_gates(hTg, gateb, wring, pg, es, eb, edec, ekw):
            G = S.sb("G", [128, NT, 16], F32, pg)
            wgb = S.sbd("wgates", [128, 8 * 16], BF16, pg)
            wg = wload(wgb, 16, D["even_w"][:, 2048:2064].rearrange("(k p) n -> p k n", p=128))
            for i in range(NT):
                g, off = tok_group(i)
                ps = psO.get()
                for k in range(8):
                    S.op("pe", [hTg[g], wgb], [ps], lambda e: e.matmul(ps[:, 0:16], lhsT=hTg[g][:, k, off:off + 128], rhs=wg[:, k, :], start=(k == 0), stop=(k == 7)))
                S.op("dve", [ps, gateb], [G], lambda e: e.tensor_tensor(out=G[:, i, :], in0=ps[:, 0:16], in1=gateb[:], op=ALU.add))
            E = S.sb("E", [128, 2, NT, 4], F32, pg)
            for d in range(2):
                S.op("act", [G], [E], lambda e: e.activation(out=E[:, d], in_=G[:, :, 4 + 8 * d:8 + 8 * d], func=AF.Exp, scale=-1.0))
            S.op("dve", [E], [E], lambda e: e.tensor_scalar(out=E[:], in0=E[:], scalar1=1.0, scalar2=None, op0=ALU.add))
            S.op("act", [E], [E], lambda e: e.activation(out=E[:], in_=E[:], func=AF.Ln))
            tg = S.sb("tg", [128, NT, 4], F32, pg)
            f72 = lambda ap: ap.rearrange("p t h -> p (t h)")
            trib = S.sb("trib", [128, 2, 128], BF16, pg)
            S.op("dve", [cst], [trib], lambda e: e.tensor_copy(out=trib[:], in_=cst[:, 128:384].rearrange("p (a b) -> p a b", a=2)))
            Ehi = S.sb("Ehi", [128, 2, NT, 4], BF16, pg)
            Ehf = S.sb("Ehf", [128, 2, NT, 4], F32, pg)
            Elo = S.sb("Elo", [128, 2, NT, 4], BF16, pg)
            S.op("dve", [E], [Ehi], lambda e: e.tensor_copy(out=Ehi[:], in_=E[:]))
            S.op("dve", [Ehi], [Ehf], lambda e: e.tensor_copy(out=Ehf[:], in_=Ehi[:]))
            S.op("dve", [E, Ehf], [Ehf], lambda e: e.tensor_tensor(out=Ehf[:], in0=E[:], in1=Ehf[:], op=ALU.subtract))
            S.op("dve", [Ehf], [Elo], lambda e: e.tensor_copy(out=Elo[:], in_=Ehf[:]))
            for d in range(2):
                psb = psO.get()
                S.op("pe", [trib, Ehi], [psb], lambda e: e.matmul(psb[:, 0:72], lhsT=trib[:, d, :], rhs=f72(Ehi[:, d]), start=True, stop=False))
                S.op("pe", [trib, Elo], [psb], lambda e: e.matmul(psb[:, 0:72], lhsT=trib[:, d, :], rhs=f72(Elo[:, d]), start=False, stop=True))
                S.op("pe", [ones_b, Ehi], [psb], lambda e: e.matmul(psb[:, 72:144], lhsT=ones_b[:], rhs=f72(Ehi[:, d]), start=True, stop=False))
                S.op("pe", [ones_b, Elo], [psb], lambda e: e.matmul(psb[:, 72:144], lhsT=ones_b[:], rhs=f72(Elo[:, d]), start=False, stop=True))
                S.op("dve", [psb, G], [tg], lambda e: e.tensor_tensor(out=tg[:], in0=psb[:, 0:72].rearrange("p (t h) -> p t h", h=4), in1=G[:, :, 8 * d:8 * d + 4], op=ALU.add))
                S.op("act", [tg], [es], lambda e: e.activation(out=es[:, d], in_=tg[:], func=AF.Exp))
                S.op("act", [psb], [eb], lambda e: e.activation(out=f72(eb[:, d]), in_=psb[:, 0:72], func=AF.Exp, scale=-1.0))
                S.op("act", [psb], [edec], lambda e: e.activation(out=f72(edec[:, d]), in_=psb[:, 72:144], func=AF.Exp, scale=-1.0))
                S.op("dve", [es, edec], [ekw], lambda e: e.tensor_tensor(out=ekw[:, d], in0=es[:, d], in1=edec[:, d], op=ALU.mult))

            pass
            pass
            pass
            pass
            pass

        def mlstm(hTg, mixTa, gateb, headg, wring, ph):
            es = S.sb("es", [128, 2, NT, 4], F32, ph)
            eb = S.sb("eb", [128, 2, NT, 4], F32, ph)
            edec = S.sb("edec", [128, 2, NT, 4], F32, ph)
            ekw = S.sb("ekw", [128, 2, NT, 4], F32, ph)
            with ExitStack() as pg:
                mlstm_gates(hTg, gateb, wring, pg, es, eb, edec, ekw)
                S.barrier()
            KS_ = ""
            KH_ = -1
            qT = S.sb("qTa", [128, NT * 128], BF16, ph)
            kT = S.sb("kTa", [128, NT * 128], BF16, ph)
            ktok = S.sb("ktok", [128, NT, 128], BF16, ph)
            vaug = S.sb("vaug", [128, NT, 130], BF16, ph)
            hraw = [S.sb(f"hraw{d}", [128, NT, 130], F32, ph) for d in range(2)]
            rnm = S.sb("rnm", [128, 2, NT], F32, ph)
            Cst = [S.sb(f"Cst{d}", [128, 129], F32, ph) for d in range(2)]
            Cbf3 = [[S.sb(f"Cbf{d}_{r}", [128, 130], BF16, ph) for r in range(3)] for d in range(2)]
            PTr = Ring([S.sb(f"PTm{i}", [128, 128], BF16, ph) for i in range(4)])
            kwr = Ring([S.sb(f"kwm{i}", [128, 128], BF16, ph) for i in range(2)])
            hss = S.sb("hss", [128, NT], F32, ph)
            hjunk = S.sb("hjunk", [128, 128], BF16, ph)
            ogr = Ring([S.sb(f"og{i}", [128, 128], F32, ph) for i in range(2)])
            t1r = Ring([S.sb(f"mt1{i}", [128, 128], F32, ph) for i in range(2)])
            mxr = Ring([S.sb(f"mmx{i}", [128, 128], BF16, ph) for i in range(2)])
            S.op("dve", [], [vaug], lambda e: e.memset(vaug[:, :, 128:129], 1.0))
            orders = [list(range(NT)), [1, 0] + list(range(NT - 1, 1, -1))]
            KS = 128.0 ** -0.5

            def load_qkv(hd):
                wb_ = wring.get()
                src = D["even_w"][:, 0:1536].rearrange("(k p) (g h n) -> p k g h n", p=128, g=3, h=4)[:, :, :, hd, :]
                wq_ = wb_[:, 0:8 * 384].rearrange("p (k g n) -> p k g n", k=8, g=3)
                for g3 in range(3):
                    S.dma("pool", wb_.sem, wq_[:, :, g3, :], src[:, :, g3, :], [], [wb_])
                return wb_, wq_

            woring = Ring([S.sbd(f"wo_{i}", [128, 8 * 128], BF16, ph) for i in range(2)])
            nxt_w = load_qkv(0)
            for h in range(4):
                wb, wq = nxt_w
                wob = woring.get()
                wo = wload(wob, 128, D["even_w"][:, 1536 + h * 128:1536 + (h + 1) * 128].rearrange("(k p) n -> p k n", p=128))
                if h + 1 < 4:
                    nxt_w = load_qkv(h + 1)
                flip = 0
                for (g, c0, n) in GROUPS:
                    for which, dst, scl in ((0, qT, 1.0), (1, kT, KS)):
                        ps = psA.get()
                        for k in range(8):
                            S.op("pe", [wb, hTg[g]], [ps], lambda e: e.matmul(ps[:, 0:n], lhsT=wq[:, k, which, :], rhs=hTg[g][:, k, 0:n], start=(k == 0), stop=(k == 7)))
                        if flip % 2 == 0:
                            S.op("act", [ps], [dst], lambda e: e.activation(out=dst[:, c0:c0 + n], in_=ps[:, 0:n], func=AF.Copy, scale=scl))
                        else:
                            S.op("dve", [ps], [dst], lambda e: e.tensor_scalar(out=dst[:, c0:c0 + n], in0=ps[:, 0:n], scalar1=scl, scalar2=None, op0=ALU.mult))
                        flip += 1
                if KS_ == "m2a" and h == KH_:
                    return
                for i in range(NT):
                    g, off = tok_group(i)
                    ps = psA.get()
                    for k in range(8):
                        S.op("pe", [wb, hTg[g]], [ps], lambda e: e.matmul(ps[:, 0:256], lhsT=hTg[g][:, k, off:off + 128], rhs=wb[:, k * 384 + 128:k * 384 + 384], start=(k == 0), stop=(k == 7)))
                    S.op("act", [ps], [ktok], lambda e: e.activation(out=ktok[:, i, :], in_=ps[:, 0:128], func=AF.Copy, scale=KS))
                    S.op("dve", [ps], [vaug], lambda e: e.tensor_copy(out=vaug[:, i, 0:128], in_=ps[:, 128:256]))
                if KS_ == "m2b" and h == KH_:
                    return
                if h == 0:
                    pass
                    pass
                    pass
                    pass
                if KS_ == "m2" and h == KH_:
                    return
                written = [False] * NT
                PTs = {}

                def emitA2(step):
                    ii = [orders[d][step] for d in range(2)]
                    col = lambda a, d: a[:, d, ii[d], h:h + 1]
                    css = [slice(i * 128, (i + 1) * 128) for i in ii]
                    pss2, kws, pscs = [], [], []
                    for d in range(2):
                        pss = psS.get()
                        S.op("pe", [kT, qT], [pss], lambda e: e.matmul(pss[:, 0:128], lhsT=kT[:, css[d]], rhs=qT[:, css[d]], start=True, stop=True))
                        pss2.append(pss)
                    if step < NT - 1:
                        for d in range(2):
                            kw = kwr.get()
                            S.op("act", [ktok, ekw], [kw], lambda e: e.activation(out=kw[:], in_=ktok[:, ii[d], :], func=AF.Copy, scale=col(ekw, d)))
                            kws.append(kw)
                        for d in range(2):
                            psc = psA.get()
                            S.op("pe", [kws[d], vaug], [psc], lambda e: e.matmul(psc[:, 0:129], lhsT=kws[d][:], rhs=vaug[:, ii[d], 0:129], start=True, stop=True))
                            pscs.append(psc)
                    for d in range(2):
                        PT = PTr.get()
                        msk = TRIU() if d == 0 else TRIL()
                        S.op("dve", [pss2[d], es, cst], [PT], lambda e: e.scalar_tensor_tensor(out=PT[:], in0=pss2[d][:, 0:128], scalar=col(es, d), in1=msk, op0=ALU.mult, op1=ALU.mult))
                        PTs[(step, d)] = PT
                    if step < NT - 1:
                        for d in range(2):
                            psc = pscs[d]
                            if step == 0:
                                S.op("dve", [psc], [Cst[d]], lambda e: e.tensor_copy(out=Cst[d][:], in_=psc[:, 0:129]))
                            else:
                                S.op("dve", [psc, Cst[d], edec], [Cst[d]], lambda e: e.scalar_tensor_tensor(out=Cst[d][:], in0=Cst[d][:], scalar=col(edec, d), in1=psc[:, 0:129], op0=ALU.mult, op1=ALU.add))
                            cb3 = Cbf3[d][(step + 1) % 3]
                            S.op("dve", [Cst[d]], [cb3], lambda e: e.tensor_copy(out=cb3[:, 0:129], in_=Cst[d][:]))

                def emitB(step, d):
                    i = orders[d][step]
                    col = lambda a: a[:, d, i, h:h + 1]
                    cs = slice(i * 128, (i + 1) * 128)
                    PT = PTs.pop((step, d))
                    acc = psO.get()
                    if step > 0:
                        cb3 = Cbf3[d][step % 3]
                        S.op("pe", [qT, cb3], [acc], lambda e: e.matmul(acc[:, 0:129], lhsT=qT[:, cs], rhs=cb3[:, 0:129], start=True, stop=False))
                    S.op("pe", [PT, vaug], [acc], lambda e: e.matmul(acc[:, 0:129], lhsT=PT[:], rhs=vaug[:, i, 0:129], start=(step == 0), stop=True))
                    S.op("act", [acc, eb], [hraw[d]], lambda e: e.activation(out=hraw[d][:, i, 0:129], in_=acc[:, 0:129], func=AF.Copy, scale=col(eb)))

                emitA2(0)
                for step in range(NT):
                    if step + 1 < NT:
                        emitA2(step + 1)
                    emitB(step, 0)
                    emitB(step, 1)
                if h == 0:
                    pass
                    pass
                if KS_ == "m3" and h == KH_:
                    return
                for d in range(2):
                    S.op("act", [hraw[d]], [rnm], lambda e: e.activation(out=rnm[:, d, :], in_=hraw[d][:, :, 128], func=AF.Abs))
                S.op("dve", [rnm], [rnm], lambda e: e.tensor_scalar(out=rnm[:], in0=rnm[:], scalar1=1.0, scalar2=None, op0=ALU.max))
                S.op("dve", [rnm], [rnm], lambda e: e.reciprocal(out=rnm[:], in_=rnm[:]))
                for d in range(2):
                    S.op("dve", [hraw[d], rnm], [hraw[d]], lambda e: e.tensor_tensor(out=hraw[d][:, :, 0:128], in0=hraw[d][:, :, 0:128], in1=rnm[:, d, :].unsqueeze(2).to_broadcast([128, NT, 128]), op=ALU.mult))
                S.op("dve", [hraw[0], hraw[1]], [hraw[0]], lambda e: e.tensor_tensor(out=hraw[0][:, :, 0:128], in0=hraw[0][:, :, 0:128], in1=hraw[1][:, :, 0:128], op=ALU.add))
                S.op("dve", [], [hss], lambda e: e.memset(hss[:], 1.0))
                for i in range(NT):
                    S.op("act", [hraw[0]], [hjunk, hss], lambda e: e.activation(out=hjunk[:], in_=hraw[0][:, i, 0:128], func=AF.Square, accum_out=hss[:, i:i + 1]))
                rstd_of(hss, lambda: hss[:], 128)

                def out_stage1(i):
                    g, off = tok_group(i)
                    ps = psA.get()
                    for k in range(8):
                        S.op("pe", [wob, hTg[g]], [ps], lambda e: e.matmul(ps[:, 0:128], lhsT=hTg[g][:, k, off:off + 128], rhs=wo[:, k, :], start=(k == 0), stop=(k == 7)))
                    og = ogr.get()
                    S.op("act", [ps], [og], lambda e: e.activation(out=og[:], in_=ps[:, 0:128], func=AF.Sigmoid))
                    return og

                def out_stage2(i, og):
                    t1 = t1r.get()
                    S.op("dve", [hraw[0], hss, headg], [t1], lambda e: e.scalar_tensor_tensor(out=t1[:], in0=hraw[0][:, i, 0:128], scalar=hss[:, i:i + 1], in1=headg[:, h * 128:(h + 1) * 128], op0=ALU.mult, op1=ALU.mult))
                    mx = mxr.get()
                    S.op("dve", [t1, og], [mx], lambda e: e.tensor_tensor(out=mx[:], in0=t1[:], in1=og[:], op=ALU.mult))
                    pt = psT.get()
                    S.op("pe", [mx, identb], [pt], lambda e: e.transpose(out=pt[:, 0, :], in_=mx[:], identity=identb[:]))
                    S.op("act", [pt], [mixTa], lambda e: e.activation(out=mixTa[:, h, i * 128:(i + 1) * 128], in_=pt[:, 0, :], func=AF.Copy))

                ogn = out_stage1(0)
                for i in range(NT):
                    ogc = ogn
                    if i + 1 < NT:
                        ogn = out_stage1(i + 1)
                    out_stage2(i, ogc)
                if KS_ == "m4" and h == KH_:
                    return

        def attn_scores_exp_pv(kv_specs, nheads_per_kv, qT, q_sl, PTr, accs, first, last):
            pass

        def gqa_attn(l, hTg, other, wring, ph):
            wname = "even_w" if l == 0 else "odd_w"
            qc0, kc0 = (2064, 2576) if l == 0 else (0, 512)
            swag = S.sbd(f"swag{l}", [128, 128], F32, ph)
            roper = Ring([S.sbd(f"rope{l}_{i}", [128, 2, 64], F32, ph) for i in range(2)])

            def get_rope(jt):
                rb = roper.get()
                S.dma("sp", rb.sem, rb[:], D["rope"][:, :, jt, :], [], [rb])
                return rb
            S.dma("sp", swag.sem, swag[:], D["swa_g_bc" if l == 0 else "gqa_g_bc"], [], [swag])
            gq = S.sb(f"gq{l}", [128, 64], F32, ph)
            S.op("dve", [swag], [gq], lambda e: e.tensor_scalar(out=gq[:], in0=swag[:, 0:64], scalar1=0.125, scalar2=None, op0=ALU.mult))
            esink = S.sb(f"esink{l}", [128, 8], F32, ph)
            if l == 0:
                sinkb = S.sbd("sinkb", [128, 8], F32, ph)
                S.dma("sp", sinkb.sem, sinkb[:], D["sink_bc"], [], [sinkb])
                S.op("act", [sinkb], [esink], lambda e: e.activation(out=esink[:], in_=sinkb[:], func=AF.Exp))
            else:
                S.op("dve", [], [esink], lambda e: e.memset(esink[:], 0.0))
            wkb = S.sbd(f"wkv{l}", [128, 8 * 256], BF16, ph)
            wkv = wload(wkb, 256, D[wname][:, kc0:kc0 + 256].rearrange("(k p) n -> p k n", p=128))
            kTd = [S.sb(f"kTd{g}", [128, NT * 128], BF16, ph) for g in range(2)]
            vb = S.sb("vb", [128, NT, 2, 66], BF16, ph)
            S.op("dve", [], [vb], lambda e: e.memset(vb[:, :, :, 64:65], 1.0))
            with ExitStack() as pk:
                TB = 9
                kraw = S.sb("kraw", [128, TB, 128], F32, pk)
                ksq = S.sb("ksq", [128, TB, 128], F32, pk)
                kt2 = S.sb("kt2", [128, TB, 128], F32, pk)
                kss = S.sb("kss", [128, TB * 2], F32, pk)
                knb = S.sb("knb", [128, TB, 128], BF16, pk)
                kd = S.sb("kd", [128, 2, 2, 64], BF16, pk)
                rtab = S.sbd("rtab", [128, 2, TB, 64], F32, pk)
                for t0 in range(0, NT, TB):
                    tl = list(range(t0, t0 + TB))
                    r0 = max(0, 2 - t0)
                    nl = TB - r0
                    j0 = t0 + r0 - 2
                    for cs_ in range(2):
                        S.dma("sp", rtab.sem, rtab[:, cs_, 0:nl, :], D["rope"][:, cs_, j0:j0 + nl, :], [], [rtab])
                    for i in tl:
                        g, off = tok_group(i)
                        ps = psA.get()
                        for k in range(8):
                            S.op("pe", [wkb, hTg[g]], [ps], lambda e: e.matmul(ps[:, 0:256], lhsT=hTg[g][:, k, off:off + 128], rhs=wkv[:, k, :], start=(k == 0), stop=(k == 7)))
                        S.op("act", [ps], [vb], lambda e: e.activation(out=vb[:, i, :, 0:64], in_=ps[:, 128:256].rearrange("p (g d) -> p g d", g=2), func=AF.Copy))
                        S.op("dve", [ps], [kraw], lambda e: e.tensor_copy(out=kraw[:, i - t0, :], in_=ps[:, 0:128]))
                    prep_batch(kraw, ksq, kss, TB, 2, swag[:, 64:128], None, None, inplace=True)
                    if r0 > 0:
                        S.op("act", [kraw], [knb], lambda e: e.activation(out=knb[:, 0:r0, :], in_=kraw[:, 0:r0, :], func=AF.Copy))
                    v4 = lambda ap: ap.rearrange("p t (h d) -> p t h d", d=64)
                    cosb = rtab[:, 0, 0:nl, :].unsqueeze(2).to_broadcast([128, nl, 2, 64])
                    S.op("dve", [kraw, rtab], [ksq], lambda e: e.tensor_tensor(out=v4(ksq[:, r0:TB, :]), in0=v4(kraw[:, r0:TB, :]), in1=cosb, op=ALU.mult))
                    v6 = lambda ap: ap.rearrange("p t (h x y d) -> p t h x y d", h=2, x=2, y=2)
                    s5 = rtab[:, 1, 0:nl, :].rearrange("p t (x y d) -> p t x y d", x=2, y=2)
                    for hh_ in range(2):
                        for y in range(2):
                            S.op("dve", [kraw, rtab], [kt2], lambda e: e.tensor_tensor(
                                out=v6(kt2[:, r0:TB, :])[:, :, hh_, :, y, :], in0=v6(kraw[:, r0:TB, :])[:, :, hh_, :, 1 - y, :],
                                in1=s5[:, :, :, y, :], op=ALU.mult))
                    S.op("dve", [ksq, kt2], [knb], lambda e: e.tensor_tensor(out=knb[:, r0:TB, :], in0=ksq[:, r0:TB, :], in1=kt2[:, r0:TB, :], op=ALU.add))
                    for i in tl:
                        S.op("dve", [knb], [kd], lambda e: e.tensor_copy(out=kd[:], in_=knb[:, i - t0, :].rearrange("p (g d) -> p g d", g=2).unsqueeze(2).to_broadcast([128, 2, 2, 64])))
                        pt = psT.get()
                        for g2 in range(2):
                            S.op("pe", [kd, identb], [pt], lambda e: e.transpose(out=pt[:, g2, :], in_=kd[:, g2].rearrange("p a d -> p (a d)"), identity=identb[:]))
                        S.op("act", [pt], [kTd[0]], lambda e: e.activation(out=kTd[0][:, i * 128:(i + 1) * 128], in_=pt[:, 0, :], func=AF.Copy))
                        S.op("act", [pt], [kTd[1]], lambda e: e.activation(out=kTd[1][:, i * 128:(i + 1) * 128], in_=pt[:, 1, :], func=AF.Copy))
                S.barrier()
            wqb = S.sbd(f"wqq{l}", [128, 8 * 512], BF16, ph)
            wq = wload(wqb, 512, D[wname][:, qc0:qc0 + 512].rearrange("(k p) n -> p k n", p=128))
            wout = S.sbd(f"wout{l}", [128, 8 * 1024], BF16, ph)
            woutv = wout[:, :].rearrange("p (k n) -> p k n", k=8)
            S.dma("pool", wout.sem, woutv, D["w_out"][l].rearrange("(k p) n -> p k n", p=128), [], [wout])
            wk = (S.sb("wk_sq", [128, 512], F32, ph), S.sb("wk_ss", [128, 8], F32, ph), S.sb("wk_qn", [128, 512], F32, ph), S.sb("wk_t1", [128, 512], F32, ph))
            import os
            KS_ = os.environ.get("KSUB", "")
            if KS_ == "w1" or (KS_ == "g1k" and l == 1):
                return
            qb = S.sb("qb", [128, 8, 64], BF16, ph)
            qz = S.sb("qz", [128, 2, 4, 128], BF16, ph)
            S.op("dve", [], [qz], lambda e: e.memset(qz[:], 0.0))
            wmb4 = S.sb("wmb4", [128, 2, 4, 128], BF16, ph)
            S.op("dve", [wmb], [wmb4], lambda e: e.tensor_copy(out=wmb4[:], in_=wmb[:, :, :].unsqueeze(2).to_broadcast([128, 2, 4, 128])))
            PTr = Ring([S.sb(f"PTw{i}", [128, 512], BF16, ph) for i in range(3)])
            den = S.sb("wden", [128, 8], F32, ph)
            mixb = S.sb("mixb", [128, 512], BF16, ph)
            mixTb = S.sb("mixTb", [128, 4, 128], BF16, ph)
            scoreS = Ring(psS.bufs + [psA.bufs[1]])
            psQ = Ring([psA.bufs[0]])

            def emit_qprep(i):
                g, off = tok_group(i)
                lat = i >= 2
                j = i - 2
                ps = psQ.get()
                for k in range(8):
                    S.op("pe", [wqb, hTg[g]], [ps], lambda e: e.matmul(ps[:, 0:512], lhsT=hTg[g][:, k, off:off + 128], rhs=wq[:, k, :], start=(k == 0), stop=(k == 7)))
                qk_prep(ps, ps[:, 0:512], 8, gq[:], j if lat else None, (qb, qb[:]), wk, get_rope(j) if lat else None)

            qtiles = list(range(NT) if l == 0 else range(2, NT))
            emit_qprep(qtiles[0])
            for qi, i in enumerate(qtiles):
                g, off = tok_group(i)
                lat = i >= 2
                j = i - 2
                pt = psT.get()
                for pr in range(4):
                    S.op("pe", [qb, identb], [pt], lambda e: e.transpose(out=pt[:, pr, :], in_=qb[:, 2 * pr:2 * pr + 2, :].rearrange("p a d -> p (a d)"), identity=identb[:]))
                S.op("act", [pt], [qz], lambda e: e.activation(out=qz[0:64, 0, :, :], in_=pt[0:64, 0:4, :], func=AF.Copy))
                S.op("dve", [pt], [qz], lambda e: e.tensor_copy(out=qz[64:128, 1, :, :], in_=pt[64:128, 0:4, :]))
                if qi + 1 < len(qtiles):
                    emit_qprep(qtiles[qi + 1])
                if KS_ == "w2a":
                    return
                if l == 1:
                    blocks = [(m, None) for m in range(NT)]
                elif lat:
                    blocks = [(0, None), (1, None)]
                    if j > 0:
                        blocks.append((i - 1, 0))
                    blocks.append((i, None))
                    if j < 15:
                        blocks.append((i + 1, 1))
                else:
                    blocks = [(0, None), (1, None)]
                for g2 in range(2):
                    acc = psO.get()

                    def emit_scores(m, msk):
                        pss = scoreS.get()
                        for half in range(2):
                            S.op("pe", [kTd[g2], qz], [pss], lambda e: e.matmul(
                                pss[:, half * 256:(half + 1) * 256], lhsT=kTd[g2][:, m * 128:(m + 1) * 128],
                                rhs=qz[:, half, 2 * g2:2 * g2 + 2, :].rearrange("p a q -> p (a q)"),
                                start=(half == 0), stop=(half == 1 and msk is None)))
                        if msk is not None:
                            S.op("pe", [identb, wmb4], [pss], lambda e: e.matmul(pss[:, 0:512], lhsT=identb[:], rhs=wmb4[:, msk, :, :].rearrange("p a q -> p (a q)"), start=False, stop=True))
                        return pss

                    queue = [emit_scores(*blocks[0])]
                    if len(blocks) > 1:
                        queue.append(emit_scores(*blocks[1]))
                    for bi, (m, msk) in enumerate(blocks):
                        pss = queue.pop(0)
                        if bi + 2 < len(blocks):
                            queue.append(emit_scores(*blocks[bi + 2]))
                        PT = PTr.get()
                        S.op("act", [pss], [PT], lambda e: e.activation(out=PT[:], in_=pss[:], func=AF.Exp))
                        for hh in range(4):
                            S.op("pe", [PT, vb], [acc], lambda e: e.matmul(acc[:, hh * 128:hh * 128 + 65], lhsT=PT[:, hh * 128:(hh + 1) * 128], rhs=vb[:, m, g2, 0:65], start=(bi == 0 and hh == 0), stop=(bi == len(blocks) - 1)))
                    if KS_ == "w2c":
                        return
                    a3 = acc[:, :].rearrange("p (h c) -> p h c", h=4)
                    S.op("dve", [acc, esink], [den], lambda e: e.tensor_tensor(out=den[:, g2 * 4:(g2 + 1) * 4].rearrange("p (b a) -> p b a", b=2), in0=a3[:, :, 64].rearrange("p (b a) -> p b a", b=2),
                                                                            in1=esink[:, g2 * 4:(g2 + 1) * 4].rearrange("p (a b) -> p b a", a=2), op=ALU.add))
                    S.op("dve", [den], [den], lambda e: e.reciprocal(out=den[:, g2 * 4:(g2 + 1) * 4], in_=den[:, g2 * 4:(g2 + 1) * 4]))
                    S.op("dve", [acc, den], [mixb], lambda e: e.tensor_tensor(
                        out=mixb[:, g2 * 256:(g2 + 1) * 256].rearrange("p (a b d) -> p b a d", a=2, b=2), in0=a3[:, :, 0:64].rearrange("p (b a) d -> p b a d", b=2),
                        in1=den[:, g2 * 4:(g2 + 1) * 4].rearrange("p (b a) -> p b a", b=2).unsqueeze(3).to_broadcast([128, 2, 2, 64]), op=ALU.mult))
                if KS_ == "w2d":
                    return
                pt2 = psT.get()
                for c in range(4):
                    S.op("pe", [mixb, identb], [pt2], lambda e: e.transpose(out=pt2[:, c, :], in_=mixb[:, c * 128:(c + 1) * 128], identity=identb[:]))
                S.op("act", [pt2], [mixTb], lambda e: e.activation(out=mixTb[:], in_=pt2[:, 0:4, :], func=AF.Copy))
                if (KS_ == "w2" and i == 2) or (KS_ == "g1q" and l == 1 and i == 3):
                    return
                for cgi in range(2):
                    pso = psQ.get()
                    for k in range(8):
                        if l == 0:
                            lhs = other[:, k, i * 128:(i + 1) * 128] if k < 4 else mixTb[:, k - 4, :]
                        else:
                            lhs = mixTb[:, k, :] if k < 4 else other[:, k - 4, j * 128:(j + 1) * 128]
                        S.op("pe", [other, mixTb, wout], [pso], lambda e: e.matmul(pso[:, 0:512], lhsT=lhs, rhs=woutv[:, k, cgi * 512:(cgi + 1) * 512], start=(k == 0), stop=(k == 7)))
                    residual(i, pso, cgi, 0 if lat else 1)

        def ffn_phase(l, tiles):
            tiles = list(tiles)
            with ExitStack() as ph:
                hTg = [S.sb(f"hF{l}_0", [128, 8, 256], BF16, ph)] + [S.sb(f"hF{l}_{g}", [128, 8, 512], BF16, ph) for g in range(1, 5)]
                with ExitStack() as ph2:
                    norm_phase(l, 1, hTg, ph2, tiles)
                    mk_gate(l, 40, ph2)
                    S.barrier()
                segs = [gg for gg in GROUPS if (gg[0] > 0 or 0 in tiles)]
                lo = segs[0][1]
                ranges = ([(0, 256)] if lo == 0 else []) + [(256, 2304)]
                GS = 3
                ur = Ring([S.sb(f"fu{l}_{i}", [128, 2304], F32, ph) for i in range(2)])
                yr = Ring([S.sb(f"fy{l}_{i}", [128, 2304], F32, ph) for i in range(2)])
                actT = S.sb(f"actT{l}", [128, GS, 2304], BF16, ph)
                wur = Ring([S.sbd(f"wu{l}_{i}", [128, 8 * 256], BF16, ph) for i in range(3)])
                wdr = Ring([S.sbd(f"wd{l}_{i}", [128, GS * 1024], BF16, ph) for i in range(2)])
                has_ctx = (lo == 0)
                wdraw = Ring([S.sb(f"wdraw{l}_{i}", [128, GS * 1024], BF16, ph) for i in range(1)]) if has_ctx else None
                upsrc = D["ffn_up"][l].rearrange("(k p) (g c n) -> p k g c n", p=128, g=2, c=22)
                wu_loaded = {}
                wd_loaded = {}

                def load_wu(cp):
                    if cp >= 22 or cp in wu_loaded:
                        return
                    wub = wur.get()
                    wu = wub[:, :].rearrange("p (k g n) -> p k g n", k=8, g=2)
                    for g3 in range(2):
                        S.dma("pool", wub.sem, wu[:, :, g3, :], upsrc[:, :, g3, cp, :], [], [wub])
                    wu_loaded[cp] = (wub, wu)

                def load_wd(c0):
                    if c0 >= 22 or c0 in wd_loaded:
                        return
                    ncg = min(GS, 22 - c0)
                    wdb = wdr.get()
                    wd = wdb[:, 0:ncg * 1024].rearrange("p (c n) -> p c n", c=ncg)
                    S.dma("pool", wdb.sem, wd, D["ffn_down"][l, c0 * 128:(c0 + ncg) * 128, :].rearrange("(c p) n -> p c n", p=128), [], [wdb])
                    wd_loaded[c0] = (wdb, wd)

                def scale_wd(c0):
                    ncg = min(GS, 22 - c0)
                    wdb, wd = wd_loaded[c0]
                    raw = None
                    if has_ctx:
                        rb = wdraw.get()
                        raw = rb[:, 0:ncg * 1024].rearrange("p (c n) -> p c n", c=ncg)
                        S.op("pool", [wdb], [rb], lambda e: e.tensor_copy(out=raw, in_=wd))
                        wd_loaded[c0] = (wdb, wd, rb, raw)
                    S.op("pool", [wdb, gbc], [wdb], lambda e: e.tensor_tensor(out=wd, in0=wd, in1=gbc[:, 0, :].unsqueeze(1).to_broadcast([128, ncg, 1024]), op=ALU.mult))
                    if not has_ctx:
                        wd_loaded[c0] = (wdb, wd, None, None)

                load_wu(0)
                load_wu(1)
                load_wd(0)

                def emit_up1(cp):
                    load_wu(cp + 2)
                    wub, wu = wu_loaded[cp]
                    ys = []
                    for gv in range(2):
                        ch = gv * 22 + cp
                        u = ur.get()
                        y = yr.get()
                        w0 = cw[:, l, 0, ch:ch + 1]
                        w1 = cw[:, l, 1, ch:ch + 1]
                        w2 = cw[:, l, 2, ch:ch + 1]
                        for (g, t0, n) in segs:
                            ps = psA.get()
                            for k in range(8):
                                S.op("pe", [wub, hTg[g]], [ps], lambda e: e.matmul(ps[:, 0:n], lhsT=wu[:, k, gv, :], rhs=hTg[g][:, k, 0:n], start=(k == 0), stop=(k == 7)))
                            S.op("act", [ps], [u], lambda e: e.activation(out=u[:, t0:t0 + n], in_=ps[:, 0:n], func=AF.Copy))
                        S.op("act", [u, cw, cb], [y], lambda e: e.activation(out=y[:, lo:2304], in_=u[:, lo:2304], func=AF.Identity, scale=w1, bias=cb[:, l, ch:ch + 1]))
                        for (a, b_) in ranges:
                            S.op("dve", [u, cw, y], [y], lambda e: e.scalar_tensor_tensor(out=y[:, a + 1:b_], in0=u[:, a:b_ - 1], scalar=w0, in1=y[:, a + 1:b_], op0=ALU.mult, op1=ALU.add))
                            S.op("dve", [u, cw, y], [y], lambda e: e.scalar_tensor_tensor(out=y[:, a:b_ - 1], in0=u[:, a + 1:b_], scalar=w2, in1=y[:, a:b_ - 1], op0=ALU.mult, op1=ALU.add))
                        ys.append(y)
                    S.op("act", [ys[0]], [ys[0]], lambda e: e.activation(out=ys[0][:, lo:2304], in_=ys[0][:, lo:2304], func=AF.Silu))
                    return ys

                def emit_up2(ys, ci):
                    S.op("dve", [ys[0], ys[1]], [actT], lambda e: e.tensor_tensor(out=actT[:, ci, lo:2304], in0=ys[0][:, lo:2304], in1=ys[1][:, lo:2304], op=ALU.mult))

                def emit_down(c0):
                    ncg = min(GS, 22 - c0)
                    scale_wd(c0)
                    wdb, wd, rb, raw = wd_loaded[c0]
                    for i in tiles:
                        for cgi in range(2):
                            ps = psO.get()
                            if i >= 2:
                                for ci in range(ncg):
                                    S.op("pe", [actT, wdb], [ps], lambda e: e.matmul(ps[:, 0:512], lhsT=actT[:, ci, i * 128:(i + 1) * 128], rhs=wd[:, ci, cgi * 512:(cgi + 1) * 512], start=(ci == 0), stop=(ci == ncg - 1)))
                                S.op("dve", [ps, xs[i]], [xs[i]], lambda e: e.tensor_tensor(out=xs[i][:, cgi * 512:(cgi + 1) * 512], in0=ps[:], in1=xs[i][:, cgi * 512:(cgi + 1) * 512], op=ALU.add))
                            else:
                                for ci in range(ncg):
                                    S.op("pe", [actT, rb], [ps], lambda e: e.matmul(ps[:, 0:512], lhsT=actT[:, ci, i * 128:(i + 1) * 128], rhs=raw[:, ci, cgi * 512:(cgi + 1) * 512], start=(ci == 0), stop=(ci == ncg - 1)))
                                residual(i, ps, cgi, 1)

                pending = None
                for c0 in range(0, 22, GS):
                    ncg = min(GS, 22 - c0)
                    ys0 = emit_up1(c0)
                    if pending is not None:
                        emit_down(pending)
                    load_wd(c0 + GS)
                    emit_up2(ys0, 0)
                    for ci in range(1, ncg):
                        emit_up2(emit_up1(c0 + ci), ci)
                    pending = c0
                emit_down(pending)
                S.barrier()

        def na_attn(hTg, mixTd, wring, ph):
            nag = S.sbd("nag", [128, 128], F32, ph)
            S.dma("sp", nag.sem, nag[:], D["na_g_bc"], [], [nag])
            gq = S.sb("nagq", [128, 64], F32, ph)
            S.op("dve", [nag], [gq], lambda e: e.tensor_scalar(out=gq[:], in0=nag[:, 0:64], scalar1=0.125, scalar2=None, op0=ALU.mult))
            kTn = S.sb("kTn", [128, NT * 128], BF16, ph)
            vn = S.sb("vn", [128, NT, 2, 66], BF16, ph)
            qTn = S.sb("qTn", [128, 2048], BF16, ph)
            S.op("dve", [], [vn], lambda e: e.memset(vn[:, :, :, 64:65], 1.0))
            TB = 9
            raw = S.sb("nraw", [128, TB, 128], F32, ph)
            sq = S.sb("nsq", [128, TB, 128], F32, ph)
            ssb = S.sb("nss", [128, TB * 2], F32, ph)
            nrm = S.sb("nnrm", [128, TB, 128], BF16, ph)
            biasr = Ring([S.sbd(f"nbias{i}", [128, 25, 128], F32, ph) for i in range(1)])
            stmp = Ring([S.sb(f"nstmp{i}", [128, 5, 128], F32, ph) for i in range(2)])
            PTr = Ring([S.sb(f"PTn{i}", [128, 7, 128], BF16, ph) for i in range(3)])
            rdn = Ring([S.sb(f"nrd{i}", [128, 1], F32, ph) for i in range(3)])
            mixd = S.sb("mixd", [128, 16, 2, 64], BF16, ph)
            naS = Ring(psS.bufs + psA.bufs)
            wsrc = D["odd_w"][:, 768:2304].rearrange("(k p) (g h n) -> p k g h n", p=128, g=3, h=4)
            wl = {}

            def load_w(pr):
                if pr >= 4 or pr in wl:
                    return
                wb = wring.get()
                wq = wb[:, 0:8 * 384].rearrange("p (k g n) -> p k g n", k=8, g=3)
                for g3 in range(3):
                    S.dma("pool", wb.sem, wq[:, :, g3, :], wsrc[:, :, g3, pr, :], [], [wb])
                wl[pr] = (wb, wq)

            load_w(0)
            for pr in range(4):
                wb, wq = wl[pr]
                load_w(pr + 1)
                for t0 in range(0, NT, TB):
                    tl = list(range(t0, min(NT, t0 + TB)))
                    for i in tl:
                        g, off = tok_group(i)
                        ps = psA.get()
                        for k in range(8):
                            S.op("pe", [wb, hTg[g]], [ps], lambda e: e.matmul(ps[:, 0:256], lhsT=hTg[g][:, k, off:off + 128], rhs=wb[:, k * 384 + 128:k * 384 + 384], start=(k == 0), stop=(k == 7)))
                        S.op("act", [ps], [vn], lambda e: e.activation(out=vn[:, i, :, 0:64], in_=ps[:, 128:256].rearrange("p (g d) -> p g d", g=2), func=AF.Copy))
                        S.op("dve", [ps], [raw], lambda e: e.tensor_copy(out=raw[:, i - t0, :], in_=ps[:, 0:128]))
                    prep_batch(raw, sq, ssb, len(tl), 2, nag[:, 64:128], nrm, nrm[:, 0:len(tl), :])
                    for i in tl:
                        pt = psT.get()
                        S.op("pe", [nrm, identb], [pt], lambda e: e.transpose(out=pt[:, 0, :], in_=nrm[:, i - t0, :], identity=identb[:]))
                        S.op("act", [pt], [kTn], lambda e: e.activation(out=kTn[:, i * 128:(i + 1) * 128], in_=pt[:, 0, :], func=AF.Copy))
                for t0 in range(2, NT, 8):
                    tl = list(range(t0, t0 + 8))
                    for i in tl:
                        g, off = tok_group(i)
                        ps2 = psA.get()
                        for k in range(8):
                            S.op("pe", [wb, hTg[g]], [ps2], lambda e: e.matmul(ps2[:, 0:128], lhsT=hTg[g][:, k, off:off + 128], rhs=wq[:, k, 0, :], start=(k == 0), stop=(k == 7)))
                        S.op("dve", [ps2], [raw], lambda e: e.tensor_copy(out=raw[:, i - t0, :], in_=ps2[:, 0:128]))
                    prep_batch(raw, sq, ssb, 8, 2, gq[:], nrm, nrm[:, 0:8, :])
                    for i in tl:
                        j = i - 2
                        pt2 = psT.get()
                        S.op("pe", [nrm, identb], [pt2], lambda e: e.transpose(out=pt2[:, 0, :], in_=nrm[:, i - t0, :], identity=identb[:]))
                        S.op("dve", [pt2], [qTn], lambda e: e.tensor_copy(out=qTn[:, j * 128:(j + 1) * 128], in_=pt2[:, 0, :]))
                for hh in range(2):
                    head = 2 * pr + hh
                    bt = biasr.get()
                    S.dma("sp", bt.sem, bt[:], D["na_bias"][head], [], [bt])
                    prs = slice(hh * 64, (hh + 1) * 64)

                    def blocks_of(j):
                        ci = 0 if j == 0 else 1 if j == 1 else 3 if j == 14 else 4 if j == 15 else 2
                        mlist = list(range(j - 2, j + 3)) if ci == 2 else na_blocks[j]
                        return ci, len(mlist), [0, 1] + [m + 2 for m in mlist]

                    def emit_scores(j):
                        ci, nb, keyt = blocks_of(j)
                        pA = naS.get()
                        pB = naS.get()
                        for bi, kt in enumerate(keyt):
                            pp, off2 = (pA, bi) if bi < 4 else (pB, bi - 4)
                            S.op("pe", [kTn, qTn], [pp], lambda e: e.matmul(pp[:, off2 * 128:(off2 + 1) * 128], lhsT=kTn[prs, kt * 128:(kt + 1) * 128], rhs=qTn[prs, j * 128:(j + 1) * 128], start=True, stop=True))
                        return pA, pB

                    def emit_norm(acc_, j_):
                        rd = rdn.get()
                        S.op("dve", [acc_], [rd], lambda e: e.reciprocal(out=rd[:], in_=acc_[:, 64:65]))
                        S.op("act", [acc_, rd], [mixd], lambda e: e.activation(out=mixd[:, j_, hh, :], in_=acc_[:, 0:64], func=AF.Copy, scale=rd[:, 0:1]))

                    pend_norm = None
                    nxt = emit_scores(0)
                    for j in range(16):
                        ci, nb, keyt = blocks_of(j)
                        pA, pB = nxt
                        if j + 1 < 16:
                            nxt = emit_scores(j + 1)
                        stp = stmp.get()
                        S.op("dve", [pA, bt], [stp], lambda e: e.tensor_tensor(out=stp[:, 0:2, :], in0=pA[:, 256:512].rearrange("p (b q) -> p b q", b=2), in1=bt[:, ci * 5:ci * 5 + 2, :], op=ALU.add))
                        S.op("dve", [pB, bt], [stp], lambda e: e.tensor_tensor(out=stp[:, 2:nb, :], in0=pB[:, 0:(nb - 2) * 128].rearrange("p (b q) -> p b q", b=nb - 2), in1=bt[:, ci * 5 + 2:ci * 5 + nb, :], op=ALU.add))
                        PT = PTr.get()
                        S.op("act", [pA], [PT], lambda e: e.activation(out=PT[:, 0:2, :], in_=pA[:, 0:256].rearrange("p (b q) -> p b q", b=2), func=AF.Exp))
                        S.op("act", [stp], [PT], lambda e: e.activation(out=PT[:, 2:2 + nb, :], in_=stp[:, 0:nb, :], func=AF.Exp))
                        acc = psO.get()
                        for bi, kt in enumerate(keyt):
                            S.op("pe", [PT, vn], [acc], lambda e: e.matmul(acc[:, 0:65], lhsT=PT[:, bi, :], rhs=vn[:, kt, hh, 0:65], start=(bi == 0), stop=(bi == len(keyt) - 1)))
                        if pend_norm is not None:
                            emit_norm(*pend_norm)
                        pend_norm = (acc, j)
                    emit_norm(*pend_norm)
                    pend_norm = None
                for j in range(16):
                    pt = psT.get()
                    S.op("pe", [mixd, identb], [pt], lambda e: e.transpose(out=pt[:, 0, :], in_=mixd[:, j, :, :].rearrange("p a d -> p (a d)"), identity=identb[:]))
                    S.op("dve", [pt], [mixTd], lambda e: e.tensor_copy(out=mixTd[:, pr, j * 128:(j + 1) * 128], in_=pt[:, 0, :]))

        def mixer1():
            with ExitStack() as ph:
                hTg = [S.sb("hT1_0", [128, 8, 256], BF16, ph)] + [S.sb(f"hT1_{g}", [128, 8, 512], BF16, ph) for g in range(1, 5)]
                with ExitStack() as ph2:
                    norm_phase(1, 0, hTg, ph2)
                    mk_gate(1, 16, ph2)
                    S.barrier()
                mixTd = S.sb("mixTd", [128, 4, 2048], BF16, ph)
                with ExitStack() as ph2:
                    wring = Ring([S.sbd(f"w1_{i}", [128, 8 * 384], BF16, ph2) for i in range(2)])
                    na_attn(hTg, mixTd, wring, ph2)
                    S.barrier()
                import os
                if os.environ.get("KSUB") == "nogqa1":
                    return
                with ExitStack() as ph2:
                    gqa_attn(1, hTg, mixTd, None, ph2)
                    S.barrier()

        mod_phase(0)
        if stage != "mod":
            mixer0()
        if stage not in ("l0mix", "mod", "norm", "mlstm"):
            ffn_phase(0, range(NT))
        if stage not in ("l0mix", "l0", "mod", "norm", "mlstm"):
            mod_phase(1)
            mixer1()
            if stage != "l1mix":
                ffn_phase(1, range(2, NT))
        osem = S.newsem("d_out")
        for i in range(2, NT):
            S.dma("sp", osem, out[(i - 2) * 128:(i - 1) * 128, :], xs[i][:], [xs[i]], [])
        if dbg:
            for i in range(2):
                S.dma("sp", osem, octx[i * 128:(i + 1) * 128, :], xs[i][:], [xs[i]], [])
        S._need("sp", osem, S.cnt[osem])
        S.barrier()
        print(f"[kernel] instructions={S.ninst} waits={S.nwait} sems={len(S.sems)}", flush=True)
    return nc


_CACHE = {}


def kernel(**inputs):
    shared, percore = host_prepare({k: np.asarray(v) for k, v in inputs.items()})
    if "nc" not in _CACHE:
        _CACHE["nc"] = build_program("full")
    nc = _CACHE["nc"]
    in_maps = []
    for b in range(8):
        m = dict(shared)
        m.update(percore[b])
        in_maps.append(m)
    res = run_bass_kernel_spmd(nc, in_maps, core_ids=list(range(8)))
    return np.stack([np.asarray(r["out"], np.float32) for r in res.results], axis=0)
```

```python
import numpy as np
from contextlib import ExitStack
import concourse.bass as bass
import concourse.mybir as mybir
from concourse.bass_utils import run_bass_kernel_spmd

F32 = mybir.dt.float32
BF16 = mybir.dt.bfloat16
AF = mybir.ActivationFunctionType
ALU = mybir.AluOpType
AX = mybir.AxisListType

ENGS = ("pe", "act", "dve", "pool", "sp")
NT = 18
EPS = 1e-6
NEGM = -30000.0


class Buf:
    __slots__ = ("t", "name", "w", "r", "sem", "psum")

    def __init__(self, t, name):
        self.t = t
        self.name = name
        self.w = None
        self.r = {}
        self.sem = None
        self.psum = False

    def __getitem__(self, idx):
        return self.t[idx]


class Ring:
    def __init__(self, bufs):
        self.bufs = bufs
        self.i = 0

    def get(self):
        b = self.bufs[self.i % len(self.bufs)]
        self.i += 1
        return b


SEM_LIMIT = 1500


class Sched:
    def __init__(self, nc, stack):
        self.nc = nc
        self.stack = stack
        self.eng = {"pe": nc.tensor, "act": nc.scalar, "dve": nc.vector,
                    "pool": nc.gpsimd, "sp": nc.sync}
        self.sems = {}
        self.cnt = {}
        self.epoch = {}
        self.cur = {}
        for e in ENGS:
            self.epoch[e] = 0
            self._new_epoch(e)
        self.seen = {e: {} for e in ENGS}
        self.ninst = 0
        self.nwait = 0
        self.nsem = 0
        self.nalloc = 0

    def _new_epoch(self, e):
        self.epoch[e] += 1
        key = f"{e}#{self.epoch[e]}"
        self.sems[key] = self.stack.enter_context(self.nc.semaphore("s_" + key.replace("#", "_")))
        self.cnt[key] = 0
        self.cur[e] = key

    def sb(self, name, shape, dt, stack=None):
        self.nalloc += 1
        name = f"{name}_{self.nalloc}"
        t = (stack or self.stack).enter_context(self.nc.sbuf_tensor(name, list(shape), dt))
        return Buf(t, name)

    def ps(self, name, shape, dt=F32):
        t = self.stack.enter_context(self.nc.psum_tensor(name, list(shape), dt))
        b = Buf(t, name)
        b.psum = True
        return b

    def newsem(self, name=None):
        self.nsem += 1
        name = name or f"d{self.nsem}"
        s = self.stack.enter_context(self.nc.semaphore(name))
        self.sems[name] = s
        self.cnt[name] = 0
        return name

    def sbd(self, name, shape, dt, stack=None):
        b = self.sb(name, shape, dt, stack)
        b.sem = self.newsem("d_" + b.name)
        return b

    @staticmethod
    def _eng_of(key):
        return key.split("#")[0] if "#" in key else None

    def _need(self, e, key, val):
        if self.seen[e].get(key, 0) >= val:
            return
        ke = self._eng_of(key)
        if ke is not None:
            ep = int(key.split("#")[1])
            for k2, v2 in self.seen[e].items():
                if v2 > 0 and self._eng_of(k2) == ke and int(k2.split("#")[1]) > ep:
                    return
        self.seen[e][key] = val
        self.eng[e].wait_ge(self.sems[key], val)
        self.nwait += 1

    def deps(self, e, reads, writes):
        for b in reads:
            if b.w is not None:
                k, v = b.w
                if not (self._eng_of(k) == e and e == "pe"):
                    self._need(e, k, v)
        for b in writes:
            if b.w is not None:
                k, v = b.w
                if self._eng_of(k) != e:
                    self._need(e, k, v)
            for k, v in b.r.items():
                if self._eng_of(k) != e:
                    self._need(e, k, v)

    def op(self, e, reads, writes, fn):
        pr = [b for b in reads if b.psum]
        if pr:
            reads = [b for b in reads if not b.psum]
            writes = list(writes) + [b for b in pr if b not in writes]
        self.deps(e, reads, writes)
        ins = fn(self.eng[e])
        if self.cnt[self.cur[e]] >= SEM_LIMIT:
            self._new_epoch(e)
        key = self.cur[e]
        self.cnt[key] += 1
        ins.then_inc(self.sems[key], 1)
        v = self.cnt[key]
        for b in reads:
            for k2 in [k2 for k2 in b.r if self._eng_of(k2) == e]:
                del b.r[k2]
            b.r[key] = v
        for b in writes:
            b.w = (key, v)
            b.r = {}
        self.ninst += 1
        return ins

    def dma(self, q, semkey, out_ap, in_ap, reads, writes, **kw):
        self.deps(q, reads, writes)
        ins = self.eng[q].dma_start(out=out_ap, in_=in_ap, **kw)
        self.cnt[semkey] += 16
        assert self.cnt[semkey] <= 2000, semkey
        ins.then_inc(self.sems[semkey], 16)
        v = self.cnt[semkey]
        for b in reads:
            b.r[semkey] = v
        for b in writes:
            b.w = (semkey, v)
            b.r = {}
        self.ninst += 1
        return ins

    def barrier(self):
        for e in ENGS:
            for k, v in list(self.cnt.items()):
                ke = self._eng_of(k)
                if ke == e or v == 0:
                    continue
                if ke is not None and k != self.cur[ke]:
                    if not (self.cnt[self.cur[ke]] == 0 and int(k.split("#")[1]) == self.epoch[ke] - 1):
                        continue
                self._need(e, k, v)


def _rope_tables():
    t = np.arange(2048)
    row = (t // 64).astype(np.float32)
    col = (t % 64).astype(np.float32)
    half = 32
    freq = (np.float32(10000.0) ** (-np.arange(0, half, 2, dtype=np.float32) / np.float32(half))).astype(np.float32)
    ang_r = row[:, None] * freq[None, :]
    ang_c = col[:, None] * freq[None, :]
    ang = np.concatenate([ang_r, ang_r, ang_c, ang_c], axis=-1).astype(np.float32)
    cos = np.cos(ang).astype(np.float32)
    sin = np.sin(ang).astype(np.float32)
    sgn = np.ones(64, np.float32)
    sgn[0:16] = -1.0
    sgn[32:48] = -1.0
    sinS = sin * sgn[None, :]
    cos = cos.reshape(16, 128, 64).transpose(1, 0, 2).copy()
    sinS = sinS.reshape(16, 128, 64).transpose(1, 0, 2).copy()
    return cos, sinS


def _na_tables(rpb):
    rows = 32
    wr = 8
    r = np.arange(rows)
    row_start = np.clip(r - wr // 2, 0, rows - wr)
    col = np.arange(64)
    col_start = np.clip(col - 8, 0, 48)
    col_ok = (col[None, :] >= col_start[:, None]) & (col[None, :] < col_start[:, None] + 16)
    dc = np.clip(col[None, :] - col[:, None] + 15, 0, 30)
    classes = [0, 1, 2, 14, 15]
    blocks = {}
    tab = np.full((8, 128, 25, 128), NEGM, np.float32)
    for ci, j in enumerate(classes):
        qrows = [2 * j, 2 * j + 1]
        lo = min(row_start[q] for q in qrows)
        hi = max(row_start[q] + wr - 1 for q in qrows)
        mlist = list(range(lo // 2, hi // 2 + 1))
        assert len(mlist) <= 5
        blocks[j] = mlist
        for si, m in enumerate(mlist):
            for kr in range(2):
                krow = 2 * m + kr
                for qr in range(2):
                    qrow = qrows[qr]
                    if not (row_start[qrow] <= krow < row_start[qrow] + wr):
                        continue
                    dr = krow - qrow + 7
                    sub = rpb[:, dr, :][:, dc]
                    sub = np.where(col_ok[None], sub, np.float32(NEGM))
                    tab[:, kr * 64:(kr + 1) * 64, ci * 5 + si, qr * 64:(qr + 1) * 64] = sub.transpose(0, 2, 1)
    return tab, blocks, classes


def _na_blocks():
    _, blocks, classes = _na_tables(np.zeros((8, 15, 31), np.float32))
    return blocks, classes


def host_prepare(inp):
    f = np.float32
    shared = {}
    shared["ada_w"] = np.ascontiguousarray(inp["ada_w"], f)
    shared["ada_bT"] = np.ascontiguousarray(inp["ada_b"].reshape(2, 48, 128).transpose(2, 0, 1), f)
    shared["norm_gT"] = np.ascontiguousarray(inp["norm_g"].reshape(2, 2, 8, 128).transpose(3, 0, 1, 2), f)
    shared["w_out"] = np.ascontiguousarray(inp["w_out"], f)
    shared["ffn_up"] = np.ascontiguousarray(inp["ffn_up"], f)
    shared["ffn_down"] = np.ascontiguousarray(inp["ffn_down"], f)
    shared["conv_wT"] = np.ascontiguousarray(inp["ffn_conv_w"].reshape(2, 3, 44, 128).transpose(3, 0, 1, 2), f)
    shared["conv_bT"] = np.ascontiguousarray(inp["ffn_conv_b"].reshape(2, 44, 128).transpose(2, 0, 1), f)
    shared["even_w"] = np.ascontiguousarray(inp["even_w_in"][0], f)
    shared["odd_w"] = np.ascontiguousarray(inp["odd_w_in"][0], f)
    bc = lambda a: np.ascontiguousarray(np.broadcast_to(np.asarray(a, f).reshape(1, -1), (128, a.size)))
    shared["gate_b_bc"] = bc(inp["mlstm_gate_b"][0])
    shared["head_g_bc"] = bc(inp["mlstm_head_g"][0])
    shared["swa_g_bc"] = bc(inp["swa_qk_g"][0])
    shared["sink_bc"] = bc(inp["swa_sink"][0])
    shared["gqa_g_bc"] = bc(inp["gqa_qk_g"][0])
    shared["na_g_bc"] = bc(inp["na_qk_g"][0])
    tab, _, _ = _na_tables(np.asarray(inp["na_rpb"][0], f))
    shared["na_bias"] = tab
    ident = np.eye(128, dtype=f)
    s = np.arange(128)
    triU = (s[:, None] <= s[None, :]).astype(f)
    triL = (s[:, None] >= s[None, :]).astype(f)
    wm = np.zeros((128, 2, 128), f)
    wm[:, 0, :] = np.where(s[None, :] <= s[:, None], 0.0, NEGM)
    wm[:, 1, :] = np.where(s[:, None] <= s[None, :], 0.0, NEGM)
    shared["consts"] = np.ascontiguousarray(np.concatenate([ident, triU, triL, wm.reshape(128, 256)], axis=1))
    cos, sinS = _rope_tables()
    shared["rope"] = np.ascontiguousarray(np.stack([cos, sinS], axis=1))
    percore = []
    for b in range(8):
        cc = np.stack([inp["c"][b].reshape(8, 128).T, inp["c_ctx"].reshape(8, 128).T], axis=-1)
        percore.append({"x": np.ascontiguousarray(inp["x"][b], f), "ctx": np.ascontiguousarray(inp["ctx"][b], f),
                        "cc": np.ascontiguousarray(cc, f)})
    return shared, percore


SHARED_SHAPES = {
    "ada_w": [2, 1024, 6144], "ada_bT": [128, 2, 48], "norm_gT": [128, 2, 2, 8], "w_out": [2, 1024, 1024],
    "ffn_up": [2, 1024, 5632], "ffn_down": [2, 2816, 1024], "conv_wT": [128, 2, 3, 44], "conv_bT": [128, 2, 44],
    "even_w": [1024, 2832], "odd_w": [1024, 2304], "gate_b_bc": [128, 16], "head_g_bc": [128, 512],
    "swa_g_bc": [128, 128], "sink_bc": [128, 8], "gqa_g_bc": [128, 128], "na_g_bc": [128, 128],
    "na_bias": [8, 128, 25, 128], "consts": [128, 640], "rope": [128, 2, 16, 64],
    "x": [2048, 1024], "ctx": [256, 1024], "cc": [128, 8, 2],
}


GROUPS = [(0, 0, 256), (1, 256, 512), (2, 768, 512), (3, 1280, 512), (4, 1792, 512)]


def tok_group(i):
    return (0, i * 128) if i < 2 else (1 + (i - 2) // 4, ((i - 2) % 4) * 128)


def build_program(stage="full"):
    nc = bass.Bass("TRN2", target_bir_lowering=False)
    D = {k: nc.dram_tensor(k, shp, F32, kind="ExternalInput").ap() for k, shp in SHARED_SHAPES.items()}
    out = nc.dram_tensor("out", [2048, 1024], F32, kind="ExternalOutput").ap()
    dbg = stage != "full"
    if dbg:
        octx = nc.dram_tensor("octx", [256, 1024], F32, kind="ExternalOutput").ap()
        dbgd = nc.dram_tensor("dbgd", [128, 8192], F32, kind="ExternalOutput").ap()
    na_blocks, na_classes = _na_blocks()

    with ExitStack() as st:
        S = Sched(nc, st)
        xs = [S.sbd(f"xs{i}", [128, 1024], F32) for i in range(NT)]
        cst = S.sbd("cst", [128, 640], F32)
        cc = S.sbd("cc", [128, 8, 2], F32)
        adab = S.sbd("adab", [128, 2, 48], F32)
        ngT = S.sbd("ngT", [128, 2, 2, 8], F32)
        cw = S.sbd("cw", [128, 2, 3, 44], F32)
        cb = S.sbd("cb", [128, 2, 44], F32)
        identb = S.sb("identb", [128, 128], BF16)
        wmb = S.sb("wmb", [128, 2, 128], BF16)
        ones_f = S.sb("ones_f", [128, 128], F32)
        ones_b = S.sb("ones_b", [128, 128], BF16)
        sc = S.sb("sc", [128, 8, 2], F32)
        modT = [S.sb(f"modT{l}", [128, 48, 2], F32) for l in range(2)]
        gbc = S.sb("gbc", [128, 2, 1024], F32)
        AB = S.sb("AB", [128, 8, 2], F32)

        psT = Ring([S.ps(f"psT{i}", [128, 8, 128], BF16) for i in range(2)])
        psA = Ring([S.ps(f"psA{i}", [128, 512], F32) for i in range(2)])
        psS = Ring([S.ps(f"psS{i}", [128, 512], F32) for i in range(2)])
        psO = Ring([S.ps(f"psO{i}", [128, 512], F32) for i in range(2)])

        IDF = lambda: cst[:, 0:128]
        TRIU = lambda: cst[:, 128:256]
        TRIL = lambda: cst[:, 256:384]

        S.dma("sp", cst.sem, cst[:], D["consts"], [], [cst])
        S.dma("sp", cc.sem, cc[:], D["cc"], [], [cc])
        S.dma("sp", adab.sem, adab[:], D["ada_bT"], [], [adab])
        S.dma("sp", ngT.sem, ngT[:], D["norm_gT"], [], [ngT])
        S.dma("sp", cw.sem, cw[:], D["conv_wT"], [], [cw])
        S.dma("sp", cb.sem, cb[:], D["conv_bT"], [], [cb])
        for i in range(NT):
            src = D["ctx"][i * 128:(i + 1) * 128, :] if i < 2 else D["x"][(i - 2) * 128:(i - 1) * 128, :]
            S.dma("sp", xs[i].sem, xs[i][:], src, [], [xs[i]])
        S.op("dve", [cst], [identb], lambda e: e.tensor_copy(out=identb[:], in_=cst[:, 0:128]))
        S.op("dve", [cst], [wmb], lambda e: e.tensor_copy(out=wmb[:], in_=cst[:, 384:640].rearrange("p (a b) -> p a b", a=2)))
        S.op("dve", [], [ones_f], lambda e: e.memset(ones_f[:], 1.0))
        S.op("dve", [], [ones_b], lambda e: e.memset(ones_b[:], 1.0))
        S.op("act", [cc], [sc], lambda e: e.activation(out=sc[:], in_=cc[:], func=AF.Silu))

        dstg = S.sb("dstg", [128, 128], F32) if dbg else None
        dstate = {"col": 0, "items": []}

        def dump(name, buf, ap, n):
            if not dbg:
                return
            stg = dstg
            sem = S.newsem()
            S.op("act", [buf], [stg], lambda e: e.activation(out=stg[:, 0:n], in_=ap, func=AF.Copy))
            c0 = dstate["col"]
            S.dma("sp", sem, dbgd[:, c0:c0 + n], stg[:, 0:n], [stg], [])
            S._need("sp", sem, S.cnt[sem])
            dstate["items"].append((name, c0, n))
            dstate["col"] = c0 + n
            print("DUMP", name, c0, n, flush=True)

        def wview(wb, shape_str, **kw):
            n = 1
            for v in kw.values():
                n *= v
            return wb

        def mod_phase(l):
            with ExitStack() as ph:
                ring = Ring([S.sbd(f"adaw{l}_{i}", [128, 8, 512], BF16, ph) for i in range(3)])
                schi = S.sb(f"schi{l}", [128, 8, 2], BF16, ph)
                schf = S.sb(f"schf{l}", [128, 8, 2], F32, ph)
                sclo = S.sb(f"sclo{l}", [128, 8, 2], BF16, ph)
                S.op("dve", [sc], [schi], lambda e: e.tensor_copy(out=schi[:], in_=sc[:]))
                S.op("dve", [schi], [schf], lambda e: e.tensor_copy(out=schf[:], in_=schi[:]))
                S.op("dve", [sc, schf], [schf], lambda e: e.tensor_tensor(out=schf[:], in0=sc[:], in1=schf[:], op=ALU.subtract))
                S.op("dve", [schf], [sclo], lambda e: e.tensor_copy(out=sclo[:], in_=schf[:]))
                wbs = {}

                def ld(cg):
                    if cg >= 12:
                        return
                    wb = ring.get()
                    S.dma("pool", wb.sem, wb[:], D["ada_w"][l, :, cg * 512:(cg + 1) * 512].rearrange("(k p) n -> p k n", p=128), [], [wb])
                    wbs[cg] = wb
                ld(0)
                ld(1)
                for cg in range(12):
                    ld(cg + 2)
                    wb = wbs[cg]
                    ps = psA.get()
                    for c4 in range(4):
                        for k in range(8):
                            S.op("pe", [wb, schi], [ps], lambda e: e.matmul(ps[:, c4 * 2:c4 * 2 + 2], lhsT=wb[:, k, c4 * 128:(c4 + 1) * 128], rhs=schi[:, k, :], start=(k == 0), stop=False))
                            S.op("pe", [wb, sclo], [ps], lambda e: e.matmul(ps[:, c4 * 2:c4 * 2 + 2], lhsT=wb[:, k, c4 * 128:(c4 + 1) * 128], rhs=sclo[:, k, :], start=False, stop=(k == 7)))
                    S.op("dve", [ps, adab], [modT[l]], lambda e: e.tensor_tensor(
                        out=modT[l][:, cg * 4:(cg + 1) * 4, :], in0=ps[:, 0:8].rearrange("p (c j) -> p c j", j=2),
                        in1=adab[:, l, cg * 4:(cg + 1) * 4].unsqueeze(2).to_broadcast([128, 4, 2]), op=ALU.add))
                S.barrier()

        def mk_AB(l, which):
            scl = 8 if which == 0 else 32
            S.op("dve", [modT[l]], [AB], lambda e: e.tensor_scalar(out=AB[:], in0=modT[l][:, scl:scl + 8, :], scalar1=1.0, scalar2=None, op0=ALU.add))
            S.op("dve", [AB, ngT], [AB], lambda e: e.tensor_tensor(out=AB[:], in0=AB[:], in1=ngT[:, l, which, :].unsqueeze(2).to_broadcast([128, 8, 2]), op=ALU.mult))

        def mk_gate(l, gchunk, ph):
            hl = S.sb(f"ghl{l}_{gchunk}", [128, 8, 2], F32, ph)
            hb = S.sb(f"ghb{l}_{gchunk}", [128, 8, 2], BF16, ph)
            hf = S.sb(f"ghf{l}_{gchunk}", [128, 8, 2], F32, ph)
            lo = S.sb(f"glo{l}_{gchunk}", [128, 8, 2], F32, ph)
            lb = S.sb(f"glb{l}_{gchunk}", [128, 8, 2], BF16, ph)
            lf = S.sb(f"glf{l}_{gchunk}", [128, 8, 2], F32, ph)
            S.op("dve", [modT[l]], [hl], lambda e: e.tensor_copy(out=hl[:], in_=modT[l][:, gchunk:gchunk + 8, :]))
            S.op("dve", [hl], [hb], lambda e: e.tensor_copy(out=hb[:], in_=hl[:]))
            S.op("dve", [hb], [hf], lambda e: e.tensor_copy(out=hf[:], in_=hb[:]))
            S.op("dve", [hl, hf], [lo], lambda e: e.tensor_tensor(out=lo[:], in0=hl[:], in1=hf[:], op=ALU.subtract))
            S.op("dve", [lo], [lb], lambda e: e.tensor_copy(out=lb[:], in_=lo[:]))
            S.op("dve", [lb], [lf], lambda e: e.tensor_copy(out=lf[:], in_=lb[:]))
            dgr = Ring([S.sb(f"dg{l}_{gchunk}_{i}", [128, 2, 128], BF16, ph) for i in range(2)])
            for j in range(2):
                for half in range(2):
                    ps = psA.get()
                    for k4 in range(4):
                        kk = half * 4 + k4
                        dg = dgr.get()
                        S.op("dve", [identb, hf], [dg], lambda e: e.tensor_scalar(out=dg[:, 0, :], in0=identb[:], scalar1=hf[:, kk, j:j + 1], scalar2=None, op0=ALU.mult))
                        S.op("dve", [identb, lf], [dg], lambda e: e.tensor_scalar(out=dg[:, 1, :], in0=identb[:], scalar1=lf[:, kk, j:j + 1], scalar2=None, op0=ALU.mult))
                        S.op("pe", [ones_b, dg], [ps], lambda e: e.matmul(ps[:, k4 * 128:(k4 + 1) * 128], lhsT=ones_b[:], rhs=dg[:, 0, :], start=True, stop=False))
                        S.op("pe", [ones_b, dg], [ps], lambda e: e.matmul(ps[:, k4 * 128:(k4 + 1) * 128], lhsT=ones_b[:], rhs=dg[:, 1, :], start=False, stop=True))
                    S.op("act", [ps], [gbc], lambda e: e.activation(out=gbc[:, j, half * 512:(half + 1) * 512], in_=ps[:], func=AF.Copy))

        def rstd_of(t, n_ap, dim):
            S.op("dve", [t], [t], lambda e: e.tensor_scalar(out=n_ap(), in0=n_ap(), scalar1=1.0 / dim, scalar2=EPS, op0=ALU.mult, op1=ALU.add))
            S.op("act", [t], [t], lambda e: e.activation(out=n_ap(), in_=n_ap(), func=AF.Ln))
            S.op("act", [t], [t], lambda e: e.activation(out=n_ap(), in_=n_ap(), func=AF.Exp, scale=-0.5))

        def norm_phase(l, which, hTg, ph, tiles=range(NT)):
            mk_AB(l, which)
            sh = 0 if which == 0 else 24
            ss = S.sb(f"nss{l}{which}", [128, NT], F32, ph)
            junk = S.sb(f"njunk{l}{which}", [128, 1024], BF16, ph)
            xnr = Ring([S.sb(f"xn{l}{which}_{i}", [128, 1024], BF16, ph) for i in range(2)])
            S.op("dve", [], [ss], lambda e: e.memset(ss[:], 1.0))
            for i in tiles:
                S.op("act", [xs[i]], [junk, ss], lambda e: e.activation(out=junk[:], in_=xs[i][:], func=AF.Square, accum_out=ss[:, i:i + 1]))
            rstd_of(ss, lambda: ss[:], 1024)
            import os
            if os.environ.get("KSUB") in ("a", "c"):
                return
            tl_ = list(tiles)

            def stage_xn(i):
                xn = xnr.get()
                S.op("dve", [xs[i], ss], [xn], lambda e: e.tensor_scalar(out=xn[:], in0=xs[i][:], scalar1=ss[:, i:i + 1], scalar2=None, op0=ALU.mult))
                pt = psT.get()
                for k in range(8):
                    S.op("pe", [xn, identb], [pt], lambda e: e.transpose(out=pt[:, k, :], in_=xn[:, k * 128:(k + 1) * 128], identity=identb[:]))
                return pt

            def stage_evac(i, pt):
                g, off = tok_group(i)
                j = 1 if i < 2 else 0
                for k in range(8):
                    if k % 2 == 0:
                        S.op("dve", [pt, AB, modT[l]], [hTg[g]], lambda e: e.tensor_scalar(
                            out=hTg[g][:, k, off:off + 128], in0=pt[:, k, :], scalar1=AB[:, k, j:j + 1], scalar2=modT[l][:, sh + k, j:j + 1], op0=ALU.mult, op1=ALU.add))
                    else:
                        S.op("act", [pt, AB, modT[l]], [hTg[g]], lambda e: e.activation(
                            out=hTg[g][:, k, off:off + 128], in_=pt[:, k, :], func=AF.Identity, scale=AB[:, k, j:j + 1], bias=modT[l][:, sh + k, j:j + 1]))

            ptn = stage_xn(tl_[0])
            for n_, i in enumerate(tl_):
                ptc = ptn
                if n_ + 1 < len(tl_):
                    ptn = stage_xn(tl_[n_ + 1])
                stage_evac(i, ptc)

        def wload(wb, n, src):
            dst = wb[:, 0:8 * n].rearrange("p (k n) -> p k n", k=8)
            S.dma("pool", wb.sem, dst, src, [], [wb])
            return dst

        def qk_prep(ps, ps_ap, nh, g_ap, rope_tile, out_ap, wk, rope):
            sq, ssq, qn, t1 = wk
            n = nh * 64
            v3 = lambda ap: ap.rearrange("p (h d) -> p h d", d=64)
            S.op("act", [ps], [sq], lambda e: e.activation(out=sq[:, 0:n], in_=ps_ap, func=AF.Square))
            S.op("dve", [sq], [ssq], lambda e: e.tensor_reduce(out=ssq[:, 0:nh], in_=v3(sq[:, 0:n]), axis=AX.X, op=ALU.add))
            rstd_of(ssq, lambda: ssq[:, 0:nh], 64)
            S.op("dve", [ps, ssq], [qn], lambda e: e.tensor_tensor(out=v3(qn[:, 0:n]), in0=v3(ps_ap), in1=ssq[:, 0:nh].unsqueeze(2).to_broadcast([128, nh, 64]), op=ALU.mult))
            if rope_tile is None:
                S.op("dve", [qn], [out_ap[0]], lambda e: e.tensor_tensor(out=out_ap[1], in0=v3(qn[:, 0:n]), in1=g_ap.unsqueeze(1).to_broadcast([128, nh, 64]), op=ALU.mult))
                return
            S.op("dve", [qn], [qn], lambda e: e.tensor_tensor(out=v3(qn[:, 0:n]), in0=v3(qn[:, 0:n]), in1=g_ap.unsqueeze(1).to_broadcast([128, nh, 64]), op=ALU.mult))
            cos_ap = rope[:, 0, :]
            sin_ap = rope[:, 1, :]
            S.op("dve", [qn, rope], [t1], lambda e: e.tensor_tensor(out=v3(t1[:, 0:n]), in0=v3(qn[:, 0:n]), in1=cos_ap.unsqueeze(1).to_broadcast([128, nh, 64]), op=ALU.mult))
            v5 = lambda ap: ap.rearrange("p (h x y d) -> p h x y d", x=2, y=2, d=16)
            s4 = sin_ap.rearrange("p (x y d) -> p x y d", x=2, y=2)
            for y in range(2):
                S.op("dve", [qn, rope], [sq], lambda e: e.tensor_tensor(
                    out=v5(sq[:, 0:n])[:, :, :, y, :], in0=v5(qn[:, 0:n])[:, :, :, 1 - y, :],
                    in1=s4[:, :, y, :].unsqueeze(1).to_broadcast([128, nh, 2, 16]), op=ALU.mult))
            S.op("dve", [t1, sq], [out_ap[0]], lambda e: e.tensor_tensor(out=out_ap[1], in0=v3(t1[:, 0:n]), in1=v3(sq[:, 0:n]), op=ALU.add))

        def prep_batch(raw, sq, ss, T, nh, g_ap, out_buf, out_ap, inplace=False):
            n = T * nh
            r3 = raw[:, 0:T, :].rearrange("p t (h d) -> p (t h) d", d=64)
            s3 = sq[:, 0:T, :].rearrange("p t (h d) -> p (t h) d", d=64)
            S.op("act", [raw], [sq], lambda e: e.activation(out=sq[:, 0:T, :], in_=raw[:, 0:T, :], func=AF.Square))
            S.op("dve", [sq], [ss], lambda e: e.tensor_reduce(out=ss[:, 0:n], in_=s3, axis=AX.X, op=ALU.add))
            rstd_of(ss, lambda: ss[:, 0:n], 64)
            S.op("dve", [raw, ss], [raw], lambda e: e.tensor_tensor(out=r3, in0=r3, in1=ss[:, 0:n].unsqueeze(2).to_broadcast([128, n, 64]), op=ALU.mult))
            if inplace:
                S.op("dve", [raw], [raw], lambda e: e.tensor_tensor(out=r3, in0=r3, in1=g_ap.unsqueeze(1).to_broadcast([128, n, 64]), op=ALU.mult))
                return
            S.op("dve", [raw], [out_buf], lambda e: e.tensor_tensor(out=out_ap.rearrange("p t (h d) -> p (t h) d", d=64), in0=r3, in1=g_ap.unsqueeze(1).to_broadcast([128, n, 64]), op=ALU.mult))

        def residual(i, ps, cgi, j):
            tmp = restmp.get()
            S.op("dve", [ps, gbc], [tmp], lambda e: e.tensor_tensor(out=tmp[:], in0=ps[:], in1=gbc[:, j, cgi * 512:(cgi + 1) * 512], op=ALU.mult))
            rstate["n"] += 1
            S.op("dve", [tmp, xs[i]], [xs[i]], lambda e: e.tensor_tensor(out=xs[i][:, cgi * 512:(cgi + 1) * 512], in0=xs[i][:, cgi * 512:(cgi + 1) * 512], in1=tmp[:], op=ALU.add))

        restmp = Ring([S.sb(f"restmp{i}", [128, 512], F32) for i in range(1)])
        rstate = {"n": 0}

        def mixer0():
            l = 0
            with ExitStack() as ph:
                hTg = [S.sb("hT0_0", [128, 8, 256], BF16, ph)] + [S.sb(f"hT0_{g}", [128, 8, 512], BF16, ph) for g in range(1, 5)]
                with ExitStack() as ph2:
                    norm_phase(0, 0, hTg, ph2)
                    import os
                    if os.environ.get("KSUB") not in ("a", "b"):
                        mk_gate(0, 16, ph2)
                    S.barrier()
                if stage == "norm":
                    return
                mixTa = S.sb("mixTa", [128, 4, NT * 128], BF16, ph)
                with ExitStack() as ph2:
                    wring = Ring([S.sbd(f"w0_{i}", [128, 8 * 384], BF16, ph2) for i in range(2)])
                    gateb = S.sbd("gateb", [128, 16], F32, ph2)
                    headg = S.sbd("headg", [128, 512], F32, ph2)
                    S.dma("sp", gateb.sem, gateb[:], D["gate_b_bc"], [], [gateb])
                    S.dma("sp", headg.sem, headg[:], D["head_g_bc"], [], [headg])
                    mlstm(hTg, mixTa, gateb, headg, wring, ph2)
                    S.barrier()
                if stage == "mlstm":
                    return
                with ExitStack() as ph2:
                    gqa_attn(0, hTg, mixTa, None, ph2)
                    S.barrier()

        def mlstm_gates(hTg, gateb, wring, pg, es, eb, edec, ekw):
            G = S.sb("G", [128, NT, 16], F32, pg)
            wgb = S.sbd("wgates", [128, 8 * 16], BF16, pg)
            wg = wload(wgb, 16, D["even_w"][:, 2048:2064].rearrange("(k p) n -> p k n", p=128))
            for i in range(NT):
                g, off = tok_group(i)
                ps = psO.get()
                for k in range(8):
                    S.op("pe", [hTg[g], wgb], [ps], lambda e: e.matmul(ps[:, 0:16], lhsT=hTg[g][:, k, off:off + 128], rhs=wg[:, k, :], start=(k == 0), stop=(k == 7)))
                S.op("dve", [ps, gateb], [G], lambda e: e.tensor_tensor(out=G[:, i, :], in0=ps[:, 0:16], in1=gateb[:], op=ALU.add))
            E = S.sb("E", [128, 2, NT, 4], F32, pg)
            for d in range(2):
                S.op("act", [G], [E], lambda e: e.activation(out=E[:, d], in_=G[:, :, 4 + 8 * d:8 + 8 * d], func=AF.Exp, scale=-1.0))
            S.op("dve", [E], [E], lambda e: e.tensor_scalar(out=E[:], in0=E[:], scalar1=1.0, scalar2=None, op0=ALU.add))
            S.op("act", [E], [E], lambda e: e.activation(out=E[:], in_=E[:], func=AF.Ln))
            tg = S.sb("tg", [128, NT, 4], F32, pg)
            f72 = lambda ap: ap.rearrange("p t h -> p (t h)")
            trib = S.sb("trib", [128, 2, 128], BF16, pg)
            S.op("dve", [cst], [trib], lambda e: e.tensor_copy(out=trib[:], in_=cst[:, 128:384].rearrange("p (a b) -> p a b", a=2)))
            Ehi = S.sb("Ehi", [128, 2, NT, 4], BF16, pg)
            Ehf = S.sb("Ehf", [128, 2, NT, 4], F32, pg)
            Elo = S.sb("Elo", [128, 2, NT, 4], BF16, pg)
            S.op("dve", [E], [Ehi], lambda e: e.tensor_copy(out=Ehi[:], in_=E[:]))
            S.op("dve", [Ehi], [Ehf], lambda e: e.tensor_copy(out=Ehf[:], in_=Ehi[:]))
            S.op("dve", [E, Ehf], [Ehf], lambda e: e.tensor_tensor(out=Ehf[:], in0=E[:], in1=Ehf[:], op=ALU.subtract))
            S.op("dve", [Ehf], [Elo], lambda e: e.tensor_copy(out=Elo[:], in_=Ehf[:]))
            for d in range(2):
                psb = psO.get()
                S.op("pe", [trib, Ehi], [psb], lambda e: e.matmul(psb[:, 0:72], lhsT=trib[:, d, :], rhs=f72(Ehi[:, d]), start=True, stop=False))
                S.op("pe", [trib, Elo], [psb], lambda e: e.matmul(psb[:, 0:72], lhsT=trib[:, d, :], rhs=f72(Elo[:, d]), start=False, stop=True))
                S.op("pe", [ones_b, Ehi], [psb], lambda e: e.matmul(psb[:, 72:144], lhsT=ones_b[:], rhs=f72(Ehi[:, d]), start=True, stop=False))
                S.op("pe", [ones_b, Elo], [psb], lambda e: e.matmul(psb[:, 72:144], lhsT=ones_b[:], rhs=f72(Elo[:, d]), start=False, stop=True))
                S.op("dve", [psb, G], [tg], lambda e: e.tensor_tensor(out=tg[:], in0=psb[:, 0:72].rearrange("p (t h) -> p t h", h=4), in1=G[:, :, 8 * d:8 * d + 4], op=ALU.add))
                S.op("act", [tg], [es], lambda e: e.activation(out=es[:, d], in_=tg[:], func=AF.Exp))
                S.op("act", [psb], [eb], lambda e: e.activation(out=f72(eb[:, d]), in_=psb[:, 0:72], func=AF.Exp, scale=-1.0))
                S.op("act", [psb], [edec], lambda e: e.activation(out=f72(edec[:, d]), in_=psb[:, 72:144], func=AF.Exp, scale=-1.0))
                S.op("dve", [es, edec], [ekw], lambda e: e.tensor_tensor(out=ekw[:, d], in0=es[:, d], in1=edec[:, d], op=ALU.mult))

            pass
            pass
            pass
            pass
            pass

        def mlstm(hTg, mixTa, gateb, headg, wring, ph):
            es = S.sb("es", [128, 2, NT, 4], F32, ph)
            eb = S.sb("eb", [128, 2, NT, 4], F32, ph)
            edec = S.sb("edec", [128, 2, NT, 4], F32, ph)
            ekw = S.sb("ekw", [128, 2, NT, 4], F32, ph)
            with ExitStack() as pg:
                mlstm_gates(hTg, gateb, wring, pg, es, eb, edec, ekw)
                S.barrier()
            KS_ = ""
            KH_ = -1
            qT = S.sb("qTa", [128, NT * 128], BF16, ph)
            kT = S.sb("kTa", [128, NT * 128], BF16, ph)
            ktok = S.sb("ktok", [128, NT, 128], BF16, ph)
            vaug = S.sb("vaug", [128, NT, 130], BF16, ph)
            hraw = [S.sb(f"hraw{d}", [128, NT, 130], F32, ph) for d in range(2)]
            rnm = S.sb("rnm", [128, 2, NT], F32, ph)
            Cst = [S.sb(f"Cst{d}", [128, 129], F32, ph) for d in range(2)]
            Cbf3 = [[S.sb(f"Cbf{d}_{r}", [128, 130], BF16, ph) for r in range(3)] for d in range(2)]
            PTr = Ring([S.sb(f"PTm{i}", [128, 128], BF16, ph) for i in range(4)])
            kwr = Ring([S.sb(f"kwm{i}", [128, 128], BF16, ph) for i in range(2)])
            hss = S.sb("hss", [128, NT], F32, ph)
            hjunk = S.sb("hjunk", [128, 128], BF16, ph)
            ogr = Ring([S.sb(f"og{i}", [128, 128], F32, ph) for i in range(2)])
            t1r = Ring([S.sb(f"mt1{i}", [128, 128], F32, ph) for i in range(2)])
            mxr = Ring([S.sb(f"mmx{i}", [128, 128], BF16, ph) for i in range(2)])
            S.op("dve", [], [vaug], lambda e: e.memset(vaug[:, :, 128:129], 1.0))
            orders = [list(range(NT)), [1, 0] + list(range(NT - 1, 1, -1))]
            KS = 128.0 ** -0.5

            def load_qkv(hd):
                wb_ = wring.get()
                src = D["even_w"][:, 0:1536].rearrange("(k p) (g h n) -> p k g h n", p=128, g=3, h=4)[:, :, :, hd, :]
                wq_ = wb_[:, 0:8 * 384].rearrange("p (k g n) -> p k g n", k=8, g=3)
                for g3 in range(3):
                    S.dma("pool", wb_.sem, wq_[:, :, g3, :], src[:, :, g3, :], [], [wb_])
                return wb_, wq_

            woring = Ring([S.sbd(f"wo_{i}", [128, 8 * 128], BF16, ph) for i in range(2)])
            nxt_w = load_qkv(0)
            for h in range(4):
                wb, wq = nxt_w
                wob = woring.get()
                wo = wload(wob, 128, D["even_w"][:, 1536 + h * 128:1536 + (h + 1) * 128].rearrange("(k p) n -> p k n", p=128))
                if h + 1 < 4:
                    nxt_w = load_qkv(h + 1)
                flip = 0
                for (g, c0, n) in GROUPS:
                    for which, dst, scl in ((0, qT, 1.0), (1, kT, KS)):
                        ps = psA.get()
                        for k in range(8):
                            S.op("pe", [wb, hTg[g]], [ps], lambda e: e.matmul(ps[:, 0:n], lhsT=wq[:, k, which, :], rhs=hTg[g][:, k, 0:n], start=(k == 0), stop=(k == 7)))
                        if flip % 2 == 0:
                            S.op("act", [ps], [dst], lambda e: e.activation(out=dst[:, c0:c0 + n], in_=ps[:, 0:n], func=AF.Copy, scale=scl))
                        else:
                            S.op("dve", [ps], [dst], lambda e: e.tensor_scalar(out=dst[:, c0:c0 + n], in0=ps[:, 0:n], scalar1=scl, scalar2=None, op0=ALU.mult))
                        flip += 1
                if KS_ == "m2a" and h == KH_:
                    return
                for i in range(NT):
                    g, off = tok_group(i)
                    ps = psA.get()
                    for k in range(8):
                        S.op("pe", [wb, hTg[g]], [ps], lambda e: e.matmul(ps[:, 0:256], lhsT=hTg[g][:, k, off:off + 128], rhs=wb[:, k * 384 + 128:k * 384 + 384], start=(k == 0), stop=(k == 7)))
                    S.op("act", [ps], [ktok], lambda e: e.activation(out=ktok[:, i, :], in_=ps[:, 0:128], func=AF.Copy, scale=KS))
                    S.op("dve", [ps], [vaug], lambda e: e.tensor_copy(out=vaug[:, i, 0:128], in_=ps[:, 128:256]))
                if KS_ == "m2b" and h == KH_:
                    return
                if h == 0:
                    pass
                    pass
                    pass
                    pass
                if KS_ == "m2" and h == KH_:
                    return
                written = [False] * NT
                PTs = {}

                def emitA2(step):
                    ii = [orders[d][step] for d in range(2)]
                    col = lambda a, d: a[:, d, ii[d], h:h + 1]
                    css = [slice(i * 128, (i + 1) * 128) for i in ii]
                    pss2, kws, pscs = [], [], []
                    for d in range(2):
                        pss = psS.get()
                        S.op("pe", [kT, qT], [pss], lambda e: e.matmul(pss[:, 0:128], lhsT=kT[:, css[d]], rhs=qT[:, css[d]], start=True, stop=True))
                        pss2.append(pss)
                    if step < NT - 1:
                        for d in range(2):
                            kw = kwr.get()
                            S.op("act", [ktok, ekw], [kw], lambda e: e.activation(out=kw[:], in_=ktok[:, ii[d], :], func=AF.Copy, scale=col(ekw, d)))
                            kws.append(kw)
                        for d in range(2):
                            psc = psA.get()
                            S.op("pe", [kws[d], vaug], [psc], lambda e: e.matmul(psc[:, 0:129], lhsT=kws[d][:], rhs=vaug[:, ii[d], 0:129], start=True, stop=True))
                            pscs.append(psc)
                    for d in range(2):
                        PT = PTr.get()
                        msk = TRIU() if d == 0 else TRIL()
                        S.op("dve", [pss2[d], es, cst], [PT], lambda e: e.scalar_tensor_tensor(out=PT[:], in0=pss2[d][:, 0:128], scalar=col(es, d), in1=msk, op0=ALU.mult, op1=ALU.mult))
                        PTs[(step, d)] = PT
                    if step < NT - 1:
                        for d in range(2):
                            psc = pscs[d]
                            if step == 0:
                                S.op("dve", [psc], [Cst[d]], lambda e: e.tensor_copy(out=Cst[d][:], in_=psc[:, 0:129]))
                            else:
                                S.op("dve", [psc, Cst[d], edec], [Cst[d]], lambda e: e.scalar_tensor_tensor(out=Cst[d][:], in0=Cst[d][:], scalar=col(edec, d), in1=psc[:, 0:129], op0=ALU.mult, op1=ALU.add))
                            cb3 = Cbf3[d][(step + 1) % 3]
                            S.op("dve", [Cst[d]], [cb3], lambda e: e.tensor_copy(out=cb3[:, 0:129], in_=Cst[d][:]))

                def emitB(step, d):
                    i = orders[d][step]
                    col = lambda a: a[:, d, i, h:h + 1]
                    cs = slice(i * 128, (i + 1) * 128)
                    PT = PTs.pop((step, d))
                    acc = psO.get()
                    if step > 0:
                        cb3 = Cbf3[d][step % 3]
                        S.op("pe", [qT, cb3], [acc], lambda e: e.matmul(acc[:, 0:129], lhsT=qT[:, cs], rhs=cb3[:, 0:129], start=True, stop=False))
                    S.op("pe", [PT, vaug], [acc], lambda e: e.matmul(acc[:, 0:129], lhsT=PT[:], rhs=vaug[:, i, 0:129], start=(step == 0), stop=True))
                    S.op("act", [acc, eb], [hraw[d]], lambda e: e.activation(out=hraw[d][:, i, 0:129], in_=acc[:, 0:129], func=AF.Copy, scale=col(eb)))

                emitA2(0)
                for step in range(NT):
                    if step + 1 < NT:
                        emitA2(step + 1)
                    emitB(step, 0)
                    emitB(step, 1)
                if h == 0:
                    pass
                    pass
                if KS_ == "m3" and h == KH_:
                    return
                for d in range(2):
                    S.op("act", [hraw[d]], [rnm], lambda e: e.activation(out=rnm[:, d, :], in_=hraw[d][:, :, 128], func=AF.Abs))
                S.op("dve", [rnm], [rnm], lambda e: e.tensor_scalar(out=rnm[:], in0=rnm[:], scalar1=1.0, scalar2=None, op0=ALU.max))
                S.op("dve", [rnm], [rnm], lambda e: e.reciprocal(out=rnm[:], in_=rnm[:]))
                for d in range(2):
                    S.op("dve", [hraw[d], rnm], [hraw[d]], lambda e: e.tensor_tensor(out=hraw[d][:, :, 0:128], in0=hraw[d][:, :, 0:128], in1=rnm[:, d, :].unsqueeze(2).to_broadcast([128, NT, 128]), op=ALU.mult))
                S.op("dve", [hraw[0], hraw[1]], [hraw[0]], lambda e: e.tensor_tensor(out=hraw[0][:, :, 0:128], in0=hraw[0][:, :, 0:128], in1=hraw[1][:, :, 0:128], op=ALU.add))
                S.op("dve", [], [hss], lambda e: e.memset(hss[:], 1.0))
                for i in range(NT):
                    S.op("act", [hraw[0]], [hjunk, hss], lambda e: e.activation(out=hjunk[:], in_=hraw[0][:, i, 0:128], func=AF.Square, accum_out=hss[:, i:i + 1]))
                rstd_of(hss, lambda: hss[:], 128)

                def out_stage1(i):
                    g, off = tok_group(i)
                    ps = psA.get()
                    for k in range(8):
                        S.op("pe", [wob, hTg[g]], [ps], lambda e: e.matmul(ps[:, 0:128], lhsT=hTg[g][:, k, off:off + 128], rhs=wo[:, k, :], start=(k == 0), stop=(k == 7)))
                    og = ogr.get()
                    S.op("act", [ps], [og], lambda e: e.activation(out=og[:], in_=ps[:, 0:128], func=AF.Sigmoid))
                    return og

                def out_stage2(i, og):
                    t1 = t1r.get()
                    S.op("dve", [hraw[0], hss, headg], [t1], lambda e: e.scalar_tensor_tensor(out=t1[:], in0=hraw[0][:, i, 0:128], scalar=hss[:, i:i + 1], in1=headg[:, h * 128:(h + 1) * 128], op0=ALU.mult, op1=ALU.mult))
                    mx = mxr.get()
                    S.op("dve", [t1, og], [mx], lambda e: e.tensor_tensor(out=mx[:], in0=t1[:], in1=og[:], op=ALU.mult))
                    pt = psT.get()
                    S.op("pe", [mx, identb], [pt], lambda e: e.transpose(out=pt[:, 0, :], in_=mx[:], identity=identb[:]))
                    S.op("act", [pt], [mixTa], lambda e: e.activation(out=mixTa[:, h, i * 128:(i + 1) * 128], in_=pt[:, 0, :], func=AF.Copy))

                ogn = out_stage1(0)
                for i in range(NT):
                    ogc = ogn
                    if i + 1 < NT:
                        ogn = out_stage1(i + 1)
                    out_stage2(i, ogc)
                if KS_ == "m4" and h == KH_:
                    return

        def attn_scores_exp_pv(kv_specs, nheads_per_kv, qT, q_sl, PTr, accs, first, last):
            pass

        def gqa_attn(l, hTg, other, wring, ph):
            wname = "even_w" if l == 0 else "odd_w"
            qc0, kc0 = (2064, 2576) if l == 0 else (0, 512)
            swag = S.sbd(f"swag{l}", [128, 128], F32, ph)
            roper = Ring([S.sbd(f"rope{l}_{i}", [128, 2, 64], F32, ph) for i in range(2)])

            def get_rope(jt):
                rb = roper.get()
                S.dma("sp", rb.sem, rb[:], D["rope"][:, :, jt, :], [], [rb])
                return rb
            S.dma("sp", swag.sem, swag[:], D["swa_g_bc" if l == 0 else "gqa_g_bc"], [], [swag])
            gq = S.sb(f"gq{l}", [128, 64], F32, ph)
            S.op("dve", [swag], [gq], lambda e: e.tensor_scalar(out=gq[:], in0=swag[:, 0:64], scalar1=0.125, scalar2=None, op0=ALU.mult))
            esink = S.sb(f"esink{l}", [128, 8], F32, ph)
            if l == 0:
                sinkb = S.sbd("sinkb", [128, 8], F32, ph)
                S.dma("sp", sinkb.sem, sinkb[:], D["sink_bc"], [], [sinkb])
                S.op("act", [sinkb], [esink], lambda e: e.activation(out=esink[:], in_=sinkb[:], func=AF.Exp))
            else:
                S.op("dve", [], [esink], lambda e: e.memset(esink[:], 0.0))
            wkb = S.sbd(f"wkv{l}", [128, 8 * 256], BF16, ph)
            wkv = wload(wkb, 256, D[wname][:, kc0:kc0 + 256].rearrange("(k p) n -> p k n", p=128))
            kTd = [S.sb(f"kTd{g}", [128, NT * 128], BF16, ph) for g in range(2)]
            vb = S.sb("vb", [128, NT, 2, 66], BF16, ph)
            S.op("dve", [], [vb], lambda e: e.memset(vb[:, :, :, 64:65], 1.0))
            with ExitStack() as pk:
                TB = 9
                kraw = S.sb("kraw", [128, TB, 128], F32, pk)
                ksq = S.sb("ksq", [128, TB, 128], F32, pk)
                kt2 = S.sb("kt2", [128, TB, 128], F32, pk)
                kss = S.sb("kss", [128, TB * 2], F32, pk)
                knb = S.sb("knb", [128, TB, 128], BF16, pk)
                kd = S.sb("kd", [128, 2, 2, 64], BF16, pk)
                rtab = S.sbd("rtab", [128, 2, TB, 64], F32, pk)
                for t0 in range(0, NT, TB):
                    tl = list(range(t0, t0 + TB))
                    r0 = max(0, 2 - t0)
                    nl = TB - r0
                    j0 = t0 + r0 - 2
                    for cs_ in range(2):
                        S.dma("sp", rtab.sem, rtab[:, cs_, 0:nl, :], D["rope"][:, cs_, j0:j0 + nl, :], [], [rtab])
                    for i in tl:
                        g, off = tok_group(i)
                        ps = psA.get()
                        for k in range(8):
                            S.op("pe", [wkb, hTg[g]], [ps], lambda e: e.matmul(ps[:, 0:256], lhsT=hTg[g][:, k, off:off + 128], rhs=wkv[:, k, :], start=(k == 0), stop=(k == 7)))
                        S.op("act", [ps], [vb], lambda e: e.activation(out=vb[:, i, :, 0:64], in_=ps[:, 128:256].rearrange("p (g d) -> p g d", g=2), func=AF.Copy))
                        S.op("dve", [ps], [kraw], lambda e: e.tensor_copy(out=kraw[:, i - t0, :], in_=ps[:, 0:128]))
                    prep_batch(kraw, ksq, kss, TB, 2, swag[:, 64:128], None, None, inplace=True)
                    if r0 > 0:
                        S.op("act", [kraw], [knb], lambda e: e.activation(out=knb[:, 0:r0, :], in_=kraw[:, 0:r0, :], func=AF.Copy))
                    v4 = lambda ap: ap.rearrange("p t (h d) -> p t h d", d=64)
                    cosb = rtab[:, 0, 0:nl, :].unsqueeze(2).to_broadcast([128, nl, 2, 64])
                    S.op("dve", [kraw, rtab], [ksq], lambda e: e.tensor_tensor(out=v4(ksq[:, r0:TB, :]), in0=v4(kraw[:, r0:TB, :]), in1=cosb, op=ALU.mult))
                    v6 = lambda ap: ap.rearrange("p t (h x y d) -> p t h x y d", h=2, x=2, y=2)
                    s5 = rtab[:, 1, 0:nl, :].rearrange("p t (x y d) -> p t x y d", x=2, y=2)
                    for hh_ in range(2):
                        for y in range(2):
                            S.op("dve", [kraw, rtab], [kt2], lambda e: e.tensor_tensor(
                                out=v6(kt2[:, r0:TB, :])[:, :, hh_, :, y, :], in0=v6(kraw[:, r0:TB, :])[:, :, hh_, :, 1 - y, :],
                                in1=s5[:, :, :, y, :], op=ALU.mult))
                    S.op("dve", [ksq, kt2], [knb], lambda e: e.tensor_tensor(out=knb[:, r0:TB, :], in0=ksq[:, r0:TB, :], in1=kt2[:, r0:TB, :], op=ALU.add))
                    for i in tl:
                        S.op("dve", [knb], [kd], lambda e: e.tensor_copy(out=kd[:], in_=knb[:, i - t0, :].rearrange("p (g d) -> p g d", g=2).unsqueeze(2).to_broadcast([128, 2, 2, 64])))
                        pt = psT.get()
                        for g2 in range(2):
                            S.op("pe", [kd, identb], [pt], lambda e: e.transpose(out=pt[:, g2, :], in_=kd[:, g2].rearrange("p a d -> p (a d)"), identity=identb[:]))
                        S.op("act", [pt], [kTd[0]], lambda e: e.activation(out=kTd[0][:, i * 128:(i + 1) * 128], in_=pt[:, 0, :], func=AF.Copy))
                        S.op("act", [pt], [kTd[1]], lambda e: e.activation(out=kTd[1][:, i * 128:(i + 1) * 128], in_=pt[:, 1, :], func=AF.Copy))
                S.barrier()
            wqb = S.sbd(f"wqq{l}", [128, 8 * 512], BF16, ph)
            wq = wload(wqb, 512, D[wname][:, qc0:qc0 + 512].rearrange("(k p) n -> p k n", p=128))
            wout = S.sbd(f"wout{l}", [128, 8 * 1024], BF16, ph)
            woutv = wout[:, :].rearrange("p (k n) -> p k n", k=8)
            S.dma("pool", wout.sem, woutv, D["w_out"][l].rearrange("(k p) n -> p k n", p=128), [], [wout])
            wk = (S.sb("wk_sq", [128, 512], F32, ph), S.sb("wk_ss", [128, 8], F32, ph), S.sb("wk_qn", [128, 512], F32, ph), S.sb("wk_t1", [128, 512], F32, ph))
            import os
            KS_ = os.environ.get("KSUB", "")
            if KS_ == "w1" or (KS_ == "g1k" and l == 1):
                return
            qb = S.sb("qb", [128, 8, 64], BF16, ph)
            qz = S.sb("qz", [128, 2, 4, 128], BF16, ph)
            S.op("dve", [], [qz], lambda e: e.memset(qz[:], 0.0))
            wmb4 = S.sb("wmb4", [128, 2, 4, 128], BF16, ph)
            S.op("dve", [wmb], [wmb4], lambda e: e.tensor_copy(out=wmb4[:], in_=wmb[:, :, :].unsqueeze(2).to_broadcast([128, 2, 4, 128])))
            PTr = Ring([S.sb(f"PTw{i}", [128, 512], BF16, ph) for i in range(3)])
            den = S.sb("wden", [128, 8], F32, ph)
            mixb = S.sb("mixb", [128, 512], BF16, ph)
            mixTb = S.sb("mixTb", [128, 4, 128], BF16, ph)
            scoreS = Ring(psS.bufs + [psA.bufs[1]])
            psQ = Ring([psA.bufs[0]])

            def emit_qprep(i):
                g, off = tok_group(i)
                lat = i >= 2
                j = i - 2
                ps = psQ.get()
                for k in range(8):
                    S.op("pe", [wqb, hTg[g]], [ps], lambda e: e.matmul(ps[:, 0:512], lhsT=hTg[g][:, k, off:off + 128], rhs=wq[:, k, :], start=(k == 0), stop=(k == 7)))
                qk_prep(ps, ps[:, 0:512], 8, gq[:], j if lat else None, (qb, qb[:]), wk, get_rope(j) if lat else None)

            qtiles = list(range(NT) if l == 0 else range(2, NT))
            emit_qprep(qtiles[0])
            for qi, i in enumerate(qtiles):
                g, off = tok_group(i)
                lat = i >= 2
                j = i - 2
                pt = psT.get()
                for pr in range(4):
                    S.op("pe", [qb, identb], [pt], lambda e: e.transpose(out=pt[:, pr, :], in_=qb[:, 2 * pr:2 * pr + 2, :].rearrange("p a d -> p (a d)"), identity=identb[:]))
                S.op("act", [pt], [qz], lambda e: e.activation(out=qz[0:64, 0, :, :], in_=pt[0:64, 0:4, :], func=AF.Copy))
                S.op("dve", [pt], [qz], lambda e: e.tensor_copy(out=qz[64:128, 1, :, :], in_=pt[64:128, 0:4, :]))
                if qi + 1 < len(qtiles):
                    emit_qprep(qtiles[qi + 1])
                if KS_ == "w2a":
                    return
                if l == 1:
                    blocks = [(m, None) for m in range(NT)]
                elif lat:
                    blocks = [(0, None), (1, None)]
                    if j > 0:
                        blocks.append((i - 1, 0))
                    blocks.append((i, None))
                    if j < 15:
                        blocks.append((i + 1, 1))
                else:
                    blocks = [(0, None), (1, None)]
                for g2 in range(2):
                    acc = psO.get()

                    def emit_scores(m, msk):
                        pss = scoreS.get()
                        for half in range(2):
                            S.op("pe", [kTd[g2], qz], [pss], lambda e: e.matmul(
                                pss[:, half * 256:(half + 1) * 256], lhsT=kTd[g2][:, m * 128:(m + 1) * 128],
                                rhs=qz[:, half, 2 * g2:2 * g2 + 2, :].rearrange("p a q -> p (a q)"),
                                start=(half == 0), stop=(half == 1 and msk is None)))
                        if msk is not None:
                            S.op("pe", [identb, wmb4], [pss], lambda e: e.matmul(pss[:, 0:512], lhsT=identb[:], rhs=wmb4[:, msk, :, :].rearrange("p a q -> p (a q)"), start=False, stop=True))
                        return pss

                    queue = [emit_scores(*blocks[0])]
                    if len(blocks) > 1:
                        queue.append(emit_scores(*blocks[1]))
                    for bi, (m, msk) in enumerate(blocks):
                        pss = queue.pop(0)
                        if bi + 2 < len(blocks):
                            queue.append(emit_scores(*blocks[bi + 2]))
                        PT = PTr.get()
                        S.op("act", [pss], [PT], lambda e: e.activation(out=PT[:], in_=pss[:], func=AF.Exp))
                        for hh in range(4):
                            S.op("pe", [PT, vb], [acc], lambda e: e.matmul(acc[:, hh * 128:hh * 128 + 65], lhsT=PT[:, hh * 128:(hh + 1) * 128], rhs=vb[:, m, g2, 0:65], start=(bi == 0 and hh == 0), stop=(bi == len(blocks) - 1)))
                    if KS_ == "w2c":
                        return
                    a3 = acc[:, :].rearrange("p (h c) -> p h c", h=4)
                    S.op("dve", [acc, esink], [den], lambda e: e.tensor_tensor(out=den[:, g2 * 4:(g2 + 1) * 4].rearrange("p (b a) -> p b a", b=2), in0=a3[:, :, 64].rearrange("p (b a) -> p b a", b=2),
                                                                            in1=esink[:, g2 * 4:(g2 + 1) * 4].rearrange("p (a b) -> p b a", a=2), op=ALU.add))
                    S.op("dve", [den], [den], lambda e: e.reciprocal(out=den[:, g2 * 4:(g2 + 1) * 4], in_=den[:, g2 * 4:(g2 + 1) * 4]))
                    S.op("dve", [acc, den], [mixb], lambda e: e.tensor_tensor(
                        out=mixb[:, g2 * 256:(g2 + 1) * 256].rearrange("p (a b d) -> p b a d", a=2, b=2), in0=a3[:, :, 0:64].rearrange("p (b a) d -> p b a d", b=2),
                        in1=den[:, g2 * 4:(g2 + 1) * 4].rearrange("p (b a) -> p b a", b=2).unsqueeze(3).to_broadcast([128, 2, 2, 64]), op=ALU.mult))
                if KS_ == "w2d":
                    return
                pt2 = psT.get()
                for c in range(4):
                    S.op("pe", [mixb, identb], [pt2], lambda e: e.transpose(out=pt2[:, c, :], in_=mixb[:, c * 128:(c + 1) * 128], identity=identb[:]))
                S.op("act", [pt2], [mixTb], lambda e: e.activation(out=mixTb[:], in_=pt2[:, 0:4, :], func=AF.Copy))
                if (KS_ == "w2" and i == 2) or (KS_ == "g1q" and l == 1 and i == 3):
                    return
                for cgi in range(2):
                    pso = psQ.get()
                    for k in range(8):
                        if l == 0:
                            lhs = other[:, k, i * 128:(i + 1) * 128] if k < 4 else mixTb[:, k - 4, :]
                        else:
                            lhs = mixTb[:, k, :] if k < 4 else other[:, k - 4, j * 128:(j + 1) * 128]
                        S.op("pe", [other, mixTb, wout], [pso], lambda e: e.matmul(pso[:, 0:512], lhsT=lhs, rhs=woutv[:, k, cgi * 512:(cgi + 1) * 512], start=(k == 0), stop=(k == 7)))
                    residual(i, pso, cgi, 0 if lat else 1)

        def ffn_phase(l, tiles):
            tiles = list(tiles)
            with ExitStack() as ph:
                hTg = [S.sb(f"hF{l}_0", [128, 8, 256], BF16, ph)] + [S.sb(f"hF{l}_{g}", [128, 8, 512], BF16, ph) for g in range(1, 5)]
                with ExitStack() as ph2:
                    norm_phase(l, 1, hTg, ph2, tiles)
                    mk_gate(l, 40, ph2)
                    S.barrier()
                segs = [gg for gg in GROUPS if (gg[0] > 0 or 0 in tiles)]
                lo = segs[0][1]
                ranges = ([(0, 256)] if lo == 0 else []) + [(256, 2304)]
                GS = 3
                ur = Ring([S.sb(f"fu{l}_{i}", [128, 2304], F32, ph) for i in range(2)])
                yr = Ring([S.sb(f"fy{l}_{i}", [128, 2304], F32, ph) for i in range(2)])
                actT = S.sb(f"actT{l}", [128, GS, 2304], BF16, ph)
                wur = Ring([S.sbd(f"wu{l}_{i}", [128, 8 * 256], BF16, ph) for i in range(3)])
                wdr = Ring([S.sbd(f"wd{l}_{i}", [128, GS * 1024], BF16, ph) for i in range(2)])
                has_ctx = (lo == 0)
                wdraw = Ring([S.sb(f"wdraw{l}_{i}", [128, GS * 1024], BF16, ph) for i in range(1)]) if has_ctx else None
                upsrc = D["ffn_up"][l].rearrange("(k p) (g c n) -> p k g c n", p=128, g=2, c=22)
                wu_loaded = {}
                wd_loaded = {}

                def load_wu(cp):
                    if cp >= 22 or cp in wu_loaded:
                        return
                    wub = wur.get()
                    wu = wub[:, :].rearrange("p (k g n) -> p k g n", k=8, g=2)
                    for g3 in range(2):
                        S.dma("pool", wub.sem, wu[:, :, g3, :], upsrc[:, :, g3, cp, :], [], [wub])
                    wu_loaded[cp] = (wub, wu)

                def load_wd(c0):
                    if c0 >= 22 or c0 in wd_loaded:
                        return
                    ncg = min(GS, 22 - c0)
                    wdb = wdr.get()
                    wd = wdb[:, 0:ncg * 1024].rearrange("p (c n) -> p c n", c=ncg)
                    S.dma("pool", wdb.sem, wd, D["ffn_down"][l, c0 * 128:(c0 + ncg) * 128, :].rearrange("(c p) n -> p c n", p=128), [], [wdb])
                    wd_loaded[c0] = (wdb, wd)

                def scale_wd(c0):
                    ncg = min(GS, 22 - c0)
                    wdb, wd = wd_loaded[c0]
                    raw = None
                    if has_ctx:
                        rb = wdraw.get()
                        raw = rb[:, 0:ncg * 1024].rearrange("p (c n) -> p c n", c=ncg)
                        S.op("pool", [wdb], [rb], lambda e: e.tensor_copy(out=raw, in_=wd))
                        wd_loaded[c0] = (wdb, wd, rb, raw)
                    S.op("pool", [wdb, gbc], [wdb], lambda e: e.tensor_tensor(out=wd, in0=wd, in1=gbc[:, 0, :].unsqueeze(1).to_broadcast([128, ncg, 1024]), op=ALU.mult))
                    if not has_ctx:
                        wd_loaded[c0] = (wdb, wd, None, None)

                load_wu(0)
                load_wu(1)
                load_wd(0)

                upS = Ring(psA.bufs + psS.bufs)

                def emit_up1(cp):
                    load_wu(cp + 2)
                    wub, wu = wu_loaded[cp]
                    ys = []
                    for gv in range(2):
                        ch = gv * 22 + cp
                        u = ur.get()
                        y = yr.get()
                        w0 = cw[:, l, 0, ch:ch + 1]
                        w1 = cw[:, l, 1, ch:ch + 1]
                        w2 = cw[:, l, 2, ch:ch + 1]
                        for (g, t0, n) in segs:
                            ps = upS.get()
                            for k in range(8):
                                S.op("pe", [wub, hTg[g]], [ps], lambda e: e.matmul(ps[:, 0:n], lhsT=wu[:, k, gv, :], rhs=hTg[g][:, k, 0:n], start=(k == 0), stop=(k == 7)))
                            S.op("act", [ps], [u], lambda e: e.activation(out=u[:, t0:t0 + n], in_=ps[:, 0:n], func=AF.Copy))
                        S.op("act", [u, cw, cb], [y], lambda e: e.activation(out=y[:, lo:2304], in_=u[:, lo:2304], func=AF.Identity, scale=w1, bias=cb[:, l, ch:ch + 1]))
                        for (a, b_) in ranges:
                            S.op("dve", [u, cw, y], [y], lambda e: e.scalar_tensor_tensor(out=y[:, a + 1:b_], in0=u[:, a:b_ - 1], scalar=w0, in1=y[:, a + 1:b_], op0=ALU.mult, op1=ALU.add))
                            S.op("dve", [u, cw, y], [y], lambda e: e.scalar_tensor_tensor(out=y[:, a:b_ - 1], in0=u[:, a + 1:b_], scalar=w2, in1=y[:, a:b_ - 1], op0=ALU.mult, op1=ALU.add))
                        ys.append(y)
                    S.op("act", [ys[0]], [ys[0]], lambda e: e.activation(out=ys[0][:, lo:2304], in_=ys[0][:, lo:2304], func=AF.Silu))
                    return ys

                def emit_up2(ys, ci):
                    S.op("dve", [ys[0], ys[1]], [actT], lambda e: e.tensor_tensor(out=actT[:, ci, lo:2304], in0=ys[0][:, lo:2304], in1=ys[1][:, lo:2304], op=ALU.mult))

                def emit_down(c0):
                    ncg = min(GS, 22 - c0)
                    scale_wd(c0)
                    wdb, wd, rb, raw = wd_loaded[c0]
                    for i in tiles:
                        for cgi in range(2):
                            ps = psO.get()
                            if i >= 2:
                                for ci in range(ncg):
                                    S.op("pe", [actT, wdb], [ps], lambda e: e.matmul(ps[:, 0:512], lhsT=actT[:, ci, i * 128:(i + 1) * 128], rhs=wd[:, ci, cgi * 512:(cgi + 1) * 512], start=(ci == 0), stop=(ci == ncg - 1)))
                                S.op("dve", [ps, xs[i]], [xs[i]], lambda e: e.tensor_tensor(out=xs[i][:, cgi * 512:(cgi + 1) * 512], in0=ps[:], in1=xs[i][:, cgi * 512:(cgi + 1) * 512], op=ALU.add))
                            else:
                                for ci in range(ncg):
                                    S.op("pe", [actT, rb], [ps], lambda e: e.matmul(ps[:, 0:512], lhsT=actT[:, ci, i * 128:(i + 1) * 128], rhs=raw[:, ci, cgi * 512:(cgi + 1) * 512], start=(ci == 0), stop=(ci == ncg - 1)))
                                residual(i, ps, cgi, 1)

                pending = None
                for c0 in range(0, 22, GS):
                    ncg = min(GS, 22 - c0)
                    ys0 = emit_up1(c0)
                    if pending is not None:
                        emit_down(pending)
                    load_wd(c0 + GS)
                    emit_up2(ys0, 0)
                    for ci in range(1, ncg):
                        emit_up2(emit_up1(c0 + ci), ci)
                    pending = c0
                emit_down(pending)
                S.barrier()

        def na_attn(hTg, mixTd, wring, ph):
            nag = S.sbd("nag", [128, 128], F32, ph)
            S.dma("sp", nag.sem, nag[:], D["na_g_bc"], [], [nag])
            gq = S.sb("nagq", [128, 64], F32, ph)
            S.op("dve", [nag], [gq], lambda e: e.tensor_scalar(out=gq[:], in0=nag[:, 0:64], scalar1=0.125, scalar2=None, op0=ALU.mult))
            kTn = S.sb("kTn", [128, NT * 128], BF16, ph)
            vn = S.sb("vn", [128, NT, 2, 66], BF16, ph)
            qTn = S.sb("qTn", [128, 2048], BF16, ph)
            S.op("dve", [], [vn], lambda e: e.memset(vn[:, :, :, 64:65], 1.0))
            TB = 9
            raw = S.sb("nraw", [128, TB, 128], F32, ph)
            sq = S.sb("nsq", [128, TB, 128], F32, ph)
            ssb = S.sb("nss", [128, TB * 2], F32, ph)
            nrm = S.sb("nnrm", [128, TB, 128], BF16, ph)
            biasr = Ring([S.sbd(f"nbias{i}", [128, 25, 128], F32, ph) for i in range(1)])
            stmp = Ring([S.sb(f"nstmp{i}", [128, 5, 128], F32, ph) for i in range(2)])
            PTr = Ring([S.sb(f"PTn{i}", [128, 7, 128], BF16, ph) for i in range(3)])
            rdn = Ring([S.sb(f"nrd{i}", [128, 1], F32, ph) for i in range(3)])
            mixd = S.sb("mixd", [128, 16, 2, 64], BF16, ph)
            naS = Ring(psS.bufs + psA.bufs)
            wsrc = D["odd_w"][:, 768:2304].rearrange("(k p) (g h n) -> p k g h n", p=128, g=3, h=4)
            wl = {}

            def load_w(pr):
                if pr >= 4 or pr in wl:
                    return
                wb = wring.get()
                wq = wb[:, 0:8 * 384].rearrange("p (k g n) -> p k g n", k=8, g=3)
                for g3 in range(3):
                    S.dma("pool", wb.sem, wq[:, :, g3, :], wsrc[:, :, g3, pr, :], [], [wb])
                wl[pr] = (wb, wq)

            load_w(0)
            for pr in range(4):
                wb, wq = wl[pr]
                load_w(pr + 1)
                for t0 in range(0, NT, TB):
                    tl = list(range(t0, min(NT, t0 + TB)))
                    for i in tl:
                        g, off = tok_group(i)
                        ps = psA.get()
                        for k in range(8):
                            S.op("pe", [wb, hTg[g]], [ps], lambda e: e.matmul(ps[:, 0:256], lhsT=hTg[g][:, k, off:off + 128], rhs=wb[:, k * 384 + 128:k * 384 + 384], start=(k == 0), stop=(k == 7)))
                        S.op("act", [ps], [vn], lambda e: e.activation(out=vn[:, i, :, 0:64], in_=ps[:, 128:256].rearrange("p (g d) -> p g d", g=2), func=AF.Copy))
                        S.op("dve", [ps], [raw], lambda e: e.tensor_copy(out=raw[:, i - t0, :], in_=ps[:, 0:128]))
                    prep_batch(raw, sq, ssb, len(tl), 2, nag[:, 64:128], nrm, nrm[:, 0:len(tl), :])
                    for i in tl:
                        pt = psT.get()
                        S.op("pe", [nrm, identb], [pt], lambda e: e.transpose(out=pt[:, 0, :], in_=nrm[:, i - t0, :], identity=identb[:]))
                        S.op("act", [pt], [kTn], lambda e: e.activation(out=kTn[:, i * 128:(i + 1) * 128], in_=pt[:, 0, :], func=AF.Copy))
                for t0 in range(2, NT, 8):
                    tl = list(range(t0, t0 + 8))
                    for i in tl:
                        g, off = tok_group(i)
                        ps2 = psA.get()
                        for k in range(8):
                            S.op("pe", [wb, hTg[g]], [ps2], lambda e: e.matmul(ps2[:, 0:128], lhsT=hTg[g][:, k, off:off + 128], rhs=wq[:, k, 0, :], start=(k == 0), stop=(k == 7)))
                        S.op("dve", [ps2], [raw], lambda e: e.tensor_copy(out=raw[:, i - t0, :], in_=ps2[:, 0:128]))
                    prep_batch(raw, sq, ssb, 8, 2, gq[:], nrm, nrm[:, 0:8, :])
                    for i in tl:
                        j = i - 2
                        pt2 = psT.get()
                        S.op("pe", [nrm, identb], [pt2], lambda e: e.transpose(out=pt2[:, 0, :], in_=nrm[:, i - t0, :], identity=identb[:]))
                        S.op("dve", [pt2], [qTn], lambda e: e.tensor_copy(out=qTn[:, j * 128:(j + 1) * 128], in_=pt2[:, 0, :]))
                for hh in range(2):
                    head = 2 * pr + hh
                    bt = biasr.get()
                    S.dma("sp", bt.sem, bt[:], D["na_bias"][head], [], [bt])
                    prs = slice(hh * 64, (hh + 1) * 64)

                    def blocks_of(j):
                        ci = 0 if j == 0 else 1 if j == 1 else 3 if j == 14 else 4 if j == 15 else 2
                        mlist = list(range(j - 2, j + 3)) if ci == 2 else na_blocks[j]
                        return ci, len(mlist), [0, 1] + [m + 2 for m in mlist]

                    def emit_scores(j):
                        ci, nb, keyt = blocks_of(j)
                        pA = naS.get()
                        pB = naS.get()
                        for bi, kt in enumerate(keyt):
                            pp, off2 = (pA, bi) if bi < 4 else (pB, bi - 4)
                            S.op("pe", [kTn, qTn], [pp], lambda e: e.matmul(pp[:, off2 * 128:(off2 + 1) * 128], lhsT=kTn[prs, kt * 128:(kt + 1) * 128], rhs=qTn[prs, j * 128:(j + 1) * 128], start=True, stop=True))
                        return pA, pB

                    def emit_norm(acc_, j_):
                        rd = rdn.get()
                        S.op("dve", [acc_], [rd], lambda e: e.reciprocal(out=rd[:], in_=acc_[:, 64:65]))
                        S.op("act", [acc_, rd], [mixd], lambda e: e.activation(out=mixd[:, j_, hh, :], in_=acc_[:, 0:64], func=AF.Copy, scale=rd[:, 0:1]))

                    pend_norm = None
                    nxt = emit_scores(0)
                    for j in range(16):
                        ci, nb, keyt = blocks_of(j)
                        pA, pB = nxt
                        if j + 1 < 16:
                            nxt = emit_scores(j + 1)
                        stp = stmp.get()
                        S.op("dve", [pA, bt], [stp], lambda e: e.tensor_tensor(out=stp[:, 0:2, :], in0=pA[:, 256:512].rearrange("p (b q) -> p b q", b=2), in1=bt[:, ci * 5:ci * 5 + 2, :], op=ALU.add))
                        S.op("dve", [pB, bt], [stp], lambda e: e.tensor_tensor(out=stp[:, 2:nb, :], in0=pB[:, 0:(nb - 2) * 128].rearrange("p (b q) -> p b q", b=nb - 2), in1=bt[:, ci * 5 + 2:ci * 5 + nb, :], op=ALU.add))
                        PT = PTr.get()
                        S.op("act", [pA], [PT], lambda e: e.activation(out=PT[:, 0:2, :], in_=pA[:, 0:256].rearrange("p (b q) -> p b q", b=2), func=AF.Exp))
                        S.op("act", [stp], [PT], lambda e: e.activation(out=PT[:, 2:2 + nb, :], in_=stp[:, 0:nb, :], func=AF.Exp))
                        acc = psO.get()
                        for bi, kt in enumerate(keyt):
                            S.op("pe", [PT, vn], [acc], lambda e: e.matmul(acc[:, 0:65], lhsT=PT[:, bi, :], rhs=vn[:, kt, hh, 0:65], start=(bi == 0), stop=(bi == len(keyt) - 1)))
                        if pend_norm is not None:
                            emit_norm(*pend_norm)
                        pend_norm = (acc, j)
                    emit_norm(*pend_norm)
                    pend_norm = None
                for j in range(16):
                    pt = psT.get()
                    S.op("pe", [mixd, identb], [pt], lambda e: e.transpose(out=pt[:, 0, :], in_=mixd[:, j, :, :].rearrange("p a d -> p (a d)"), identity=identb[:]))
                    S.op("dve", [pt], [mixTd], lambda e: e.tensor_copy(out=mixTd[:, pr, j * 128:(j + 1) * 128], in_=pt[:, 0, :]))

        def mixer1():
            with ExitStack() as ph:
                hTg = [S.sb("hT1_0", [128, 8, 256], BF16, ph)] + [S.sb(f"hT1_{g}", [128, 8, 512], BF16, ph) for g in range(1, 5)]
                with ExitStack() as ph2:
                    norm_phase(1, 0, hTg, ph2)
                    mk_gate(1, 16, ph2)
                    S.barrier()
                mixTd = S.sb("mixTd", [128, 4, 2048], BF16, ph)
                with ExitStack() as ph2:
                    wring = Ring([S.sbd(f"w1_{i}", [128, 8 * 384], BF16, ph2) for i in range(2)])
                    na_attn(hTg, mixTd, wring, ph2)
                    S.barrier()
                import os
                if os.environ.get("KSUB") == "nogqa1":
                    return
                with ExitStack() as ph2:
                    gqa_attn(1, hTg, mixTd, None, ph2)
                    S.barrier()

        mod_phase(0)
        if stage != "mod":
            mixer0()
        if stage not in ("l0mix", "mod", "norm", "mlstm"):
            ffn_phase(0, range(NT))
        if stage not in ("l0mix", "l0", "mod", "norm", "mlstm"):
            mod_phase(1)
            mixer1()
            if stage != "l1mix":
                ffn_phase(1, range(2, NT))
        osem = S.newsem("d_out")
        for i in range(2, NT):
            S.dma("sp", osem, out[(i - 2) * 128:(i - 1) * 128, :], xs[i][:], [xs[i]], [])
        if dbg:
            for i in range(2):
                S.dma("sp", osem, octx[i * 128:(i + 1) * 128, :], xs[i][:], [xs[i]], [])
        S._need("sp", osem, S.cnt[osem])
        S.barrier()
        print(f"[kernel] instructions={S.ninst} waits={S.nwait} sems={len(S.sems)}", flush=True)
    return nc


_CACHE = {}


def kernel(**inputs):
    shared, percore = host_prepare({k: np.asarray(v) for k, v in inputs.items()})
    if "nc" not in _CACHE:
        _CACHE["nc"] = build_program("full")
    nc = _CACHE["nc"]
    in_maps = []
    for b in range(8):
        m = dict(shared)
        m.update(percore[b])
        in_maps.append(m)
    res = run_bass_kernel_spmd(nc, in_maps, core_ids=list(range(8)))
    return np.stack([np.asarray(r["out"], np.float32) for r in res.results], axis=0)
```

```python
import numpy as np
from contextlib import ExitStack
import concourse.bass as bass
import concourse.mybir as mybir
from concourse.bass_utils import run_bass_kernel_spmd

F32 = mybir.dt.float32
BF16 = mybir.dt.bfloat16
AF = mybir.ActivationFunctionType
ALU = mybir.AluOpType
AX = mybir.AxisListType

ENGS = ("pe", "act", "dve", "pool", "sp")
NT = 18
EPS = 1e-6
NEGM = -30000.0


class Buf:
    __slots__ = ("t", "name", "w", "r", "sem", "psum")

    def __init__(self, t, name):
        self.t = t
        self.name = name
        self.w = None
        self.r = {}
        self.sem = None
        self.psum = False

    def __getitem__(self, idx):
        return self.t[idx]


class Ring:
    def __init__(self, bufs):
        self.bufs = bufs
        self.i = 0

    def get(self):
        b = self.bufs[self.i % len(self.bufs)]
        self.i += 1
        return b


SEM_LIMIT = 1500


class Sched:
    def __init__(self, nc, stack):
        self.nc = nc
        self.stack = stack
        self.eng = {"pe": nc.tensor, "act": nc.scalar, "dve": nc.vector,
                    "pool": nc.gpsimd, "sp": nc.sync}
        self.sems = {}
        self.cnt = {}
        self.epoch = {}
        self.cur = {}
        for e in ENGS:
            self.epoch[e] = 0
            self._new_epoch(e)
        self.seen = {e: {} for e in ENGS}
        self.ninst = 0
        self.nwait = 0
        self.nsem = 0
        self.nalloc = 0

    def _new_epoch(self, e):
        self.epoch[e] += 1
        key = f"{e}#{self.epoch[e]}"
        self.sems[key] = self.stack.enter_context(self.nc.semaphore("s_" + key.replace("#", "_")))
        self.cnt[key] = 0
        self.cur[e] = key

    def sb(self, name, shape, dt, stack=None):
        self.nalloc += 1
        name = f"{name}_{self.nalloc}"
        t = (stack or self.stack).enter_context(self.nc.sbuf_tensor(name, list(shape), dt))
        return Buf(t, name)

    def ps(self, name, shape, dt=F32):
        t = self.stack.enter_context(self.nc.psum_tensor(name, list(shape), dt))
        b = Buf(t, name)
        b.psum = True
        return b

    def newsem(self, name=None):
        self.nsem += 1
        name = name or f"d{self.nsem}"
        s = self.stack.enter_context(self.nc.semaphore(name))
        self.sems[name] = s
        self.cnt[name] = 0
        return name

    def sbd(self, name, shape, dt, stack=None):
        b = self.sb(name, shape, dt, stack)
        b.sem = self.newsem("d_" + b.name)
        return b

    @staticmethod
    def _eng_of(key):
        return key.split("#")[0] if "#" in key else None

    def _need(self, e, key, val):
        if self.seen[e].get(key, 0) >= val:
            return
        ke = self._eng_of(key)
        if ke is not None:
            ep = int(key.split("#")[1])
            for k2, v2 in self.seen[e].items():
                if v2 > 0 and self._eng_of(k2) == ke and int(k2.split("#")[1]) > ep:
                    return
        self.seen[e][key] = val
        self.eng[e].wait_ge(self.sems[key], val)
        self.nwait += 1

    def deps(self, e, reads, writes):
        for b in reads:
            if b.w is not None:
                k, v = b.w
                if not (self._eng_of(k) == e and e == "pe"):
                    self._need(e, k, v)
        for b in writes:
            if b.w is not None:
                k, v = b.w
                if self._eng_of(k) != e:
                    self._need(e, k, v)
            for k, v in b.r.items():
                if self._eng_of(k) != e:
                    self._need(e, k, v)

    def op(self, e, reads, writes, fn):
        pr = [b for b in reads if b.psum]
        if pr:
            reads = [b for b in reads if not b.psum]
            writes = list(writes) + [b for b in pr if b not in writes]
        self.deps(e, reads, writes)
        ins = fn(self.eng[e])
        if self.cnt[self.cur[e]] >= SEM_LIMIT:
            self._new_epoch(e)
        key = self.cur[e]
        self.cnt[key] += 1
        ins.then_inc(self.sems[key], 1)
        v = self.cnt[key]
        for b in reads:
            for k2 in [k2 for k2 in b.r if self._eng_of(k2) == e]:
                del b.r[k2]
            b.r[key] = v
        for b in writes:
            b.w = (key, v)
            b.r = {}
        self.ninst += 1
        return ins

    def dma(self, q, semkey, out_ap, in_ap, reads, writes, **kw):
        self.deps(q, reads, writes)
        ins = self.eng[q].dma_start(out=out_ap, in_=in_ap, **kw)
        self.cnt[semkey] += 16
        assert self.cnt[semkey] <= 2000, semkey
        ins.then_inc(self.sems[semkey], 16)
        v = self.cnt[semkey]
        for b in reads:
            b.r[semkey] = v
        for b in writes:
            b.w = (semkey, v)
            b.r = {}
        self.ninst += 1
        return ins

    def barrier(self):
        for e in ENGS:
            for k, v in list(self.cnt.items()):
                ke = self._eng_of(k)
                if ke == e or v == 0:
                    continue
                if ke is not None and k != self.cur[ke]:
                    if not (self.cnt[self.cur[ke]] == 0 and int(k.split("#")[1]) == self.epoch[ke] - 1):
                        continue
                self._need(e, k, v)


def _rope_tables():
    t = np.arange(2048)
    row = (t // 64).astype(np.float32)
    col = (t % 64).astype(np.float32)
    half = 32
    freq = (np.float32(10000.0) ** (-np.arange(0, half, 2, dtype=np.float32) / np.float32(half))).astype(np.float32)
    ang_r = row[:, None] * freq[None, :]
    ang_c = col[:, None] * freq[None, :]
    ang = np.concatenate([ang_r, ang_r, ang_c, ang_c], axis=-1).astype(np.float32)
    cos = np.cos(ang).astype(np.float32)
    sin = np.sin(ang).astype(np.float32)
    sgn = np.ones(64, np.float32)
    sgn[0:16] = -1.0
    sgn[32:48] = -1.0
    sinS = sin * sgn[None, :]
    cos = cos.reshape(16, 128, 64).transpose(1, 0, 2).copy()
    sinS = sinS.reshape(16, 128, 64).transpose(1, 0, 2).copy()
    return cos, sinS


def _na_tables(rpb):
    rows = 32
    wr = 8
    r = np.arange(rows)
    row_start = np.clip(r - wr // 2, 0, rows - wr)
    col = np.arange(64)
    col_start = np.clip(col - 8, 0, 48)
    col_ok = (col[None, :] >= col_start[:, None]) & (col[None, :] < col_start[:, None] + 16)
    dc = np.clip(col[None, :] - col[:, None] + 15, 0, 30)
    classes = [0, 1, 2, 14, 15]
    blocks = {}
    tab = np.full((8, 128, 25, 128), NEGM, np.float32)
    for ci, j in enumerate(classes):
        qrows = [2 * j, 2 * j + 1]
        lo = min(row_start[q] for q in qrows)
        hi = max(row_start[q] + wr - 1 for q in qrows)
        mlist = list(range(lo // 2, hi // 2 + 1))
        assert len(mlist) <= 5
        blocks[j] = mlist
        for si, m in enumerate(mlist):
            for kr in range(2):
                krow = 2 * m + kr
                for qr in range(2):
                    qrow = qrows[qr]
                    if not (row_start[qrow] <= krow < row_start[qrow] + wr):
                        continue
                    dr = krow - qrow + 7
                    sub = rpb[:, dr, :][:, dc]
                    sub = np.where(col_ok[None], sub, np.float32(NEGM))
                    tab[:, kr * 64:(kr + 1) * 64, ci * 5 + si, qr * 64:(qr + 1) * 64] = sub.transpose(0, 2, 1)
    return tab, blocks, classes


def _na_blocks():
    _, blocks, classes = _na_tables(np.zeros((8, 15, 31), np.float32))
    return blocks, classes


def host_prepare(inp):
    f = np.float32
    shared = {}
    shared["ada_w"] = np.ascontiguousarray(inp["ada_w"], f)
    shared["ada_bT"] = np.ascontiguousarray(inp["ada_b"].reshape(2, 48, 128).transpose(2, 0, 1), f)
    shared["norm_gT"] = np.ascontiguousarray(inp["norm_g"].reshape(2, 2, 8, 128).transpose(3, 0, 1, 2), f)
    shared["w_out"] = np.ascontiguousarray(inp["w_out"], f)
    shared["ffn_up"] = np.ascontiguousarray(inp["ffn_up"], f)
    shared["ffn_down"] = np.ascontiguousarray(inp["ffn_down"], f)
    shared["conv_wT"] = np.ascontiguousarray(inp["ffn_conv_w"].reshape(2, 3, 44, 128).transpose(3, 0, 1, 2), f)
    shared["conv_bT"] = np.ascontiguousarray(inp["ffn_conv_b"].reshape(2, 44, 128).transpose(2, 0, 1), f)
    shared["even_w"] = np.ascontiguousarray(inp["even_w_in"][0], f)
    shared["odd_w"] = np.ascontiguousarray(inp["odd_w_in"][0], f)
    bc = lambda a: np.ascontiguousarray(np.broadcast_to(np.asarray(a, f).reshape(1, -1), (128, a.size)))
    shared["gate_b_bc"] = bc(inp["mlstm_gate_b"][0])
    shared["head_g_bc"] = bc(inp["mlstm_head_g"][0])
    shared["swa_g_bc"] = bc(inp["swa_qk_g"][0])
    shared["sink_bc"] = bc(inp["swa_sink"][0])
    shared["gqa_g_bc"] = bc(inp["gqa_qk_g"][0])
    shared["na_g_bc"] = bc(inp["na_qk_g"][0])
    tab, _, _ = _na_tables(np.asarray(inp["na_rpb"][0], f))
    shared["na_bias"] = tab
    ident = np.eye(128, dtype=f)
    s = np.arange(128)
    triU = (s[:, None] <= s[None, :]).astype(f)
    triL = (s[:, None] >= s[None, :]).astype(f)
    wm = np.zeros((128, 2, 128), f)
    wm[:, 0, :] = np.where(s[None, :] <= s[:, None], 0.0, NEGM)
    wm[:, 1, :] = np.where(s[:, None] <= s[None, :], 0.0, NEGM)
    shared["consts"] = np.ascontiguousarray(np.concatenate([ident, triU, triL, wm.reshape(128, 256)], axis=1))
    cos, sinS = _rope_tables()
    shared["rope"] = np.ascontiguousarray(np.stack([cos, sinS], axis=1))
    percore = []
    for b in range(8):
        cc = np.stack([inp["c"][b].reshape(8, 128).T, inp["c_ctx"].reshape(8, 128).T], axis=-1)
        percore.append({"x": np.ascontiguousarray(inp["x"][b], f), "ctx": np.ascontiguousarray(inp["ctx"][b], f),
                        "cc": np.ascontiguousarray(cc, f)})
    return shared, percore


SHARED_SHAPES = {
    "ada_w": [2, 1024, 6144], "ada_bT": [128, 2, 48], "norm_gT": [128, 2, 2, 8], "w_out": [2, 1024, 1024],
    "ffn_up": [2, 1024, 5632], "ffn_down": [2, 2816, 1024], "conv_wT": [128, 2, 3, 44], "conv_bT": [128, 2, 44],
    "even_w": [1024, 2832], "odd_w": [1024, 2304], "gate_b_bc": [128, 16], "head_g_bc": [128, 512],
    "swa_g_bc": [128, 128], "sink_bc": [128, 8], "gqa_g_bc": [128, 128], "na_g_bc": [128, 128],
    "na_bias": [8, 128, 25, 128], "consts": [128, 640], "rope": [128, 2, 16, 64],
    "x": [2048, 1024], "ctx": [256, 1024], "cc": [128, 8, 2],
}


GROUPS = [(0, 0, 256), (1, 256, 512), (2, 768, 512), (3, 1280, 512), (4, 1792, 512)]


def tok_group(i):
    return (0, i * 128) if i < 2 else (1 + (i - 2) // 4, ((i - 2) % 4) * 128)


def build_program(stage="full"):
    nc = bass.Bass("TRN2", target_bir_lowering=False)
    D = {k: nc.dram_tensor(k, shp, F32, kind="ExternalInput").ap() for k, shp in SHARED_SHAPES.items()}
    out = nc.dram_tensor("out", [2048, 1024], F32, kind="ExternalOutput").ap()
    dbg = stage != "full"
    if dbg:
        octx = nc.dram_tensor("octx", [256, 1024], F32, kind="ExternalOutput").ap()
        dbgd = nc.dram_tensor("dbgd", [128, 8192], F32, kind="ExternalOutput").ap()
    na_blocks, na_classes = _na_blocks()

    with ExitStack() as st:
        S = Sched(nc, st)
        xs = [S.sbd(f"xs{i}", [128, 1024], F32) for i in range(NT)]
        cst = S.sbd("cst", [128, 640], F32)
        cc = S.sbd("cc", [128, 8, 2], F32)
        adab = S.sbd("adab", [128, 2, 48], F32)
        ngT = S.sbd("ngT", [128, 2, 2, 8], F32)
        cw = S.sbd("cw", [128, 2, 3, 44], F32)
        cb = S.sbd("cb", [128, 2, 44], F32)
        identb = S.sb("identb", [128, 128], BF16)
        wmb = S.sb("wmb", [128, 2, 128], BF16)
        ones_f = S.sb("ones_f", [128, 128], F32)
        ones_b = S.sb("ones_b", [128, 128], BF16)
        sc = S.sb("sc", [128, 8, 2], F32)
        modT = [S.sb(f"modT{l}", [128, 48, 2], F32) for l in range(2)]
        gbc = S.sb("gbc", [128, 2, 1024], F32)
        AB = S.sb("AB", [128, 8, 2], F32)

        psT = Ring([S.ps(f"psT{i}", [128, 8, 128], BF16) for i in range(2)])
        psA = Ring([S.ps(f"psA{i}", [128, 512], F32) for i in range(2)])
        psS = Ring([S.ps(f"psS{i}", [128, 512], F32) for i in range(2)])
        psO = Ring([S.ps(f"psO{i}", [128, 512], F32) for i in range(2)])

        IDF = lambda: cst[:, 0:128]
        TRIU = lambda: cst[:, 128:256]
        TRIL = lambda: cst[:, 256:384]

        S.dma("sp", cst.sem, cst[:], D["consts"], [], [cst])
        S.dma("sp", cc.sem, cc[:], D["cc"], [], [cc])
        S.dma("sp", adab.sem, adab[:], D["ada_bT"], [], [adab])
        S.dma("sp", ngT.sem, ngT[:], D["norm_gT"], [], [ngT])
        S.dma("sp", cw.sem, cw[:], D["conv_wT"], [], [cw])
        S.dma("sp", cb.sem, cb[:], D["conv_bT"], [], [cb])
        for i in range(NT):
            src = D["ctx"][i * 128:(i + 1) * 128, :] if i < 2 else D["x"][(i - 2) * 128:(i - 1) * 128, :]
            S.dma("sp", xs[i].sem, xs[i][:], src, [], [xs[i]])
        S.op("dve", [cst], [identb], lambda e: e.tensor_copy(out=identb[:], in_=cst[:, 0:128]))
        S.op("dve", [cst], [wmb], lambda e: e.tensor_copy(out=wmb[:], in_=cst[:, 384:640].rearrange("p (a b) -> p a b", a=2)))
        S.op("dve", [], [ones_f], lambda e: e.memset(ones_f[:], 1.0))
        S.op("dve", [], [ones_b], lambda e: e.memset(ones_b[:], 1.0))
        S.op("act", [cc], [sc], lambda e: e.activation(out=sc[:], in_=cc[:], func=AF.Silu))

        dstg = S.sb("dstg", [128, 128], F32) if dbg else None
        dstate = {"col": 0, "items": []}

        def dump(name, buf, ap, n):
            if not dbg:
                return
            stg = dstg
            sem = S.newsem()
            S.op("act", [buf], [stg], lambda e: e.activation(out=stg[:, 0:n], in_=ap, func=AF.Copy))
            c0 = dstate["col"]
            S.dma("sp", sem, dbgd[:, c0:c0 + n], stg[:, 0:n], [stg], [])
            S._need("sp", sem, S.cnt[sem])
            dstate["items"].append((name, c0, n))
            dstate["col"] = c0 + n
            print("DUMP", name, c0, n, flush=True)

        def wview(wb, shape_str, **kw):
            n = 1
            for v in kw.values():
                n *= v
            return wb

        def mod_phase(l):
            with ExitStack() as ph:
                ring = Ring([S.sbd(f"adaw{l}_{i}", [128, 8, 512], BF16, ph) for i in range(3)])
                schi = S.sb(f"schi{l}", [128, 8, 2], BF16, ph)
                schf = S.sb(f"schf{l}", [128, 8, 2], F32, ph)
                sclo = S.sb(f"sclo{l}", [128, 8, 2], BF16, ph)
                S.op("dve", [sc], [schi], lambda e: e.tensor_copy(out=schi[:], in_=sc[:]))
                S.op("dve", [schi], [schf], lambda e: e.tensor_copy(out=schf[:], in_=schi[:]))
                S.op("dve", [sc, schf], [schf], lambda e: e.tensor_tensor(out=schf[:], in0=sc[:], in1=schf[:], op=ALU.subtract))
                S.op("dve", [schf], [sclo], lambda e: e.tensor_copy(out=sclo[:], in_=schf[:]))
                wbs = {}

                def ld(cg):
                    if cg >= 12:
                        return
                    wb = ring.get()
                    S.dma("pool", wb.sem, wb[:], D["ada_w"][l, :, cg * 512:(cg + 1) * 512].rearrange("(k p) n -> p k n", p=128), [], [wb])
                    wbs[cg] = wb
                ld(0)
                ld(1)
                for cg in range(12):
                    ld(cg + 2)
                    wb = wbs[cg]
                    ps = psA.get()
                    for c4 in range(4):
                        for k in range(8):
                            S.op("pe", [wb, schi], [ps], lambda e: e.matmul(ps[:, c4 * 2:c4 * 2 + 2], lhsT=wb[:, k, c4 * 128:(c4 + 1) * 128], rhs=schi[:, k, :], start=(k == 0), stop=False))
                            S.op("pe", [wb, sclo], [ps], lambda e: e.matmul(ps[:, c4 * 2:c4 * 2 + 2], lhsT=wb[:, k, c4 * 128:(c4 + 1) * 128], rhs=sclo[:, k, :], start=False, stop=(k == 7)))
                    S.op("dve", [ps, adab], [modT[l]], lambda e: e.tensor_tensor(
                        out=modT[l][:, cg * 4:(cg + 1) * 4, :], in0=ps[:, 0:8].rearrange("p (c j) -> p c j", j=2),
                        in1=adab[:, l, cg * 4:(cg + 1) * 4].unsqueeze(2).to_broadcast([128, 4, 2]), op=ALU.add))
                S.barrier()

        def mk_AB(l, which):
            scl = 8 if which == 0 else 32
            S.op("dve", [modT[l]], [AB], lambda e: e.tensor_scalar(out=AB[:], in0=modT[l][:, scl:scl + 8, :], scalar1=1.0, scalar2=None, op0=ALU.add))
            S.op("dve", [AB, ngT], [AB], lambda e: e.tensor_tensor(out=AB[:], in0=AB[:], in1=ngT[:, l, which, :].unsqueeze(2).to_broadcast([128, 8, 2]), op=ALU.mult))

        def mk_gate(l, gchunk, ph):
            hl = S.sb(f"ghl{l}_{gchunk}", [128, 8, 2], F32, ph)
            hb = S.sb(f"ghb{l}_{gchunk}", [128, 8, 2], BF16, ph)
            hf = S.sb(f"ghf{l}_{gchunk}", [128, 8, 2], F32, ph)
            lo = S.sb(f"glo{l}_{gchunk}", [128, 8, 2], F32, ph)
            lb = S.sb(f"glb{l}_{gchunk}", [128, 8, 2], BF16, ph)
            lf = S.sb(f"glf{l}_{gchunk}", [128, 8, 2], F32, ph)
            S.op("dve", [modT[l]], [hl], lambda e: e.tensor_copy(out=hl[:], in_=modT[l][:, gchunk:gchunk + 8, :]))
            S.op("dve", [hl], [hb], lambda e: e.tensor_copy(out=hb[:], in_=hl[:]))
            S.op("dve", [hb], [hf], lambda e: e.tensor_copy(out=hf[:], in_=hb[:]))
            S.op("dve", [hl, hf], [lo], lambda e: e.tensor_tensor(out=lo[:], in0=hl[:], in1=hf[:], op=ALU.subtract))
            S.op("dve", [lo], [lb], lambda e: e.tensor_copy(out=lb[:], in_=lo[:]))
            S.op("dve", [lb], [lf], lambda e: e.tensor_copy(out=lf[:], in_=lb[:]))
            dgr = Ring([S.sb(f"dg{l}_{gchunk}_{i}", [128, 2, 128], BF16, ph) for i in range(2)])
            for j in range(2):
                for half in range(2):
                    ps = psA.get()
                    for k4 in range(4):
                        kk = half * 4 + k4
                        dg = dgr.get()
                        S.op("dve", [identb, hf], [dg], lambda e: e.tensor_scalar(out=dg[:, 0, :], in0=identb[:], scalar1=hf[:, kk, j:j + 1], scalar2=None, op0=ALU.mult))
                        S.op("dve", [identb, lf], [dg], lambda e: e.tensor_scalar(out=dg[:, 1, :], in0=identb[:], scalar1=lf[:, kk, j:j + 1], scalar2=None, op0=ALU.mult))
                        S.op("pe", [ones_b, dg], [ps], lambda e: e.matmul(ps[:, k4 * 128:(k4 + 1) * 128], lhsT=ones_b[:], rhs=dg[:, 0, :], start=True, stop=False))
                        S.op("pe", [ones_b, dg], [ps], lambda e: e.matmul(ps[:, k4 * 128:(k4 + 1) * 128], lhsT=ones_b[:], rhs=dg[:, 1, :], start=False, stop=True))
                    S.op("act", [ps], [gbc], lambda e: e.activation(out=gbc[:, j, half * 512:(half + 1) * 512], in_=ps[:], func=AF.Copy))

        def rstd_of(t, n_ap, dim):
            S.op("dve", [t], [t], lambda e: e.tensor_scalar(out=n_ap(), in0=n_ap(), scalar1=1.0 / dim, scalar2=EPS, op0=ALU.mult, op1=ALU.add))
            S.op("act", [t], [t], lambda e: e.activation(out=n_ap(), in_=n_ap(), func=AF.Ln))
            S.op("act", [t], [t], lambda e: e.activation(out=n_ap(), in_=n_ap(), func=AF.Exp, scale=-0.5))

        def norm_phase(l, which, hTg, ph, tiles=range(NT)):
            mk_AB(l, which)
            sh = 0 if which == 0 else 24
            ss = S.sb(f"nss{l}{which}", [128, NT], F32, ph)
            junk = S.sb(f"njunk{l}{which}", [128, 1024], BF16, ph)
            xnr = Ring([S.sb(f"xn{l}{which}_{i}", [128, 1024], BF16, ph) for i in range(2)])
            S.op("dve", [], [ss], lambda e: e.memset(ss[:], 1.0))
            for i in tiles:
                S.op("act", [xs[i]], [junk, ss], lambda e: e.activation(out=junk[:], in_=xs[i][:], func=AF.Square, accum_out=ss[:, i:i + 1]))
            rstd_of(ss, lambda: ss[:], 1024)
            import os
            if os.environ.get("KSUB") in ("a", "c"):
                return
            tl_ = list(tiles)

            def stage_xn(i):
                xn = xnr.get()
                S.op("dve", [xs[i], ss], [xn], lambda e: e.tensor_scalar(out=xn[:], in0=xs[i][:], scalar1=ss[:, i:i + 1], scalar2=None, op0=ALU.mult))
                pt = psT.get()
                for k in range(8):
                    S.op("pe", [xn, identb], [pt], lambda e: e.transpose(out=pt[:, k, :], in_=xn[:, k * 128:(k + 1) * 128], identity=identb[:]))
                return pt

            def stage_evac(i, pt):
                g, off = tok_group(i)
                j = 1 if i < 2 else 0
                for k in range(8):
                    if k % 2 == 0:
                        S.op("dve", [pt, AB, modT[l]], [hTg[g]], lambda e: e.tensor_scalar(
                            out=hTg[g][:, k, off:off + 128], in0=pt[:, k, :], scalar1=AB[:, k, j:j + 1], scalar2=modT[l][:, sh + k, j:j + 1], op0=ALU.mult, op1=ALU.add))
                    else:
                        S.op("act", [pt, AB, modT[l]], [hTg[g]], lambda e: e.activation(
                            out=hTg[g][:, k, off:off + 128], in_=pt[:, k, :], func=AF.Identity, scale=AB[:, k, j:j + 1], bias=modT[l][:, sh + k, j:j + 1]))

            ptn = stage_xn(tl_[0])
            for n_, i in enumerate(tl_):
                ptc = ptn
                if n_ + 1 < len(tl_):
                    ptn = stage_xn(tl_[n_ + 1])
                stage_evac(i, ptc)

        def wload(wb, n, src):
            dst = wb[:, 0:8 * n].rearrange("p (k n) -> p k n", k=8)
            S.dma("pool", wb.sem, dst, src, [], [wb])
            return dst

        def qk_prep(ps, ps_ap, nh, g_ap, rope_tile, out_ap, wk, rope):
            sq, ssq, qn, t1 = wk
            n = nh * 64
            v3 = lambda ap: ap.rearrange("p (h d) -> p h d", d=64)
            S.op("act", [ps], [sq], lambda e: e.activation(out=sq[:, 0:n], in_=ps_ap, func=AF.Square))
            S.op("dve", [sq], [ssq], lambda e: e.tensor_reduce(out=ssq[:, 0:nh], in_=v3(sq[:, 0:n]), axis=AX.X, op=ALU.add))
            rstd_of(ssq, lambda: ssq[:, 0:nh], 64)
            S.op("dve", [ps, ssq], [qn], lambda e: e.tensor_tensor(out=v3(qn[:, 0:n]), in0=v3(ps_ap), in1=ssq[:, 0:nh].unsqueeze(2).to_broadcast([128, nh, 64]), op=ALU.mult))
            if rope_tile is None:
                S.op("dve", [qn], [out_ap[0]], lambda e: e.tensor_tensor(out=out_ap[1], in0=v3(qn[:, 0:n]), in1=g_ap.unsqueeze(1).to_broadcast([128, nh, 64]), op=ALU.mult))
                return
            S.op("dve", [qn], [qn], lambda e: e.tensor_tensor(out=v3(qn[:, 0:n]), in0=v3(qn[:, 0:n]), in1=g_ap.unsqueeze(1).to_broadcast([128, nh, 64]), op=ALU.mult))
            cos_ap = rope[:, 0, :]
            sin_ap = rope[:, 1, :]
            S.op("dve", [qn, rope], [t1], lambda e: e.tensor_tensor(out=v3(t1[:, 0:n]), in0=v3(qn[:, 0:n]), in1=cos_ap.unsqueeze(1).to_broadcast([128, nh, 64]), op=ALU.mult))
            v5 = lambda ap: ap.rearrange("p (h x y d) -> p h x y d", x=2, y=2, d=16)
            s4 = sin_ap.rearrange("p (x y d) -> p x y d", x=2, y=2)
            for y in range(2):
                S.op("dve", [qn, rope], [sq], lambda e: e.tensor_tensor(
                    out=v5(sq[:, 0:n])[:, :, :, y, :], in0=v5(qn[:, 0:n])[:, :, :, 1 - y, :],
                    in1=s4[:, :, y, :].unsqueeze(1).to_broadcast([128, nh, 2, 16]), op=ALU.mult))
            S.op("dve", [t1, sq], [out_ap[0]], lambda e: e.tensor_tensor(out=out_ap[1], in0=v3(t1[:, 0:n]), in1=v3(sq[:, 0:n]), op=ALU.add))

        def prep_batch(raw, sq, ss, T, nh, g_ap, out_buf, out_ap, inplace=False):
            n = T * nh
            r3 = raw[:, 0:T, :].rearrange("p t (h d) -> p (t h) d", d=64)
            s3 = sq[:, 0:T, :].rearrange("p t (h d) -> p (t h) d", d=64)
            S.op("act", [raw], [sq], lambda e: e.activation(out=sq[:, 0:T, :], in_=raw[:, 0:T, :], func=AF.Square))
            S.op("dve", [sq], [ss], lambda e: e.tensor_reduce(out=ss[:, 0:n], in_=s3, axis=AX.X, op=ALU.add))
            rstd_of(ss, lambda: ss[:, 0:n], 64)
            S.op("dve", [raw, ss], [raw], lambda e: e.tensor_tensor(out=r3, in0=r3, in1=ss[:, 0:n].unsqueeze(2).to_broadcast([128, n, 64]), op=ALU.mult))
            if inplace:
                S.op("dve", [raw], [raw], lambda e: e.tensor_tensor(out=r3, in0=r3, in1=g_ap.unsqueeze(1).to_broadcast([128, n, 64]), op=ALU.mult))
                return
            S.op("dve", [raw], [out_buf], lambda e: e.tensor_tensor(out=out_ap.rearrange("p t (h d) -> p (t h) d", d=64), in0=r3, in1=g_ap.unsqueeze(1).to_broadcast([128, n, 64]), op=ALU.mult))

        def residual(i, ps, cgi, j):
            tmp = restmp.get()
            S.op("dve", [ps, gbc], [tmp], lambda e: e.tensor_tensor(out=tmp[:], in0=ps[:], in1=gbc[:, j, cgi * 512:(cgi + 1) * 512], op=ALU.mult))
            rstate["n"] += 1
            S.op("dve", [tmp, xs[i]], [xs[i]], lambda e: e.tensor_tensor(out=xs[i][:, cgi * 512:(cgi + 1) * 512], in0=xs[i][:, cgi * 512:(cgi + 1) * 512], in1=tmp[:], op=ALU.add))

        restmp = Ring([S.sb(f"restmp{i}", [128, 512], F32) for i in range(1)])
        rstate = {"n": 0}

        def mixer0():
            l = 0
            with ExitStack() as ph:
                hTg = [S.sb("hT0_0", [128, 8, 256], BF16, ph)] + [S.sb(f"hT0_{g}", [128, 8, 512], BF16, ph) for g in range(1, 5)]
                with ExitStack() as ph2:
                    norm_phase(0, 0, hTg, ph2)
                    import os
                    if os.environ.get("KSUB") not in ("a", "b"):
                        mk_gate(0, 16, ph2)
                    S.barrier()
                if stage == "norm":
                    return
                mixTa = S.sb("mixTa", [128, 4, NT * 128], BF16, ph)
                with ExitStack() as ph2:
                    wring = Ring([S.sbd(f"w0_{i}", [128, 8 * 384], BF16, ph2) for i in range(2)])
                    gateb = S.sbd("gateb", [128, 16], F32, ph2)
                    headg = S.sbd("headg", [128, 512], F32, ph2)
                    S.dma("sp", gateb.sem, gateb[:], D["gate_b_bc"], [], [gateb])
                    S.dma("sp", headg.sem, headg[:], D["head_g_bc"], [], [headg])
                    mlstm(hTg, mixTa, gateb, headg, wring, ph2)
                    S.barrier()
                if stage == "mlstm":
                    return
                with ExitStack() as ph2:
                    gqa_attn(0, hTg, mixTa, None, ph2)
                    S.barrier()

        def mlstm_gates(hTg, gateb, wring, pg, es, eb, edec, ekw):
            G = S.sb("G", [128, NT, 16], F32, pg)
            wgb = S.sbd("wgates", [128, 8 * 16], BF16, pg)
            wg = wload(wgb, 16, D["even_w"][:, 2048:2064].rearrange("(k p) n -> p k n", p=128))
            for i in range(NT):
                g, off = tok_group(i)
                ps = psO.get()
                for k in range(8):
                    S.op("pe", [hTg[g], wgb], [ps], lambda e: e.matmul(ps[:, 0:16], lhsT=hTg[g][:, k, off:off + 128], rhs=wg[:, k, :], start=(k == 0), stop=(k == 7)))
                S.op("dve", [ps, gateb], [G], lambda e: e.tensor_tensor(out=G[:, i, :], in0=ps[:, 0:16], in1=gateb[:], op=ALU.add))
            E = S.sb("E", [128, 2, NT, 4], F32, pg)
            for d in range(2):
                S.op("act", [G], [E], lambda e: e.activation(out=E[:, d], in_=G[:, :, 4 + 8 * d:8 + 8 * d], func=AF.Exp, scale=-1.0))
            S.op("dve", [E], [E], lambda e: e.tensor_scalar(out=E[:], in0=E[:], scalar1=1.0, scalar2=None, op0=ALU.add))
            S.op("act", [E], [E], lambda e: e.activation(out=E[:], in_=E[:], func=AF.Ln))
            tg = S.sb("tg", [128, NT, 4], F32, pg)
            f72 = lambda ap: ap.rearrange("p t h -> p (t h)")
            trib = S.sb("trib", [128, 2, 128], BF16, pg)
            S.op("dve", [cst], [trib], lambda e: e.tensor_copy(out=trib[:], in_=cst[:, 128:384].rearrange("p (a b) -> p a b", a=2)))
            Ehi = S.sb("Ehi", [128, 2, NT, 4], BF16, pg)
            Ehf = S.sb("Ehf", [128, 2, NT, 4], F32, pg)
            Elo = S.sb("Elo", [128, 2, NT, 4], BF16, pg)
            S.op("dve", [E], [Ehi], lambda e: e.tensor_copy(out=Ehi[:], in_=E[:]))
            S.op("dve", [Ehi], [Ehf], lambda e: e.tensor_copy(out=Ehf[:], in_=Ehi[:]))
            S.op("dve", [E, Ehf], [Ehf], lambda e: e.tensor_tensor(out=Ehf[:], in0=E[:], in1=Ehf[:], op=ALU.subtract))
            S.op("dve", [Ehf], [Elo], lambda e: e.tensor_copy(out=Elo[:], in_=Ehf[:]))
            for d in range(2):
                psb = psO.get()
                S.op("pe", [trib, Ehi], [psb], lambda e: e.matmul(psb[:, 0:72], lhsT=trib[:, d, :], rhs=f72(Ehi[:, d]), start=True, stop=False))
                S.op("pe", [trib, Elo], [psb], lambda e: e.matmul(psb[:, 0:72], lhsT=trib[:, d, :], rhs=f72(Elo[:, d]), start=False, stop=True))
                S.op("pe", [ones_b, Ehi], [psb], lambda e: e.matmul(psb[:, 72:144], lhsT=ones_b[:], rhs=f72(Ehi[:, d]), start=True, stop=False))
                S.op("pe", [ones_b, Elo], [psb], lambda e: e.matmul(psb[:, 72:144], lhsT=ones_b[:], rhs=f72(Elo[:, d]), start=False, stop=True))
                S.op("dve", [psb, G], [tg], lambda e: e.tensor_tensor(out=tg[:], in0=psb[:, 0:72].rearrange("p (t h) -> p t h", h=4), in1=G[:, :, 8 * d:8 * d + 4], op=ALU.add))
                S.op("act", [tg], [es], lambda e: e.activation(out=es[:, d], in_=tg[:], func=AF.Exp))
                S.op("act", [psb], [eb], lambda e: e.activation(out=f72(eb[:, d]), in_=psb[:, 0:72], func=AF.Exp, scale=-1.0))
                S.op("act", [psb], [edec], lambda e: e.activation(out=f72(edec[:, d]), in_=psb[:, 72:144], func=AF.Exp, scale=-1.0))
                S.op("dve", [es, edec], [ekw], lambda e: e.tensor_tensor(out=ekw[:, d], in0=es[:, d], in1=edec[:, d], op=ALU.mult))

            pass
            pass
            pass
            pass
            pass

        def mlstm(hTg, mixTa, gateb, headg, wring, ph):
            es = S.sb("es", [128, 2, NT, 4], F32, ph)
            eb = S.sb("eb", [128, 2, NT, 4], F32, ph)
            edec = S.sb("edec", [128, 2, NT, 4], F32, ph)
            ekw = S.sb("ekw", [128, 2, NT, 4], F32, ph)
            with ExitStack() as pg:
                mlstm_gates(hTg, gateb, wring, pg, es, eb, edec, ekw)
                S.barrier()
            KS_ = ""
            KH_ = -1
            qT = S.sb("qTa", [128, NT * 128], BF16, ph)
            kT = S.sb("kTa", [128, NT * 128], BF16, ph)
            ktok = S.sb("ktok", [128, NT, 128], BF16, ph)
            vaug = S.sb("vaug", [128, NT, 130], BF16, ph)
            hraw = [S.sb(f"hraw{d}", [128, NT, 130], F32, ph) for d in range(2)]
            rnm = S.sb("rnm", [128, 2, NT], F32, ph)
            Cst = [S.sb(f"Cst{d}", [128, 129], F32, ph) for d in range(2)]
            Cbf3 = [[S.sb(f"Cbf{d}_{r}", [128, 130], BF16, ph) for r in range(3)] for d in range(2)]
            PTr = Ring([S.sb(f"PTm{i}", [128, 128], BF16, ph) for i in range(4)])
            kwr = Ring([S.sb(f"kwm{i}", [128, 128], BF16, ph) for i in range(2)])
            hss = S.sb("hss", [128, NT], F32, ph)
            hjunk = S.sb("hjunk", [128, 128], BF16, ph)
            ogr = Ring([S.sb(f"og{i}", [128, 128], F32, ph) for i in range(2)])
            t1r = Ring([S.sb(f"mt1{i}", [128, 128], F32, ph) for i in range(2)])
            mxr = Ring([S.sb(f"mmx{i}", [128, 128], BF16, ph) for i in range(2)])
            S.op("dve", [], [vaug], lambda e: e.memset(vaug[:, :, 128:129], 1.0))
            orders = [list(range(NT)), [1, 0] + list(range(NT - 1, 1, -1))]
            KS = 128.0 ** -0.5

            def load_qkv(hd):
                wb_ = wring.get()
                src = D["even_w"][:, 0:1536].rearrange("(k p) (g h n) -> p k g h n", p=128, g=3, h=4)[:, :, :, hd, :]
                wq_ = wb_[:, 0:8 * 384].rearrange("p (k g n) -> p k g n", k=8, g=3)
                for g3 in range(3):
                    S.dma("pool", wb_.sem, wq_[:, :, g3, :], src[:, :, g3, :], [], [wb_])
                return wb_, wq_

            woring = Ring([S.sbd(f"wo_{i}", [128, 8 * 128], BF16, ph) for i in range(2)])
            nxt_w = load_qkv(0)
            for h in range(4):
                wb, wq = nxt_w
                wob = woring.get()
                wo = wload(wob, 128, D["even_w"][:, 1536 + h * 128:1536 + (h + 1) * 128].rearrange("(k p) n -> p k n", p=128))
                if h + 1 < 4:
                    nxt_w = load_qkv(h + 1)
                flip = 0
                for (g, c0, n) in GROUPS:
                    for which, dst, scl in ((0, qT, 1.0), (1, kT, KS)):
                        ps = psA.get()
                        for k in range(8):
                            S.op("pe", [wb, hTg[g]], [ps], lambda e: e.matmul(ps[:, 0:n], lhsT=wq[:, k, which, :], rhs=hTg[g][:, k, 0:n], start=(k == 0), stop=(k == 7)))
                        if flip % 2 == 0:
                            S.op("act", [ps], [dst], lambda e: e.activation(out=dst[:, c0:c0 + n], in_=ps[:, 0:n], func=AF.Copy, scale=scl))
                        else:
                            S.op("dve", [ps], [dst], lambda e: e.tensor_scalar(out=dst[:, c0:c0 + n], in0=ps[:, 0:n], scalar1=scl, scalar2=None, op0=ALU.mult))
                        flip += 1
                if KS_ == "m2a" and h == KH_:
                    return
                for i in range(NT):
                    g, off = tok_group(i)
                    ps = psA.get()
                    for k in range(8):
                        S.op("pe", [wb, hTg[g]], [ps], lambda e: e.matmul(ps[:, 0:256], lhsT=hTg[g][:, k, off:off + 128], rhs=wb[:, k * 384 + 128:k * 384 + 384], start=(k == 0), stop=(k == 7)))
                    S.op("act", [ps], [ktok], lambda e: e.activation(out=ktok[:, i, :], in_=ps[:, 0:128], func=AF.Copy, scale=KS))
                    S.op("dve", [ps], [vaug], lambda e: e.tensor_copy(out=vaug[:, i, 0:128], in_=ps[:, 128:256]))
                if KS_ == "m2b" and h == KH_:
                    return
                if h == 0:
                    pass
                    pass
                    pass
                    pass
                if KS_ == "m2" and h == KH_:
                    return
                written = [False] * NT
                PTs = {}

                def emitA2(step):
                    ii = [orders[d][step] for d in range(2)]
                    col = lambda a, d: a[:, d, ii[d], h:h + 1]
                    css = [slice(i * 128, (i + 1) * 128) for i in ii]
                    pss2, kws, pscs = [], [], []
                    for d in range(2):
                        pss = psS.get()
                        S.op("pe", [kT, qT], [pss], lambda e: e.matmul(pss[:, 0:128], lhsT=kT[:, css[d]], rhs=qT[:, css[d]], start=True, stop=True))
                        pss2.append(pss)
                    if step < NT - 1:
                        for d in range(2):
                            kw = kwr.get()
                            S.op("act", [ktok, ekw], [kw], lambda e: e.activation(out=kw[:], in_=ktok[:, ii[d], :], func=AF.Copy, scale=col(ekw, d)))
                            kws.append(kw)
                        for d in range(2):
                            psc = psA.get()
                            S.op("pe", [kws[d], vaug], [psc], lambda e: e.matmul(psc[:, 0:129], lhsT=kws[d][:], rhs=vaug[:, ii[d], 0:129], start=True, stop=True))
                            pscs.append(psc)
                    for d in range(2):
                        PT = PTr.get()
                        msk = TRIU() if d == 0 else TRIL()
                        S.op("dve", [pss2[d], es, cst], [PT], lambda e: e.scalar_tensor_tensor(out=PT[:], in0=pss2[d][:, 0:128], scalar=col(es, d), in1=msk, op0=ALU.mult, op1=ALU.mult))
                        PTs[(step, d)] = PT
                    if step < NT - 1:
                        for d in range(2):
                            psc = pscs[d]
                            if step == 0:
                                S.op("dve", [psc], [Cst[d]], lambda e: e.tensor_copy(out=Cst[d][:], in_=psc[:, 0:129]))
                            else:
                                S.op("dve", [psc, Cst[d], edec], [Cst[d]], lambda e: e.scalar_tensor_tensor(out=Cst[d][:], in0=Cst[d][:], scalar=col(edec, d), in1=psc[:, 0:129], op0=ALU.mult, op1=ALU.add))
                            cb3 = Cbf3[d][(step + 1) % 3]
                            S.op("dve", [Cst[d]], [cb3], lambda e: e.tensor_copy(out=cb3[:, 0:129], in_=Cst[d][:]))

                def emitB(step, d):
                    i = orders[d][step]
                    col = lambda a: a[:, d, i, h:h + 1]
                    cs = slice(i * 128, (i + 1) * 128)
                    PT = PTs.pop((step, d))
                    acc = psO.get()
                    if step > 0:
                        cb3 = Cbf3[d][step % 3]
                        S.op("pe", [qT, cb3], [acc], lambda e: e.matmul(acc[:, 0:129], lhsT=qT[:, cs], rhs=cb3[:, 0:129], start=True, stop=False))
                    S.op("pe", [PT, vaug], [acc], lambda e: e.matmul(acc[:, 0:129], lhsT=PT[:], rhs=vaug[:, i, 0:129], start=(step == 0), stop=True))
                    S.op("act", [acc, eb], [hraw[d]], lambda e: e.activation(out=hraw[d][:, i, 0:129], in_=acc[:, 0:129], func=AF.Copy, scale=col(eb)))

                emitA2(0)
                for step in range(NT):
                    if step + 1 < NT:
                        emitA2(step + 1)
                    emitB(step, 0)
                    emitB(step, 1)
                if h == 0:
                    pass
                    pass
                if KS_ == "m3" and h == KH_:
                    return
                for d in range(2):
                    S.op("act", [hraw[d]], [rnm], lambda e: e.activation(out=rnm[:, d, :], in_=hraw[d][:, :, 128], func=AF.Abs))
                S.op("dve", [rnm], [rnm], lambda e: e.tensor_scalar(out=rnm[:], in0=rnm[:], scalar1=1.0, scalar2=None, op0=ALU.max))
                S.op("dve", [rnm], [rnm], lambda e: e.reciprocal(out=rnm[:], in_=rnm[:]))
                for d in range(2):
                    S.op("dve", [hraw[d], rnm], [hraw[d]], lambda e: e.tensor_tensor(out=hraw[d][:, :, 0:128], in0=hraw[d][:, :, 0:128], in1=rnm[:, d, :].unsqueeze(2).to_broadcast([128, NT, 128]), op=ALU.mult))
                S.op("dve", [hraw[0], hraw[1]], [hraw[0]], lambda e: e.tensor_tensor(out=hraw[0][:, :, 0:128], in0=hraw[0][:, :, 0:128], in1=hraw[1][:, :, 0:128], op=ALU.add))
                S.op("dve", [], [hss], lambda e: e.memset(hss[:], 1.0))
                for i in range(NT):
                    S.op("act", [hraw[0]], [hjunk, hss], lambda e: e.activation(out=hjunk[:], in_=hraw[0][:, i, 0:128], func=AF.Square, accum_out=hss[:, i:i + 1]))
                rstd_of(hss, lambda: hss[:], 128)

                def out_stage1(i):
                    g, off = tok_group(i)
                    ps = psA.get()
                    for k in range(8):
                        S.op("pe", [wob, hTg[g]], [ps], lambda e: e.matmul(ps[:, 0:128], lhsT=hTg[g][:, k, off:off + 128], rhs=wo[:, k, :], start=(k == 0), stop=(k == 7)))
                    og = ogr.get()
                    S.op("act", [ps], [og], lambda e: e.activation(out=og[:], in_=ps[:, 0:128], func=AF.Sigmoid))
                    return og

                def out_stage2(i, og):
                    t1 = t1r.get()
                    S.op("dve", [hraw[0], hss, headg], [t1], lambda e: e.scalar_tensor_tensor(out=t1[:], in0=hraw[0][:, i, 0:128], scalar=hss[:, i:i + 1], in1=headg[:, h * 128:(h + 1) * 128], op0=ALU.mult, op1=ALU.mult))
                    mx = mxr.get()
                    S.op("dve", [t1, og], [mx], lambda e: e.tensor_tensor(out=mx[:], in0=t1[:], in1=og[:], op=ALU.mult))
                    pt = psT.get()
                    S.op("pe", [mx, identb], [pt], lambda e: e.transpose(out=pt[:, 0, :], in_=mx[:], identity=identb[:]))
                    S.op("act", [pt], [mixTa], lambda e: e.activation(out=mixTa[:, h, i * 128:(i + 1) * 128], in_=pt[:, 0, :], func=AF.Copy))

                ogn = out_stage1(0)
                for i in range(NT):
                    ogc = ogn
                    if i + 1 < NT:
                        ogn = out_stage1(i + 1)
                    out_stage2(i, ogc)
                if KS_ == "m4" and h == KH_:
                    return

        def attn_scores_exp_pv(kv_specs, nheads_per_kv, qT, q_sl, PTr, accs, first, last):
            pass

        def gqa_attn(l, hTg, other, wring, ph):
            wname = "even_w" if l == 0 else "odd_w"
            qc0, kc0 = (2064, 2576) if l == 0 else (0, 512)
            swag = S.sbd(f"swag{l}", [128, 128], F32, ph)
            roper = Ring([S.sbd(f"rope{l}_{i}", [128, 2, 64], F32, ph) for i in range(2)])

            def get_rope(jt):
                rb = roper.get()
                S.dma("sp", rb.sem, rb[:], D["rope"][:, :, jt, :], [], [rb])
                return rb
            S.dma("sp", swag.sem, swag[:], D["swa_g_bc" if l == 0 else "gqa_g_bc"], [], [swag])
            gq = S.sb(f"gq{l}", [128, 64], F32, ph)
            S.op("dve", [swag], [gq], lambda e: e.tensor_scalar(out=gq[:], in0=swag[:, 0:64], scalar1=0.125, scalar2=None, op0=ALU.mult))
            esink = S.sb(f"esink{l}", [128, 8], F32, ph)
            if l == 0:
                sinkb = S.sbd("sinkb", [128, 8], F32, ph)
                S.dma("sp", sinkb.sem, sinkb[:], D["sink_bc"], [], [sinkb])
                S.op("act", [sinkb], [esink], lambda e: e.activation(out=esink[:], in_=sinkb[:], func=AF.Exp))
            else:
                S.op("dve", [], [esink], lambda e: e.memset(esink[:], 0.0))
            wkb = S.sbd(f"wkv{l}", [128, 8 * 256], BF16, ph)
            wkv = wload(wkb, 256, D[wname][:, kc0:kc0 + 256].rearrange("(k p) n -> p k n", p=128))
            kTd = [S.sb(f"kTd{g}", [128, NT * 128], BF16, ph) for g in range(2)]
            vb = S.sb("vb", [128, NT, 2, 66], BF16, ph)
            S.op("dve", [], [vb], lambda e: e.memset(vb[:, :, :, 64:65], 1.0))
            with ExitStack() as pk:
                TB = 9
                kraw = S.sb("kraw", [128, TB, 128], F32, pk)
                ksq = S.sb("ksq", [128, TB, 128], F32, pk)
                kt2 = S.sb("kt2", [128, TB, 128], F32, pk)
                kss = S.sb("kss", [128, TB * 2], F32, pk)
                knb = S.sb("knb", [128, TB, 128], BF16, pk)
                kd = S.sb("kd", [128, 2, 2, 64], BF16, pk)
                rtab = S.sbd("rtab", [128, 2, TB, 64], F32, pk)
                for t0 in range(0, NT, TB):
                    tl = list(range(t0, t0 + TB))
                    r0 = max(0, 2 - t0)
                    nl = TB - r0
                    j0 = t0 + r0 - 2
                    for cs_ in range(2):
                        S.dma("sp", rtab.sem, rtab[:, cs_, 0:nl, :], D["rope"][:, cs_, j0:j0 + nl, :], [], [rtab])
                    for i in tl:
                        g, off = tok_group(i)
                        ps = psA.get()
                        for k in range(8):
                            S.op("pe", [wkb, hTg[g]], [ps], lambda e: e.matmul(ps[:, 0:256], lhsT=hTg[g][:, k, off:off + 128], rhs=wkv[:, k, :], start=(k == 0), stop=(k == 7)))
                        S.op("act", [ps], [vb], lambda e: e.activation(out=vb[:, i, :, 0:64], in_=ps[:, 128:256].rearrange("p (g d) -> p g d", g=2), func=AF.Copy))
                        S.op("dve", [ps], [kraw], lambda e: e.tensor_copy(out=kraw[:, i - t0, :], in_=ps[:, 0:128]))
                    prep_batch(kraw, ksq, kss, TB, 2, swag[:, 64:128], None, None, inplace=True)
                    if r0 > 0:
                        S.op("act", [kraw], [knb], lambda e: e.activation(out=knb[:, 0:r0, :], in_=kraw[:, 0:r0, :], func=AF.Copy))
                    v4 = lambda ap: ap.rearrange("p t (h d) -> p t h d", d=64)
                    cosb = rtab[:, 0, 0:nl, :].unsqueeze(2).to_broadcast([128, nl, 2, 64])
                    S.op("dve", [kraw, rtab], [ksq], lambda e: e.tensor_tensor(out=v4(ksq[:, r0:TB, :]), in0=v4(kraw[:, r0:TB, :]), in1=cosb, op=ALU.mult))
                    v6 = lambda ap: ap.rearrange("p t (h x y d) -> p t h x y d", h=2, x=2, y=2)
                    s5 = rtab[:, 1, 0:nl, :].rearrange("p t (x y d) -> p t x y d", x=2, y=2)
                    for hh_ in range(2):
                        for y in range(2):
                            S.op("dve", [kraw, rtab], [kt2], lambda e: e.tensor_tensor(
                                out=v6(kt2[:, r0:TB, :])[:, :, hh_, :, y, :], in0=v6(kraw[:, r0:TB, :])[:, :, hh_, :, 1 - y, :],
                                in1=s5[:, :, :, y, :], op=ALU.mult))
                    S.op("dve", [ksq, kt2], [knb], lambda e: e.tensor_tensor(out=knb[:, r0:TB, :], in0=ksq[:, r0:TB, :], in1=kt2[:, r0:TB, :], op=ALU.add))
                    for i in tl:
                        S.op("dve", [knb], [kd], lambda e: e.tensor_copy(out=kd[:], in_=knb[:, i - t0, :].rearrange("p (g d) -> p g d", g=2).unsqueeze(2).to_broadcast([128, 2, 2, 64])))
                        pt = psT.get()
                        for g2 in range(2):
                            S.op("pe", [kd, identb], [pt], lambda e: e.transpose(out=pt[:, g2, :], in_=kd[:, g2].rearrange("p a d -> p (a d)"), identity=identb[:]))
                        S.op("act", [pt], [kTd[0]], lambda e: e.activation(out=kTd[0][:, i * 128:(i + 1) * 128], in_=pt[:, 0, :], func=AF.Copy))
                        S.op("act", [pt], [kTd[1]], lambda e: e.activation(out=kTd[1][:, i * 128:(i + 1) * 128], in_=pt[:, 1, :], func=AF.Copy))
                S.barrier()
            wqb = S.sbd(f"wqq{l}", [128, 8 * 512], BF16, ph)
            wq = wload(wqb, 512, D[wname][:, qc0:qc0 + 512].rearrange("(k p) n -> p k n", p=128))
            wout = S.sbd(f"wout{l}", [128, 8 * 1024], BF16, ph)
            woutv = wout[:, :].rearrange("p (k n) -> p k n", k=8)
            S.dma("pool", wout.sem, woutv, D["w_out"][l].rearrange("(k p) n -> p k n", p=128), [], [wout])
            wk = (S.sb("wk_sq", [128, 512], F32, ph), S.sb("wk_ss", [128, 8], F32, ph), S.sb("wk_qn", [128, 512], F32, ph), S.sb("wk_t1", [128, 512], F32, ph))
            import os
            KS_ = os.environ.get("KSUB", "")
            if KS_ == "w1" or (KS_ == "g1k" and l == 1):
                return
            qb = S.sb("qb", [128, 8, 64], BF16, ph)
            qz = S.sb("qz", [128, 2, 4, 128], BF16, ph)
            S.op("dve", [], [qz], lambda e: e.memset(qz[:], 0.0))
            wmb4 = S.sb("wmb4", [128, 2, 4, 128], BF16, ph)
            S.op("dve", [wmb], [wmb4], lambda e: e.tensor_copy(out=wmb4[:], in_=wmb[:, :, :].unsqueeze(2).to_broadcast([128, 2, 4, 128])))
            PTr = Ring([S.sb(f"PTw{i}", [128, 512], BF16, ph) for i in range(3)])
            den = S.sb("wden", [128, 8], F32, ph)
            mixb = S.sb("mixb", [128, 512], BF16, ph)
            mixTb = S.sb("mixTb", [128, 4, 128], BF16, ph)
            scoreS = Ring(psS.bufs + [psA.bufs[1]])
            psQ = Ring([psA.bufs[0]])

            def emit_qprep(i):
                g, off = tok_group(i)
                lat = i >= 2
                j = i - 2
                ps = psQ.get()
                for k in range(8):
                    S.op("pe", [wqb, hTg[g]], [ps], lambda e: e.matmul(ps[:, 0:512], lhsT=hTg[g][:, k, off:off + 128], rhs=wq[:, k, :], start=(k == 0), stop=(k == 7)))
                qk_prep(ps, ps[:, 0:512], 8, gq[:], j if lat else None, (qb, qb[:]), wk, get_rope(j) if lat else None)

            qtiles = list(range(NT) if l == 0 else range(2, NT))
            emit_qprep(qtiles[0])
            for qi, i in enumerate(qtiles):
                g, off = tok_group(i)
                lat = i >= 2
                j = i - 2
                pt = psT.get()
                for pr in range(4):
                    S.op("pe", [qb, identb], [pt], lambda e: e.transpose(out=pt[:, pr, :], in_=qb[:, 2 * pr:2 * pr + 2, :].rearrange("p a d -> p (a d)"), identity=identb[:]))
                S.op("act", [pt], [qz], lambda e: e.activation(out=qz[0:64, 0, :, :], in_=pt[0:64, 0:4, :], func=AF.Copy))
                S.op("dve", [pt], [qz], lambda e: e.tensor_copy(out=qz[64:128, 1, :, :], in_=pt[64:128, 0:4, :]))
                if qi + 1 < len(qtiles):
                    emit_qprep(qtiles[qi + 1])
                if KS_ == "w2a":
                    return
                if l == 1:
                    blocks = [(m, None) for m in range(NT)]
                elif lat:
                    blocks = [(0, None), (1, None)]
                    if j > 0:
                        blocks.append((i - 1, 0))
                    blocks.append((i, None))
                    if j < 15:
                        blocks.append((i + 1, 1))
                else:
                    blocks = [(0, None), (1, None)]
                for g2 in range(2):
                    acc = psO.get()

                    def emit_scores(m, msk):
                        pss = scoreS.get()
                        for half in range(2):
                            S.op("pe", [kTd[g2], qz], [pss], lambda e: e.matmul(
                                pss[:, half * 256:(half + 1) * 256], lhsT=kTd[g2][:, m * 128:(m + 1) * 128],
                                rhs=qz[:, half, 2 * g2:2 * g2 + 2, :].rearrange("p a q -> p (a q)"),
                                start=(half == 0), stop=(half == 1 and msk is None)))
                        if msk is not None:
                            S.op("pe", [identb, wmb4], [pss], lambda e: e.matmul(pss[:, 0:512], lhsT=identb[:], rhs=wmb4[:, msk, :, :].rearrange("p a q -> p (a q)"), start=False, stop=True))
                        return pss

                    queue = [emit_scores(*blocks[0])]
                    if len(blocks) > 1:
                        queue.append(emit_scores(*blocks[1]))
                    for bi, (m, msk) in enumerate(blocks):
                        pss = queue.pop(0)
                        if bi + 2 < len(blocks):
                            queue.append(emit_scores(*blocks[bi + 2]))
                        PT = PTr.get()
                        S.op("act", [pss], [PT], lambda e: e.activation(out=PT[:], in_=pss[:], func=AF.Exp))
                        for hh in range(4):
                            S.op("pe", [PT, vb], [acc], lambda e: e.matmul(acc[:, hh * 128:hh * 128 + 65], lhsT=PT[:, hh * 128:(hh + 1) * 128], rhs=vb[:, m, g2, 0:65], start=(bi == 0 and hh == 0), stop=(bi == len(blocks) - 1)))
                    if KS_ == "w2c":
                        return
                    a3 = acc[:, :].rearrange("p (h c) -> p h c", h=4)
                    S.op("dve", [acc, esink], [den], lambda e: e.tensor_tensor(out=den[:, g2 * 4:(g2 + 1) * 4].rearrange("p (b a) -> p b a", b=2), in0=a3[:, :, 64].rearrange("p (b a) -> p b a", b=2),
                                                                            in1=esink[:, g2 * 4:(g2 + 1) * 4].rearrange("p (a b) -> p b a", a=2), op=ALU.add))
                    S.op("dve", [den], [den], lambda e: e.reciprocal(out=den[:, g2 * 4:(g2 + 1) * 4], in_=den[:, g2 * 4:(g2 + 1) * 4]))
                    S.op("dve", [acc, den], [mixb], lambda e: e.tensor_tensor(
                        out=mixb[:, g2 * 256:(g2 + 1) * 256].rearrange("p (a b d) -> p b a d", a=2, b=2), in0=a3[:, :, 0:64].rearrange("p (b a) d -> p b a d", b=2),
                        in1=den[:, g2 * 4:(g2 + 1) * 4].rearrange("p (b a) -> p b a", b=2).unsqueeze(3).to_broadcast([128, 2, 2, 64]), op=ALU.mult))
                if KS_ == "w2d":
                    return
                pt2 = psT.get()
                for c in range(4):
                    S.op("pe", [mixb, identb], [pt2], lambda e: e.transpose(out=pt2[:, c, :], in_=mixb[:, c * 128:(c + 1) * 128], identity=identb[:]))
                S.op("act", [pt2], [mixTb], lambda e: e.activation(out=mixTb[:], in_=pt2[:, 0:4, :], func=AF.Copy))
                if (KS_ == "w2" and i == 2) or (KS_ == "g1q" and l == 1 and i == 3):
                    return
                for cgi in range(2):
                    pso = psQ.get() if cgi == 0 else scoreS.get()
                    for k in range(8):
                        if l == 0:
                            lhs = other[:, k, i * 128:(i + 1) * 128] if k < 4 else mixTb[:, k - 4, :]
                        else:
                            lhs = mixTb[:, k, :] if k < 4 else other[:, k - 4, j * 128:(j + 1) * 128]
                        S.op("pe", [other, mixTb, wout], [pso], lambda e: e.matmul(pso[:, 0:512], lhsT=lhs, rhs=woutv[:, k, cgi * 512:(cgi + 1) * 512], start=(k == 0), stop=(k == 7)))
                    residual(i, pso, cgi, 0 if lat else 1)

        def ffn_phase(l, tiles):
            tiles = list(tiles)
            with ExitStack() as ph:
                hTg = [S.sb(f"hF{l}_0", [128, 8, 256], BF16, ph)] + [S.sb(f"hF{l}_{g}", [128, 8, 512], BF16, ph) for g in range(1, 5)]
                with ExitStack() as ph2:
                    norm_phase(l, 1, hTg, ph2, tiles)
                    mk_gate(l, 40, ph2)
                    S.barrier()
                segs = [gg for gg in GROUPS if (gg[0] > 0 or 0 in tiles)]
                lo = segs[0][1]
                ranges = ([(0, 256)] if lo == 0 else []) + [(256, 2304)]
                GS = 3
                ur = Ring([S.sb(f"fu{l}_{i}", [128, 2304], F32, ph) for i in range(2)])
                yr = Ring([S.sb(f"fy{l}_{i}", [128, 2304], F32, ph) for i in range(2)])
                actT = S.sb(f"actT{l}", [128, GS, 2304], BF16, ph)
                wur = Ring([S.sbd(f"wu{l}_{i}", [128, 8 * 256], BF16, ph) for i in range(3)])
                wdr = Ring([S.sbd(f"wd{l}_{i}", [128, GS * 1024], BF16, ph) for i in range(2)])
                has_ctx = (lo == 0)
                wdraw = Ring([S.sb(f"wdraw{l}_{i}", [128, GS * 1024], BF16, ph) for i in range(1)]) if has_ctx else None
                upsrc = D["ffn_up"][l].rearrange("(k p) (g c n) -> p k g c n", p=128, g=2, c=22)
                wu_loaded = {}
                wd_loaded = {}

                def load_wu(cp):
                    if cp >= 22 or cp in wu_loaded:
                        return
                    wub = wur.get()
                    wu = wub[:, :].rearrange("p (k g n) -> p k g n", k=8, g=2)
                    for g3 in range(2):
                        S.dma("pool", wub.sem, wu[:, :, g3, :], upsrc[:, :, g3, cp, :], [], [wub])
                    wu_loaded[cp] = (wub, wu)

                def load_wd(c0):
                    if c0 >= 22 or c0 in wd_loaded:
                        return
                    ncg = min(GS, 22 - c0)
                    wdb = wdr.get()
                    wd = wdb[:, 0:ncg * 1024].rearrange("p (c n) -> p c n", c=ncg)
                    S.dma("pool", wdb.sem, wd, D["ffn_down"][l, c0 * 128:(c0 + ncg) * 128, :].rearrange("(c p) n -> p c n", p=128), [], [wdb])
                    wd_loaded[c0] = (wdb, wd)

                def scale_wd(c0):
                    ncg = min(GS, 22 - c0)
                    wdb, wd = wd_loaded[c0]
                    raw = None
                    if has_ctx:
                        rb = wdraw.get()
                        raw = rb[:, 0:ncg * 1024].rearrange("p (c n) -> p c n", c=ncg)
                        S.op("pool", [wdb], [rb], lambda e: e.tensor_copy(out=raw, in_=wd))
                        wd_loaded[c0] = (wdb, wd, rb, raw)
                    S.op("pool", [wdb, gbc], [wdb], lambda e: e.tensor_tensor(out=wd, in0=wd, in1=gbc[:, 0, :].unsqueeze(1).to_broadcast([128, ncg, 1024]), op=ALU.mult))
                    if not has_ctx:
                        wd_loaded[c0] = (wdb, wd, None, None)

                load_wu(0)
                load_wu(1)
                load_wd(0)

                upS = Ring(psA.bufs + psS.bufs)

                def emit_up1(cp):
                    load_wu(cp + 2)
                    wub, wu = wu_loaded[cp]
                    ys = []
                    for gv in range(2):
                        ch = gv * 22 + cp
                        u = ur.get()
                        y = yr.get()
                        w0 = cw[:, l, 0, ch:ch + 1]
                        w1 = cw[:, l, 1, ch:ch + 1]
                        w2 = cw[:, l, 2, ch:ch + 1]
                        for (g, t0, n) in segs:
                            ps = upS.get()
                            for k in range(8):
                                S.op("pe", [wub, hTg[g]], [ps], lambda e: e.matmul(ps[:, 0:n], lhsT=wu[:, k, gv, :], rhs=hTg[g][:, k, 0:n], start=(k == 0), stop=(k == 7)))
                            S.op("act", [ps], [u], lambda e: e.activation(out=u[:, t0:t0 + n], in_=ps[:, 0:n], func=AF.Copy))
                        S.op("act", [u, cw, cb], [y], lambda e: e.activation(out=y[:, lo:2304], in_=u[:, lo:2304], func=AF.Identity, scale=w1, bias=cb[:, l, ch:ch + 1]))
                        for (a, b_) in ranges:
                            S.op("dve", [u, cw, y], [y], lambda e: e.scalar_tensor_tensor(out=y[:, a + 1:b_], in0=u[:, a:b_ - 1], scalar=w0, in1=y[:, a + 1:b_], op0=ALU.mult, op1=ALU.add))
                            S.op("dve", [u, cw, y], [y], lambda e: e.scalar_tensor_tensor(out=y[:, a:b_ - 1], in0=u[:, a + 1:b_], scalar=w2, in1=y[:, a:b_ - 1], op0=ALU.mult, op1=ALU.add))
                        ys.append(y)
                    S.op("act", [ys[0]], [ys[0]], lambda e: e.activation(out=ys[0][:, lo:2304], in_=ys[0][:, lo:2304], func=AF.Silu))
                    return ys

                def emit_up2(ys, ci):
                    S.op("dve", [ys[0], ys[1]], [actT], lambda e: e.tensor_tensor(out=actT[:, ci, lo:2304], in0=ys[0][:, lo:2304], in1=ys[1][:, lo:2304], op=ALU.mult))

                def emit_down(c0):
                    ncg = min(GS, 22 - c0)
                    scale_wd(c0)
                    wdb, wd, rb, raw = wd_loaded[c0]
                    for i in tiles:
                        for cgi in range(2):
                            ps = psO.get()
                            if i >= 2:
                                for ci in range(ncg):
                                    S.op("pe", [actT, wdb], [ps], lambda e: e.matmul(ps[:, 0:512], lhsT=actT[:, ci, i * 128:(i + 1) * 128], rhs=wd[:, ci, cgi * 512:(cgi + 1) * 512], start=(ci == 0), stop=(ci == ncg - 1)))
                                S.op("dve", [ps, xs[i]], [xs[i]], lambda e: e.tensor_tensor(out=xs[i][:, cgi * 512:(cgi + 1) * 512], in0=ps[:], in1=xs[i][:, cgi * 512:(cgi + 1) * 512], op=ALU.add))
                            else:
                                for ci in range(ncg):
                                    S.op("pe", [actT, rb], [ps], lambda e: e.matmul(ps[:, 0:512], lhsT=actT[:, ci, i * 128:(i + 1) * 128], rhs=raw[:, ci, cgi * 512:(cgi + 1) * 512], start=(ci == 0), stop=(ci == ncg - 1)))
                                residual(i, ps, cgi, 1)

                pending = None
                for c0 in range(0, 22, GS):
                    ncg = min(GS, 22 - c0)
                    ys0 = emit_up1(c0)
                    if pending is not None:
                        emit_down(pending)
                    load_wd(c0 + GS)
                    emit_up2(ys0, 0)
                    for ci in range(1, ncg):
                        emit_up2(emit_up1(c0 + ci), ci)
                    pending = c0
                emit_down(pending)
                S.barrier()

        def na_attn(hTg, mixTd, wring, ph):
            nag = S.sbd("nag", [128, 128], F32, ph)
            S.dma("sp", nag.sem, nag[:], D["na_g_bc"], [], [nag])
            gq = S.sb("nagq", [128, 64], F32, ph)
            S.op("dve", [nag], [gq], lambda e: e.tensor_scalar(out=gq[:], in0=nag[:, 0:64], scalar1=0.125, scalar2=None, op0=ALU.mult))
            kTn = S.sb("kTn", [128, NT * 128], BF16, ph)
            vn = S.sb("vn", [128, NT, 2, 66], BF16, ph)
            qTn = S.sb("qTn", [128, 2048], BF16, ph)
            S.op("dve", [], [vn], lambda e: e.memset(vn[:, :, :, 64:65], 1.0))
            TB = 9
            raw = S.sb("nraw", [128, TB, 128], F32, ph)
            sq = S.sb("nsq", [128, TB, 128], F32, ph)
            ssb = S.sb("nss", [128, TB * 2], F32, ph)
            nrm = S.sb("nnrm", [128, TB, 128], BF16, ph)
            biasr = Ring([S.sbd(f"nbias{i}", [128, 25, 128], F32, ph) for i in range(1)])
            stmp = Ring([S.sb(f"nstmp{i}", [128, 5, 128], F32, ph) for i in range(2)])
            PTr = Ring([S.sb(f"PTn{i}", [128, 7, 128], BF16, ph) for i in range(3)])
            rdn = Ring([S.sb(f"nrd{i}", [128, 1], F32, ph) for i in range(3)])
            mixd = S.sb("mixd", [128, 16, 2, 64], BF16, ph)
            naS = Ring(psS.bufs + psA.bufs)
            wsrc = D["odd_w"][:, 768:2304].rearrange("(k p) (g h n) -> p k g h n", p=128, g=3, h=4)
            wl = {}

            def load_w(pr):
                if pr >= 4 or pr in wl:
                    return
                wb = wring.get()
                wq = wb[:, 0:8 * 384].rearrange("p (k g n) -> p k g n", k=8, g=3)
                for g3 in range(3):
                    S.dma("pool", wb.sem, wq[:, :, g3, :], wsrc[:, :, g3, pr, :], [], [wb])
                wl[pr] = (wb, wq)

            load_w(0)
            for pr in range(4):
                wb, wq = wl[pr]
                load_w(pr + 1)
                for t0 in range(0, NT, TB):
                    tl = list(range(t0, min(NT, t0 + TB)))
                    for i in tl:
                        g, off = tok_group(i)
                        ps = psA.get()
                        for k in range(8):
                            S.op("pe", [wb, hTg[g]], [ps], lambda e: e.matmul(ps[:, 0:256], lhsT=hTg[g][:, k, off:off + 128], rhs=wb[:, k * 384 + 128:k * 384 + 384], start=(k == 0), stop=(k == 7)))
                        S.op("act", [ps], [vn], lambda e: e.activation(out=vn[:, i, :, 0:64], in_=ps[:, 128:256].rearrange("p (g d) -> p g d", g=2), func=AF.Copy))
                        S.op("dve", [ps], [raw], lambda e: e.tensor_copy(out=raw[:, i - t0, :], in_=ps[:, 0:128]))
                    prep_batch(raw, sq, ssb, len(tl), 2, nag[:, 64:128], nrm, nrm[:, 0:len(tl), :])
                    for i in tl:
                        pt = psT.get()
                        S.op("pe", [nrm, identb], [pt], lambda e: e.transpose(out=pt[:, 0, :], in_=nrm[:, i - t0, :], identity=identb[:]))
                        S.op("act", [pt], [kTn], lambda e: e.activation(out=kTn[:, i * 128:(i + 1) * 128], in_=pt[:, 0, :], func=AF.Copy))
                for t0 in range(2, NT, 8):
                    tl = list(range(t0, t0 + 8))
                    for i in tl:
                        g, off = tok_group(i)
                        ps2 = psA.get()
                        for k in range(8):
                            S.op("pe", [wb, hTg[g]], [ps2], lambda e: e.matmul(ps2[:, 0:128], lhsT=hTg[g][:, k, off:off + 128], rhs=wq[:, k, 0, :], start=(k == 0), stop=(k == 7)))
                        S.op("dve", [ps2], [raw], lambda e: e.tensor_copy(out=raw[:, i - t0, :], in_=ps2[:, 0:128]))
                    prep_batch(raw, sq, ssb, 8, 2, gq[:], nrm, nrm[:, 0:8, :])
                    for i in tl:
                        j = i - 2
                        pt2 = psT.get()
                        S.op("pe", [nrm, identb], [pt2], lambda e: e.transpose(out=pt2[:, 0, :], in_=nrm[:, i - t0, :], identity=identb[:]))
                        S.op("dve", [pt2], [qTn], lambda e: e.tensor_copy(out=qTn[:, j * 128:(j + 1) * 128], in_=pt2[:, 0, :]))
                for hh in range(2):
                    head = 2 * pr + hh
                    bt = biasr.get()
                    S.dma("sp", bt.sem, bt[:], D["na_bias"][head], [], [bt])
                    prs = slice(hh * 64, (hh + 1) * 64)

                    def blocks_of(j):
                        ci = 0 if j == 0 else 1 if j == 1 else 3 if j == 14 else 4 if j == 15 else 2
                        mlist = list(range(j - 2, j + 3)) if ci == 2 else na_blocks[j]
                        return ci, len(mlist), [0, 1] + [m + 2 for m in mlist]

                    def emit_scores(j):
                        ci, nb, keyt = blocks_of(j)
                        pA = naS.get()
                        pB = naS.get()
                        for bi, kt in enumerate(keyt):
                            pp, off2 = (pA, bi) if bi < 4 else (pB, bi - 4)
                            S.op("pe", [kTn, qTn], [pp], lambda e: e.matmul(pp[:, off2 * 128:(off2 + 1) * 128], lhsT=kTn[prs, kt * 128:(kt + 1) * 128], rhs=qTn[prs, j * 128:(j + 1) * 128], start=True, stop=True))
                        return pA, pB

                    def emit_norm(acc_, j_):
                        rd = rdn.get()
                        S.op("dve", [acc_], [rd], lambda e: e.reciprocal(out=rd[:], in_=acc_[:, 64:65]))
                        S.op("act", [acc_, rd], [mixd], lambda e: e.activation(out=mixd[:, j_, hh, :], in_=acc_[:, 0:64], func=AF.Copy, scale=rd[:, 0:1]))

                    pend_norm = None
                    nxt = emit_scores(0)
                    for j in range(16):
                        ci, nb, keyt = blocks_of(j)
                        pA, pB = nxt
                        if j + 1 < 16:
                            nxt = emit_scores(j + 1)
                        stp = stmp.get()
                        S.op("dve", [pA, bt], [stp], lambda e: e.tensor_tensor(out=stp[:, 0:2, :], in0=pA[:, 256:512].rearrange("p (b q) -> p b q", b=2), in1=bt[:, ci * 5:ci * 5 + 2, :], op=ALU.add))
                        S.op("dve", [pB, bt], [stp], lambda e: e.tensor_tensor(out=stp[:, 2:nb, :], in0=pB[:, 0:(nb - 2) * 128].rearrange("p (b q) -> p b q", b=nb - 2), in1=bt[:, ci * 5 + 2:ci * 5 + nb, :], op=ALU.add))
                        PT = PTr.get()
                        S.op("act", [pA], [PT], lambda e: e.activation(out=PT[:, 0:2, :], in_=pA[:, 0:256].rearrange("p (b q) -> p b q", b=2), func=AF.Exp))
                        S.op("act", [stp], [PT], lambda e: e.activation(out=PT[:, 2:2 + nb, :], in_=stp[:, 0:nb, :], func=AF.Exp))
                        acc = psO.get()
                        for bi, kt in enumerate(keyt):
                            S.op("pe", [PT, vn], [acc], lambda e: e.matmul(acc[:, 0:65], lhsT=PT[:, bi, :], rhs=vn[:, kt, hh, 0:65], start=(bi == 0), stop=(bi == len(keyt) - 1)))
                        if pend_norm is not None:
                            emit_norm(*pend_norm)
                        pend_norm = (acc, j)
                    emit_norm(*pend_norm)
                    pend_norm = None
                for j in range(16):
                    pt = psT.get()
                    S.op("pe", [mixd, identb], [pt], lambda e: e.transpose(out=pt[:, 0, :], in_=mixd[:, j, :, :].rearrange("p a d -> p (a d)"), identity=identb[:]))
                    S.op("dve", [pt], [mixTd], lambda e: e.tensor_copy(out=mixTd[:, pr, j * 128:(j + 1) * 128], in_=pt[:, 0, :]))

        def mixer1():
            with ExitStack() as ph:
                hTg = [S.sb("hT1_0", [128, 8, 256], BF16, ph)] + [S.sb(f"hT1_{g}", [128, 8, 512], BF16, ph) for g in range(1, 5)]
                with ExitStack() as ph2:
                    norm_phase(1, 0, hTg, ph2)
                    mk_gate(1, 16, ph2)
                    S.barrier()
                mixTd = S.sb("mixTd", [128, 4, 2048], BF16, ph)
                with ExitStack() as ph2:
                    wring = Ring([S.sbd(f"w1_{i}", [128, 8 * 384], BF16, ph2) for i in range(2)])
                    na_attn(hTg, mixTd, wring, ph2)
                    S.barrier()
                import os
                if os.environ.get("KSUB") == "nogqa1":
                    return
                with ExitStack() as ph2:
                    gqa_attn(1, hTg, mixTd, None, ph2)
                    S.barrier()

        mod_phase(0)
        if stage != "mod":
            mixer0()
        if stage not in ("l0mix", "mod", "norm", "mlstm"):
            ffn_phase(0, range(NT))
        if stage not in ("l0mix", "l0", "mod", "norm", "mlstm"):
            mod_phase(1)
            mixer1()
            if stage != "l1mix":
                ffn_phase(1, range(2, NT))
        osem = S.newsem("d_out")
        for i in range(2, NT):
            S.dma("sp", osem, out[(i - 2) * 128:(i - 1) * 128, :], xs[i][:], [xs[i]], [])
        if dbg:
            for i in range(2):
                S.dma("sp", osem, octx[i * 128:(i + 1) * 128, :], xs[i][:], [xs[i]], [])
        S._need("sp", osem, S.cnt[osem])
        S.barrier()
        print(f"[kernel] instructions={S.ninst} waits={S.nwait} sems={len(S.sems)}", flush=True)
    return nc


_CACHE = {}


def kernel(**inputs):
    shared, percore = host_prepare({k: np.asarray(v) for k, v in inputs.items()})
    if "nc" not in _CACHE:
        _CACHE["nc"] = build_program("full")
    nc = _CACHE["nc"]
    in_maps = []
    for b in range(8):
        m = dict(shared)
        m.update(percore[b])
        in_maps.append(m)
    res = run_bass_kernel_spmd(nc, in_maps, core_ids=list(range(8)))
    return np.stack([np.asarray(r["out"], np.float32) for r in res.results], axis=0)
```
